# Optimizing a Trainium2 kernel written in Bass

```python
import math
import functools
import jax
import jax.numpy as jnp
from jax import lax
import numpy as np

D_MODEL = 1024
BATCH = 8
SEQ = 8192
DEPTH = 4

GRID_W = 64
CTX_LEN = 256
N_MIXERS = 4
NORM_EPS = 1e-6
ROPE_THETA = 10000.0

LRU_WIDTH = 2 * D_MODEL
LRU_BLOCKS = 16
LRU_BLOCK = LRU_WIDTH // LRU_BLOCKS
LRU_C = 8.0
LRU_CONV = 4
LRU_PAD = (2, 1)

RET_HEAD_QK = 256
RET_HEADS = D_MODEL // RET_HEAD_QK
RET_HEAD_V = 2 * RET_HEAD_QK
RET_CHUNK = 128

HY_WIDTH = D_MODEL
HY_ORDER = 2
HY_SHORT = 3
HY_EMB = 33
HY_HIDDEN = 64
HY_FAST_DECAY = 0.3
HY_SLOW_DECAY = 1.5
HY_TARGET = 1e-2

ATT_HEADS = 8
ATT_KV_HEADS = 2
ATT_HEAD_DIM = D_MODEL // ATT_HEADS
ATT_BLOCK = 128

kernel_name = 'hybrid_lru_ret_hyena_gqa_prefix_dit'

F32 = jnp.float32


def _rms(x):
    xf = x.astype(F32)
    return xf * lax.rsqrt(jnp.mean(xf * xf, axis=-1, keepdims=True) + NORM_EPS)


def rmsnorm(x, g):
    return (_rms(x) * g.astype(F32)).astype(x.dtype)


def dwconv(x, w, b, pad):
    y = lax.conv_general_dilated(x, w[:, None, :].astype(x.dtype), window_strides=(1,), padding=[pad],
                                 dimension_numbers=('NWC', 'WIO', 'NWC'), feature_group_count=x.shape[-1])
    return y + b.astype(x.dtype)


def rope_angles(pos, dim):
    half = dim // 2
    inv = ROPE_THETA ** (-jnp.arange(half, dtype=F32) / half)
    return pos[:, None] * inv[None, :]


def axial_angles(L, dim):
    rows = L // GRID_W
    row = jnp.repeat(jnp.arange(rows, dtype=F32), GRID_W)
    col = jnp.tile(jnp.arange(GRID_W, dtype=F32), rows)
    return jnp.concatenate([rope_angles(row, dim // 2), rope_angles(col, dim // 2)], axis=-1)


def apply_rotary(x, ang):
    xf = x.astype(F32)
    half = xf.shape[-1] // 2
    cos = jnp.cos(ang)[None, :, None, :]
    sin = jnp.sin(ang)[None, :, None, :]
    x1, x2 = xf[..., :half], xf[..., half:]
    return jnp.concatenate([x1 * cos - x2 * sin, x1 * sin + x2 * cos], axis=-1).astype(x.dtype)


def modulation(cond, mod_w, mod_b):
    m = jax.nn.silu(cond) @ mod_w + mod_b
    return jnp.split(m, 3, axis=-1)


def _lin_combine(e1, e2):
    a1, b1 = e1
    a2, b2 = e2
    return a1 * a2, a2 * b1 + b2


def rglru_scan(xs, w_r, b_r, w_i, b_i, lam, h0, reverse):
    bsz, L, W = xs.shape
    xb = xs.reshape(bsz, L, LRU_BLOCKS, LRU_BLOCK)
    r = jax.nn.sigmoid(jnp.einsum('blnj,njk->blnk', xb, w_r.astype(F32)).reshape(bsz, L, W) + b_r.astype(F32))
    i = jax.nn.sigmoid(jnp.einsum('blnj,njk->blnk', xb, w_i.astype(F32)).reshape(bsz, L, W) + b_i.astype(F32))
    log_a = -LRU_C * r * jax.nn.softplus(-lam.astype(F32))
    a = jnp.exp(log_a)
    b = jnp.sqrt(-jnp.expm1(2.0 * log_a)) * (i * xs)
    a_cum, b_cum = lax.associative_scan(_lin_combine, (a, b), axis=1, reverse=reverse)
    return a_cum * h0[:, None, :] + b_cum


def mixer_lru(h, hc, need_ctx_out, w_in, conv_w, conv_b, w_r, b_r, w_i, b_i, lam, w_out):
    W = LRU_WIDTH
    u = h @ w_in
    xl = dwconv(u[..., :W], conv_w, conv_b, LRU_PAD).astype(F32)
    uc = hc @ (w_in if need_ctx_out else w_in[:, :W])
    xc = dwconv(uc[..., :W], conv_w, conv_b, LRU_PAD).astype(F32)
    zero = jnp.zeros((h.shape[0], W), F32)
    lat, con = [], []
    for d, rev in enumerate((False, True)):
        hcs = rglru_scan(xc, w_r[d], b_r[d], w_i[d], b_i[d], lam[d], zero, rev)
        h0 = hcs[:, 0] if rev else hcs[:, -1]
        lat.append(rglru_scan(xl, w_r[d], b_r[d], w_i[d], b_i[d], lam[d], h0, rev))
        con.append(hcs)
    y = ((lat[0] + lat[1]).astype(h.dtype) * jax.nn.silu(u[..., W:])) @ w_out
    yc = None
    if need_ctx_out:
        yc = ((con[0] + con[1]).astype(hc.dtype) * jax.nn.silu(uc[..., W:])) @ w_out
    return y, yc


def retention_context_states(k, v, logg_f, logg_b):
    Lc = k.shape[1]
    m = jnp.arange(Lc, dtype=F32)[:, None]
    wf = jnp.exp((Lc - 1 - m) * logg_f[None, :])
    wb = jnp.exp(m * logg_b[None, :])
    sf = jnp.einsum('bmhd,bmhe,mh->bhde', k, v, wf)
    sb = jnp.einsum('bmhd,bmhe,mh->bhde', k, v, wb)
    return sf, sb


def retention_bidir(q, k, v, logg_f, logg_b, s0_f, s0_b):
    bsz, L, H, dk = q.shape
    dv = v.shape[-1]
    C = RET_CHUNK
    n = L // C
    qc = q.reshape(bsz, n, C, H, dk)
    kc = k.reshape(bsz, n, C, H, dk)
    vc = v.reshape(bsz, n, C, H, dv)
    pos = jnp.arange(C, dtype=F32)
    diff = pos[:, None] - pos[None, :]
    dist = jnp.abs(diff)[None]
    mask = jnp.where((diff >= 0)[None], jnp.exp(dist * logg_f[:, None, None]),
                     jnp.exp(dist * logg_b[:, None, None]))
    s = jnp.einsum('bnihd,bnjhd->bnhij', qc, kc) * mask
    o = jnp.einsum('bnhij,bnjhe->bnihe', s, vc)
    qf_dec = jnp.exp((pos[:, None] + 1.0) * logg_f[None, :])
    kf_dec = jnp.exp((C - 1.0 - pos[:, None]) * logg_f[None, :])
    qb_dec = jnp.exp((C - pos[:, None]) * logg_b[None, :])
    kb_dec = jnp.exp(pos[:, None] * logg_b[None, :])
    blk_f = jnp.exp(C * logg_f)[None, :, None, None]
    blk_b = jnp.exp(C * logg_b)[None, :, None, None]
    xs = (jnp.moveaxis(qc, 1, 0), jnp.moveaxis(kc, 1, 0), jnp.moveaxis(vc, 1, 0))

    def make_step(q_dec, k_dec, blk):
        def step(state, qkv):
            qq, kk, vv = qkv
            out = jnp.einsum('bihd,bhde->bihe', qq, state) * q_dec[None, :, :, None]
            state = state * blk + jnp.einsum('bjhd,bjhe,jh->bhde', kk, vv, k_dec)
            return state, out
        return step

    _, of = lax.scan(make_step(qf_dec, kf_dec, blk_f), s0_f, xs)
    _, ob = lax.scan(make_step(qb_dec, kb_dec, blk_b), s0_b, xs, reverse=True)
    o = o + jnp.moveaxis(of + ob, 0, 1)
    return o.reshape(bsz, L, H, dv)


def mixer_ret(h, hc, need_ctx_out, ang, w_in, decay_logit, w_out):
    H, dk, dv = RET_HEADS, RET_HEAD_QK, RET_HEAD_V
    nk, nv = H * dk, H * dv
    bsz, L, _ = h.shape
    Lc = hc.shape[1]
    logg = -jax.nn.softplus(-decay_logit.astype(F32))
    u = h @ w_in
    q = apply_rotary(u[..., :nk].reshape(bsz, L, H, dk), ang).astype(F32)
    k = apply_rotary(u[..., nk:2 * nk].reshape(bsz, L, H, dk), ang).astype(F32) * dk ** -0.5
    v = u[..., 2 * nk:2 * nk + nv].reshape(bsz, L, H, dv).astype(F32)
    if need_ctx_out:
        uc = hc @ w_in
        kvc = uc[..., nk:2 * nk + nv]
    else:
        kvc = hc @ w_in[:, nk:2 * nk + nv]
    kc = kvc[..., :nk].reshape(bsz, Lc, H, dk).astype(F32) * dk ** -0.5
    vc = kvc[..., nk:].reshape(bsz, Lc, H, dv).astype(F32)
    sf, sb = retention_context_states(kc, vc, logg[0], logg[1])
    o = retention_bidir(q, k, v, logg[0], logg[1], sf, sb)
    y = (jax.nn.silu(u[..., 2 * nk + nv:]) * _rms(o).reshape(bsz, L, nv).astype(h.dtype)) @ w_out
    yc = None
    if need_ctx_out:
        qc = uc[..., :nk].reshape(bsz, Lc, H, dk).astype(F32)
        zero = jnp.zeros((bsz, H, dk, dv), F32)
        oc = retention_bidir(qc, kc, vc, logg[0], logg[1], zero, zero)
        yc = (jax.nn.silu(uc[..., 2 * nk + nv:]) * _rms(oc).reshape(bsz, Lc, nv).astype(hc.dtype)) @ w_out
    return y, yc


def hyena_kernel_fft(L, fw1, fb1, fw2, fb2, fw3, freq):
    W = HY_WIDTH
    bands = (HY_EMB - 1) // 2
    t = jnp.linspace(0.0, 1.0, L, dtype=F32)[:, None]
    w = 2.0 * math.pi * jnp.arange(L, dtype=F32)[:, None] / L
    fr = jnp.linspace(1e-4, bands - 1, bands, dtype=F32)[None, :]
    z = jnp.concatenate([t, jnp.cos(fr * w), -jnp.sin(fr * w)], axis=-1)
    fq = freq.astype(F32)
    hid = jnp.sin(fq * (z @ fw1.astype(F32) + fb1.astype(F32)))
    hid = jnp.sin(fq * (hid @ fw2.astype(F32) + fb2.astype(F32)))
    filt = (hid @ fw3.astype(F32)).reshape(L, HY_ORDER, 2, W)
    deltas = jnp.abs(jnp.linspace(math.log(HY_TARGET) / HY_SLOW_DECAY, math.log(HY_TARGET) / HY_FAST_DECAY, W, dtype=F32))
    filt = filt * jnp.exp(-t * deltas[None, :])[:, None, None, :]
    kern = jnp.concatenate([filt[:, :, 0], jnp.zeros((1, HY_ORDER, W), F32), filt[:0:-1, :, 1]], axis=0)
    kern = kern / jnp.sum(jnp.abs(kern), axis=0, keepdims=True)
    return jnp.fft.rfft(kern, axis=0)


def long_conv(u, kf, bias):
    L = u.shape[1]
    y = jnp.fft.irfft(jnp.fft.rfft(u, n=2 * L, axis=1) * kf[None], n=2 * L, axis=1)[:, :L]
    return y + u * bias


def mixer_hyena(h, hc, need_ctx_out, w_in, conv_w, conv_b, fw1, fb1, fw2, fb2, fw3, freq, skip, w_out):
    W = HY_WIDTH
    pad = (HY_SHORT // 2, HY_SHORT // 2)

    def branch(s):
        u = s @ w_in
        z = dwconv(u[..., :3 * W], conv_w, conv_b, pad).astype(F32)
        v, x1, x2 = z[..., :W], z[..., W:2 * W], z[..., 2 * W:]
        kf = hyena_kernel_fft(s.shape[1], fw1, fb1, fw2, fb2, fw3, freq)
        sk = skip.astype(F32)
        y = x1 * long_conv(v, kf[:, 0], sk[0])
        y = x2 * long_conv(y, kf[:, 1], sk[1])
        return (y.astype(s.dtype) * jax.nn.silu(u[..., 3 * W:])) @ w_out

    return branch(h), (branch(hc) if need_ctx_out else None)


def attend(q, k, v):
    s = jnp.einsum('bqgrd,bkgd->bgrqk', q, k).astype(F32) * (q.shape[-1] ** -0.5)
    p = jax.nn.softmax(s, axis=-1).astype(v.dtype)
    return jnp.einsum('bgrqk,bkgd->bqgrd', p, v)


def mixer_attn(h, hc, need_ctx_out, ang, w_in, q_norm_g, k_norm_g, w_out):
    Hq, G, d = ATT_HEADS, ATT_KV_HEADS, ATT_HEAD_DIM
    R = Hq // G
    nq, nk = Hq * d, G * d
    bsz, L, _ = h.shape
    Lc = hc.shape[1]
    u = h @ w_in
    q = apply_rotary(rmsnorm(u[..., :nq].reshape(bsz, L, Hq, d), q_norm_g), ang).reshape(bsz, L, G, R, d)
    k = apply_rotary(rmsnorm(u[..., nq:nq + nk].reshape(bsz, L, G, d), k_norm_g), ang)
    v = u[..., nq + nk:nq + 2 * nk].reshape(bsz, L, G, d)
    uc = hc @ (w_in if need_ctx_out else w_in[:, nq:nq + 2 * nk])
    off = nq if need_ctx_out else 0
    kc = rmsnorm(uc[..., off:off + nk].reshape(bsz, Lc, G, d), k_norm_g)
    vc = uc[..., off + nk:off + 2 * nk].reshape(bsz, Lc, G, d)
    keys = jnp.concatenate([k, kc], axis=1)
    vals = jnp.concatenate([v, vc], axis=1)
    qb = jnp.moveaxis(q.reshape(bsz, L // ATT_BLOCK, ATT_BLOCK, G, R, d), 1, 0)
    o = lax.map(lambda blk: attend(blk, keys, vals), qb)
    o = jnp.moveaxis(o, 0, 1).reshape(bsz, L, nq)
    y = (jax.nn.silu(u[..., nq + 2 * nk:]) * o) @ w_out
    yc = None
    if need_ctx_out:
        qc = rmsnorm(uc[..., :nq].reshape(bsz, Lc, Hq, d), q_norm_g).reshape(bsz, Lc, G, R, d)
        oc = attend(qc, kc, vc).reshape(bsz, Lc, nq)
        yc = (jax.nn.silu(uc[..., nq + 2 * nk:]) * oc) @ w_out
    return y, yc


def trunk_layer(x, ctx, c, c_ctx, mod_w, mod_b, norm_g, mixer, need_ctx_out):
    sh, sc, gt = modulation(c[:, None, :], mod_w, mod_b)
    shc, scc, gtc = modulation(c_ctx, mod_w, mod_b)
    h = rmsnorm(x, norm_g) * (1.0 + sc) + sh
    hc = rmsnorm(ctx, norm_g) * (1.0 + scc) + shc
    y, yc = mixer(h, hc, need_ctx_out)
    x = x + gt * y
    if need_ctx_out:
        ctx = ctx + gtc * yc
    return x, ctx


def setup_inputs(seed: int = 0) -> dict:
    key = jax.random.key(seed)
    ks = iter(jax.random.split(key, 64))
    D = D_MODEL

    def nrm(shape, scale):
        return scale * jax.random.normal(next(ks), shape, F32)

    def gain(n):
        return 1.0 + nrm((n,), 0.05)

    inp = {}
    inp['x'] = nrm((BATCH, SEQ, D), 1.0)
    inp['c'] = nrm((BATCH, D), 1.0)
    inp['ctx'] = nrm((BATCH, CTX_LEN, D), 1.0)
    inp['c_ctx'] = nrm((D,), 1.0)

    def adaln(p):
        inp[p + '_mod_w'] = nrm((D, 3 * D), 0.5 * D ** -0.5)
        inp[p + '_mod_b'] = nrm((3 * D,), 0.01)
        inp[p + '_norm_g'] = gain(D)

    adaln('lru')
    W = LRU_WIDTH
    inp['lru_w_in'] = nrm((D, 2 * W), D ** -0.5)
    inp['lru_conv_w'] = nrm((LRU_CONV, W), LRU_CONV ** -0.5)
    inp['lru_conv_b'] = nrm((W,), 0.01)
    inp['lru_w_r'] = nrm((2, LRU_BLOCKS, LRU_BLOCK, LRU_BLOCK), LRU_BLOCK ** -0.5)
    inp['lru_b_r'] = nrm((2, W), 0.01)
    inp['lru_w_i'] = nrm((2, LRU_BLOCKS, LRU_BLOCK, LRU_BLOCK), LRU_BLOCK ** -0.5)
    inp['lru_b_i'] = nrm((2, W), 0.01)
    a_c = jax.random.uniform(next(ks), (2, W), F32, 0.9, 0.999)
    s = a_c ** (1.0 / LRU_C)
    inp['lru_lambda'] = jnp.log(s) - jnp.log1p(-s)
    inp['lru_w_out'] = nrm((W, D), W ** -0.5)

    adaln('ret')
    nk, nv = RET_HEADS * RET_HEAD_QK, RET_HEADS * RET_HEAD_V
    inp['ret_w_in'] = nrm((D, 2 * nk + 2 * nv), D ** -0.5)
    hh = jnp.arange(RET_HEADS, dtype=F32)
    gam = 1.0 - 2.0 ** (-5.0 - hh)
    inp['ret_decay_logit'] = (jnp.log(gam) - jnp.log1p(-gam))[None, :] + nrm((2, RET_HEADS), 0.01)
    inp['ret_w_out'] = nrm((nv, D), nv ** -0.5)

    adaln('hy')
    Wh = HY_WIDTH
    inp['hy_w_in'] = nrm((D, 4 * Wh), D ** -0.5)
    inp['hy_conv_w'] = nrm((HY_SHORT, 3 * Wh), HY_SHORT ** -0.5)
    inp['hy_conv_b'] = nrm((3 * Wh,), 0.01)
    inp['hy_fw1'] = nrm((HY_EMB, HY_HIDDEN), HY_EMB ** -0.5)
    inp['hy_fb1'] = nrm((HY_HIDDEN,), 0.02)
    inp['hy_fw2'] = nrm((HY_HIDDEN, HY_HIDDEN), HY_HIDDEN ** -0.5)
    inp['hy_fb2'] = nrm((HY_HIDDEN,), 0.02)
    inp['hy_fw3'] = nrm((HY_HIDDEN, HY_ORDER * 2 * Wh), HY_HIDDEN ** -0.5)
    inp['hy_freq'] = gain(HY_HIDDEN)
    inp['hy_skip'] = nrm((HY_ORDER, Wh), 0.5)
    inp['hy_w_out'] = nrm((Wh, D), Wh ** -0.5)

    adaln('att')
    nq, nkv = ATT_HEADS * ATT_HEAD_DIM, ATT_KV_HEADS * ATT_HEAD_DIM
    inp['att_w_in'] = nrm((D, 2 * nq + 2 * nkv), D ** -0.5)
    inp['att_q_norm_g'] = gain(ATT_HEAD_DIM)
    inp['att_k_norm_g'] = gain(ATT_HEAD_DIM)
    inp['att_w_out'] = nrm((nq, D), nq ** -0.5)

    inp['final_norm_g'] = gain(D)
    return inp


def reference(x, c, ctx, c_ctx,
              lru_mod_w, lru_mod_b, lru_norm_g, lru_w_in, lru_conv_w, lru_conv_b, lru_w_r, lru_b_r,
              lru_w_i, lru_b_i, lru_lambda, lru_w_out,
              ret_mod_w, ret_mod_b, ret_norm_g, ret_w_in, ret_decay_logit, ret_w_out,
              hy_mod_w, hy_mod_b, hy_norm_g, hy_w_in, hy_conv_w, hy_conv_b, hy_fw1, hy_fb1, hy_fw2, hy_fb2,
              hy_fw3, hy_freq, hy_skip, hy_w_out,
              att_mod_w, att_mod_b, att_norm_g, att_w_in, att_q_norm_g, att_k_norm_g, att_w_out,
              final_norm_g):
    L = x.shape[1]
    ret_ang = rope_angles(jnp.arange(L, dtype=F32), RET_HEAD_QK)
    att_ang = axial_angles(L, ATT_HEAD_DIM)
    layers = [
        (lru_mod_w, lru_mod_b, lru_norm_g,
         functools.partial(mixer_lru, w_in=lru_w_in, conv_w=lru_conv_w, conv_b=lru_conv_b, w_r=lru_w_r,
                           b_r=lru_b_r, w_i=lru_w_i, b_i=lru_b_i, lam=lru_lambda, w_out=lru_w_out)),
        (ret_mod_w, ret_mod_b, ret_norm_g,
         functools.partial(mixer_ret, ang=ret_ang, w_in=ret_w_in, decay_logit=ret_decay_logit, w_out=ret_w_out)),
        (hy_mod_w, hy_mod_b, hy_norm_g,
         functools.partial(mixer_hyena, w_in=hy_w_in, conv_w=hy_conv_w, conv_b=hy_conv_b, fw1=hy_fw1, fb1=hy_fb1,
                           fw2=hy_fw2, fb2=hy_fb2, fw3=hy_fw3, freq=hy_freq, skip=hy_skip, w_out=hy_w_out)),
        (att_mod_w, att_mod_b, att_norm_g,
         functools.partial(mixer_attn, ang=att_ang, w_in=att_w_in, q_norm_g=att_q_norm_g,
                           k_norm_g=att_k_norm_g, w_out=att_w_out)),
    ]
    for i in range(DEPTH):
        mod_w, mod_b, norm_g, mixer = layers[i]
        x, ctx = trunk_layer(x, ctx, c, c_ctx, mod_w, mod_b, norm_g, mixer, i < DEPTH - 1)
    return rmsnorm(x, final_norm_g)
```

```python
import math
import numpy as np
import ml_dtypes
from contextlib import ExitStack
import concourse.bass as bass
import concourse.mybir as mybir
from concourse.bass_utils import run_bass_kernel_spmd

F32 = mybir.dt.float32
BF16 = mybir.dt.bfloat16
AF = mybir.ActivationFunctionType
ALU = mybir.AluOpType
AX = mybir.AxisListType

P = 128
D = 1024
KD = 8
EPS = 1e-6


class Res:
    __slots__ = ("lw", "rd")

    def __init__(self):
        self.lw = None
        self.rd = {}


class Tl:
    __slots__ = ("ap", "res")

    def __init__(self, ap, res=None):
        self.ap = ap
        self.res = res if res is not None else Res()


class Ring:
    def __init__(self, tiles):
        self.t = tiles
        self.i = 0

    def next(self):
        t = self.t[self.i % len(self.t)]
        self.i += 1
        return t


class Sched:
    ENGS = ("pe", "act", "dve", "pool", "sp")
    NS = {"sp": 14, "act": 8}

    def __init__(self, nc, es):
        self.nc = nc
        self.ops = {e: [] for e in self.ENGS}
        self.cnt = {e: 0 for e in self.ENGS}
        self.known = {e: {} for e in self.ENGS}
        self.sems = {}
        for e in ("pe", "act", "dve", "pool"):
            self.sems[("c", e)] = es.enter_context(nc.semaphore("c_" + e))
        self.dq = {}
        for q, n in self.NS.items():
            for i in range(n):
                self.sems[("d", q, i)] = es.enter_context(nc.semaphore("d_%s_%d" % (q, i)))
            self.dq[q] = {"next": 0, "val": [0] * n}
        self.nops = 0

    def _collect(self, eng, reads, writes):
        deps = {}

        def add(tok, raw):
            if tok is None:
                return
            semkey, val, teng, seq = tok
            if semkey[0] == "c" and teng == eng:
                if eng == "pe" or not raw:
                    return
                if self.cnt[eng] - seq > 3:
                    return
            if deps.get(semkey, 0) < val:
                deps[semkey] = val

        for r in reads:
            add(r.lw, True)
        for w in writes:
            add(w.lw, False)
            for t in w.rd.values():
                add(t, False)
        return deps

    def _waits(self, eng, deps):
        out = []
        kn = self.known[eng]
        for semkey, val in deps.items():
            if kn.get(semkey, 0) >= val:
                continue
            kn[semkey] = val
            out.append((semkey, val))
        return out

    def _commit(self, tok, reads, writes):
        semkey = tok[0]
        for r in reads:
            old = r.rd.get(semkey)
            if old is None or old[1] < tok[1]:
                r.rd[semkey] = tok
        for w in writes:
            w.lw = tok
            w.rd = {}

    @staticmethod
    def _res(lst):
        return [x.res if isinstance(x, Tl) else x for x in lst]

    def op(self, eng, emit, reads=(), writes=()):
        reads = self._res(reads)
        writes = self._res(writes)
        waits = self._waits(eng, self._collect(eng, reads, writes))
        self.cnt[eng] += 1
        semkey = ("c", eng)
        tok = (semkey, self.cnt[eng], eng, self.cnt[eng])
        self.ops[eng].append((waits, emit, semkey, 1))
        self._commit(tok, reads, writes)
        self.nops += 1

    def dma(self, q, out, in_, reads=(), writes=(), slow=False):
        reads = self._res(reads)
        writes = self._res(writes)
        deps = self._collect(q, reads, writes)
        st = self.dq[q]
        i = st["next"] % self.NS[q]
        st["next"] += 1
        semkey = ("d", q, i)
        prev = st["val"][i]
        if prev > 0 and deps.get(semkey, 0) < prev:
            deps[semkey] = prev
        waits = self._waits(q, deps)
        st["val"][i] = prev + 16
        tok = (semkey, prev + 16, q, 0)
        if slow:
            emit = lambda e: e.dma_start(out=out, in_=in_, allow_slow_non_contiguous=True)
        else:
            emit = lambda e: e.dma_start(out=out, in_=in_)
        self.ops[q].append((waits, emit, semkey, 16))
        self._commit(tok, reads, writes)
        self.nops += 1

    def barrier(self):
        deps = {}
        for e in ("pe", "act", "dve", "pool"):
            if self.cnt[e] > 0:
                deps[("c", e)] = self.cnt[e]
        for q, st in self.dq.items():
            for i, v in enumerate(st["val"]):
                if v > 0:
                    deps[("d", q, i)] = v
        for e in self.ENGS:
            d = {k: v for k, v in deps.items() if not (k[0] == "c" and k[1] == e)}
            waits = self._waits(e, d)
            if waits:
                self.ops[e].append((waits, None, None, 0))

    def emit(self):
        self.barrier()
        sems = self.sems
        ops = self.ops

        def run(name):
            def f(eng):
                for waits, emit, semkey, inc in ops[name]:
                    for sk, val in waits:
                        eng.wait_ge(sems[sk], val)
                    if emit is not None:
                        emit(eng).then_inc(sems[semkey], inc)
            return f

        with self.nc.Block() as block:
            block.tensor(run("pe"))
            block.scalar(run("act"))
            block.vector(run("dve"))
            block.gpsimd(run("pool"))
            block.sync(run("sp"))


class Arena:
    def __init__(self, ap, ncols):
        self.ap = ap
        self.n = ncols
        self.top = 0
        self.ptop = ncols

    def reset(self):
        self.top = 0

    def f32(self, cols):
        a = self.ap[:, self.top:self.top + cols]
        self.top += cols
        assert self.top <= self.ptop, "SBUF arena overflow %d > %d" % (self.top, self.ptop)
        return a

    def bf16(self, cols):
        c = (cols + 1) // 2
        return self.f32(c).bitcast(BF16)

    def pf32(self, cols):
        self.ptop -= cols
        assert self.top <= self.ptop
        return self.ap[:, self.ptop:self.ptop + cols]

    def pbf16(self, cols):
        c = (cols + 1) // 2
        return self.pf32(c).bitcast(BF16)


ARENA_COLS = 48 * 1024

LRU_W = 2048
RET_H, RET_DK, RET_DV = 4, 256, 512
ATT_HQ, ATT_G, ATT_D = 8, 2, 128
GRID_W = 64


class Builder:
    def __init__(self, T, TC, nlayers=4, dbg=False):
        self.T, self.TC, self.nlayers, self.dbg = T, TC, nlayers, dbg
        self.nc = bass.Bass("TRN2", target_bir_lowering=False)
        self.ins = {}
        self.host_consts = {}

    def inp(self, name, shape, dt=F32):
        t = self.nc.dram_tensor(name, list(shape), dt, kind="ExternalInput").ap()
        self.ins[name] = t
        return t

    def scratch(self, name, shape, dt=F32):
        return self.nc.dram_tensor(name, list(shape), dt, kind="Internal").ap()

    def const(self, name, arr):
        arr = np.ascontiguousarray(arr)
        dt = BF16 if arr.dtype == ml_dtypes.bfloat16 else F32
        self.host_consts[name] = arr
        return self.inp(name, arr.shape, dt)

    def psum(self, b0, nb=1, dt=F32):
        a = self.ps[:, b0 * 512:(b0 + nb) * 512]
        return a.bitcast(BF16) if dt == BF16 else a

    def build(self):
        nc = self.nc
        T, TC = self.T, self.TC
        I = self.inp
        x = I("x", [T, D]); c = I("c", [D]); ctx = I("ctx", [TC, D]); c_ctx = I("c_ctx", [D])
        W = {}
        for p in ("lru", "ret", "hy", "att"):
            W[p + "_mod_w"] = I(p + "_mod_w", [D, 3 * D]); W[p + "_mod_b"] = I(p + "_mod_b", [3 * D])
            W[p + "_norm_g"] = I(p + "_norm_g", [D])
        W["lru_w_in"] = I("lru_w_in", [D, 4096]); W["lru_conv_w"] = I("lru_conv_w", [4, 2048])
        W["lru_conv_b"] = I("lru_conv_b", [2048]); W["lru_w_r"] = I("lru_w_r", [2, 16, 128, 128])
        W["lru_b_r"] = I("lru_b_r", [2, 2048]); W["lru_w_i"] = I("lru_w_i", [2, 16, 128, 128])
        W["lru_b_i"] = I("lru_b_i", [2, 2048]); W["lru_lambda"] = I("lru_lambda", [2, 2048])
        W["lru_w_out"] = I("lru_w_out", [2048, D])
        W["ret_w_in"] = I("ret_w_in", [D, 6144]); W["ret_decay_logit"] = I("ret_decay_logit", [2, 4])
        W["ret_w_out"] = I("ret_w_out", [2048, D])
        W["hy_w_in"] = I("hy_w_in", [D, 4096]); W["hy_conv_w"] = I("hy_conv_w", [3, 3072])
        W["hy_conv_b"] = I("hy_conv_b", [3072]); W["hy_fw1"] = I("hy_fw1", [33, 64]); W["hy_fb1"] = I("hy_fb1", [64])
        W["hy_fw2"] = I("hy_fw2", [64, 64]); W["hy_fb2"] = I("hy_fb2", [64]); W["hy_fw3"] = I("hy_fw3", [64, 4096])
        W["hy_freq"] = I("hy_freq", [64]); W["hy_skip"] = I("hy_skip", [2, 1024]); W["hy_w_out"] = I("hy_w_out", [1024, D])
        W["att_w_in"] = I("att_w_in", [D, 2560]); W["att_q_norm_g"] = I("att_q_norm_g", [128])
        W["att_k_norm_g"] = I("att_k_norm_g", [128]); W["att_w_out"] = I("att_w_out", [1024, D])
        W["final_norm_g"] = I("final_norm_g", [D])
        self.W = W
        self.x_in, self.ctx_in, self.c_in, self.cctx_in = x, ctx, c, c_ctx
        ident_d = self.const("c_ident", np.eye(128, dtype=np.float32).astype(ml_dtypes.bfloat16))
        self.out = nc.dram_tensor("out", [T, D], F32, kind="ExternalOutput").ap()
        if self.dbg:
            self.ctx_out = nc.dram_tensor("ctx_out", [TC, D], F32, kind="ExternalOutput").ap()
        self.xs = self.scratch("xs", [T, D]); self.cs = self.scratch("cs", [TC, D])
        self.x_res = [Res() for _ in range(T // 128)]
        self.c_res = [Res() for _ in range(TC // 128)]

        with ExitStack() as es:
            self.S = S = Sched(nc, es)
            arena_t = es.enter_context(nc.sbuf_tensor("arena", [P, ARENA_COLS], F32))
            self.A = A = Arena(arena_t, ARENA_COLS)
            self.ps = es.enter_context(nc.psum_tensor("ps", [P, 4096], F32))
            self.ps_res = [Res() for _ in range(8)]
            self.ident = Tl(A.pbf16(128))
            S.dma("sp", self.ident.ap, ident_d, writes=[self.ident])
            self.ones_row = Tl(A.pf32(128))
            S.op("pool", lambda e: e.memset(self.ones_row.ap, 1.0), writes=[self.ones_row])
            self.ones_bf = Tl(A.pbf16(128))
            self.eps_col = Tl(A.pf32(1))
            S.op("pool", lambda e: e.memset(self.eps_col.ap, EPS), writes=[self.eps_col])
            S.op("pool", lambda e: e.memset(self.ones_bf.ap, 1.0), writes=[self.ones_bf])
            self.modcol = Tl(A.pf32(48))
            self.gs = Tl(A.pf32(16)); self.sh = Tl(A.pf32(16))
            self.gt_bc = Tl(A.pf32(1024)); self.gtc_bc = Tl(A.pf32(1024))
            self.scol = Tl(A.pf32(16))
            self.prep_cond()

            x_src, c_src = self.x_in, self.ctx_in
            layers = [("lru", self.mix_lru), ("ret", self.mix_ret), ("hy", self.mix_hy), ("att", self.mix_att)]
            for li in range(self.nlayers):
                pfx, fn = layers[li]
                self.layer_idx = li
                self.need_ctx = li < 3
                S.barrier(); A.reset()
                self.modulation(pfx)
                S.barrier(); A.reset()
                fn(x_src, c_src)
                x_src, c_src = self.xs, self.cs
            S.barrier(); A.reset()
            self.final_norm(x_src)
            if self.dbg:
                S.barrier(); A.reset()
                t = Tl(A.f32(1024))
                for i in range(TC // 128):
                    S.dma("sp", t.ap, c_src[i * 128:(i + 1) * 128, :], reads=[self.c_res[i]], writes=[t])
                    S.dma("sp", self.ctx_out[i * 128:(i + 1) * 128, :], t.ap, reads=[t])
            S.emit()
        return nc

    def prep_cond(self):
        S, A = self.S, self.A
        raw = Tl(A.f32(16))
        r3 = raw.ap.rearrange("p (k n) -> p k n", n=2)
        S.dma("sp", r3[:, :, 0], self.c_in.rearrange("(k p) -> p k", p=128), writes=[raw], slow=True)
        S.dma("sp", r3[:, :, 1], self.cctx_in.rearrange("(k p) -> p k", p=128), writes=[raw], slow=True)
        S.op("act", lambda e: e.activation(out=self.scol.ap, in_=raw.ap, func=AF.Silu), reads=[raw], writes=[self.scol])

    def modulation(self, pfx):
        S, A, W = self.S, self.A, self.W
        mw = Tl(A.f32(8 * 3072))
        mw3 = mw.ap.rearrange("p (k n) -> p k n", k=8)
        src = W[pfx + "_mod_w"].rearrange("(k p) n -> p k n", p=128)
        for k in range(8):
            S.dma("sp" if k % 2 == 0 else "act", mw3[:, k, :], src[:, k, :], writes=[mw])
        mb = Tl(A.f32(24))
        S.dma("sp", mb.ap, W[pfx + "_mod_b"].rearrange("(j p) -> p j", p=128), writes=[mb], slow=True)
        g = Tl(A.f32(8))
        S.dma("sp", g.ap, W[pfx + "_norm_g"].rearrange("(j p) -> p j", p=128), writes=[g], slow=True)
        mbrow = Tl(A.f32(1024))
        S.dma("sp", mbrow.ap[0:1, :], W[pfx + "_mod_b"][2048:3072].rearrange("(o n) -> o n", o=1), writes=[mbrow])
        sc3 = self.scol.ap.rearrange("p (k n) -> p k n", n=2)
        mc3 = self.modcol.ap.rearrange("p (j n) -> p j n", n=2)
        for j in range(24):
            b = j % 4
            pt = self.psum(b)
            for k in range(8):
                S.op("pe", lambda e, k=k, j=j, pt=pt: e.matmul(pt[:, 0:2], lhsT=mw3[:, k, j * 128:(j + 1) * 128], rhs=sc3[:, k, :],
                                                            start=(k == 0), stop=(k == 7)),
                     reads=[mw, self.scol], writes=[self.ps_res[b]])
            S.op("dve", lambda e, j=j, pt=pt: e.tensor_scalar(out=mc3[:, j, :], in0=pt[:, 0:2], scalar1=mb.ap[:, j:j + 1], scalar2=None,
                                                           op0=ALU.add), reads=[self.ps_res[b], mb], writes=[self.modcol])
        gs3 = self.gs.ap.rearrange("p (k n) -> p k n", n=2)
        sh3 = self.sh.ap.rearrange("p (k n) -> p k n", n=2)
        for n in range(2):
            S.op("dve", lambda e, n=n: e.tensor_scalar(out=gs3[:, :, n], in0=mc3[:, 8:16, n], scalar1=1.0, scalar2=None, op0=ALU.add),
                 reads=[self.modcol], writes=[self.gs])
            S.op("dve", lambda e, n=n: e.tensor_tensor(out=gs3[:, :, n], in0=gs3[:, :, n], in1=g.ap, op=ALU.mult),
                 reads=[self.gs, g], writes=[self.gs])
            S.op("dve", lambda e, n=n: e.tensor_copy(out=sh3[:, :, n], in_=mc3[:, 0:8, n]), reads=[self.modcol], writes=[self.sh])
        for n, dst in ((0, self.gt_bc), (1, self.gtc_bc)):
            row = Tl(A.f32(1024))
            for hh in range(2):
                b = 4 + hh
                pt = self.psum(b)
                for k in range(8):
                    S.op("pe", lambda e, k=k, hh=hh, n=n, pt=pt: e.matmul(pt[0:1, :], lhsT=sc3[:, k, n:n + 1],
                                                                       rhs=mw3[:, k, 2048 + hh * 512:2048 + (hh + 1) * 512],
                                                                       start=(k == 0), stop=(k == 7)),
                         reads=[mw, self.scol], writes=[self.ps_res[b]])
                S.op("dve", lambda e, hh=hh, pt=pt, row=row: e.tensor_tensor(out=row.ap[0:1, hh * 512:(hh + 1) * 512], in0=pt[0:1, :],
                                                                           in1=mbrow.ap[0:1, hh * 512:(hh + 1) * 512], op=ALU.add),
                     reads=[self.ps_res[b], mbrow], writes=[row])
            for hh in range(2):
                b = 6 + hh
                pt = self.psum(b)
                S.op("pe", lambda e, hh=hh, pt=pt, row=row: e.matmul(pt, lhsT=self.ones_row.ap[0:1, :], rhs=row.ap[0:1, hh * 512:(hh + 1) * 512],
                                                                   start=True, stop=True),
                     reads=[row, self.ones_row], writes=[self.ps_res[b]])
                S.op("act", lambda e, hh=hh, pt=pt, dst=dst: e.activation(out=dst.ap[:, hh * 512:(hh + 1) * 512], in_=pt, func=AF.Copy),
                     reads=[self.ps_res[b]], writes=[dst])

    def load_w_bf16(self, w_dram, K, N, dst3, dst_tl, col0=0, ncols=None, stage_ring=None):
        S = self.S
        ncols = N if ncols is None else ncols
        src = w_dram.rearrange("(k p) n -> p k n", p=128)
        CH = 2048
        i = 0
        for k in range(K // 128):
            for c0 in range(0, ncols, CH):
                cw = min(CH, ncols - c0)
                st = stage_ring.next()
                S.dma("sp" if i % 2 == 0 else "act", st.ap[:, 0:cw], src[:, k, col0 + c0:col0 + c0 + cw], writes=[st])
                eng = ("act", "dve", "pool")[i % 3]
                if eng == "act":
                    S.op("act", lambda e, st=st, k=k, c0=c0, cw=cw: e.activation(out=dst3[:, k, c0:c0 + cw], in_=st.ap[:, 0:cw], func=AF.Copy),
                         reads=[st], writes=[dst_tl])
                else:
                    S.op(eng, lambda e, st=st, k=k, c0=c0, cw=cw: e.tensor_copy(out=dst3[:, k, c0:c0 + cw], in_=st.ap[:, 0:cw]),
                         reads=[st], writes=[dst_tl])
                i += 1

    def norm_tiles(self, src, src_res, Tn, which, xring, hring, small, TT):
        S = self.S
        gs3 = self.gs.ap.rearrange("p (k n) -> p k n", n=2)
        sh3 = self.sh.ap.rearrange("p (k n) -> p k n", n=2)
        for tt in range(Tn // TT):
            hT = hring.next()
            h3 = hT.ap.rearrange("p (k t) -> p k t", k=8)
            for s in range(TT // 128):
                ti = (tt * TT) // 128 + s
                xt = xring.next()
                xf, xb, ss = xt["xf"], xt["xb"], xt["ss"]
                S.dma("sp", xf.ap, src[ti * 128:(ti + 1) * 128, :], reads=[src_res[ti]], writes=[xf])
                S.op("act", lambda e, xf=xf, xb=xb, ss=ss: e.activation(out=xb.ap, in_=xf.ap, func=AF.Square, accum_out=ss.ap),
                     reads=[xf], writes=[xb, ss])
                S.op("dve", lambda e, ss=ss: e.tensor_scalar(out=ss.ap, in0=ss.ap, scalar1=1.0 / D, scalar2=EPS, op0=ALU.mult, op1=ALU.add),
                     reads=[ss], writes=[ss])
                S.op("act", lambda e, ss=ss: e.activation(out=ss.ap, in_=ss.ap, func=AF.Sqrt), reads=[ss], writes=[ss])
                S.op("dve", lambda e, ss=ss: e.reciprocal(out=ss.ap, in_=ss.ap), reads=[ss], writes=[ss])
                S.op("act", lambda e, xf=xf, xb=xb, ss=ss: e.activation(out=xb.ap, in_=xf.ap, func=AF.Copy, scale=ss.ap),
                     reads=[xf, ss], writes=[xb])
                b = self.tp_bank
                self.tp_bank = 6 + (self.tp_bank - 6 + 1) % 2
                pb = self.psum(b, 1, BF16)
                for j in range(8):
                    S.op("pe", lambda e, j=j, pb=pb, xb=xb: e.transpose(out=pb[:, j * 128:(j + 1) * 128], in_=xb.ap[:, j * 128:(j + 1) * 128],
                                                                        identity=self.ident.ap),
                         reads=[xb, self.ident], writes=[self.ps_res[b]])
                for j in range(8):
                    eng = "dve" if j % 2 == 0 else "pool"
                    eng = "dve"
                    S.op(eng, lambda e, j=j, pb=pb, s=s, h3=h3: e.tensor_scalar(out=h3[:, j, s * 128:(s + 1) * 128], in0=pb[:, j * 128:(j + 1) * 128],
                                                                         scalar1=gs3[:, j, which:which + 1], scalar2=sh3[:, j, which:which + 1],
                                                                         op0=ALU.mult, op1=ALU.add),
                         reads=[self.ps_res[b], self.gs, self.sh], writes=[hT])
            yield tt, hT, h3

    def make_norm_rings(self, TT, nh=2, nx=3):
        A = self.A
        xring = Ring([{"xf": Tl(A.f32(1024)), "xb": Tl(A.bf16(1024)), "ss": Tl(A.f32(1))} for _ in range(nx)])
        hring = Ring([Tl(A.bf16(8 * TT)) for _ in range(nh)])
        self.tp_bank = 6
        return xring, hring

    def out_proj(self, Y, C, w_bf3, w_tl, Tn, gt, src, dst, res_list, TT=512):
        S, A = self.S, self.A
        kc = C // 128
        TT = min(TT, Tn)
        yring = Ring([Tl(A.bf16(kc * TT)) for _ in range(2)])
        xring = Ring([Tl(A.f32(1024)) for _ in range(3)])
        Yv = Y.rearrange("(k p) t -> p k t", p=128)
        pbi = 0
        for tt in range(Tn // TT):
            yt = yring.next()
            y3 = yt.ap.rearrange("p (k t) -> p k t", k=kc)
            S.dma("sp", y3, Yv[:, :, tt * TT:(tt + 1) * TT], writes=[yt], reads=[self.Y_res])
            for s in range(TT // 128):
                ti = (tt * TT) // 128 + s
                xt = xring.next()
                S.dma("act", xt.ap, src[ti * 128:(ti + 1) * 128, :], reads=[res_list[ti]], writes=[xt])
                b0 = (pbi % 3) * 2
                pbi += 1
                for hh in range(2):
                    pt = self.psum(b0 + hh)
                    for k in range(kc):
                        S.op("pe", lambda e, k=k, hh=hh, pt=pt, s=s, y3=y3: e.matmul(pt, lhsT=y3[:, k, s * 128:(s + 1) * 128],
                                                                                 rhs=w_bf3[:, k, hh * 512:(hh + 1) * 512],
                                                                                 start=(k == 0), stop=(k == kc - 1)),
                             reads=[yt, w_tl], writes=[self.ps_res[b0 + hh]])
                pt2 = self.psum(b0, 2)
                S.op("dve", lambda e, pt2=pt2: e.tensor_tensor(out=pt2, in0=pt2, in1=gt.ap, op=ALU.mult),
                     reads=[self.ps_res[b0], self.ps_res[b0 + 1], gt], writes=[self.ps_res[b0], self.ps_res[b0 + 1]])
                S.op("dve", lambda e, pt2=pt2, xt=xt: e.tensor_tensor(out=xt.ap, in0=pt2, in1=xt.ap, op=ALU.add),
                     reads=[self.ps_res[b0], self.ps_res[b0 + 1], xt], writes=[xt])
                S.dma("sp", dst[ti * 128:(ti + 1) * 128, :], xt.ap, reads=[xt], writes=[res_list[ti]])

    def final_norm(self, src):
        S, A, T = self.S, self.A, self.T
        g = Tl(A.f32(1024))
        S.dma("sp", g.ap, self.W["final_norm_g"].partition_broadcast(128), writes=[g])
        ring = Ring([{"xf": Tl(A.f32(1024)), "sq": Tl(A.f32(1024)), "ss": Tl(A.f32(1))} for _ in range(3)])
        ores = Res()
        for ti in range(T // 128):
            r = ring.next()
            xf, sq, ss = r["xf"], r["sq"], r["ss"]
            S.dma("sp", xf.ap, src[ti * 128:(ti + 1) * 128, :], reads=[self.x_res[ti]], writes=[xf])
            S.op("act", lambda e, xf=xf, sq=sq, ss=ss: e.activation(out=sq.ap, in_=xf.ap, func=AF.Square, accum_out=ss.ap),
                 reads=[xf], writes=[sq, ss])
            S.op("dve", lambda e, ss=ss: e.tensor_scalar(out=ss.ap, in0=ss.ap, scalar1=1.0 / D, scalar2=EPS, op0=ALU.mult, op1=ALU.add),
                 reads=[ss], writes=[ss])
            S.op("act", lambda e, ss=ss: e.activation(out=ss.ap, in_=ss.ap, func=AF.Sqrt), reads=[ss], writes=[ss])
            S.op("dve", lambda e, ss=ss: e.reciprocal(out=ss.ap, in_=ss.ap), reads=[ss], writes=[ss])
            S.op("dve", lambda e, xf=xf, sq=sq, ss=ss: e.scalar_tensor_tensor(out=sq.ap, in0=xf.ap, scalar=ss.ap, in1=g.ap,
                                                                             op0=ALU.mult, op1=ALU.mult),
                 reads=[xf, ss, g], writes=[sq])
            S.dma("act", self.out[ti * 128:(ti + 1) * 128, :], sq.ap, reads=[sq], writes=[ores])

    def proj_to_dram(self, src, src_res, Tn, which, w3, w_tl, nchunks, U, U_res, TT):
        S, A = self.S, self.A
        xring, hring = self.make_norm_rings(TT)
        stg = Ring([Tl(A.f32(TT)) for _ in range(4)])
        bi = 0
        for tt, hT, h3 in self.norm_tiles(src, src_res, Tn, which, xring, hring, None, TT):
            for oc in range(nchunks):
                b = bi % 6
                bi += 1
                pt = self.psum(b)[:, 0:TT]
                for k in range(8):
                    S.op("pe", lambda e, k=k, oc=oc, pt=pt, h3=h3: e.matmul(pt, lhsT=w3[:, k, oc * 128:(oc + 1) * 128], rhs=h3[:, k, :],
                                                                         start=(k == 0), stop=(k == 7)),
                         reads=[hT, w_tl], writes=[self.ps_res[b]])
                st = stg.next()
                if oc % 2 == 0:
                    S.op("act", lambda e, pt=pt, st=st: e.activation(out=st.ap, in_=pt, func=AF.Copy), reads=[self.ps_res[b]], writes=[st])
                else:
                    S.op("dve", lambda e, pt=pt, st=st: e.tensor_copy(out=st.ap, in_=pt), reads=[self.ps_res[b]], writes=[st])
                S.dma("sp" if oc % 2 == 0 else "act", U[oc * 128:(oc + 1) * 128, tt * TT:(tt + 1) * TT], st.ap, reads=[st], writes=[U_res])

    def mix_lru(self, x_src, c_src):
        S, A, W, T, TC = self.S, self.A, self.W, self.T, self.TC
        U = self.scratch("lru_U", [4096, T]); UC = self.scratch("lru_UC", [4096, TC])
        Y = self.scratch("lru_Y", [2048, T], BF16); YC = self.scratch("lru_YC", [2048, TC], BF16)
        U_res, UC_res = Res(), Res()
        self.Y_res = Res()
        w_tl = Tl(A.bf16(8 * 4096))
        w3 = w_tl.ap.rearrange("p (k n) -> p k n", k=8)
        stage = Ring([Tl(A.f32(2048)) for _ in range(3)])
        self.load_w_bf16(W["lru_w_in"], 1024, 4096, w3, w_tl, stage_ring=stage)
        mark = A.top
        self.proj_to_dram(c_src, self.c_res, TC, 1, w3, w_tl, 32, UC, UC_res, min(512, TC))
        A.top = mark
        S.barrier()
        self.proj_to_dram(x_src, self.x_res, T, 0, w3, w_tl, 32, U, U_res, min(512, T))
        S.barrier(); A.reset()
        def col(src_ap, n, pattern="(n p) -> p n"):
            t = Tl(A.f32(n))
            S.dma("sp", t.ap, src_ap.rearrange(pattern, p=128), writes=[t], slow=True)
            return t
        cw = [col(W["lru_conv_w"][k], 16) for k in range(4)]
        cb = col(W["lru_conv_b"], 16)
        br = [col(W["lru_b_r"][d], 16) for d in range(2)]
        bi_ = [col(W["lru_b_i"][d], 16) for d in range(2)]
        lam = [col(W["lru_lambda"][d], 16) for d in range(2)]
        cdec, cdec2 = [], []
        for d in range(2):
            t = Tl(A.f32(16)); t2 = Tl(A.f32(16))
            S.op("act", lambda e, t=t, d=d: e.activation(out=t.ap, in_=lam[d].ap, func=AF.Exp, scale=-1.0), reads=[lam[d]], writes=[t])
            S.op("act", lambda e, t=t: e.activation(out=t.ap, in_=t.ap, func=AF.Ln, bias=1.0), reads=[t], writes=[t])
            S.op("dve", lambda e, t=t, t2=t2: e.tensor_scalar(out=t2.ap, in0=t.ap, scalar1=-16.0, scalar2=None, op0=ALU.mult), reads=[t], writes=[t2])
            S.op("dve", lambda e, t=t: e.tensor_scalar(out=t.ap, in0=t.ap, scalar1=-8.0, scalar2=None, op0=ALU.mult), reads=[t], writes=[t])
            cdec.append(t); cdec2.append(t2)
        gw = Tl(A.bf16(2 * 2 * 16 * 128))
        gw5 = gw.ap.rearrange("p (d g n k) -> p d g n k", d=2, g=2, n=16)
        gst = Ring([Tl(A.f32(16 * 128)) for _ in range(2)])
        for d in range(2):
            for g, nm in enumerate(("lru_w_r", "lru_w_i")):
                st = gst.next()
                st3 = st.ap.rearrange("p (n k) -> p n k", n=16)
                S.dma("sp", st3, W[nm][d].rearrange("n j k -> j n k"), writes=[st])
                S.op("dve", lambda e, st3=st3, d=d, g=g: e.tensor_copy(out=gw5[:, d, g, :, :], in_=st3), reads=[st], writes=[gw])
        TM = max(T, TC)
        ubuf = Tl(A.f32(TM + 3))
        xl = Tl(A.f32(TM)); xlb = Tl(A.bf16(TM))
        TTS = min(1024, TM)
        tring = Ring([{k: Tl(A.f32(TTS)) for k in ("r", "i", "a", "hb", "g")} for _ in range(2)])
        yring = Ring([Tl(A.bf16(TTS)) for _ in range(2)])
        fin = [Tl(A.f32(1)), Tl(A.f32(1))]
        pb = [0]

        def seq(n, Useq, Useq_res, Tn, init, Yseq):
            tts = min(TTS, Tn)
            nt = Tn // tts
            S.op("pool", lambda e: e.memset(ubuf.ap[:, 0:2], 0.0), writes=[ubuf])
            S.op("pool", lambda e: e.memset(ubuf.ap[:, Tn + 2:Tn + 3], 0.0), writes=[ubuf])
            S.dma("sp", ubuf.ap[:, 2:Tn + 2], Useq[n * 128:(n + 1) * 128, :], reads=[Useq_res], writes=[ubuf])
            S.op("dve", lambda e: e.tensor_scalar(out=xl.ap[:, 0:Tn], in0=ubuf.ap[:, 0:Tn], scalar1=cw[0].ap[:, n:n + 1],
                                                  scalar2=cb.ap[:, n:n + 1], op0=ALU.mult, op1=ALU.add),
                 reads=[ubuf, cw[0], cb], writes=[xl])
            for k in range(1, 4):
                S.op("dve", lambda e, k=k: e.scalar_tensor_tensor(out=xl.ap[:, 0:Tn], in0=ubuf.ap[:, k:k + Tn], scalar=cw[k].ap[:, n:n + 1],
                                                                  in1=xl.ap[:, 0:Tn], op0=ALU.mult, op1=ALU.add),
                     reads=[ubuf, cw[k], xl], writes=[xl])
            S.op("act", lambda e: e.activation(out=xlb.ap[:, 0:Tn], in_=xl.ap[:, 0:Tn], func=AF.Copy), reads=[xl], writes=[xlb])
            hf = ubuf
            for d in range(2):
                order = list(range(nt)) if d == 0 else list(range(nt - 1, -1, -1))
                state = init[d]
                for tt in order:
                    sl = slice(tt * tts, (tt + 1) * tts)
                    tb = tring.next()
                    r, i_, a, hb, g = tb["r"], tb["i"], tb["a"], tb["hb"], tb["g"]
                    pb[0] += 1
                    nb = (tts + 511) // 512
                    bank0 = (pb[0] % 2) * 4
                    for gi in range(2):
                        for h in range(nb):
                            c0 = h * 512
                            cwid = min(512, tts - c0)
                            bank = bank0 + gi * 2 + h
                            pt = self.psum(bank)[:, 0:cwid]
                            S.op("pe", lambda e, pt=pt, gi=gi, c0=c0, cwid=cwid, d=d, tt=tt: e.matmul(
                                pt, lhsT=gw5[:, d, gi, n, :], rhs=xlb.ap[:, tt * tts + c0:tt * tts + c0 + cwid], start=True, stop=True),
                                 reads=[gw, xlb], writes=[self.ps_res[bank]])
                            dst = r if gi == 0 else i_
                            bias = (br if gi == 0 else bi_)[d]
                            S.op("act", lambda e, pt=pt, dst=dst, bias=bias, c0=c0, cwid=cwid: e.activation(
                                out=dst.ap[:, c0:c0 + cwid], in_=pt, func=AF.Sigmoid, bias=bias.ap[:, n:n + 1]),
                                 reads=[self.ps_res[bank], bias], writes=[dst])
                    S.op("act", lambda e, r=r, a=a, d=d: e.activation(out=a.ap[:, 0:tts], in_=r.ap[:, 0:tts], func=AF.Exp, scale=cdec[d].ap[:, n:n + 1]),
                         reads=[r, cdec[d]], writes=[a])
                    S.op("act", lambda e, r=r, d=d: e.activation(out=r.ap[:, 0:tts], in_=r.ap[:, 0:tts], func=AF.Exp, scale=cdec2[d].ap[:, n:n + 1]),
                         reads=[r, cdec2[d]], writes=[r])
                    S.op("act", lambda e, r=r: e.activation(out=r.ap[:, 0:tts], in_=r.ap[:, 0:tts], func=AF.Sqrt, scale=-1.0, bias=1.0),
                         reads=[r], writes=[r])
                    S.op("pool", lambda e, i_=i_, sl=sl: e.tensor_tensor(out=i_.ap[:, 0:tts], in0=i_.ap[:, 0:tts], in1=xl.ap[:, sl], op=ALU.mult),
                         reads=[i_, xl], writes=[i_])
                    S.op("dve", lambda e, i_=i_, r=r: e.tensor_tensor(out=i_.ap[:, 0:tts], in0=i_.ap[:, 0:tts], in1=r.ap[:, 0:tts], op=ALU.mult),
                         reads=[i_, r], writes=[i_])
                    ini = state if isinstance(state, float) else state.ap
                    rds = [a, i_] + ([] if isinstance(state, float) else [state])
                    if d == 0:
                        S.op("dve", lambda e, a=a, i_=i_, ini=ini, sl=sl: e.tensor_tensor_scan(out=hf.ap[:, sl], data0=a.ap[:, 0:tts], data1=i_.ap[:, 0:tts],
                                                                                         initial=ini, op0=ALU.mult, op1=ALU.add),
                             reads=rds, writes=[hf])
                        state = Tl(hf.ap[:, (tt + 1) * tts - 1:(tt + 1) * tts], hf.res)
                    else:
                        S.op("dve", lambda e, a=a, i_=i_, ini=ini, hb=hb: e.tensor_tensor_scan(out=hb.ap[:, 0:tts][:, ::-1], data0=a.ap[:, 0:tts][:, ::-1],
                                                                                         data1=i_.ap[:, 0:tts][:, ::-1], initial=ini,
                                                                                         op0=ALU.mult, op1=ALU.add),
                             reads=rds, writes=[hb])
                        state = Tl(hb.ap[:, 0:1], hb.res)
                        if tt == 0:
                            S.op("act", lambda e, hb=hb: e.activation(out=fin[1].ap, in_=hb.ap[:, 0:1], func=AF.Copy), reads=[hb], writes=[fin[1]])
                        if Yseq is not None:
                            S.dma("act", g.ap[:, 0:tts], Useq[2048 + n * 128:2048 + (n + 1) * 128, sl], reads=[Useq_res], writes=[g])
                            S.op("act", lambda e, g=g: e.activation(out=g.ap[:, 0:tts], in_=g.ap[:, 0:tts], func=AF.Silu), reads=[g], writes=[g])
                            S.op("pool", lambda e, hb=hb, sl=sl, a=a: e.tensor_tensor(out=a.ap[:, 0:tts], in0=hb.ap[:, 0:tts], in1=hf.ap[:, sl], op=ALU.add),
                                 reads=[hb, hf], writes=[a])
                            yt = yring.next()
                            S.op("dve", lambda e, a=a, g=g, yt=yt: e.tensor_tensor(out=yt.ap[:, 0:tts], in0=a.ap[:, 0:tts], in1=g.ap[:, 0:tts], op=ALU.mult),
                                 reads=[a, g], writes=[yt])
                            S.dma("sp", Yseq[n * 128:(n + 1) * 128, sl], yt.ap[:, 0:tts], reads=[yt], writes=[self.Y_res])
                if d == 0:
                    S.op("act", lambda e: e.activation(out=fin[0].ap, in_=hf.ap[:, Tn - 1:Tn], func=AF.Copy), reads=[hf], writes=[fin[0]])

        for n in range(16):
            seq(n, UC, UC_res, TC, [0.0, 0.0], YC if self.need_ctx else None)
            seq(n, U, U_res, T, [fin[0], fin[1]], Y)
        S.barrier(); A.reset()
        wo = Tl(A.bf16(16 * 1024))
        wo3 = wo.ap.rearrange("p (k n) -> p k n", k=16)
        stage = Ring([Tl(A.f32(2048)) for _ in range(3)])
        self.load_w_bf16(W["lru_w_out"], 2048, 1024, wo3, wo, stage_ring=stage)
        self.out_proj(Y, 2048, wo3, wo, T, self.gt_bc, x_src, self.xs, self.x_res)
        if self.need_ctx:
            self.out_proj(YC, 2048, wo3, wo, TC, self.gtc_bc, c_src, self.cs, self.c_res)

    def mix_ret(self, x_src, c_src):
        S, A, W, T, TC = self.S, self.A, self.W, self.T, self.TC
        C = 128
        f32 = np.float32
        p_ = np.arange(128, dtype=f32)
        cst = np.zeros((128, 4 + 4 * 128), f32)
        cst[:, 0] = C - 1 - p_; cst[:, 1] = p_; cst[:, 2] = C
        ii = np.arange(128, dtype=f32)[None, :]; jj = np.arange(128, dtype=f32)[:, None]
        cst[:, 4:132] = ii + 1.0
        cst[:, 132:260] = C - ii
        cst[:, 260:388] = np.maximum(ii - jj, 0.0)
        cst[:, 388:516] = np.maximum(jj - ii, 0.0)
        cst_d = self.const("c_ret_cst", cst)
        inv = (f32(10000.0) ** (-np.arange(128, dtype=f32) / f32(128))).astype(f32)
        ang = (np.arange(T, dtype=f32)[:, None] * inv[None, :]).astype(f32)
        cs_, sn_ = np.cos(ang.astype(np.float64)).astype(f32), np.sin(ang.astype(np.float64)).astype(f32)
        tabs_x = dict(cF=self.const("c_ret_cF", cs_.T), sF=self.const("c_ret_sF", sn_.T),
                      cT=self.const("c_ret_cT", np.tile(cs_, (1, 2))), sT=self.const("c_ret_sT", np.tile(sn_, (1, 2))))
        tabs_c = dict(cF=self.const("c_ret_cFc", np.ones((128, TC), f32)), sF=self.const("c_ret_sFc", np.zeros((128, TC), f32)),
                      cT=self.const("c_ret_cTc", np.ones((TC, 256), f32)), sT=self.const("c_ret_sTc", np.zeros((TC, 256), f32)))

        def mk(T_, sfx):
            d = {}
            for nm in ("Q", "QF", "QB", "KT"):
                d[nm] = self.scratch("ret_%s%s" % (nm, sfx), [1024, T_], BF16)
            for nm in ("KF", "KB"):
                d[nm] = self.scratch("ret_%s%s" % (nm, sfx), [T_, 1024], BF16)
            d["V"] = self.scratch("ret_V" + sfx, [T_, 2048], BF16); d["G"] = self.scratch("ret_G" + sfx, [T_, 2048], BF16)
            d["OB"] = self.scratch("ret_OB" + sfx, [T_, 2048], F32); d["Y"] = self.scratch("ret_Y" + sfx, [2048, T_], BF16)
            d["res"] = Res()
            return d
        DX, DC = mk(T, "x"), mk(TC, "c")
        self.Y_res = Res()

        cstt = Tl(A.f32(516))
        S.dma("sp", cstt.ap, cst_d, writes=[cstt])
        lg = Tl(A.f32(8))
        S.dma("sp", lg.ap, W["ret_decay_logit"].rearrange("d h -> (d h)").partition_broadcast(128), writes=[lg])
        S.op("act", lambda e: e.activation(out=lg.ap, in_=lg.ap, func=AF.Exp, scale=-1.0), reads=[lg], writes=[lg])
        S.op("act", lambda e: e.activation(out=lg.ap, in_=lg.ap, func=AF.Ln, bias=1.0), reads=[lg], writes=[lg])
        S.op("dve", lambda e: e.tensor_scalar(out=lg.ap, in0=lg.ap, scalar1=-1.0, scalar2=None, op0=ALU.mult), reads=[lg], writes=[lg])
        kd = Tl(A.f32(8)); blk = Tl(A.f32(8))
        for d in range(2):
            for h in range(4):
                ix = d * 4 + h
                S.op("act", lambda e, d=d, ix=ix: e.activation(out=kd.ap[:, ix:ix + 1], in_=cstt.ap[:, d:d + 1], func=AF.Exp, scale=lg.ap[:, ix:ix + 1]),
                     reads=[cstt, lg], writes=[kd])
                S.op("act", lambda e, ix=ix: e.activation(out=blk.ap[:, ix:ix + 1], in_=cstt.ap[:, 2:3], func=AF.Exp, scale=lg.ap[:, ix:ix + 1]),
                     reads=[cstt, lg], writes=[blk])
        S.op("dve", lambda e: e.tensor_scalar(out=kd.ap, in0=kd.ap, scalar1=1.0 / 16.0, scalar2=None, op0=ALU.mult), reads=[kd], writes=[kd])
        onesf = Tl(A.f32(256))
        S.op("pool", lambda e: e.memset(onesf.ap, 1.0), writes=[onesf])
        dect = [Tl(A.f32(1024)), Tl(A.f32(1024))]
        qdt = [Tl(A.f32(4 * 512)), Tl(A.f32(4 * 512))]
        maskT = Tl(A.f32(4 * 128))
        mtmp = Tl(A.f32(128))
        for d in range(2):
            for h in range(4):
                ix = d * 4 + h
                S.op("dve", lambda e, d=d, h=h, ix=ix: e.tensor_scalar(out=dect[d].ap[:, h * 256:(h + 1) * 256], in0=onesf.ap, scalar1=kd.ap[:, ix:ix + 1],
                                                                   scalar2=None, op0=ALU.mult), reads=[onesf, kd], writes=[dect[d]])
                for r in range(4):
                    S.op("act", lambda e, d=d, h=h, ix=ix, r=r: e.activation(out=qdt[d].ap[:, h * 512 + r * 128:h * 512 + (r + 1) * 128],
                                                                          in_=cstt.ap[:, 4 + d * 128:4 + (d + 1) * 128], func=AF.Exp,
                                                                          scale=lg.ap[:, ix:ix + 1]), reads=[cstt, lg], writes=[qdt[d]])
        for h in range(4):
            S.op("act", lambda e, h=h: e.activation(out=maskT.ap[:, h * 128:(h + 1) * 128], in_=cstt.ap[:, 260:388], func=AF.Exp, scale=lg.ap[:, h:h + 1]),
                 reads=[cstt, lg], writes=[maskT])
            S.op("act", lambda e, h=h: e.activation(out=mtmp.ap, in_=cstt.ap[:, 388:516], func=AF.Exp, scale=lg.ap[:, 4 + h:5 + h]),
                 reads=[cstt, lg], writes=[mtmp])
            S.op("dve", lambda e, h=h: e.scalar_tensor_tensor(out=maskT.ap[:, h * 128:(h + 1) * 128], in0=maskT.ap[:, h * 128:(h + 1) * 128],
                                                              scalar=1.0 / 16.0, in1=mtmp.ap, op0=ALU.mult, op1=ALU.mult),
                 reads=[maskT, mtmp], writes=[maskT])
        base_mark = A.top

        def ret_proj(src, src_res, Tn, which, tabs, Dd, part, w3, w_tl):
            TT = min(512, Tn)
            xring, hring = self.make_norm_rings(TT, nh=2, nx=2)
            if part == 0:
                tabF = Ring([{"c": Tl(A.f32(TT)), "s": Tl(A.f32(TT))} for _ in range(2)])
                tmp = Ring([{k: Tl(A.f32(TT)) for k in ("t1", "t2", "o1", "o2")} for _ in range(2)])
                stb = Ring([Tl(A.bf16(TT)) for _ in range(6)])
            else:
                tabT = Ring([{"c": Tl(A.f32(256)), "s": Tl(A.f32(256))} for _ in range(2)])
                tk = Ring([{"u1": Tl(A.f32(256)), "u2": Tl(A.f32(256)), "ko": Tl(A.f32(512))} for _ in range(2)])
                kst = Ring([{"f": Tl(A.bf16(1024)), "b": Tl(A.bf16(1024))} for _ in range(2)])
                vst = Ring([Tl(A.bf16(2048)) for _ in range(2)])
                gst = Ring([Tl(A.bf16(2048)) for _ in range(2)])
            dres = Dd["res"]
            cnt = [0, 0]
            for tt, hT, h3 in self.norm_tiles(src, src_res, Tn, which, xring, hring, None, TT):
                cols = slice(tt * TT, (tt + 1) * TT)
                if part == 0:
                    tf = tabF.next()
                    S.dma("act", tf["c"].ap, tabs["cF"][:, cols], writes=[tf["c"]])
                    S.dma("act", tf["s"].ap, tabs["sF"][:, cols], writes=[tf["s"]])
                for qk in (range(2) if part == 0 else ()):
                    for h in range(4):
                        b0 = (cnt[0] % 2) * 2
                        cnt[0] += 1
                        for half in range(2):
                            oc = qk * 8 + h * 2 + half
                            pt = self.psum(b0 + half)[:, 0:TT]
                            for kk in range(8):
                                S.op("pe", lambda e, kk=kk, oc=oc, pt=pt, h3=h3: e.matmul(pt, lhsT=w3[:, kk, oc * 128:(oc + 1) * 128], rhs=h3[:, kk, :],
                                                                                      start=(kk == 0), stop=(kk == 7)),
                                     reads=[hT, w_tl], writes=[self.ps_res[b0 + half]])
                        x1 = self.psum(b0)[:, 0:TT]; x2 = self.psum(b0 + 1)[:, 0:TT]
                        r1, r2 = self.ps_res[b0], self.ps_res[b0 + 1]
                        tm = tmp.next()
                        t1, t2, o1, o2 = tm["t1"], tm["t2"], tm["o1"], tm["o2"]
                        S.op("dve", lambda e, x1=x1, t1=t1, tf=tf: e.tensor_tensor(out=t1.ap, in0=x1, in1=tf["c"].ap, op=ALU.mult), reads=[r1, tf["c"]], writes=[t1])
                        S.op("dve", lambda e, x2=x2, t2=t2, tf=tf: e.tensor_tensor(out=t2.ap, in0=x2, in1=tf["s"].ap, op=ALU.mult), reads=[r2, tf["s"]], writes=[t2])
                        S.op("pool", lambda e, t1=t1, t2=t2, o1=o1: e.tensor_tensor(out=o1.ap, in0=t1.ap, in1=t2.ap, op=ALU.subtract), reads=[t1, t2], writes=[o1])
                        S.op("dve", lambda e, x1=x1, t1=t1, tf=tf: e.tensor_tensor(out=t1.ap, in0=x1, in1=tf["s"].ap, op=ALU.mult), reads=[r1, tf["s"], o1], writes=[t1])
                        S.op("dve", lambda e, x2=x2, t2=t2, tf=tf: e.tensor_tensor(out=t2.ap, in0=x2, in1=tf["c"].ap, op=ALU.mult), reads=[r2, tf["c"], o1], writes=[t2])
                        S.op("pool", lambda e, t1=t1, t2=t2, o2=o2: e.tensor_tensor(out=o2.ap, in0=t1.ap, in1=t2.ap, op=ALU.add), reads=[t1, t2], writes=[o2])
                        for half, o in ((0, o1), (1, o2)):
                            row0 = h * 256 + half * 128
                            if qk == 0:
                                st = stb.next()
                                S.op("act", lambda e, st=st, o=o: e.activation(out=st.ap, in_=o.ap, func=AF.Copy), reads=[o], writes=[st])
                                S.dma("sp", Dd["Q"][row0:row0 + 128, cols], st.ap, reads=[st], writes=[dres])
                                for d, nm in ((0, "QF"), (1, "QB")):
                                    st = stb.next()
                                    S.op("pool" if d == 0 else "dve", lambda e, st=st, o=o, d=d, h=h: e.tensor_tensor(out=st.ap, in0=o.ap, in1=qdt[d].ap[:, h * 512:h * 512 + TT],
                                                                                                              op=ALU.mult), reads=[o, qdt[d]], writes=[st])
                                    S.dma("sp", Dd[nm][row0:row0 + 128, cols], st.ap, reads=[st], writes=[dres])
                            else:
                                st = stb.next()
                                S.op("act", lambda e, st=st, o=o: e.activation(out=st.ap, in_=o.ap, func=AF.Copy), reads=[o], writes=[st])
                                S.dma("sp", Dd["KT"][row0:row0 + 128, cols], st.ap, reads=[st], writes=[dres])
                for s in (range(TT // 128) if part == 1 else ()):
                    ti = tt * (TT // 128) + s
                    rows = slice(ti * 128, (ti + 1) * 128)
                    tT = tabT.next()
                    S.dma("act", tT["c"].ap, tabs["cT"][rows, :], writes=[tT["c"]])
                    S.dma("act", tT["s"].ap, tabs["sT"][rows, :], writes=[tT["s"]])
                    cT3 = tT["c"].ap.rearrange("p (a c) -> p a c", a=2); sT3 = tT["s"].ap.rearrange("p (a c) -> p a c", a=2)
                    ks = kst.next()

                    def tok_mm(col0, bank):
                        pt = self.psum(bank)
                        for kk in range(8):
                            S.op("pe", lambda e, kk=kk, pt=pt, s=s, h3=h3, col0=col0: e.matmul(pt, lhsT=h3[:, kk, s * 128:(s + 1) * 128],
                                                                                          rhs=w3[:, kk, col0:col0 + 512], start=(kk == 0), stop=(kk == 7)),
                                 reads=[hT, w_tl], writes=[self.ps_res[bank]])
                        return pt
                    for g in range(2):
                        bank = 4 + cnt[1] % 2
                        cnt[1] += 1
                        pt = tok_mm(g * 512, bank)
                        pr = self.ps_res[bank]
                        pv = pt.rearrange("p (a b c) -> p a b c", a=2, b=2)
                        x1, x2 = pv[:, :, 0, :], pv[:, :, 1, :]
                        t = tk.next()
                        u1, u2, ko = t["u1"], t["u2"], t["ko"]
                        u13 = u1.ap.rearrange("p (a c) -> p a c", a=2); u23 = u2.ap.rearrange("p (a c) -> p a c", a=2)
                        ko4 = ko.ap.rearrange("p (a b c) -> p a b c", a=2, b=2)
                        S.op("dve", lambda e, x1=x1, u13=u13, cT3=cT3: e.tensor_tensor(out=u13, in0=x1, in1=cT3, op=ALU.mult), reads=[pr, tT["c"]], writes=[u1])
                        S.op("dve", lambda e, x2=x2, u23=u23, sT3=sT3: e.tensor_tensor(out=u23, in0=x2, in1=sT3, op=ALU.mult), reads=[pr, tT["s"]], writes=[u2])
                        S.op("pool", lambda e, u13=u13, u23=u23, ko4=ko4: e.tensor_tensor(out=ko4[:, :, 0, :], in0=u13, in1=u23, op=ALU.subtract), reads=[u1, u2], writes=[ko])
                        S.op("dve", lambda e, x1=x1, u13=u13, sT3=sT3: e.tensor_tensor(out=u13, in0=x1, in1=sT3, op=ALU.mult), reads=[pr, tT["s"], ko], writes=[u1])
                        S.op("dve", lambda e, x2=x2, u23=u23, cT3=cT3: e.tensor_tensor(out=u23, in0=x2, in1=cT3, op=ALU.mult), reads=[pr, tT["c"], ko], writes=[u2])
                        S.op("pool", lambda e, u13=u13, u23=u23, ko4=ko4: e.tensor_tensor(out=ko4[:, :, 1, :], in0=u13, in1=u23, op=ALU.add), reads=[u1, u2], writes=[ko])
                        S.op("pool", lambda e, ko=ko, ks=ks, g=g: e.tensor_tensor(out=ks["f"].ap[:, g * 512:(g + 1) * 512], in0=ko.ap, in1=dect[0].ap[:, g * 512:(g + 1) * 512], op=ALU.mult),
                             reads=[ko, dect[0]], writes=[ks["f"]])
                        S.op("dve", lambda e, ko=ko, ks=ks, g=g: e.tensor_tensor(out=ks["b"].ap[:, g * 512:(g + 1) * 512], in0=ko.ap, in1=dect[1].ap[:, g * 512:(g + 1) * 512], op=ALU.mult),
                             reads=[ko, dect[1]], writes=[ks["b"]])
                    S.dma("sp", Dd["KF"][rows, :], ks["f"].ap, reads=[ks["f"]], writes=[dres])
                    S.dma("sp", Dd["KB"][rows, :], ks["b"].ap, reads=[ks["b"]], writes=[dres])
                    vs = vst.next(); gs_ = gst.next()
                    for g in range(4):
                        bank = 4 + cnt[1] % 2
                        cnt[1] += 1
                        pt = tok_mm(1024 + g * 512, bank)
                        S.op("act", lambda e, pt=pt, vs=vs, g=g: e.activation(out=vs.ap[:, g * 512:(g + 1) * 512], in_=pt, func=AF.Copy), reads=[self.ps_res[bank]], writes=[vs])
                    S.dma("sp", Dd["V"][rows, :], vs.ap, reads=[vs], writes=[dres])
                    for g in range(4):
                        bank = 4 + cnt[1] % 2
                        cnt[1] += 1
                        pt = tok_mm(3072 + g * 512, bank)
                        S.op("act", lambda e, pt=pt, gs_=gs_, g=g: e.activation(out=gs_.ap[:, g * 512:(g + 1) * 512], in_=pt, func=AF.Silu), reads=[self.ps_res[bank]], writes=[gs_])
                    S.dma("sp", Dd["G"][rows, :], gs_.ap, reads=[gs_], writes=[dres])

        for part, (c0, ncol) in enumerate(((0, 2048), (1024, 5120))):
            A.top = base_mark
            w_tl = Tl(A.bf16(8 * ncol))
            w3 = w_tl.ap.rearrange("p (k n) -> p k n", k=8)
            wmark = A.top
            stage = Ring([Tl(A.f32(2048)) for _ in range(2)])
            self.load_w_bf16(W["ret_w_in"], 1024, 6144, w3, w_tl, col0=c0, ncols=ncol, stage_ring=stage)
            S.barrier(); A.top = wmark
            ret_proj(c_src, self.c_res, TC, 1, tabs_c, DC, part, w3, w_tl)
            S.barrier(); A.top = wmark
            ret_proj(x_src, self.x_res, T, 0, tabs_x, DX, part, w3, w_tl)
            S.barrier()
        A.top = base_mark

        Sst = [[Tl(A.f32(512)) for _ in range(8)] for _ in range(2)]
        Sbf = [[Tl(A.bf16(512)) for _ in range(8)] for _ in range(2)]
        for d in range(2):
            for hd in range(8):
                S.op("pool", lambda e, d=d, hd=hd: e.memset(Sst[d][hd].ap, 0.0), writes=[Sst[d][hd]])
                S.op("pool", lambda e, d=d, hd=hd: e.memset(Sbf[d][hd].ap, 0.0), writes=[Sbf[d][hd]])
        pmark = A.top
        ucnt = [0]

        def state_update(d, kt_, vt_):
            for h in range(4):
                for dc in range(2):
                    hd = h * 2 + dc
                    bank = 6 + ucnt[0] % 2
                    ucnt[0] += 1
                    pt = self.psum(bank)
                    S.op("pe", lambda e, pt=pt, h=h, dc=dc, kt_=kt_, vt_=vt_: e.matmul(pt, lhsT=kt_.ap[:, h * 256 + dc * 128:h * 256 + (dc + 1) * 128],
                                                                                   rhs=vt_.ap[:, h * 512:(h + 1) * 512], start=True, stop=True),
                         reads=[kt_, vt_], writes=[self.ps_res[bank]])
                    st = Sst[d][hd]
                    S.op("dve", lambda e, pt=pt, st=st, d=d, h=h: e.scalar_tensor_tensor(out=st.ap, in0=st.ap, scalar=blk.ap[:, d * 4 + h:d * 4 + h + 1], in1=pt,
                                                                                    op0=ALU.mult, op1=ALU.add), reads=[st, blk, self.ps_res[bank]], writes=[st])
                    S.op("act", lambda e, st=st, d=d, hd=hd: e.activation(out=Sbf[d][hd].ap, in_=st.ap, func=AF.Copy), reads=[st], writes=[Sbf[d][hd]])

        def passB(Dd, Tn):
            nch = Tn // 128
            ring = Ring([{"q": Tl(A.bf16(1024)), "k": Tl(A.bf16(1024)), "v": Tl(A.bf16(2048)), "ob": Tl(A.f32(2048))} for _ in range(2)])
            bc = 0
            for c in range(nch - 1, -1, -1):
                r = ring.next()
                q, k, v, ob = r["q"], r["k"], r["v"], r["ob"]
                cs = slice(c * 128, (c + 1) * 128)
                q3 = q.ap.rearrange("p (a t) -> p a t", a=8)
                S.dma("sp", q3, Dd["QB"].rearrange("(a p) t -> p a t", p=128)[:, :, cs], reads=[Dd["res"]], writes=[q])
                S.dma("act", k.ap, Dd["KB"][cs, :], reads=[Dd["res"]], writes=[k])
                S.dma("sp", v.ap, Dd["V"][cs, :], reads=[Dd["res"]], writes=[v])
                for h in range(4):
                    bank = bc % 4
                    bc += 1
                    pt = self.psum(bank)
                    for dc in range(2):
                        hd = h * 2 + dc
                        S.op("pe", lambda e, pt=pt, hd=hd, dc=dc, q3=q3: e.matmul(pt, lhsT=q3[:, hd, :], rhs=Sbf[1][hd].ap, start=(dc == 0), stop=(dc == 1)),
                             reads=[q, Sbf[1][hd]], writes=[self.ps_res[bank]])
                    if h % 2 == 0:
                        S.op("act", lambda e, pt=pt, ob=ob, h=h: e.activation(out=ob.ap[:, h * 512:(h + 1) * 512], in_=pt, func=AF.Copy), reads=[self.ps_res[bank]], writes=[ob])
                    else:
                        S.op("dve", lambda e, pt=pt, ob=ob, h=h: e.tensor_copy(out=ob.ap[:, h * 512:(h + 1) * 512], in_=pt), reads=[self.ps_res[bank]], writes=[ob])
                S.dma("sp", Dd["OB"][cs, :], ob.ap, reads=[ob], writes=[Dd["res"]])
                state_update(1, k, v)

        def passF(Dd, Tn, want_y):
            nch = Tn // 128
            ring = Ring([{"q": Tl(A.bf16(1024)), "qf": Tl(A.bf16(1024)), "kt": Tl(A.bf16(1024)), "k": Tl(A.bf16(1024)), "v": Tl(A.bf16(2048)),
                          "g": Tl(A.bf16(2048)), "ob": Tl(A.f32(2048)), "o": Tl(A.f32(2048)), "y": Tl(A.bf16(2048)), "yT": Tl(A.bf16(2048)),
                          "ss": Tl(A.f32(4)), "junk": Tl(A.bf16(512))} for _ in range(2)])
            pTr = Ring([Tl(A.bf16(128)) for _ in range(3)])
            bs = bo = 0
            for c in range(nch):
                r = ring.next()
                q, qf, kt, k, v, g, ob, o, y, yT, ss, junk = (r[n_] for n_ in ("q", "qf", "kt", "k", "v", "g", "ob", "o", "y", "yT", "ss", "junk"))
                cs = slice(c * 128, (c + 1) * 128)
                q3 = q.ap.rearrange("p (a t) -> p a t", a=8); qf3 = qf.ap.rearrange("p (a t) -> p a t", a=8)
                kt3 = kt.ap.rearrange("p (a t) -> p a t", a=8)
                fm = lambda nm: Dd[nm].rearrange("(a p) t -> p a t", p=128)[:, :, cs]
                S.dma("sp", q3, fm("Q"), reads=[Dd["res"]], writes=[q])
                S.dma("act", qf3, fm("QF"), reads=[Dd["res"]], writes=[qf])
                S.dma("sp", kt3, fm("KT"), reads=[Dd["res"]], writes=[kt])
                S.dma("act", k.ap, Dd["KF"][cs, :], reads=[Dd["res"]], writes=[k])
                S.dma("sp", v.ap, Dd["V"][cs, :], reads=[Dd["res"]], writes=[v])
                if want_y:
                    S.dma("act", g.ap, Dd["G"][cs, :], reads=[Dd["res"]], writes=[g])
                    S.dma("sp", ob.ap, Dd["OB"][cs, :], reads=[Dd["res"]], writes=[ob])
                    for h in range(4):
                        bS = bs % 2
                        bs += 1
                        pS = self.psum(bS)[:, 0:128]
                        for dc in range(2):
                            hd = h * 2 + dc
                            S.op("pe", lambda e, pS=pS, hd=hd, dc=dc, kt3=kt3, q3=q3: e.matmul(pS, lhsT=kt3[:, hd, :], rhs=q3[:, hd, :], start=(dc == 0), stop=(dc == 1)),
                                 reads=[kt, q], writes=[self.ps_res[bS]])
                        pT = pTr.next()
                        S.op("dve", lambda e, pS=pS, pT=pT, h=h: e.tensor_tensor(out=pT.ap, in0=pS, in1=maskT.ap[:, h * 128:(h + 1) * 128], op=ALU.mult),
                             reads=[self.ps_res[bS], maskT], writes=[pT])
                        bO = 2 + bo % 2
                        bo += 1
                        pO = self.psum(bO)
                        S.op("pe", lambda e, pO=pO, pT=pT, v=v, h=h: e.matmul(pO, lhsT=pT.ap, rhs=v.ap[:, h * 512:(h + 1) * 512], start=True, stop=False),
                             reads=[pT, v], writes=[self.ps_res[bO]])
                        for dc in range(2):
                            hd = h * 2 + dc
                            S.op("pe", lambda e, pO=pO, hd=hd, dc=dc, qf3=qf3: e.matmul(pO, lhsT=qf3[:, hd, :], rhs=Sbf[0][hd].ap, start=False, stop=(dc == 1)),
                                 reads=[qf, Sbf[0][hd]], writes=[self.ps_res[bO]])
                        S.op("dve", lambda e, pO=pO, o=o, ob=ob, h=h: e.tensor_tensor(out=o.ap[:, h * 512:(h + 1) * 512], in0=pO, in1=ob.ap[:, h * 512:(h + 1) * 512], op=ALU.add),
                             reads=[self.ps_res[bO], ob], writes=[o])
                        S.op("act", lambda e, o=o, junk=junk, ss=ss, h=h: e.activation(out=junk.ap, in_=o.ap[:, h * 512:(h + 1) * 512], func=AF.Square, accum_out=ss.ap[:, h:h + 1]),
                             reads=[o], writes=[junk, ss])
                    S.op("dve", lambda e, ss=ss: e.tensor_scalar(out=ss.ap, in0=ss.ap, scalar1=1.0 / 512.0, scalar2=EPS, op0=ALU.mult, op1=ALU.add), reads=[ss], writes=[ss])
                    S.op("act", lambda e, ss=ss: e.activation(out=ss.ap, in_=ss.ap, func=AF.Sqrt), reads=[ss], writes=[ss])
                    S.op("dve", lambda e, ss=ss: e.reciprocal(out=ss.ap, in_=ss.ap), reads=[ss], writes=[ss])
                    for h in range(4):
                        S.op("dve", lambda e, o=o, y=y, g=g, ss=ss, h=h: e.scalar_tensor_tensor(
                            out=y.ap[:, h * 512:(h + 1) * 512], in0=o.ap[:, h * 512:(h + 1) * 512], scalar=ss.ap[:, h:h + 1], in1=g.ap[:, h * 512:(h + 1) * 512],
                            op0=ALU.mult, op1=ALU.mult), reads=[o, ss, g], writes=[y])
                    for half in range(2):
                        bank = 4 + half
                        pb = self.psum(bank, 1, BF16)
                        for j in range(8):
                            jj_ = half * 8 + j
                            S.op("pe", lambda e, pb=pb, j=j, jj_=jj_, y=y: e.transpose(out=pb[:, j * 128:(j + 1) * 128], in_=y.ap[:, jj_ * 128:(jj_ + 1) * 128], identity=self.ident.ap),
                                 reads=[y, self.ident], writes=[self.ps_res[bank]])
                        if half == 0:
                            S.op("act", lambda e, pb=pb, yT=yT: e.activation(out=yT.ap[:, 0:1024], in_=pb, func=AF.Copy), reads=[self.ps_res[bank]], writes=[yT])
                        else:
                            S.op("dve", lambda e, pb=pb, yT=yT: e.tensor_copy(out=yT.ap[:, 1024:2048], in_=pb), reads=[self.ps_res[bank]], writes=[yT])
                    S.dma("sp", Dd["Y"].rearrange("(a p) t -> p a t", p=128)[:, :, cs], yT.ap.rearrange("p (a t) -> p a t", a=16), reads=[yT], writes=[self.Y_res])
                state_update(0, k, v)

        passB(DC, TC)
        S.barrier(); A.top = pmark
        passF(DC, TC, self.need_ctx)
        S.barrier(); A.top = pmark
        passB(DX, T)
        S.barrier(); A.top = pmark
        passF(DX, T, True)
        S.barrier(); A.top = base_mark
        wo = Tl(A.bf16(16 * 1024))
        wo3 = wo.ap.rearrange("p (k n) -> p k n", k=16)
        stage = Ring([Tl(A.f32(2048)) for _ in range(3)])
        self.load_w_bf16(W["ret_w_out"], 2048, 1024, wo3, wo, stage_ring=stage)
        self.out_proj(DX["Y"], 2048, wo3, wo, T, self.gt_bc, x_src, self.xs, self.x_res)
        if self.need_ctx:
            self.out_proj(DC["Y"], 2048, wo3, wo, TC, self.gtc_bc, c_src, self.cs, self.c_res)

    def mix_hy(self, x_src, c_src):
        S, A, W, T, TC = self.S, self.A, self.W, self.T, self.TC
        f32 = np.float32
        bf = ml_dtypes.bfloat16
        N = 16384
        TWO_PI = 2.0 * math.pi
        n_ = np.arange(128, dtype=np.float64)
        Fc = np.exp(-2j * np.pi * np.outer(n_, n_) / 128.0)
        Fre, Fim = Fc.real, Fc.imag
        twc = np.exp(-2j * np.pi * np.outer(n_, n_) / N)
        FS = np.concatenate([Fre, Fim], axis=1)
        cmat = np.concatenate([Fre, Fim, -Fim, Fre, -Fim, Fim, Fre, Fre / N, Fim / N], axis=1)
        cmat_d = self.const("c_hy_cmat", cmat.astype(bf))
        tw_d = self.const("c_hy_tw", np.concatenate([np.tile(twc.real, (1, 4)), np.tile(twc.imag, (1, 4))], axis=1).astype(f32))

        def fs_rows(L):
            M = L // 128
            rows = list(range(M)) + ([] if M == 64 else [])
            kr = list(range(M)) + list(range(128 - M, 128))
            return FS[:M].astype(bf), FS[kr].astype(bf)
        fss_x, fsk_x = fs_rows(T); fss_c, fsk_c = fs_rows(TC)
        fs_d = {"sx": self.const("c_hy_fssx", fss_x), "kx": self.const("c_hy_fskx", fsk_x),
                "sc": self.const("c_hy_fssc", fss_c), "kc": self.const("c_hy_fskc", fsk_c)}

        def zfeat(L):
            bands = 16
            t = np.linspace(0.0, 1.0, L, dtype=f32)[:, None]
            w = (f32(2.0 * math.pi) * np.arange(L, dtype=f32)[:, None] / f32(L)).astype(f32)
            fr = np.linspace(1e-4, bands - 1, bands, dtype=f32)[None, :]
            arg = (fr * w).astype(f32).astype(np.float64)
            z = np.concatenate([t, np.cos(arg).astype(f32), -np.sin(arg).astype(f32)], axis=-1)
            return np.ascontiguousarray(z.T.astype(f32)), np.ascontiguousarray(np.tile(t.T, (128, 1)).astype(f32))
        zT_x, tn_x = zfeat(T); zT_c, tn_c = zfeat(TC)
        zt_d = {"x": self.const("c_hy_zTx", zT_x), "c": self.const("c_hy_zTc", zT_c)}
        tn_d = {"x": self.const("c_hy_tnx", tn_x), "c": self.const("c_hy_tnc", tn_c)}
        deltas = np.abs(np.linspace(math.log(1e-2) / 1.5, math.log(1e-2) / 0.3, 1024, dtype=f32))
        nd_d = self.const("c_hy_ndelta", np.ascontiguousarray((-deltas).reshape(8, 128).T.astype(f32)))

        U = {"x": self.scratch("hy_Ux", [4096, T]), "c": self.scratch("hy_Uc", [4096, TC])}
        Z = {"x": self.scratch("hy_Zx", [3072, T]), "c": self.scratch("hy_Zc", [3072, TC])}
        KERN = {"x": self.scratch("hy_KERNx", [2, 1024, N]), "c": self.scratch("hy_KERNc", [2, 1024, N])}
        KF = {"x": self.scratch("hy_KFx", [2, 128, 1024, 256]), "c": self.scratch("hy_KFc", [2, 128, 1024, 256])}
        Y = {"x": self.scratch("hy_Yx", [1024, T], BF16), "c": self.scratch("hy_Yc", [1024, TC], BF16)}
        Ures = {"x": Res(), "c": Res()}
        dres = Res()
        self.Y_res = Res()
        LEN = {"x": T, "c": TC}
        seqs = ["c", "x"] if self.need_ctx else ["c", "x"]

        w_tl = Tl(A.bf16(8 * 4096))
        w3 = w_tl.ap.rearrange("p (k n) -> p k n", k=8)
        stage = Ring([Tl(A.f32(2048)) for _ in range(3)])
        self.load_w_bf16(W["hy_w_in"], 1024, 4096, w3, w_tl, stage_ring=stage)
        mark = A.top
        self.proj_to_dram(c_src, self.c_res, TC, 1, w3, w_tl, 32, U["c"], Ures["c"], min(512, TC))
        A.top = mark
        S.barrier()
        self.proj_to_dram(x_src, self.x_res, T, 0, w3, w_tl, 32, U["x"], Ures["x"], min(512, T))
        S.barrier(); A.reset()

        def col(src_ap, n):
            t = Tl(A.f32(n))
            S.dma("sp", t.ap, src_ap.rearrange("(n p) -> p n", p=128), writes=[t], slow=True)
            return t
        cw = [col(W["hy_conv_w"][k], 24) for k in range(3)]
        cb = col(W["hy_conv_b"], 24)
        TM = max(T, TC)
        ur = Ring([Tl(A.f32(TM + 2)) for _ in range(2)])
        zr = Ring([Tl(A.f32(TM)) for _ in range(2)])
        for sq in ("c", "x"):
            Tn = LEN[sq]
            for n in range(24):
                ub = ur.next(); zb = zr.next()
                S.op("pool", lambda e, ub=ub: e.memset(ub.ap[:, 0:1], 0.0), writes=[ub])
                S.op("pool", lambda e, ub=ub, Tn=Tn: e.memset(ub.ap[:, Tn + 1:Tn + 2], 0.0), writes=[ub])
                S.dma("sp", ub.ap[:, 1:Tn + 1], U[sq][n * 128:(n + 1) * 128, :], reads=[Ures[sq]], writes=[ub])
                S.op("dve", lambda e, ub=ub, zb=zb, n=n, Tn=Tn: e.tensor_scalar(out=zb.ap[:, 0:Tn], in0=ub.ap[:, 0:Tn], scalar1=cw[0].ap[:, n:n + 1],
                                                                         scalar2=cb.ap[:, n:n + 1], op0=ALU.mult, op1=ALU.add),
                     reads=[ub, cw[0], cb], writes=[zb])
                for k in (1, 2):
                    S.op("dve", lambda e, ub=ub, zb=zb, n=n, k=k, Tn=Tn: e.scalar_tensor_tensor(out=zb.ap[:, 0:Tn], in0=ub.ap[:, k:k + Tn], scalar=cw[k].ap[:, n:n + 1],
                                                                                      in1=zb.ap[:, 0:Tn], op0=ALU.mult, op1=ALU.add),
                         reads=[ub, cw[k], zb], writes=[zb])
                S.dma("act", Z[sq][n * 128:(n + 1) * 128, :], zb.ap[:, 0:Tn], reads=[zb], writes=[dres])
        S.barrier(); A.reset()

        fw1 = Tl(A.f32(64)); fw2 = Tl(A.f32(64)); fw3 = Tl(A.f32(4096))
        S.dma("sp", fw1.ap[0:33, :], W["hy_fw1"], writes=[fw1])
        S.dma("sp", fw2.ap[0:64, :], W["hy_fw2"], writes=[fw2])
        S.dma("sp", fw3.ap[0:64, :], W["hy_fw3"], writes=[fw3])
        sm = Tl(A.f32(8))
        S.dma("sp", sm.ap[0:64, 0:1], W["hy_freq"].rearrange("(p o) -> p o", o=1), writes=[sm])
        S.dma("sp", sm.ap[0:64, 1:2], W["hy_fb1"].rearrange("(p o) -> p o", o=1), writes=[sm])
        S.dma("sp", sm.ap[0:64, 2:3], W["hy_fb2"].rearrange("(p o) -> p o", o=1), writes=[sm])
        OFF = math.pi + TWO_PI * 16
        for i in range(2):
            S.op("dve", lambda e, i=i: e.tensor_scalar(out=sm.ap[0:64, 3 + i:4 + i], in0=sm.ap[0:64, 1 + i:2 + i], scalar1=sm.ap[0:64, 0:1], scalar2=None, op0=ALU.mult),
                 reads=[sm], writes=[sm])
        nd = Tl(A.f32(8))
        S.dma("sp", nd.ap, nd_d, writes=[nd])
        skc = Tl(A.f32(16))
        S.dma("sp", skc.ap.rearrange("p (o n) -> p o n", o=2), W["hy_skip"].rearrange("o (n p) -> p o n", p=128), writes=[skc], slow=True)
        zero = Tl(A.f32(1))
        S.op("pool", lambda e: e.memset(zero.ap, 0.0), writes=[zero])
        MAGIC = 12582912.0
        hidT = Tl(A.f32(TM))
        f0 = Tl(A.f32(TM)); f1 = Tl(A.f32(TM)); rev = Tl(A.f32(TM))
        ztr = Ring([Tl(A.f32(512)) for _ in range(2)])
        h1r = Ring([{"a": Tl(A.f32(512)), "r": Tl(A.f32(512))} for _ in range(2)])
        tnr = Ring([Tl(A.f32(512)) for _ in range(2)])
        decr = Ring([Tl(A.f32(512)) for _ in range(2)])
        sums = Tl(A.f32(4))

        def sin_layer(pt, bias_col, dst_ap, TW, hb):
            a, r = hb["a"], hb["r"]
            S.op("dve", lambda e: e.tensor_scalar(out=a.ap[0:64, 0:TW], in0=pt, scalar1=sm.ap[0:64, 0:1], scalar2=sm.ap[0:64, bias_col:bias_col + 1],
                                                  op0=ALU.mult, op1=ALU.add), reads=[self.ps_res[0], sm], writes=[a])
            S.op("dve", lambda e: e.tensor_scalar(out=r.ap[0:64, 0:TW], in0=a.ap[0:64, 0:TW], scalar1=1.0 / TWO_PI, scalar2=MAGIC, op0=ALU.mult, op1=ALU.add),
                 reads=[a], writes=[r])
            S.op("dve", lambda e: e.tensor_scalar(out=r.ap[0:64, 0:TW], in0=r.ap[0:64, 0:TW], scalar1=-MAGIC, scalar2=None, op0=ALU.add), reads=[r], writes=[r])
            S.op("dve", lambda e: e.scalar_tensor_tensor(out=a.ap[0:64, 0:TW], in0=r.ap[0:64, 0:TW], scalar=-TWO_PI, in1=a.ap[0:64, 0:TW], op0=ALU.mult, op1=ALU.add),
                 reads=[r, a], writes=[a])
            S.op("act", lambda e: e.activation(out=dst_ap, in_=a.ap[0:64, 0:TW], func=AF.Sin), reads=[a], writes=[hidT, hb["r"]])

        for sq in ("c", "x"):
            L = LEN[sq]
            M = L // 128
            TW = min(512, L)
            for tt in range(L // TW):
                cs = slice(tt * TW, (tt + 1) * TW)
                zt = ztr.next(); hb = h1r.next()
                S.dma("sp", zt.ap[0:33, 0:TW], zt_d[sq][:, cs], writes=[zt])
                pt = self.psum(0)[0:64, 0:TW]
                S.op("pe", lambda e, pt=pt, zt=zt, TW=TW: e.matmul(pt, lhsT=fw1.ap[0:33, 0:64], rhs=zt.ap[0:33, 0:TW], start=True, stop=True),
                     reads=[fw1, zt], writes=[self.ps_res[0]])
                h1 = hb["r"]
                sin_layer(pt, 3, h1.ap[0:64, 0:TW], TW, hb)
                S.op("pe", lambda e, pt=pt, h1=h1, TW=TW: e.matmul(pt, lhsT=fw2.ap[0:64, 0:64], rhs=h1.ap[0:64, 0:TW], start=True, stop=True),
                     reads=[fw2, h1], writes=[self.ps_res[0]])
                hb2 = h1r.next()
                sin_layer(pt, 4, hidT.ap[0:64, cs], TW, hb2)
            for o in range(2):
                for cc in range(8):
                    fd = (f0, f1)
                    for dr in range(2):
                        j = o * 16 + dr * 8 + cc
                        for tt in range(L // TW):
                            cs = slice(tt * TW, (tt + 1) * TW)
                            b = 1 + (tt % 2)
                            pt = self.psum(b)[:, 0:TW]
                            S.op("pe", lambda e, pt=pt, j=j, cs=cs: e.matmul(pt, lhsT=fw3.ap[0:64, j * 128:(j + 1) * 128], rhs=hidT.ap[0:64, cs], start=True, stop=True),
                                 reads=[fw3, hidT], writes=[self.ps_res[b]])
                            tnt = tnr.next(); dc_ = decr.next()
                            S.dma("sp", tnt.ap[:, 0:TW], tn_d[sq][:, cs], writes=[tnt])
                            S.op("act", lambda e, tnt=tnt, dc_=dc_, cc=cc, TW=TW: e.activation(out=dc_.ap[:, 0:TW], in_=tnt.ap[:, 0:TW], func=AF.Exp, scale=nd.ap[:, cc:cc + 1]),
                                 reads=[tnt, nd], writes=[dc_])
                            S.op("dve", lambda e, pt=pt, dc_=dc_, dr=dr, cs=cs, TW=TW, fd=fd: e.tensor_tensor(out=fd[dr].ap[:, cs], in0=pt, in1=dc_.ap[:, 0:TW], op=ALU.mult),
                                 reads=[self.ps_res[b], dc_], writes=[fd[dr]])
                        s0 = dr
                        S.op("act", lambda e, dr=dr, s0=s0, L=L, fd=fd: e.activation(out=rev.ap[:, s0:L], in_=fd[dr].ap[:, s0:L], func=AF.Abs, accum_out=sums.ap[:, dr:dr + 1]),
                             reads=[fd[dr]], writes=[rev, sums])
                    S.op("dve", lambda e: e.tensor_tensor(out=sums.ap[:, 2:3], in0=sums.ap[:, 0:1], in1=sums.ap[:, 1:2], op=ALU.add), reads=[sums], writes=[sums])
                    S.op("dve", lambda e: e.reciprocal(out=sums.ap[:, 3:4], in_=sums.ap[:, 2:3]), reads=[sums], writes=[sums])
                    S.op("dve", lambda e, L=L: e.tensor_scalar(out=f0.ap[:, 0:L], in0=f0.ap[:, 0:L], scalar1=sums.ap[:, 3:4], scalar2=None, op0=ALU.mult),
                         reads=[f0, sums], writes=[f0])
                    S.op("dve", lambda e, o=o, cc=cc: e.tensor_scalar(out=f0.ap[:, 0:1], in0=f0.ap[:, 0:1], scalar1=skc.ap[:, o * 8 + cc:o * 8 + cc + 1], scalar2=None, op0=ALU.add),
                         reads=[f0, skc], writes=[f0])
                    rows = slice(cc * 128, (cc + 1) * 128)
                    S.dma("sp", KERN[sq][o, rows, 0:L], f0.ap[:, 0:L], reads=[f0], writes=[dres])
                    S.op("dve", lambda e, L=L: e.tensor_scalar(out=rev.ap[:, 0:L - 1], in0=f1.ap[:, 1:L][:, ::-1], scalar1=sums.ap[:, 3:4], scalar2=None, op0=ALU.mult),
                         reads=[f1, sums], writes=[rev])
                    S.dma("act", KERN[sq][o, rows, N - L + 1:N], rev.ap[:, 0:L - 1], reads=[rev], writes=[dres])
                    S.dma("sp", KERN[sq][o, rows, N - L:N - L + 1], zero.ap, reads=[zero], writes=[dres], slow=True)
        S.barrier(); A.reset()

        cm = Tl(A.bf16(9 * 128))
        S.dma("sp", cm.ap, cmat_d, writes=[cm])
        Fre_, Fim_, nFim_ = cm.ap[:, 0:128], cm.ap[:, 128:256], cm.ap[:, 256:384]
        GS1_, GS2_ = cm.ap[:, 384:640], cm.ap[:, 640:896]
        FreN_, FimN_ = cm.ap[:, 896:1024], cm.ap[:, 1024:1152]
        tw = Tl(A.f32(1024))
        S.dma("sp", tw.ap, tw_d, writes=[tw])
        twR = tw.ap[:, 0:512].rearrange("p (c k) -> p c k", c=4); twI = tw.ap[:, 512:1024].rearrange("p (c k) -> p c k", c=4)
        fs_t = {}
        for key, d in fs_d.items():
            t = Tl(A.bf16(256))
            R = d.shape[0]
            S.dma("sp", t.ap[0:R, :], d, writes=[t])
            fs_t[key] = (t, R)
        NCH = 2
        chs = []
        for ci in range(NCH):
            chs.append({"b": ci * 4,
                        "t": [Tl(A.f32(512)) for _ in range(4)],
                        "cre": Tl(A.bf16(512)), "cim": Tl(A.bf16(512)), "pre": Tl(A.bf16(512)), "pim": Tl(A.bf16(512)),
                        "zre": Tl(A.bf16(512)), "zim": Tl(A.bf16(512)),
                        "kf": Ring([Tl(A.f32(1024)) for _ in range(2)]),
                        "af": Ring([Tl(A.f32(512)) for _ in range(2)]), "ab": Ring([Tl(A.bf16(512)) for _ in range(2)]),
                        "x1": Ring([Tl(A.f32(512)) for _ in range(2)]), "x2": Ring([Tl(A.f32(512)) for _ in range(2)]),
                        "g": Ring([Tl(A.f32(512)) for _ in range(2)]), "y1": Ring([Tl(A.bf16(512)) for _ in range(2)]),
                        "y2": Tl(A.f32(512)), "yo": Ring([Tl(A.bf16(512)) for _ in range(2)]),
                        "kfo": Ring([Tl(A.f32(1024)) for _ in range(2)])})

        def v3(ap, c=4):
            return ap.rearrange("p (c k) -> p c k", c=c)

        def cmul(ch, src_re, src_im, rr, ri, are, aim, bre, bim, conj, out_re, out_im):
            t1, t2, t3, t4 = ch["t"]
            pr = [rr, ri]
            S.op("dve", lambda e: e.tensor_tensor(out=v3(t1.ap), in0=are, in1=bre, op=ALU.mult), reads=pr + src_re, writes=[t1])
            S.op("dve", lambda e: e.tensor_tensor(out=v3(t2.ap), in0=aim, in1=bim, op=ALU.mult), reads=pr + src_im, writes=[t2])
            S.op("pool", lambda e: e.tensor_tensor(out=out_re.ap, in0=t1.ap, in1=t2.ap, op=(ALU.add if conj else ALU.subtract)), reads=[t1, t2], writes=[out_re])
            S.op("dve", lambda e: e.tensor_tensor(out=v3(t3.ap), in0=aim, in1=bre, op=ALU.mult), reads=pr + src_re, writes=[t3])
            S.op("dve", lambda e: e.tensor_tensor(out=v3(t4.ap), in0=are, in1=bim, op=ALU.mult), reads=pr + src_im, writes=[t4])
            S.op("pool", lambda e: e.tensor_tensor(out=out_im.ap, in0=t3.ap, in1=t4.ap, op=(ALU.subtract if conj else ALU.add)), reads=[t3, t4], writes=[out_im])

        def fwd_fft(ch, a_tl, a3, R, fs):
            b = ch["b"]
            ps01 = self.psum(b, 2)
            for ci in range(4):
                S.op("pe", lambda e, ci=ci: e.matmul(ps01[:, ci * 256:(ci + 1) * 256], lhsT=a3[0:R, ci, :], rhs=fs.ap[0:R, :], start=True, stop=True),
                     reads=[a_tl, fs], writes=[self.ps_res[b + ci // 2]])
            yield
            p4 = ps01.rearrange("p (c r k) -> p c r k", c=4, r=2)
            rr = self.ps_res[b]; ri = self.ps_res[b + 1]
            cmul(ch, [tw], [tw], rr, ri, p4[:, :, 0, :], p4[:, :, 1, :], twR, twI, False, ch["cre"], ch["cim"])
            yield
            for bank, l1, l2 in ((b + 2, Fre_, nFim_), (b + 3, Fim_, Fre_)):
                pt = self.psum(bank)
                S.op("pe", lambda e, pt=pt, l1=l1: e.matmul(pt, lhsT=l1, rhs=ch["cre"].ap, start=True, stop=False), reads=[cm, ch["cre"]], writes=[self.ps_res[bank]])
                S.op("pe", lambda e, pt=pt, l2=l2: e.matmul(pt, lhsT=l2, rhs=ch["cim"].ap, start=False, stop=True), reads=[cm, ch["cim"]], writes=[self.ps_res[bank]])
            yield

        def inv_fft(ch, kf_tl, M):
            b = ch["b"]
            k4 = kf_tl.ap.rearrange("p (c r k) -> p c r k", c=4, r=2)
            cmul(ch, [kf_tl], [kf_tl], self.ps_res[b + 2], self.ps_res[b + 3], v3(self.psum(b + 2)), v3(self.psum(b + 3)), k4[:, :, 0, :], k4[:, :, 1, :],
                 False, ch["pre"], ch["pim"])
            yield
            ps01 = self.psum(b, 2)
            pre3, pim3 = v3(ch["pre"].ap), v3(ch["pim"].ap)
            for ci in range(4):
                S.op("pe", lambda e, ci=ci: e.matmul(ps01[:, ci * 256:(ci + 1) * 256], lhsT=pre3[:, ci, :], rhs=GS1_, start=True, stop=False),
                     reads=[ch["pre"], cm], writes=[self.ps_res[b + ci // 2]])
                S.op("pe", lambda e, ci=ci: e.matmul(ps01[:, ci * 256:(ci + 1) * 256], lhsT=pim3[:, ci, :], rhs=GS2_, start=False, stop=True),
                     reads=[ch["pim"], cm], writes=[self.ps_res[b + ci // 2]])
            yield
            p4 = ps01.rearrange("p (c r k) -> p c r k", c=4, r=2)
            cmul(ch, [tw], [tw], self.ps_res[b], self.ps_res[b + 1], p4[:, :, 0, :], p4[:, :, 1, :], twR, twI, True, ch["zre"], ch["zim"])
            yield
            pt = self.psum(b + 2)[0:M, :]
            S.op("pe", lambda e: e.matmul(pt, lhsT=FreN_[:, 0:M], rhs=ch["zre"].ap, start=True, stop=False), reads=[cm, ch["zre"]], writes=[self.ps_res[b + 2]])
            S.op("pe", lambda e: e.matmul(pt, lhsT=FimN_[:, 0:M], rhs=ch["zim"].ap, start=False, stop=True), reads=[cm, ch["zim"]], writes=[self.ps_res[b + 2]])
            yield

        def run_chains(gens):
            act = list(gens)
            while act:
                for g in list(act):
                    try:
                        next(g)
                    except StopIteration:
                        act.remove(g)

        def kern_chain(ch, sq, groups):
            L = LEN[sq]; M = L // 128
            fs, R = fs_t["k" + sq]
            b = ch["b"]
            for (o, g) in groups:
                c0 = g * 4
                af = ch["kfo"].next()
                a3f = af.ap[:, 0:512].rearrange("p (c k) -> p c k", c=4)
                src = KERN[sq][o, c0:c0 + 4, :].rearrange("c (n1 n2) -> n1 c n2", n2=128)
                S.dma("sp", a3f[0:M], src[0:M], reads=[dres], writes=[af])
                if R > M:
                    S.dma("act", a3f[M:2 * M], src[128 - M:128], reads=[dres], writes=[af])
                ab = ch["ab"].next()
                S.op("act", lambda e, af=af, ab=ab: e.activation(out=ab.ap[0:R, :], in_=af.ap[0:R, 0:512], func=AF.Copy), reads=[af], writes=[ab])
                yield from fwd_fft(ch, ab, v3(ab.ap), R, fs)
                ko = ch["kf"].next()
                k4 = ko.ap.rearrange("p (c r k) -> p c r k", c=4, r=2)
                S.op("act", lambda e, k4=k4: e.activation(out=k4[:, :, 0, :], in_=v3(self.psum(b + 2)), func=AF.Copy), reads=[self.ps_res[b + 2]], writes=[ko])
                S.op("act", lambda e, k4=k4: e.activation(out=k4[:, :, 1, :], in_=v3(self.psum(b + 3)), func=AF.Copy), reads=[self.ps_res[b + 3]], writes=[ko])
                S.dma("sp", KF[sq][o, :, c0:c0 + 4, :], ko.ap.rearrange("p (c x) -> p c x", c=4), reads=[ko], writes=[dres])
                yield

        for sq in ("c", "x"):
            allg = [(o, g) for o in range(2) for g in range(256)]
            run_chains([kern_chain(chs[i], sq, allg[i::NCH]) for i in range(NCH)])
        S.barrier()

        def conv_chain(ch, sq, groups):
            L = LEN[sq]; M = L // 128
            fs, R = fs_t["s" + sq]
            b = ch["b"]
            Zv = Z[sq].rearrange("c (n1 n2) -> n1 c n2", n2=128)
            Uv = U[sq].rearrange("c (n1 n2) -> n1 c n2", n2=128)
            Yv = Y[sq].rearrange("c (n1 n2) -> n1 c n2", n2=128)
            for g in groups:
                c0 = g * 4
                af = ch["af"].next(); x1 = ch["x1"].next(); x2 = ch["x2"].next(); gg = ch["g"].next(); ab = ch["ab"].next()
                S.dma("sp", v3(af.ap)[0:M], Zv[:, c0:c0 + 4, :], reads=[dres], writes=[af])
                S.dma("act", v3(x1.ap)[0:M], Zv[:, 1024 + c0:1024 + c0 + 4, :], reads=[dres], writes=[x1])
                S.dma("sp", v3(x2.ap)[0:M], Zv[:, 2048 + c0:2048 + c0 + 4, :], reads=[dres], writes=[x2])
                S.dma("act", v3(gg.ap)[0:M], Uv[:, 3072 + c0:3072 + c0 + 4, :], reads=[Ures[sq]], writes=[gg])
                S.op("act", lambda e, af=af, ab=ab: e.activation(out=ab.ap[0:M, :], in_=af.ap[0:M, :], func=AF.Copy), reads=[af], writes=[ab])
                S.op("act", lambda e, gg=gg: e.activation(out=gg.ap[0:M, :], in_=gg.ap[0:M, :], func=AF.Silu), reads=[gg], writes=[gg])
                kf0 = ch["kf"].next()
                S.dma("sp", kf0.ap.rearrange("p (c x) -> p c x", c=4), KF[sq][0, :, c0:c0 + 4, :], reads=[dres], writes=[kf0])
                yield from fwd_fft(ch, ab, v3(ab.ap), M, fs)
                yield from inv_fft(ch, kf0, M)
                y1 = ch["y1"].next()
                S.op("dve", lambda e, y1=y1, x1=x1: e.tensor_tensor(out=y1.ap[0:M, :], in0=self.psum(b + 2)[0:M, :], in1=x1.ap[0:M, :], op=ALU.mult),
                     reads=[self.ps_res[b + 2], x1], writes=[y1])
                kf1 = ch["kf"].next()
                S.dma("act", kf1.ap.rearrange("p (c x) -> p c x", c=4), KF[sq][1, :, c0:c0 + 4, :], reads=[dres], writes=[kf1])
                yield
                yield from fwd_fft(ch, y1, v3(y1.ap), M, fs)
                yield from inv_fft(ch, kf1, M)
                y2 = ch["y2"]; yo = ch["yo"].next()
                S.op("dve", lambda e, y2=y2, x2=x2: e.tensor_tensor(out=y2.ap[0:M, :], in0=self.psum(b + 2)[0:M, :], in1=x2.ap[0:M, :], op=ALU.mult),
                     reads=[self.ps_res[b + 2], x2], writes=[y2])
                S.op("pool", lambda e, y2=y2, yo=yo, gg=gg: e.tensor_tensor(out=yo.ap[0:M, :], in0=y2.ap[0:M, :], in1=gg.ap[0:M, :], op=ALU.mult),
                     reads=[y2, gg], writes=[yo])
                S.dma("sp", Yv[:, c0:c0 + 4, :], v3(yo.ap)[0:M], reads=[yo], writes=[self.Y_res])
                yield

        for sq in (("c", "x") if self.need_ctx else ("x",)):
            allg = list(range(256))
            run_chains([conv_chain(chs[i], sq, allg[i::NCH]) for i in range(NCH)])
        S.barrier(); A.reset()

        wo = Tl(A.bf16(8 * 1024))
        wo3 = wo.ap.rearrange("p (k n) -> p k n", k=8)
        stage = Ring([Tl(A.f32(2048)) for _ in range(3)])
        self.load_w_bf16(W["hy_w_out"], 1024, 1024, wo3, wo, stage_ring=stage)
        self.out_proj(Y["x"], 1024, wo3, wo, T, self.gt_bc, x_src, self.xs, self.x_res)
        if self.need_ctx:
            self.out_proj(Y["c"], 1024, wo3, wo, TC, self.gtc_bc, c_src, self.cs, self.c_res)

    def mix_att(self, x_src, c_src):
        S, A, W, T, TC = self.S, self.A, self.W, self.T, self.TC
        f32 = np.float32
        inv = (f32(10000.0) ** (-np.arange(32, dtype=f32) / f32(32))).astype(f32)
        rows = np.repeat(np.arange(T // GRID_W, dtype=f32), GRID_W); cols = np.tile(np.arange(GRID_W, dtype=f32), T // GRID_W)
        ang = np.concatenate([(rows[:, None] * inv[None, :]).astype(f32), (cols[:, None] * inv[None, :]).astype(f32)], axis=-1).astype(np.float64)
        cs_, sn_ = np.cos(ang).astype(f32).T, np.sin(ang).astype(f32).T
        tabs_x = (self.const("c_att_cos", np.concatenate([cs_, cs_], axis=0)), self.const("c_att_sin", np.concatenate([-sn_, sn_], axis=0)))
        tabs_c = (self.const("c_att_cosc", np.ones((128, TC), f32)), self.const("c_att_sinc", np.zeros((128, TC), f32)))
        QT = self.scratch("att_QT", [1024, T], BF16); GT = self.scratch("att_GT", [1024, T], BF16)
        Y = self.scratch("att_Y", [1024, T], BF16)
        dres = Res()
        self.Y_res = Res()
        NK = T + TC
        nkt = NK // 128
        KT = [Tl(A.bf16(NK)) for _ in range(2)]
        V = Tl(A.bf16(nkt * 256))
        V3 = V.ap.rearrange("p (k d) -> p k d", d=256)
        gq = Tl(A.f32(2))
        S.dma("sp", gq.ap[:, 0:1], W["att_q_norm_g"].rearrange("(p o) -> p o", o=1), writes=[gq])
        S.dma("sp", gq.ap[:, 1:2], W["att_k_norm_g"].rearrange("(p o) -> p o", o=1), writes=[gq])
        base_mark = A.top
        w_tl = Tl(A.bf16(8 * 2560))
        w3 = w_tl.ap.rearrange("p (k n) -> p k n", k=8)
        wmark = A.top
        stage = Ring([Tl(A.f32(2048)) for _ in range(2)])
        self.load_w_bf16(W["att_w_in"], 1024, 2560, w3, w_tl, stage_ring=stage)
        S.barrier(); A.top = wmark

        def att_proj(src, src_res, Tn, which, tabs, latent, key0):
            TT = min(512, Tn)
            xring, hring = self.make_norm_rings(TT, nh=2, nx=2)
            tab = Ring([{"c": Tl(A.f32(TT)), "s": Tl(A.f32(TT))} for _ in range(2)])
            tmp = Ring([{"sq": Tl(A.bf16(TT)), "rs": Tl(A.f32(TT)), "qn": Tl(A.f32(TT)), "sw": Tl(A.f32(TT)), "t1": Tl(A.f32(TT)), "t2": Tl(A.f32(TT))}
                        for _ in range(2)])
            stq = Ring([Tl(A.bf16(TT)) for _ in range(3)])
            cnt = [0, 0]
            for tt, hT, h3 in self.norm_tiles(src, src_res, Tn, which, xring, hring, None, TT):
                cols = slice(tt * TT, (tt + 1) * TT)
                tb = tab.next()
                S.dma("act", tb["c"].ap, tabs[0][:, cols], writes=[tb["c"]])
                S.dma("act", tb["s"].ap, tabs[1][:, cols], writes=[tb["s"]])
                heads = ([("q", i) for i in range(8)] if latent else []) + [("k", i) for i in range(2)]
                for kind, idx in heads:
                    oc = idx if kind == "q" else 8 + idx
                    gi = 0 if kind == "q" else 1
                    b = cnt[0] % 4
                    cnt[0] += 1
                    pt = self.psum(b)[:, 0:TT]
                    for kk in range(8):
                        S.op("pe", lambda e, kk=kk, oc=oc, pt=pt, h3=h3: e.matmul(pt, lhsT=w3[:, kk, oc * 128:(oc + 1) * 128], rhs=h3[:, kk, :],
                                                                              start=(kk == 0), stop=(kk == 7)), reads=[hT, w_tl], writes=[self.ps_res[b]])
                    tm = tmp.next()
                    sq, rs, qn, sw, t1, t2 = tm["sq"], tm["rs"], tm["qn"], tm["sw"], tm["t1"], tm["t2"]
                    S.op("act", lambda e, pt=pt, sq=sq: e.activation(out=sq.ap, in_=pt, func=AF.Square), reads=[self.ps_res[b]], writes=[sq])
                    b2 = 4 + cnt[1] % 2
                    cnt[1] += 1
                    p2 = self.psum(b2)[:, 0:TT]
                    S.op("pe", lambda e, p2=p2, sq=sq: e.matmul(p2, lhsT=self.ones_bf.ap, rhs=sq.ap, start=True, stop=True), reads=[sq, self.ones_bf], writes=[self.ps_res[b2]])
                    S.op("act", lambda e, p2=p2, rs=rs: e.activation(out=rs.ap, in_=p2, func=AF.Sqrt, scale=1.0 / 128.0, bias=self.eps_col.ap), reads=[self.ps_res[b2], self.eps_col], writes=[rs])
                    S.op("dve", lambda e, rs=rs: e.reciprocal(out=rs.ap, in_=rs.ap), reads=[rs], writes=[rs])
                    S.op("dve", lambda e, pt=pt, qn=qn, rs=rs, gi=gi: e.scalar_tensor_tensor(out=qn.ap, in0=pt, scalar=gq.ap[:, gi:gi + 1], in1=rs.ap, op0=ALU.mult, op1=ALU.mult),
                         reads=[self.ps_res[b], gq, rs], writes=[qn])
                    S.op("act", lambda e, qn=qn, sw=sw: e.activation(out=sw.ap[0:64, :], in_=qn.ap[64:128, :], func=AF.Copy), reads=[qn], writes=[sw])
                    S.op("pool", lambda e, qn=qn, sw=sw: e.tensor_copy(out=sw.ap[64:128, :], in_=qn.ap[0:64, :]), reads=[qn], writes=[sw])
                    S.op("pool", lambda e, qn=qn, t1=t1, tb=tb: e.tensor_tensor(out=t1.ap, in0=qn.ap, in1=tb["c"].ap, op=ALU.mult), reads=[qn, tb["c"]], writes=[t1])
                    S.op("dve", lambda e, sw=sw, t2=t2, tb=tb: e.tensor_tensor(out=t2.ap, in0=sw.ap, in1=tb["s"].ap, op=ALU.mult), reads=[sw, tb["s"]], writes=[t2])
                    if kind == "q":
                        st = stq.next()
                        S.op("pool", lambda e, t1=t1, t2=t2, st=st: e.tensor_tensor(out=st.ap, in0=t1.ap, in1=t2.ap, op=ALU.add), reads=[t1, t2], writes=[st])
                        S.dma("sp", QT[idx * 128:(idx + 1) * 128, cols], st.ap, reads=[st], writes=[dres])
                    else:
                        kdst = KT[idx].ap[:, key0 + tt * TT:key0 + (tt + 1) * TT]
                        S.op("pool", lambda e, t1=t1, t2=t2, kdst=kdst: e.tensor_tensor(out=kdst, in0=t1.ap, in1=t2.ap, op=ALU.add), reads=[t1, t2], writes=[KT[idx]])
                if latent:
                    for j in range(8):
                        b = cnt[0] % 4
                        cnt[0] += 1
                        pt = self.psum(b)[:, 0:TT]
                        for kk in range(8):
                            S.op("pe", lambda e, kk=kk, j=j, pt=pt, h3=h3: e.matmul(pt, lhsT=w3[:, kk, 1536 + j * 128:1536 + (j + 1) * 128], rhs=h3[:, kk, :],
                                                                                 start=(kk == 0), stop=(kk == 7)), reads=[hT, w_tl], writes=[self.ps_res[b]])
                        st = stq.next()
                        S.op("act", lambda e, pt=pt, st=st: e.activation(out=st.ap, in_=pt, func=AF.Silu), reads=[self.ps_res[b]], writes=[st])
                        S.dma("sp", GT[j * 128:(j + 1) * 128, cols], st.ap, reads=[st], writes=[dres])
                for s_ in range(TT // 128):
                    kti = (key0 + tt * TT) // 128 + s_
                    b2 = 6 + cnt[1] % 2
                    cnt[1] += 1
                    pv = self.psum(b2)[:, 0:256]
                    for kk in range(8):
                        S.op("pe", lambda e, kk=kk, pv=pv, h3=h3, s_=s_: e.matmul(pv, lhsT=h3[:, kk, s_ * 128:(s_ + 1) * 128], rhs=w3[:, kk, 1280:1536],
                                                                              start=(kk == 0), stop=(kk == 7)), reads=[hT, w_tl], writes=[self.ps_res[b2]])
                    S.op("act", lambda e, pv=pv, kti=kti: e.activation(out=V3[:, kti, :], in_=pv, func=AF.Copy), reads=[self.ps_res[b2]], writes=[V])

        att_proj(c_src, self.c_res, TC, 1, tabs_c, False, T)
        S.barrier(); A.top = wmark
        att_proj(x_src, self.x_res, T, 0, tabs_x, True, 0)
        S.barrier(); A.top = base_mark

        scale = 1.0 / math.sqrt(128.0)
        qr = Ring([{"q": Tl(A.bf16(1024)), "g": Tl(A.bf16(1024)), "y": Tl(A.bf16(1024))} for _ in range(2)])
        pTr = Ring([Tl(A.bf16(512)) for _ in range(4)])
        accs = Ring([{"d": Tl(A.f32(512)), "p": Tl(A.f32(512)), "rs": Tl(A.f32(512)), "o": Tl(A.f32(512))} for _ in range(2)])
        QTv = QT.rearrange("(h p) t -> p h t", p=128); GTv = GT.rearrange("(h p) t -> p h t", p=128); Yv = Y.rearrange("(h p) t -> p h t", p=128)
        gcnt = 0
        for qt in range(T // 128):
            cs = slice(qt * 128, (qt + 1) * 128)
            qb = qr.next()
            q, gt_, y = qb["q"], qb["g"], qb["y"]
            q3 = q.ap.rearrange("p (h t) -> p h t", h=8); g3 = gt_.ap.rearrange("p (h t) -> p h t", h=8)
            S.dma("sp", q3, QTv[:, :, cs], reads=[dres], writes=[q])
            S.dma("act", g3, GTv[:, :, cs], reads=[dres], writes=[gt_])
            for g in range(2):
                q2 = q.ap[:, g * 512:(g + 1) * 512]
                bO = 4 + gcnt % 2
                bSum = 6 + gcnt % 2
                gcnt += 1
                pO = self.psum(bO)
                ac = accs.next()
                for kt in range(nkt):
                    bS = kt % 4
                    pS = self.psum(bS)
                    S.op("pe", lambda e, pS=pS, g=g, kt=kt, q2=q2: e.matmul(pS, lhsT=KT[g].ap[:, kt * 128:(kt + 1) * 128], rhs=q2, start=True, stop=True),
                         reads=[KT[g], q], writes=[self.ps_res[bS]])
                    pT = pTr.next()
                    S.op("act", lambda e, pS=pS, pT=pT: e.activation(out=pT.ap, in_=pS, func=AF.Exp, scale=scale), reads=[self.ps_res[bS]], writes=[pT])
                    S.op("pe", lambda e, pO=pO, g=g, kt=kt, pT=pT: e.matmul(pO, lhsT=V3[:, kt, g * 128:(g + 1) * 128], rhs=pT.ap, start=(kt == 0), stop=(kt == nkt - 1)),
                         reads=[V, pT], writes=[self.ps_res[bO]])
                    eng, at = ("dve", ac["d"]) if kt % 2 == 0 else ("pool", ac["p"])
                    if kt < 2:
                        S.op(eng, lambda e, at=at, pT=pT: e.tensor_copy(out=at.ap, in_=pT.ap), reads=[pT], writes=[at])
                    else:
                        S.op(eng, lambda e, at=at, pT=pT: e.tensor_tensor(out=at.ap, in0=at.ap, in1=pT.ap, op=ALU.add), reads=[at, pT], writes=[at])
                S.op("dve", lambda e, ac=ac: e.tensor_tensor(out=ac["d"].ap, in0=ac["d"].ap, in1=ac["p"].ap, op=ALU.add), reads=[ac["d"], ac["p"]], writes=[ac["d"]])
                pSum = self.psum(bSum)
                S.op("pe", lambda e, pSum=pSum, ac=ac: e.matmul(pSum, lhsT=self.ones_row.ap, rhs=ac["d"].ap, start=True, stop=True),
                     reads=[self.ones_row, ac["d"]], writes=[self.ps_res[bSum]])
                S.op("dve", lambda e, pSum=pSum, ac=ac: e.reciprocal(out=ac["rs"].ap, in_=pSum), reads=[self.ps_res[bSum]], writes=[ac["rs"]])
                S.op("dve", lambda e, pO=pO, ac=ac: e.tensor_tensor(out=ac["o"].ap, in0=pO, in1=ac["rs"].ap, op=ALU.mult), reads=[self.ps_res[bO], ac["rs"]], writes=[ac["o"]])
                S.op("pool", lambda e, ac=ac, y=y, gt_=gt_, g=g: e.tensor_tensor(out=y.ap[:, g * 512:(g + 1) * 512], in0=ac["o"].ap, in1=gt_.ap[:, g * 512:(g + 1) * 512], op=ALU.mult),
                     reads=[ac["o"], gt_], writes=[y])
            S.dma("sp", Yv[:, :, cs], y.ap.rearrange("p (h t) -> p h t", h=8), reads=[y], writes=[self.Y_res])
        S.barrier(); A.top = base_mark
        wo = Tl(A.bf16(8 * 1024))
        wo3 = wo.ap.rearrange("p (k n) -> p k n", k=8)
        stage = Ring([Tl(A.f32(2048)) for _ in range(3)])
        self.load_w_bf16(W["att_w_out"], 1024, 1024, wo3, wo, stage_ring=stage)
        self.out_proj(Y, 1024, wo3, wo, T, self.gt_bc, x_src, self.xs, self.x_res)


_CACHE = {}


def get_program(T, TC, nlayers=4, dbg=False):
    key = (T, TC, nlayers, dbg)
    if key not in _CACHE:
        b = Builder(T, TC, nlayers, dbg)
        nc = b.build()
        _CACHE[key] = (nc, b)
    return _CACHE[key]


def kernel(**inputs):
    x = np.asarray(inputs["x"], dtype=np.float32)
    B, T, _ = x.shape
    TC = inputs["ctx"].shape[1]
    nc, b = get_program(T, TC)
    in_maps = []
    for i in range(B):
        m = {}
        for name in b.ins:
            if name in b.host_consts:
                m[name] = b.host_consts[name]
            elif name == "x":
                m[name] = np.ascontiguousarray(x[i])
            elif name == "c":
                m[name] = np.ascontiguousarray(np.asarray(inputs["c"], dtype=np.float32)[i])
            elif name == "ctx":
                m[name] = np.ascontiguousarray(np.asarray(inputs["ctx"], dtype=np.float32)[i])
            else:
                m[name] = np.ascontiguousarray(np.asarray(inputs[name], dtype=np.float32))
        in_maps.append(m)
    res = run_bass_kernel_spmd(nc, in_maps, core_ids=list(range(B)))
    return np.stack([np.asarray(r["out"]) for r in res.results], axis=0).astype(np.float32)
```

```python
import math
import numpy as np
import ml_dtypes
from contextlib import ExitStack
import concourse.bass as bass
import concourse.mybir as mybir
from concourse.bass_utils import run_bass_kernel_spmd

F32 = mybir.dt.float32
BF16 = mybir.dt.bfloat16
AF = mybir.ActivationFunctionType
ALU = mybir.AluOpType
AX = mybir.AxisListType

P = 128
D = 1024
KD = 8
EPS = 1e-6


class Res:
    __slots__ = ("lw", "rd")

    def __init__(self):
        self.lw = None
        self.rd = {}


class FreeRes(Res):
    __slots__ = ()


class Tl:
    __slots__ = ("ap", "res")

    def __init__(self, ap, res=None):
        self.ap = ap
        self.res = res if res is not None else Res()


class Ring:
    def __init__(self, tiles):
        self.t = tiles
        self.i = 0

    def next(self):
        t = self.t[self.i % len(self.t)]
        self.i += 1
        return t


class Sched:
    ENGS = ("pe", "act", "dve", "pool", "sp")
    NS = {"sp": 14, "act": 8}

    def __init__(self, nc, es):
        self.nc = nc
        self.ops = {e: [] for e in self.ENGS}
        self.cnt = {e: 0 for e in self.ENGS}
        self.known = {e: {} for e in self.ENGS}
        self.sems = {}
        for e in ("pe", "act", "dve", "pool"):
            self.sems[("c", e)] = es.enter_context(nc.semaphore("c_" + e))
        self.dq = {}
        for q, n in self.NS.items():
            for i in range(n):
                self.sems[("d", q, i)] = es.enter_context(nc.semaphore("d_%s_%d" % (q, i)))
            self.dq[q] = {"next": 0, "val": [0] * n}
        self.nops = 0

    def _collect(self, eng, reads, writes):
        deps = {}

        def add(tok, raw):
            if tok is None:
                return
            semkey, val, teng, seq = tok
            if semkey[0] == "c" and teng == eng:
                if eng == "pe" or not raw:
                    return
                if self.cnt[eng] - seq > 3:
                    return
            if deps.get(semkey, 0) < val:
                deps[semkey] = val

        for r in reads:
            add(r.lw, True)
        for w in writes:
            add(w.lw, False)
            for t in w.rd.values():
                add(t, False)
        return deps

    def _waits(self, eng, deps):
        out = []
        kn = self.known[eng]
        for semkey, val in deps.items():
            if kn.get(semkey, 0) >= val:
                continue
            kn[semkey] = val
            out.append((semkey, val))
        return out

    def _commit(self, tok, reads, writes):
        semkey = tok[0]
        for r in reads:
            old = r.rd.get(semkey)
            if old is None or old[1] < tok[1]:
                r.rd[semkey] = tok
        for w in writes:
            w.lw = tok
            w.rd = {}

    @staticmethod
    def _res(lst):
        return [r for r in (x.res if isinstance(x, Tl) else x for x in lst) if not isinstance(r, FreeRes)]

    def op(self, eng, emit, reads=(), writes=()):
        reads = self._res(reads)
        writes = self._res(writes)
        waits = self._waits(eng, self._collect(eng, reads, writes))
        self.cnt[eng] += 1
        semkey = ("c", eng)
        tok = (semkey, self.cnt[eng], eng, self.cnt[eng])
        self.ops[eng].append((waits, emit, semkey, 1))
        self._commit(tok, reads, writes)
        self.nops += 1

    def dma(self, q, out, in_, reads=(), writes=(), slow=False):
        reads = self._res(reads)
        writes = self._res(writes)
        deps = self._collect(q, reads, writes)
        st = self.dq[q]
        i = st["next"] % self.NS[q]
        st["next"] += 1
        semkey = ("d", q, i)
        prev = st["val"][i]
        if prev > 0 and deps.get(semkey, 0) < prev:
            deps[semkey] = prev
        waits = self._waits(q, deps)
        st["val"][i] = prev + 16
        tok = (semkey, prev + 16, q, 0)
        if slow:
            emit = lambda e: e.dma_start(out=out, in_=in_, allow_slow_non_contiguous=True)
        else:
            emit = lambda e: e.dma_start(out=out, in_=in_)
        self.ops[q].append((waits, emit, semkey, 16))
        self._commit(tok, reads, writes)
        self.nops += 1

    def barrier(self):
        deps = {}
        for e in ("pe", "act", "dve", "pool"):
            if self.cnt[e] > 0:
                deps[("c", e)] = self.cnt[e]
        for q, st in self.dq.items():
            for i, v in enumerate(st["val"]):
                if v > 0:
                    deps[("d", q, i)] = v
        for e in self.ENGS:
            d = {k: v for k, v in deps.items() if not (k[0] == "c" and k[1] == e)}
            waits = self._waits(e, d)
            if waits:
                self.ops[e].append((waits, None, None, 0))

    def emit(self):
        self.barrier()
        sems = self.sems
        ops = self.ops

        def run(name):
            def f(eng):
                for waits, emit, semkey, inc in ops[name]:
                    for sk, val in waits:
                        eng.wait_ge(sems[sk], val)
                    if emit is not None:
                        emit(eng).then_inc(sems[semkey], inc)
            return f

        with self.nc.Block() as block:
            block.tensor(run("pe"))
            block.scalar(run("act"))
            block.vector(run("dve"))
            block.gpsimd(run("pool"))
            block.sync(run("sp"))


class Arena:
    def __init__(self, ap, ncols):
        self.ap = ap
        self.n = ncols
        self.top = 0
        self.ptop = ncols

    def reset(self):
        self.top = 0

    def f32(self, cols):
        a = self.ap[:, self.top:self.top + cols]
        self.top += cols
        assert self.top <= self.ptop, "SBUF arena overflow %d > %d" % (self.top, self.ptop)
        return a

    def bf16(self, cols):
        c = (cols + 1) // 2
        return self.f32(c).bitcast(BF16)

    def pf32(self, cols):
        self.ptop -= cols
        assert self.top <= self.ptop
        return self.ap[:, self.ptop:self.ptop + cols]

    def pbf16(self, cols):
        c = (cols + 1) // 2
        return self.pf32(c).bitcast(BF16)


ARENA_COLS = 48 * 1024

LRU_W = 2048
RET_H, RET_DK, RET_DV = 4, 256, 512
ATT_HQ, ATT_G, ATT_D = 8, 2, 128
GRID_W = 64


class Builder:
    def __init__(self, T, TC, nlayers=4, dbg=False):
        self.T, self.TC, self.nlayers, self.dbg = T, TC, nlayers, dbg
        self.nc = bass.Bass("TRN2", target_bir_lowering=False)
        self.ins = {}
        self.host_consts = {}

    def inp(self, name, shape, dt=F32):
        t = self.nc.dram_tensor(name, list(shape), dt, kind="ExternalInput").ap()
        self.ins[name] = t
        return t

    def scratch(self, name, shape, dt=F32):
        return self.nc.dram_tensor(name, list(shape), dt, kind="Internal").ap()

    def const(self, name, arr):
        arr = np.ascontiguousarray(arr)
        dt = BF16 if arr.dtype == ml_dtypes.bfloat16 else F32
        self.host_consts[name] = arr
        return self.inp(name, arr.shape, dt)

    def psum(self, b0, nb=1, dt=F32):
        a = self.ps[:, b0 * 512:(b0 + nb) * 512]
        return a.bitcast(BF16) if dt == BF16 else a

    def build(self):
        nc = self.nc
        T, TC = self.T, self.TC
        I = self.inp
        x = I("x", [T, D]); c = I("c", [D]); ctx = I("ctx", [TC, D]); c_ctx = I("c_ctx", [D])
        W = {}
        for p in ("lru", "ret", "hy", "att"):
            W[p + "_mod_w"] = I(p + "_mod_w", [D, 3 * D]); W[p + "_mod_b"] = I(p + "_mod_b", [3 * D])
            W[p + "_norm_g"] = I(p + "_norm_g", [D])
        W["lru_w_in"] = I("lru_w_in", [D, 4096]); W["lru_conv_w"] = I("lru_conv_w", [4, 2048])
        W["lru_conv_b"] = I("lru_conv_b", [2048]); W["lru_w_r"] = I("lru_w_r", [2, 16, 128, 128])
        W["lru_b_r"] = I("lru_b_r", [2, 2048]); W["lru_w_i"] = I("lru_w_i", [2, 16, 128, 128])
        W["lru_b_i"] = I("lru_b_i", [2, 2048]); W["lru_lambda"] = I("lru_lambda", [2, 2048])
        W["lru_w_out"] = I("lru_w_out", [2048, D])
        W["ret_w_in"] = I("ret_w_in", [D, 6144]); W["ret_decay_logit"] = I("ret_decay_logit", [2, 4])
        W["ret_w_out"] = I("ret_w_out", [2048, D])
        W["hy_w_in"] = I("hy_w_in", [D, 4096]); W["hy_conv_w"] = I("hy_conv_w", [3, 3072])
        W["hy_conv_b"] = I("hy_conv_b", [3072]); W["hy_fw1"] = I("hy_fw1", [33, 64]); W["hy_fb1"] = I("hy_fb1", [64])
        W["hy_fw2"] = I("hy_fw2", [64, 64]); W["hy_fb2"] = I("hy_fb2", [64]); W["hy_fw3"] = I("hy_fw3", [64, 4096])
        W["hy_freq"] = I("hy_freq", [64]); W["hy_skip"] = I("hy_skip", [2, 1024]); W["hy_w_out"] = I("hy_w_out", [1024, D])
        W["att_w_in"] = I("att_w_in", [D, 2560]); W["att_q_norm_g"] = I("att_q_norm_g", [128])
        W["att_k_norm_g"] = I("att_k_norm_g", [128]); W["att_w_out"] = I("att_w_out", [1024, D])
        W["final_norm_g"] = I("final_norm_g", [D])
        self.W = W
        self.x_in, self.ctx_in, self.c_in, self.cctx_in = x, ctx, c, c_ctx
        ident_d = self.const("c_ident", np.eye(128, dtype=np.float32).astype(ml_dtypes.bfloat16))
        self.out = nc.dram_tensor("out", [T, D], F32, kind="ExternalOutput").ap()
        if self.dbg:
            self.ctx_out = nc.dram_tensor("ctx_out", [TC, D], F32, kind="ExternalOutput").ap()
        self.xs = self.scratch("xs", [T, D]); self.cs = self.scratch("cs", [TC, D])
        self.x_res = [Res() for _ in range(T // 128)]
        self.c_res = [Res() for _ in range(TC // 128)]

        with ExitStack() as es:
            self.S = S = Sched(nc, es)
            arena_t = es.enter_context(nc.sbuf_tensor("arena", [P, ARENA_COLS], F32))
            self.A = A = Arena(arena_t, ARENA_COLS)
            self.ps = es.enter_context(nc.psum_tensor("ps", [P, 4096], F32))
            self.ps_res = [Res() for _ in range(8)]
            self.ident = Tl(A.pbf16(128))
            S.dma("sp", self.ident.ap, ident_d, writes=[self.ident])
            self.ones_row = Tl(A.pf32(128))
            S.op("pool", lambda e: e.memset(self.ones_row.ap, 1.0), writes=[self.ones_row])
            self.ones_bf = Tl(A.pbf16(128))
            self.eps_col = Tl(A.pf32(1))
            S.op("pool", lambda e: e.memset(self.eps_col.ap, EPS), writes=[self.eps_col])
            S.op("pool", lambda e: e.memset(self.ones_bf.ap, 1.0), writes=[self.ones_bf])
            self.modcol = Tl(A.pf32(48))
            self.gs = Tl(A.pf32(16)); self.sh = Tl(A.pf32(16))
            self.gt_bc = Tl(A.pf32(1024)); self.gtc_bc = Tl(A.pf32(1024))
            self.scol = Tl(A.pf32(16))
            self.prep_cond()

            x_src, c_src = self.x_in, self.ctx_in
            layers = [("lru", self.mix_lru), ("ret", self.mix_ret), ("hy", self.mix_hy), ("att", self.mix_att)]
            for li in range(self.nlayers):
                pfx, fn = layers[li]
                self.layer_idx = li
                self.need_ctx = li < 3
                S.barrier(); A.reset()
                self.modulation(pfx)
                S.barrier(); A.reset()
                fn(x_src, c_src)
                x_src, c_src = self.xs, self.cs
            S.barrier(); A.reset()
            self.final_norm(x_src)
            if self.dbg:
                S.barrier(); A.reset()
                t = Tl(A.f32(1024))
                for i in range(TC // 128):
                    S.dma("sp", t.ap, c_src[i * 128:(i + 1) * 128, :], reads=[self.c_res[i]], writes=[t])
                    S.dma("sp", self.ctx_out[i * 128:(i + 1) * 128, :], t.ap, reads=[t])
            S.emit()
        return nc

    def prep_cond(self):
        S, A = self.S, self.A
        raw = Tl(A.f32(16))
        r3 = raw.ap.rearrange("p (k n) -> p k n", n=2)
        S.dma("sp", r3[:, :, 0], self.c_in.rearrange("(k p) -> p k", p=128), writes=[raw], slow=True)
        S.dma("sp", r3[:, :, 1], self.cctx_in.rearrange("(k p) -> p k", p=128), writes=[raw], slow=True)
        S.op("act", lambda e: e.activation(out=self.scol.ap, in_=raw.ap, func=AF.Silu), reads=[raw], writes=[self.scol])

    def modulation(self, pfx):
        S, A, W = self.S, self.A, self.W
        mw = Tl(A.f32(8 * 3072))
        mw3 = mw.ap.rearrange("p (k n) -> p k n", k=8)
        src = W[pfx + "_mod_w"].rearrange("(k p) n -> p k n", p=128)
        for k in range(8):
            S.dma("sp" if k % 2 == 0 else "act", mw3[:, k, :], src[:, k, :], writes=[mw])
        mb = Tl(A.f32(24))
        S.dma("sp", mb.ap, W[pfx + "_mod_b"].rearrange("(j p) -> p j", p=128), writes=[mb], slow=True)
        g = Tl(A.f32(8))
        S.dma("sp", g.ap, W[pfx + "_norm_g"].rearrange("(j p) -> p j", p=128), writes=[g], slow=True)
        mbrow = Tl(A.f32(1024))
        S.dma("sp", mbrow.ap[0:1, :], W[pfx + "_mod_b"][2048:3072].rearrange("(o n) -> o n", o=1), writes=[mbrow])
        sc3 = self.scol.ap.rearrange("p (k n) -> p k n", n=2)
        mc3 = self.modcol.ap.rearrange("p (j n) -> p j n", n=2)
        for j in range(24):
            b = j % 4
            pt = self.psum(b)
            for k in range(8):
                S.op("pe", lambda e, k=k, j=j, pt=pt: e.matmul(pt[:, 0:2], lhsT=mw3[:, k, j * 128:(j + 1) * 128], rhs=sc3[:, k, :],
                                                            start=(k == 0), stop=(k == 7)),
                     reads=[mw, self.scol], writes=[self.ps_res[b]])
            S.op("dve", lambda e, j=j, pt=pt: e.tensor_scalar(out=mc3[:, j, :], in0=pt[:, 0:2], scalar1=mb.ap[:, j:j + 1], scalar2=None,
                                                           op0=ALU.add), reads=[self.ps_res[b], mb], writes=[self.modcol])
        gs3 = self.gs.ap.rearrange("p (k n) -> p k n", n=2)
        sh3 = self.sh.ap.rearrange("p (k n) -> p k n", n=2)
        for n in range(2):
            S.op("dve", lambda e, n=n: e.tensor_scalar(out=gs3[:, :, n], in0=mc3[:, 8:16, n], scalar1=1.0, scalar2=None, op0=ALU.add),
                 reads=[self.modcol], writes=[self.gs])
            S.op("dve", lambda e, n=n: e.tensor_tensor(out=gs3[:, :, n], in0=gs3[:, :, n], in1=g.ap, op=ALU.mult),
                 reads=[self.gs, g], writes=[self.gs])
            S.op("dve", lambda e, n=n: e.tensor_copy(out=sh3[:, :, n], in_=mc3[:, 0:8, n]), reads=[self.modcol], writes=[self.sh])
        for n, dst in ((0, self.gt_bc), (1, self.gtc_bc)):
            row = Tl(A.f32(1024))
            for hh in range(2):
                b = 4 + hh
                pt = self.psum(b)
                for k in range(8):
                    S.op("pe", lambda e, k=k, hh=hh, n=n, pt=pt: e.matmul(pt[0:1, :], lhsT=sc3[:, k, n:n + 1],
                                                                       rhs=mw3[:, k, 2048 + hh * 512:2048 + (hh + 1) * 512],
                                                                       start=(k == 0), stop=(k == 7)),
                         reads=[mw, self.scol], writes=[self.ps_res[b]])
                S.op("dve", lambda e, hh=hh, pt=pt, row=row: e.tensor_tensor(out=row.ap[0:1, hh * 512:(hh + 1) * 512], in0=pt[0:1, :],
                                                                           in1=mbrow.ap[0:1, hh * 512:(hh + 1) * 512], op=ALU.add),
                     reads=[self.ps_res[b], mbrow], writes=[row])
            for hh in range(2):
                b = 6 + hh
                pt = self.psum(b)
                S.op("pe", lambda e, hh=hh, pt=pt, row=row: e.matmul(pt, lhsT=self.ones_row.ap[0:1, :], rhs=row.ap[0:1, hh * 512:(hh + 1) * 512],
                                                                   start=True, stop=True),
                     reads=[row, self.ones_row], writes=[self.ps_res[b]])
                S.op("act", lambda e, hh=hh, pt=pt, dst=dst: e.activation(out=dst.ap[:, hh * 512:(hh + 1) * 512], in_=pt, func=AF.Copy),
                     reads=[self.ps_res[b]], writes=[dst])

    def load_w_bf16(self, w_dram, K, N, dst3, dst_tl, col0=0, ncols=None, stage_ring=None):
        S = self.S
        ncols = N if ncols is None else ncols
        src = w_dram.rearrange("(k p) n -> p k n", p=128)
        CH = 2048
        i = 0
        for k in range(K // 128):
            for c0 in range(0, ncols, CH):
                cw = min(CH, ncols - c0)
                st = stage_ring.next()
                S.dma("sp" if i % 2 == 0 else "act", st.ap[:, 0:cw], src[:, k, col0 + c0:col0 + c0 + cw], writes=[st])
                eng = ("act", "dve", "pool")[i % 3]
                if eng == "act":
                    S.op("act", lambda e, st=st, k=k, c0=c0, cw=cw: e.activation(out=dst3[:, k, c0:c0 + cw], in_=st.ap[:, 0:cw], func=AF.Copy),
                         reads=[st], writes=[dst_tl])
                else:
                    S.op(eng, lambda e, st=st, k=k, c0=c0, cw=cw: e.tensor_copy(out=dst3[:, k, c0:c0 + cw], in_=st.ap[:, 0:cw]),
                         reads=[st], writes=[dst_tl])
                i += 1

    def norm_tiles(self, src, src_res, Tn, which, xring, hring, small, TT):
        S = self.S
        gs3 = self.gs.ap.rearrange("p (k n) -> p k n", n=2)
        sh3 = self.sh.ap.rearrange("p (k n) -> p k n", n=2)
        for tt in range(Tn // TT):
            hT = hring.next()
            h3 = hT.ap.rearrange("p (k t) -> p k t", k=8)
            for s in range(TT // 128):
                ti = (tt * TT) // 128 + s
                xt = xring.next()
                xf, xb, ss = xt["xf"], xt["xb"], xt["ss"]
                S.dma("sp", xf.ap, src[ti * 128:(ti + 1) * 128, :], reads=[src_res[ti]], writes=[xf])
                S.op("act", lambda e, xf=xf, xb=xb, ss=ss: e.activation(out=xb.ap, in_=xf.ap, func=AF.Square, accum_out=ss.ap),
                     reads=[xf], writes=[xb, ss])
                S.op("dve", lambda e, ss=ss: e.tensor_scalar(out=ss.ap, in0=ss.ap, scalar1=1.0 / D, scalar2=EPS, op0=ALU.mult, op1=ALU.add),
                     reads=[ss], writes=[ss])
                S.op("act", lambda e, ss=ss: e.activation(out=ss.ap, in_=ss.ap, func=AF.Sqrt), reads=[ss], writes=[ss])
                S.op("dve", lambda e, ss=ss: e.reciprocal(out=ss.ap, in_=ss.ap), reads=[ss], writes=[ss])
                S.op("act", lambda e, xf=xf, xb=xb, ss=ss: e.activation(out=xb.ap, in_=xf.ap, func=AF.Copy, scale=ss.ap),
                     reads=[xf, ss], writes=[xb])
                b = self.tp_bank
                self.tp_bank = 6 + (self.tp_bank - 6 + 1) % 2
                pb = self.psum(b, 1, BF16)
                for j in range(8):
                    S.op("pe", lambda e, j=j, pb=pb, xb=xb: e.transpose(out=pb[:, j * 128:(j + 1) * 128], in_=xb.ap[:, j * 128:(j + 1) * 128],
                                                                        identity=self.ident.ap),
                         reads=[xb, self.ident], writes=[self.ps_res[b]])
                for j in range(8):
                    eng = "dve" if j % 2 == 0 else "pool"
                    eng = "dve"
                    S.op(eng, lambda e, j=j, pb=pb, s=s, h3=h3: e.tensor_scalar(out=h3[:, j, s * 128:(s + 1) * 128], in0=pb[:, j * 128:(j + 1) * 128],
                                                                         scalar1=gs3[:, j, which:which + 1], scalar2=sh3[:, j, which:which + 1],
                                                                         op0=ALU.mult, op1=ALU.add),
                         reads=[self.ps_res[b], self.gs, self.sh], writes=[hT])
            yield tt, hT, h3

    def make_norm_rings(self, TT, nh=2, nx=3):
        A = self.A
        xring = Ring([{"xf": Tl(A.f32(1024)), "xb": Tl(A.bf16(1024)), "ss": Tl(A.f32(1))} for _ in range(nx)])
        hring = Ring([Tl(A.bf16(8 * TT)) for _ in range(nh)])
        self.tp_bank = 6
        return xring, hring

    def out_proj(self, Y, C, w_bf3, w_tl, Tn, gt, src, dst, res_list, TT=512):
        S, A = self.S, self.A
        kc = C // 128
        TT = min(TT, Tn)
        yring = Ring([Tl(A.bf16(kc * TT)) for _ in range(2)])
        xring = Ring([Tl(A.f32(1024)) for _ in range(3)])
        Yv = Y.rearrange("(k p) t -> p k t", p=128)
        pbi = 0
        for tt in range(Tn // TT):
            yt = yring.next()
            y3 = yt.ap.rearrange("p (k t) -> p k t", k=kc)
            S.dma("sp", y3, Yv[:, :, tt * TT:(tt + 1) * TT], writes=[yt], reads=[self.Y_res])
            for s in range(TT // 128):
                ti = (tt * TT) // 128 + s
                xt = xring.next()
                S.dma("act", xt.ap, src[ti * 128:(ti + 1) * 128, :], reads=[res_list[ti]], writes=[xt])
                b0 = (pbi % 3) * 2
                pbi += 1
                for hh in range(2):
                    pt = self.psum(b0 + hh)
                    for k in range(kc):
                        S.op("pe", lambda e, k=k, hh=hh, pt=pt, s=s, y3=y3: e.matmul(pt, lhsT=y3[:, k, s * 128:(s + 1) * 128],
                                                                                 rhs=w_bf3[:, k, hh * 512:(hh + 1) * 512],
                                                                                 start=(k == 0), stop=(k == kc - 1)),
                             reads=[yt, w_tl], writes=[self.ps_res[b0 + hh]])
                pt2 = self.psum(b0, 2)
                S.op("dve", lambda e, pt2=pt2: e.tensor_tensor(out=pt2, in0=pt2, in1=gt.ap, op=ALU.mult),
                     reads=[self.ps_res[b0], self.ps_res[b0 + 1], gt], writes=[self.ps_res[b0], self.ps_res[b0 + 1]])
                S.op("dve", lambda e, pt2=pt2, xt=xt: e.tensor_tensor(out=xt.ap, in0=pt2, in1=xt.ap, op=ALU.add),
                     reads=[self.ps_res[b0], self.ps_res[b0 + 1], xt], writes=[xt])
                S.dma("sp", dst[ti * 128:(ti + 1) * 128, :], xt.ap, reads=[xt], writes=[res_list[ti]])

    def final_norm(self, src):
        S, A, T = self.S, self.A, self.T
        g = Tl(A.f32(1024))
        S.dma("sp", g.ap, self.W["final_norm_g"].partition_broadcast(128), writes=[g])
        ring = Ring([{"xf": Tl(A.f32(1024)), "sq": Tl(A.f32(1024)), "ss": Tl(A.f32(1))} for _ in range(3)])
        ores = Res()
        for ti in range(T // 128):
            r = ring.next()
            xf, sq, ss = r["xf"], r["sq"], r["ss"]
            S.dma("sp", xf.ap, src[ti * 128:(ti + 1) * 128, :], reads=[self.x_res[ti]], writes=[xf])
            S.op("act", lambda e, xf=xf, sq=sq, ss=ss: e.activation(out=sq.ap, in_=xf.ap, func=AF.Square, accum_out=ss.ap),
                 reads=[xf], writes=[sq, ss])
            S.op("dve", lambda e, ss=ss: e.tensor_scalar(out=ss.ap, in0=ss.ap, scalar1=1.0 / D, scalar2=EPS, op0=ALU.mult, op1=ALU.add),
                 reads=[ss], writes=[ss])
            S.op("act", lambda e, ss=ss: e.activation(out=ss.ap, in_=ss.ap, func=AF.Sqrt), reads=[ss], writes=[ss])
            S.op("dve", lambda e, ss=ss: e.reciprocal(out=ss.ap, in_=ss.ap), reads=[ss], writes=[ss])
            S.op("dve", lambda e, xf=xf, sq=sq, ss=ss: e.scalar_tensor_tensor(out=sq.ap, in0=xf.ap, scalar=ss.ap, in1=g.ap,
                                                                             op0=ALU.mult, op1=ALU.mult),
                 reads=[xf, ss, g], writes=[sq])
            S.dma("act", self.out[ti * 128:(ti + 1) * 128, :], sq.ap, reads=[sq], writes=[ores])

    def proj_to_dram(self, src, src_res, Tn, which, w3, w_tl, nchunks, U, U_res, TT):
        S, A = self.S, self.A
        xring, hring = self.make_norm_rings(TT)
        stg = Ring([Tl(A.f32(TT)) for _ in range(4)])
        bi = 0
        for tt, hT, h3 in self.norm_tiles(src, src_res, Tn, which, xring, hring, None, TT):
            for oc in range(nchunks):
                b = bi % 6
                bi += 1
                pt = self.psum(b)[:, 0:TT]
                for k in range(8):
                    S.op("pe", lambda e, k=k, oc=oc, pt=pt, h3=h3: e.matmul(pt, lhsT=w3[:, k, oc * 128:(oc + 1) * 128], rhs=h3[:, k, :],
                                                                         start=(k == 0), stop=(k == 7)),
                         reads=[hT, w_tl], writes=[self.ps_res[b]])
                st = stg.next()
                if oc % 2 == 0:
                    S.op("act", lambda e, pt=pt, st=st: e.activation(out=st.ap, in_=pt, func=AF.Copy), reads=[self.ps_res[b]], writes=[st])
                else:
                    S.op("dve", lambda e, pt=pt, st=st: e.tensor_copy(out=st.ap, in_=pt), reads=[self.ps_res[b]], writes=[st])
                S.dma("sp" if oc % 2 == 0 else "act", U[oc * 128:(oc + 1) * 128, tt * TT:(tt + 1) * TT], st.ap, reads=[st], writes=[U_res])

    def mix_lru(self, x_src, c_src):
        S, A, W, T, TC = self.S, self.A, self.W, self.T, self.TC
        U = self.scratch("lru_U", [4096, T]); UC = self.scratch("lru_UC", [4096, TC])
        Y = self.scratch("lru_Y", [2048, T], BF16); YC = self.scratch("lru_YC", [2048, TC], BF16)
        U_res, UC_res = FreeRes(), FreeRes()
        self.Y_res = FreeRes()
        w_tl = Tl(A.bf16(8 * 4096))
        w3 = w_tl.ap.rearrange("p (k n) -> p k n", k=8)
        stage = Ring([Tl(A.f32(2048)) for _ in range(3)])
        self.load_w_bf16(W["lru_w_in"], 1024, 4096, w3, w_tl, stage_ring=stage)
        mark = A.top
        self.proj_to_dram(c_src, self.c_res, TC, 1, w3, w_tl, 32, UC, UC_res, min(512, TC))
        A.top = mark
        S.barrier()
        self.proj_to_dram(x_src, self.x_res, T, 0, w3, w_tl, 32, U, U_res, min(512, T))
        S.barrier(); A.reset()
        def col(src_ap, n, pattern="(n p) -> p n"):
            t = Tl(A.f32(n))
            S.dma("sp", t.ap, src_ap.rearrange(pattern, p=128), writes=[t], slow=True)
            return t
        cw = [col(W["lru_conv_w"][k], 16) for k in range(4)]
        cb = col(W["lru_conv_b"], 16)
        br = [col(W["lru_b_r"][d], 16) for d in range(2)]
        bi_ = [col(W["lru_b_i"][d], 16) for d in range(2)]
        lam = [col(W["lru_lambda"][d], 16) for d in range(2)]
        cdec, cdec2 = [], []
        for d in range(2):
            t = Tl(A.f32(16)); t2 = Tl(A.f32(16))
            S.op("act", lambda e, t=t, d=d: e.activation(out=t.ap, in_=lam[d].ap, func=AF.Exp, scale=-1.0), reads=[lam[d]], writes=[t])
            S.op("act", lambda e, t=t: e.activation(out=t.ap, in_=t.ap, func=AF.Ln, bias=1.0), reads=[t], writes=[t])
            S.op("dve", lambda e, t=t, t2=t2: e.tensor_scalar(out=t2.ap, in0=t.ap, scalar1=-16.0, scalar2=None, op0=ALU.mult), reads=[t], writes=[t2])
            S.op("dve", lambda e, t=t: e.tensor_scalar(out=t.ap, in0=t.ap, scalar1=-8.0, scalar2=None, op0=ALU.mult), reads=[t], writes=[t])
            cdec.append(t); cdec2.append(t2)
        gw = Tl(A.bf16(2 * 2 * 16 * 128))
        gw5 = gw.ap.rearrange("p (d g n k) -> p d g n k", d=2, g=2, n=16)
        gst = Ring([Tl(A.f32(16 * 128)) for _ in range(2)])
        for d in range(2):
            for g, nm in enumerate(("lru_w_r", "lru_w_i")):
                st = gst.next()
                st3 = st.ap.rearrange("p (n k) -> p n k", n=16)
                S.dma("sp", st3, W[nm][d].rearrange("n j k -> j n k"), writes=[st])
                S.op("dve", lambda e, st3=st3, d=d, g=g: e.tensor_copy(out=gw5[:, d, g, :, :], in_=st3), reads=[st], writes=[gw])
        TM = max(T, TC)
        ubuf = Tl(A.f32(TM + 3))
        xl = Tl(A.f32(TM)); xlb = Tl(A.bf16(TM))
        TTS = min(1024, TM)
        tring = Ring([{k: Tl(A.f32(TTS)) for k in ("r", "i", "a", "hb", "g")} for _ in range(2)])
        yring = Ring([Tl(A.bf16(TTS)) for _ in range(2)])
        fin = [Tl(A.f32(1)), Tl(A.f32(1))]
        pb = [0]

        def seq(n, Useq, Useq_res, Tn, init, Yseq):
            tts = min(TTS, Tn)
            nt = Tn // tts
            S.op("pool", lambda e: e.memset(ubuf.ap[:, 0:2], 0.0), writes=[ubuf])
            S.op("pool", lambda e: e.memset(ubuf.ap[:, Tn + 2:Tn + 3], 0.0), writes=[ubuf])
            S.dma("sp", ubuf.ap[:, 2:Tn + 2], Useq[n * 128:(n + 1) * 128, :], reads=[Useq_res], writes=[ubuf])
            S.op("dve", lambda e: e.tensor_scalar(out=xl.ap[:, 0:Tn], in0=ubuf.ap[:, 0:Tn], scalar1=cw[0].ap[:, n:n + 1],
                                                  scalar2=cb.ap[:, n:n + 1], op0=ALU.mult, op1=ALU.add),
                 reads=[ubuf, cw[0], cb], writes=[xl])
            for k in range(1, 4):
                S.op("dve", lambda e, k=k: e.scalar_tensor_tensor(out=xl.ap[:, 0:Tn], in0=ubuf.ap[:, k:k + Tn], scalar=cw[k].ap[:, n:n + 1],
                                                                  in1=xl.ap[:, 0:Tn], op0=ALU.mult, op1=ALU.add),
                     reads=[ubuf, cw[k], xl], writes=[xl])
            S.op("act", lambda e: e.activation(out=xlb.ap[:, 0:Tn], in_=xl.ap[:, 0:Tn], func=AF.Copy), reads=[xl], writes=[xlb])
            hf = ubuf
            for d in range(2):
                order = list(range(nt)) if d == 0 else list(range(nt - 1, -1, -1))
                state = init[d]
                for tt in order:
                    sl = slice(tt * tts, (tt + 1) * tts)
                    tb = tring.next()
                    r, i_, a, hb, g = tb["r"], tb["i"], tb["a"], tb["hb"], tb["g"]
                    pb[0] += 1
                    nb = (tts + 511) // 512
                    bank0 = (pb[0] % 2) * 4
                    for gi in range(2):
                        for h in range(nb):
                            c0 = h * 512
                            cwid = min(512, tts - c0)
                            bank = bank0 + gi * 2 + h
                            pt = self.psum(bank)[:, 0:cwid]
                            S.op("pe", lambda e, pt=pt, gi=gi, c0=c0, cwid=cwid, d=d, tt=tt: e.matmul(
                                pt, lhsT=gw5[:, d, gi, n, :], rhs=xlb.ap[:, tt * tts + c0:tt * tts + c0 + cwid], start=True, stop=True),
                                 reads=[gw, xlb], writes=[self.ps_res[bank]])
                            dst = r if gi == 0 else i_
                            bias = (br if gi == 0 else bi_)[d]
                            S.op("act", lambda e, pt=pt, dst=dst, bias=bias, c0=c0, cwid=cwid: e.activation(
                                out=dst.ap[:, c0:c0 + cwid], in_=pt, func=AF.Sigmoid, bias=bias.ap[:, n:n + 1]),
                                 reads=[self.ps_res[bank], bias], writes=[dst])
                    S.op("act", lambda e, r=r, a=a, d=d: e.activation(out=a.ap[:, 0:tts], in_=r.ap[:, 0:tts], func=AF.Exp, scale=cdec[d].ap[:, n:n + 1]),
                         reads=[r, cdec[d]], writes=[a])
                    S.op("act", lambda e, r=r, d=d: e.activation(out=r.ap[:, 0:tts], in_=r.ap[:, 0:tts], func=AF.Exp, scale=cdec2[d].ap[:, n:n + 1]),
                         reads=[r, cdec2[d]], writes=[r])
                    S.op("act", lambda e, r=r: e.activation(out=r.ap[:, 0:tts], in_=r.ap[:, 0:tts], func=AF.Sqrt, scale=-1.0, bias=1.0),
                         reads=[r], writes=[r])
                    S.op("pool", lambda e, i_=i_, sl=sl: e.tensor_tensor(out=i_.ap[:, 0:tts], in0=i_.ap[:, 0:tts], in1=xl.ap[:, sl], op=ALU.mult),
                         reads=[i_, xl], writes=[i_])
                    S.op("dve", lambda e, i_=i_, r=r: e.tensor_tensor(out=i_.ap[:, 0:tts], in0=i_.ap[:, 0:tts], in1=r.ap[:, 0:tts], op=ALU.mult),
                         reads=[i_, r], writes=[i_])
                    ini = state if isinstance(state, float) else state.ap
                    rds = [a, i_] + ([] if isinstance(state, float) else [state])
                    if d == 0:
                        S.op("dve", lambda e, a=a, i_=i_, ini=ini, sl=sl: e.tensor_tensor_scan(out=hf.ap[:, sl], data0=a.ap[:, 0:tts], data1=i_.ap[:, 0:tts],
                                                                                         initial=ini, op0=ALU.mult, op1=ALU.add),
                             reads=rds, writes=[hf])
                        state = Tl(hf.ap[:, (tt + 1) * tts - 1:(tt + 1) * tts], hf.res)
                    else:
                        S.op("dve", lambda e, a=a, i_=i_, ini=ini, hb=hb: e.tensor_tensor_scan(out=hb.ap[:, 0:tts][:, ::-1], data0=a.ap[:, 0:tts][:, ::-1],
                                                                                         data1=i_.ap[:, 0:tts][:, ::-1], initial=ini,
                                                                                         op0=ALU.mult, op1=ALU.add),
                             reads=rds, writes=[hb])
                        state = Tl(hb.ap[:, 0:1], hb.res)
                        if tt == 0:
                            S.op("act", lambda e, hb=hb: e.activation(out=fin[1].ap, in_=hb.ap[:, 0:1], func=AF.Copy), reads=[hb], writes=[fin[1]])
                        if Yseq is not None:
                            S.dma("act", g.ap[:, 0:tts], Useq[2048 + n * 128:2048 + (n + 1) * 128, sl], reads=[Useq_res], writes=[g])
                            S.op("act", lambda e, g=g: e.activation(out=g.ap[:, 0:tts], in_=g.ap[:, 0:tts], func=AF.Silu), reads=[g], writes=[g])
                            S.op("pool", lambda e, hb=hb, sl=sl, a=a: e.tensor_tensor(out=a.ap[:, 0:tts], in0=hb.ap[:, 0:tts], in1=hf.ap[:, sl], op=ALU.add),
                                 reads=[hb, hf], writes=[a])
                            yt = yring.next()
                            S.op("dve", lambda e, a=a, g=g, yt=yt: e.tensor_tensor(out=yt.ap[:, 0:tts], in0=a.ap[:, 0:tts], in1=g.ap[:, 0:tts], op=ALU.mult),
                                 reads=[a, g], writes=[yt])
                            S.dma("sp", Yseq[n * 128:(n + 1) * 128, sl], yt.ap[:, 0:tts], reads=[yt], writes=[self.Y_res])
                if d == 0:
                    S.op("act", lambda e: e.activation(out=fin[0].ap, in_=hf.ap[:, Tn - 1:Tn], func=AF.Copy), reads=[hf], writes=[fin[0]])

        for n in range(16):
            seq(n, UC, UC_res, TC, [0.0, 0.0], YC if self.need_ctx else None)
            seq(n, U, U_res, T, [fin[0], fin[1]], Y)
        S.barrier(); A.reset()
        wo = Tl(A.bf16(16 * 1024))
        wo3 = wo.ap.rearrange("p (k n) -> p k n", k=16)
        stage = Ring([Tl(A.f32(2048)) for _ in range(3)])
        self.load_w_bf16(W["lru_w_out"], 2048, 1024, wo3, wo, stage_ring=stage)
        self.out_proj(Y, 2048, wo3, wo, T, self.gt_bc, x_src, self.xs, self.x_res)
        if self.need_ctx:
            self.out_proj(YC, 2048, wo3, wo, TC, self.gtc_bc, c_src, self.cs, self.c_res)

    def mix_ret(self, x_src, c_src):
        S, A, W, T, TC = self.S, self.A, self.W, self.T, self.TC
        C = 128
        f32 = np.float32
        p_ = np.arange(128, dtype=f32)
        cst = np.zeros((128, 4 + 4 * 128), f32)
        cst[:, 0] = C - 1 - p_; cst[:, 1] = p_; cst[:, 2] = C
        ii = np.arange(128, dtype=f32)[None, :]; jj = np.arange(128, dtype=f32)[:, None]
        cst[:, 4:132] = ii + 1.0
        cst[:, 132:260] = C - ii
        cst[:, 260:388] = np.maximum(ii - jj, 0.0)
        cst[:, 388:516] = np.maximum(jj - ii, 0.0)
        cst_d = self.const("c_ret_cst", cst)
        inv = (f32(10000.0) ** (-np.arange(128, dtype=f32) / f32(128))).astype(f32)
        ang = (np.arange(T, dtype=f32)[:, None] * inv[None, :]).astype(f32)
        cs_, sn_ = np.cos(ang.astype(np.float64)).astype(f32), np.sin(ang.astype(np.float64)).astype(f32)
        tabs_x = dict(cF=self.const("c_ret_cF", cs_.T), sF=self.const("c_ret_sF", sn_.T),
                      cT=self.const("c_ret_cT", np.tile(cs_, (1, 2))), sT=self.const("c_ret_sT", np.tile(sn_, (1, 2))))
        tabs_c = dict(cF=self.const("c_ret_cFc", np.ones((128, TC), f32)), sF=self.const("c_ret_sFc", np.zeros((128, TC), f32)),
                      cT=self.const("c_ret_cTc", np.ones((TC, 256), f32)), sT=self.const("c_ret_sTc", np.zeros((TC, 256), f32)))

        def mk(T_, sfx):
            d = {}
            for nm in ("Q", "QF", "QB", "KT"):
                d[nm] = self.scratch("ret_%s%s" % (nm, sfx), [1024, T_], BF16)
            for nm in ("KF", "KB"):
                d[nm] = self.scratch("ret_%s%s" % (nm, sfx), [T_, 1024], BF16)
            d["V"] = self.scratch("ret_V" + sfx, [T_, 2048], BF16); d["G"] = self.scratch("ret_G" + sfx, [T_, 2048], BF16)
            d["OB"] = self.scratch("ret_OB" + sfx, [T_, 2048], F32); d["Y"] = self.scratch("ret_Y" + sfx, [2048, T_], BF16)
            d["res"] = FreeRes()
            return d
        DX, DC = mk(T, "x"), mk(TC, "c")
        self.Y_res = FreeRes()

        cstt = Tl(A.f32(516))
        S.dma("sp", cstt.ap, cst_d, writes=[cstt])
        lg = Tl(A.f32(8))
        S.dma("sp", lg.ap, W["ret_decay_logit"].rearrange("d h -> (d h)").partition_broadcast(128), writes=[lg])
        S.op("act", lambda e: e.activation(out=lg.ap, in_=lg.ap, func=AF.Exp, scale=-1.0), reads=[lg], writes=[lg])
        S.op("act", lambda e: e.activation(out=lg.ap, in_=lg.ap, func=AF.Ln, bias=1.0), reads=[lg], writes=[lg])
        S.op("dve", lambda e: e.tensor_scalar(out=lg.ap, in0=lg.ap, scalar1=-1.0, scalar2=None, op0=ALU.mult), reads=[lg], writes=[lg])
        kd = Tl(A.f32(8)); blk = Tl(A.f32(8))
        for d in range(2):
            for h in range(4):
                ix = d * 4 + h
                S.op("act", lambda e, d=d, ix=ix: e.activation(out=kd.ap[:, ix:ix + 1], in_=cstt.ap[:, d:d + 1], func=AF.Exp, scale=lg.ap[:, ix:ix + 1]),
                     reads=[cstt, lg], writes=[kd])
                S.op("act", lambda e, ix=ix: e.activation(out=blk.ap[:, ix:ix + 1], in_=cstt.ap[:, 2:3], func=AF.Exp, scale=lg.ap[:, ix:ix + 1]),
                     reads=[cstt, lg], writes=[blk])
        S.op("dve", lambda e: e.tensor_scalar(out=kd.ap, in0=kd.ap, scalar1=1.0 / 16.0, scalar2=None, op0=ALU.mult), reads=[kd], writes=[kd])
        onesf = Tl(A.f32(256))
        S.op("pool", lambda e: e.memset(onesf.ap, 1.0), writes=[onesf])
        dect = [Tl(A.f32(1024)), Tl(A.f32(1024))]
        qdt = [Tl(A.f32(4 * 512)), Tl(A.f32(4 * 512))]
        maskT = Tl(A.f32(4 * 128))
        mtmp = Tl(A.f32(128))
        for d in range(2):
            for h in range(4):
                ix = d * 4 + h
                S.op("dve", lambda e, d=d, h=h, ix=ix: e.tensor_scalar(out=dect[d].ap[:, h * 256:(h + 1) * 256], in0=onesf.ap, scalar1=kd.ap[:, ix:ix + 1],
                                                                   scalar2=None, op0=ALU.mult), reads=[onesf, kd], writes=[dect[d]])
                for r in range(4):
                    S.op("act", lambda e, d=d, h=h, ix=ix, r=r: e.activation(out=qdt[d].ap[:, h * 512 + r * 128:h * 512 + (r + 1) * 128],
                                                                          in_=cstt.ap[:, 4 + d * 128:4 + (d + 1) * 128], func=AF.Exp,
                                                                          scale=lg.ap[:, ix:ix + 1]), reads=[cstt, lg], writes=[qdt[d]])
        for h in range(4):
            S.op("act", lambda e, h=h: e.activation(out=maskT.ap[:, h * 128:(h + 1) * 128], in_=cstt.ap[:, 260:388], func=AF.Exp, scale=lg.ap[:, h:h + 1]),
                 reads=[cstt, lg], writes=[maskT])
            S.op("act", lambda e, h=h: e.activation(out=mtmp.ap, in_=cstt.ap[:, 388:516], func=AF.Exp, scale=lg.ap[:, 4 + h:5 + h]),
                 reads=[cstt, lg], writes=[mtmp])
            S.op("dve", lambda e, h=h: e.scalar_tensor_tensor(out=maskT.ap[:, h * 128:(h + 1) * 128], in0=maskT.ap[:, h * 128:(h + 1) * 128],
                                                              scalar=1.0 / 16.0, in1=mtmp.ap, op0=ALU.mult, op1=ALU.mult),
                 reads=[maskT, mtmp], writes=[maskT])
        base_mark = A.top

        def ret_proj(src, src_res, Tn, which, tabs, Dd, part, w3, w_tl):
            TT = min(512, Tn)
            xring, hring = self.make_norm_rings(TT, nh=2, nx=2)
            if part == 0:
                tabF = Ring([{"c": Tl(A.f32(TT)), "s": Tl(A.f32(TT))} for _ in range(2)])
                tmp = Ring([{k: Tl(A.f32(TT)) for k in ("t1", "t2", "o1", "o2")} for _ in range(2)])
                stb = Ring([Tl(A.bf16(TT)) for _ in range(6)])
            else:
                tabT = Ring([{"c": Tl(A.f32(256)), "s": Tl(A.f32(256))} for _ in range(2)])
                tk = Ring([{"u1": Tl(A.f32(256)), "u2": Tl(A.f32(256)), "ko": Tl(A.f32(512))} for _ in range(2)])
                kst = Ring([{"f": Tl(A.bf16(1024)), "b": Tl(A.bf16(1024))} for _ in range(2)])
                vst = Ring([Tl(A.bf16(2048)) for _ in range(2)])
                gst = Ring([Tl(A.bf16(2048)) for _ in range(2)])
            dres = Dd["res"]
            cnt = [0, 0]
            for tt, hT, h3 in self.norm_tiles(src, src_res, Tn, which, xring, hring, None, TT):
                cols = slice(tt * TT, (tt + 1) * TT)
                if part == 0:
                    tf = tabF.next()
                    S.dma("act", tf["c"].ap, tabs["cF"][:, cols], writes=[tf["c"]])
                    S.dma("act", tf["s"].ap, tabs["sF"][:, cols], writes=[tf["s"]])
                for qk in (range(2) if part == 0 else ()):
                    for h in range(4):
                        b0 = (cnt[0] % 2) * 2
                        cnt[0] += 1
                        for half in range(2):
                            oc = qk * 8 + h * 2 + half
                            pt = self.psum(b0 + half)[:, 0:TT]
                            for kk in range(8):
                                S.op("pe", lambda e, kk=kk, oc=oc, pt=pt, h3=h3: e.matmul(pt, lhsT=w3[:, kk, oc * 128:(oc + 1) * 128], rhs=h3[:, kk, :],
                                                                                      start=(kk == 0), stop=(kk == 7)),
                                     reads=[hT, w_tl], writes=[self.ps_res[b0 + half]])
                        x1 = self.psum(b0)[:, 0:TT]; x2 = self.psum(b0 + 1)[:, 0:TT]
                        r1, r2 = self.ps_res[b0], self.ps_res[b0 + 1]
                        tm = tmp.next()
                        t1, t2, o1, o2 = tm["t1"], tm["t2"], tm["o1"], tm["o2"]
                        S.op("dve", lambda e, x1=x1, t1=t1, tf=tf: e.tensor_tensor(out=t1.ap, in0=x1, in1=tf["c"].ap, op=ALU.mult), reads=[r1, tf["c"]], writes=[t1])
                        S.op("dve", lambda e, x2=x2, t2=t2, tf=tf: e.tensor_tensor(out=t2.ap, in0=x2, in1=tf["s"].ap, op=ALU.mult), reads=[r2, tf["s"]], writes=[t2])
                        S.op("pool", lambda e, t1=t1, t2=t2, o1=o1: e.tensor_tensor(out=o1.ap, in0=t1.ap, in1=t2.ap, op=ALU.subtract), reads=[t1, t2], writes=[o1])
                        S.op("dve", lambda e, x1=x1, t1=t1, tf=tf: e.tensor_tensor(out=t1.ap, in0=x1, in1=tf["s"].ap, op=ALU.mult), reads=[r1, tf["s"], o1], writes=[t1])
                        S.op("dve", lambda e, x2=x2, t2=t2, tf=tf: e.tensor_tensor(out=t2.ap, in0=x2, in1=tf["c"].ap, op=ALU.mult), reads=[r2, tf["c"], o1], writes=[t2])
                        S.op("pool", lambda e, t1=t1, t2=t2, o2=o2: e.tensor_tensor(out=o2.ap, in0=t1.ap, in1=t2.ap, op=ALU.add), reads=[t1, t2], writes=[o2])
                        for half, o in ((0, o1), (1, o2)):
                            row0 = h * 256 + half * 128
                            if qk == 0:
                                st = stb.next()
                                S.op("act", lambda e, st=st, o=o: e.activation(out=st.ap, in_=o.ap, func=AF.Copy), reads=[o], writes=[st])
                                S.dma("sp", Dd["Q"][row0:row0 + 128, cols], st.ap, reads=[st], writes=[dres])
                                for d, nm in ((0, "QF"), (1, "QB")):
                                    st = stb.next()
                                    S.op("pool" if d == 0 else "dve", lambda e, st=st, o=o, d=d, h=h: e.tensor_tensor(out=st.ap, in0=o.ap, in1=qdt[d].ap[:, h * 512:h * 512 + TT],
                                                                                                              op=ALU.mult), reads=[o, qdt[d]], writes=[st])
                                    S.dma("sp", Dd[nm][row0:row0 + 128, cols], st.ap, reads=[st], writes=[dres])
                            else:
                                st = stb.next()
                                S.op("act", lambda e, st=st, o=o: e.activation(out=st.ap, in_=o.ap, func=AF.Copy), reads=[o], writes=[st])
                                S.dma("sp", Dd["KT"][row0:row0 + 128, cols], st.ap, reads=[st], writes=[dres])
                for s in (range(TT // 128) if part == 1 else ()):
                    ti = tt * (TT // 128) + s
                    rows = slice(ti * 128, (ti + 1) * 128)
                    tT = tabT.next()
                    S.dma("act", tT["c"].ap, tabs["cT"][rows, :], writes=[tT["c"]])
                    S.dma("act", tT["s"].ap, tabs["sT"][rows, :], writes=[tT["s"]])
                    cT3 = tT["c"].ap.rearrange("p (a c) -> p a c", a=2); sT3 = tT["s"].ap.rearrange("p (a c) -> p a c", a=2)
                    ks = kst.next()

                    def tok_mm(col0, bank):
                        pt = self.psum(bank)
                        for kk in range(8):
                            S.op("pe", lambda e, kk=kk, pt=pt, s=s, h3=h3, col0=col0: e.matmul(pt, lhsT=h3[:, kk, s * 128:(s + 1) * 128],
                                                                                          rhs=w3[:, kk, col0:col0 + 512], start=(kk == 0), stop=(kk == 7)),
                                 reads=[hT, w_tl], writes=[self.ps_res[bank]])
                        return pt
                    for g in range(2):
                        bank = 4 + cnt[1] % 2
                        cnt[1] += 1
                        pt = tok_mm(g * 512, bank)
                        pr = self.ps_res[bank]
                        pv = pt.rearrange("p (a b c) -> p a b c", a=2, b=2)
                        x1, x2 = pv[:, :, 0, :], pv[:, :, 1, :]
                        t = tk.next()
                        u1, u2, ko = t["u1"], t["u2"], t["ko"]
                        u13 = u1.ap.rearrange("p (a c) -> p a c", a=2); u23 = u2.ap.rearrange("p (a c) -> p a c", a=2)
                        ko4 = ko.ap.rearrange("p (a b c) -> p a b c", a=2, b=2)
                        S.op("dve", lambda e, x1=x1, u13=u13, cT3=cT3: e.tensor_tensor(out=u13, in0=x1, in1=cT3, op=ALU.mult), reads=[pr, tT["c"]], writes=[u1])
                        S.op("dve", lambda e, x2=x2, u23=u23, sT3=sT3: e.tensor_tensor(out=u23, in0=x2, in1=sT3, op=ALU.mult), reads=[pr, tT["s"]], writes=[u2])
                        S.op("pool", lambda e, u13=u13, u23=u23, ko4=ko4: e.tensor_tensor(out=ko4[:, :, 0, :], in0=u13, in1=u23, op=ALU.subtract), reads=[u1, u2], writes=[ko])
                        S.op("dve", lambda e, x1=x1, u13=u13, sT3=sT3: e.tensor_tensor(out=u13, in0=x1, in1=sT3, op=ALU.mult), reads=[pr, tT["s"], ko], writes=[u1])
                        S.op("dve", lambda e, x2=x2, u23=u23, cT3=cT3: e.tensor_tensor(out=u23, in0=x2, in1=cT3, op=ALU.mult), reads=[pr, tT["c"], ko], writes=[u2])
                        S.op("pool", lambda e, u13=u13, u23=u23, ko4=ko4: e.tensor_tensor(out=ko4[:, :, 1, :], in0=u13, in1=u23, op=ALU.add), reads=[u1, u2], writes=[ko])
                        S.op("pool", lambda e, ko=ko, ks=ks, g=g: e.tensor_tensor(out=ks["f"].ap[:, g * 512:(g + 1) * 512], in0=ko.ap, in1=dect[0].ap[:, g * 512:(g + 1) * 512], op=ALU.mult),
                             reads=[ko, dect[0]], writes=[ks["f"]])
                        S.op("dve", lambda e, ko=ko, ks=ks, g=g: e.tensor_tensor(out=ks["b"].ap[:, g * 512:(g + 1) * 512], in0=ko.ap, in1=dect[1].ap[:, g * 512:(g + 1) * 512], op=ALU.mult),
                             reads=[ko, dect[1]], writes=[ks["b"]])
                    S.dma("sp", Dd["KF"][rows, :], ks["f"].ap, reads=[ks["f"]], writes=[dres])
                    S.dma("sp", Dd["KB"][rows, :], ks["b"].ap, reads=[ks["b"]], writes=[dres])
                    vs = vst.next(); gs_ = gst.next()
                    for g in range(4):
                        bank = 4 + cnt[1] % 2
                        cnt[1] += 1
                        pt = tok_mm(1024 + g * 512, bank)
                        S.op("act", lambda e, pt=pt, vs=vs, g=g: e.activation(out=vs.ap[:, g * 512:(g + 1) * 512], in_=pt, func=AF.Copy), reads=[self.ps_res[bank]], writes=[vs])
                    S.dma("sp", Dd["V"][rows, :], vs.ap, reads=[vs], writes=[dres])
                    for g in range(4):
                        bank = 4 + cnt[1] % 2
                        cnt[1] += 1
                        pt = tok_mm(3072 + g * 512, bank)
                        S.op("act", lambda e, pt=pt, gs_=gs_, g=g: e.activation(out=gs_.ap[:, g * 512:(g + 1) * 512], in_=pt, func=AF.Silu), reads=[self.ps_res[bank]], writes=[gs_])
                    S.dma("sp", Dd["G"][rows, :], gs_.ap, reads=[gs_], writes=[dres])

        for part, (c0, ncol) in enumerate(((0, 2048), (1024, 5120))):
            A.top = base_mark
            w_tl = Tl(A.bf16(8 * ncol))
            w3 = w_tl.ap.rearrange("p (k n) -> p k n", k=8)
            wmark = A.top
            stage = Ring([Tl(A.f32(2048)) for _ in range(2)])
            self.load_w_bf16(W["ret_w_in"], 1024, 6144, w3, w_tl, col0=c0, ncols=ncol, stage_ring=stage)
            S.barrier(); A.top = wmark
            ret_proj(c_src, self.c_res, TC, 1, tabs_c, DC, part, w3, w_tl)
            S.barrier(); A.top = wmark
            ret_proj(x_src, self.x_res, T, 0, tabs_x, DX, part, w3, w_tl)
            S.barrier()
        A.top = base_mark

        Sst = [[Tl(A.f32(512)) for _ in range(8)] for _ in range(2)]
        Sbf = [[Tl(A.bf16(512)) for _ in range(8)] for _ in range(2)]
        for d in range(2):
            for hd in range(8):
                S.op("pool", lambda e, d=d, hd=hd: e.memset(Sst[d][hd].ap, 0.0), writes=[Sst[d][hd]])
                S.op("pool", lambda e, d=d, hd=hd: e.memset(Sbf[d][hd].ap, 0.0), writes=[Sbf[d][hd]])
        pmark = A.top
        ucnt = [0]

        def state_update(d, kt_, vt_):
            for h in range(4):
                for dc in range(2):
                    hd = h * 2 + dc
                    bank = 6 + ucnt[0] % 2
                    ucnt[0] += 1
                    pt = self.psum(bank)
                    S.op("pe", lambda e, pt=pt, h=h, dc=dc, kt_=kt_, vt_=vt_: e.matmul(pt, lhsT=kt_.ap[:, h * 256 + dc * 128:h * 256 + (dc + 1) * 128],
                                                                                   rhs=vt_.ap[:, h * 512:(h + 1) * 512], start=True, stop=True),
                         reads=[kt_, vt_], writes=[self.ps_res[bank]])
                    st = Sst[d][hd]
                    S.op("dve", lambda e, pt=pt, st=st, d=d, h=h: e.scalar_tensor_tensor(out=st.ap, in0=st.ap, scalar=blk.ap[:, d * 4 + h:d * 4 + h + 1], in1=pt,
                                                                                    op0=ALU.mult, op1=ALU.add), reads=[st, blk, self.ps_res[bank]], writes=[st])
                    S.op("act", lambda e, st=st, d=d, hd=hd: e.activation(out=Sbf[d][hd].ap, in_=st.ap, func=AF.Copy), reads=[st], writes=[Sbf[d][hd]])

        def passB(Dd, Tn):
            nch = Tn // 128
            ring = Ring([{"q": Tl(A.bf16(1024)), "k": Tl(A.bf16(1024)), "v": Tl(A.bf16(2048)), "ob": Tl(A.f32(2048))} for _ in range(2)])
            bc = 0
            for c in range(nch - 1, -1, -1):
                r = ring.next()
                q, k, v, ob = r["q"], r["k"], r["v"], r["ob"]
                cs = slice(c * 128, (c + 1) * 128)
                q3 = q.ap.rearrange("p (a t) -> p a t", a=8)
                S.dma("sp", q3, Dd["QB"].rearrange("(a p) t -> p a t", p=128)[:, :, cs], reads=[Dd["res"]], writes=[q])
                S.dma("act", k.ap, Dd["KB"][cs, :], reads=[Dd["res"]], writes=[k])
                S.dma("sp", v.ap, Dd["V"][cs, :], reads=[Dd["res"]], writes=[v])
                for h in range(4):
                    bank = bc % 4
                    bc += 1
                    pt = self.psum(bank)
                    for dc in range(2):
                        hd = h * 2 + dc
                        S.op("pe", lambda e, pt=pt, hd=hd, dc=dc, q3=q3: e.matmul(pt, lhsT=q3[:, hd, :], rhs=Sbf[1][hd].ap, start=(dc == 0), stop=(dc == 1)),
                             reads=[q, Sbf[1][hd]], writes=[self.ps_res[bank]])
                    if h % 2 == 0:
                        S.op("act", lambda e, pt=pt, ob=ob, h=h: e.activation(out=ob.ap[:, h * 512:(h + 1) * 512], in_=pt, func=AF.Copy), reads=[self.ps_res[bank]], writes=[ob])
                    else:
                        S.op("dve", lambda e, pt=pt, ob=ob, h=h: e.tensor_copy(out=ob.ap[:, h * 512:(h + 1) * 512], in_=pt), reads=[self.ps_res[bank]], writes=[ob])
                S.dma("sp", Dd["OB"][cs, :], ob.ap, reads=[ob], writes=[Dd["res"]])
                state_update(1, k, v)

        def passF(Dd, Tn, want_y):
            nch = Tn // 128
            ring = Ring([{"q": Tl(A.bf16(1024)), "qf": Tl(A.bf16(1024)), "kt": Tl(A.bf16(1024)), "k": Tl(A.bf16(1024)), "v": Tl(A.bf16(2048)),
                          "g": Tl(A.bf16(2048)), "ob": Tl(A.f32(2048)), "o": Tl(A.f32(2048)), "y": Tl(A.bf16(2048)), "yT": Tl(A.bf16(2048)),
                          "ss": Tl(A.f32(4)), "junk": Tl(A.bf16(512))} for _ in range(2)])
            pTr = Ring([Tl(A.bf16(128)) for _ in range(3)])
            bs = bo = 0
            for c in range(nch):
                r = ring.next()
                q, qf, kt, k, v, g, ob, o, y, yT, ss, junk = (r[n_] for n_ in ("q", "qf", "kt", "k", "v", "g", "ob", "o", "y", "yT", "ss", "junk"))
                cs = slice(c * 128, (c + 1) * 128)
                q3 = q.ap.rearrange("p (a t) -> p a t", a=8); qf3 = qf.ap.rearrange("p (a t) -> p a t", a=8)
                kt3 = kt.ap.rearrange("p (a t) -> p a t", a=8)
                fm = lambda nm: Dd[nm].rearrange("(a p) t -> p a t", p=128)[:, :, cs]
                S.dma("sp", q3, fm("Q"), reads=[Dd["res"]], writes=[q])
                S.dma("act", qf3, fm("QF"), reads=[Dd["res"]], writes=[qf])
                S.dma("sp", kt3, fm("KT"), reads=[Dd["res"]], writes=[kt])
                S.dma("act", k.ap, Dd["KF"][cs, :], reads=[Dd["res"]], writes=[k])
                S.dma("sp", v.ap, Dd["V"][cs, :], reads=[Dd["res"]], writes=[v])
                if want_y:
                    S.dma("act", g.ap, Dd["G"][cs, :], reads=[Dd["res"]], writes=[g])
                    S.dma("sp", ob.ap, Dd["OB"][cs, :], reads=[Dd["res"]], writes=[ob])
                    def emit_S(h):
                        nonlocal bs
                        bS = bs % 2
                        bs += 1
                        pS = self.psum(bS)[:, 0:128]
                        for dc in range(2):
                            hd = h * 2 + dc
                            S.op("pe", lambda e, pS=pS, hd=hd, dc=dc, kt3=kt3, q3=q3: e.matmul(pS, lhsT=kt3[:, hd, :], rhs=q3[:, hd, :], start=(dc == 0), stop=(dc == 1)),
                                 reads=[kt, q], writes=[self.ps_res[bS]])
                        pT = pTr.next()
                        S.op("dve", lambda e, pS=pS, pT=pT, h=h: e.tensor_tensor(out=pT.ap, in0=pS, in1=maskT.ap[:, h * 128:(h + 1) * 128], op=ALU.mult),
                             reads=[self.ps_res[bS], maskT], writes=[pT])
                        return pT
                    pT_next = emit_S(0)
                    for h in range(4):
                        pT = pT_next
                        if h + 1 < 4:
                            pT_next = emit_S(h + 1)
                        bO = 2 + bo % 2
                        bo += 1
                        pO = self.psum(bO)
                        S.op("pe", lambda e, pO=pO, pT=pT, v=v, h=h: e.matmul(pO, lhsT=pT.ap, rhs=v.ap[:, h * 512:(h + 1) * 512], start=True, stop=False),
                             reads=[pT, v], writes=[self.ps_res[bO]])
                        for dc in range(2):
                            hd = h * 2 + dc
                            S.op("pe", lambda e, pO=pO, hd=hd, dc=dc, qf3=qf3: e.matmul(pO, lhsT=qf3[:, hd, :], rhs=Sbf[0][hd].ap, start=False, stop=(dc == 1)),
                                 reads=[qf, Sbf[0][hd]], writes=[self.ps_res[bO]])
                        S.op("dve", lambda e, pO=pO, o=o, ob=ob, h=h: e.tensor_tensor(out=o.ap[:, h * 512:(h + 1) * 512], in0=pO, in1=ob.ap[:, h * 512:(h + 1) * 512], op=ALU.add),
                             reads=[self.ps_res[bO], ob], writes=[o])
                        S.op("act", lambda e, o=o, junk=junk, ss=ss, h=h: e.activation(out=junk.ap, in_=o.ap[:, h * 512:(h + 1) * 512], func=AF.Square, accum_out=ss.ap[:, h:h + 1]),
                             reads=[o], writes=[junk, ss])
                    S.op("dve", lambda e, ss=ss: e.tensor_scalar(out=ss.ap, in0=ss.ap, scalar1=1.0 / 512.0, scalar2=EPS, op0=ALU.mult, op1=ALU.add), reads=[ss], writes=[ss])
                    S.op("act", lambda e, ss=ss: e.activation(out=ss.ap, in_=ss.ap, func=AF.Sqrt), reads=[ss], writes=[ss])
                    S.op("dve", lambda e, ss=ss: e.reciprocal(out=ss.ap, in_=ss.ap), reads=[ss], writes=[ss])
                    for h in range(4):
                        S.op("dve", lambda e, o=o, y=y, g=g, ss=ss, h=h: e.scalar_tensor_tensor(
                            out=y.ap[:, h * 512:(h + 1) * 512], in0=o.ap[:, h * 512:(h + 1) * 512], scalar=ss.ap[:, h:h + 1], in1=g.ap[:, h * 512:(h + 1) * 512],
                            op0=ALU.mult, op1=ALU.mult), reads=[o, ss, g], writes=[y])
                    for half in range(2):
                        bank = 4 + half
                        pb = self.psum(bank, 1, BF16)
                        for j in range(8):
                            jj_ = half * 8 + j
                            S.op("pe", lambda e, pb=pb, j=j, jj_=jj_, y=y: e.transpose(out=pb[:, j * 128:(j + 1) * 128], in_=y.ap[:, jj_ * 128:(jj_ + 1) * 128], identity=self.ident.ap),
                                 reads=[y, self.ident], writes=[self.ps_res[bank]])
                        if half == 0:
                            S.op("act", lambda e, pb=pb, yT=yT: e.activation(out=yT.ap[:, 0:1024], in_=pb, func=AF.Copy), reads=[self.ps_res[bank]], writes=[yT])
                        else:
                            S.op("dve", lambda e, pb=pb, yT=yT: e.tensor_copy(out=yT.ap[:, 1024:2048], in_=pb), reads=[self.ps_res[bank]], writes=[yT])
                    S.dma("sp", Dd["Y"].rearrange("(a p) t -> p a t", p=128)[:, :, cs], yT.ap.rearrange("p (a t) -> p a t", a=16), reads=[yT], writes=[self.Y_res])
                state_update(0, k, v)

        passB(DC, TC)
        S.barrier(); A.top = pmark
        passF(DC, TC, self.need_ctx)
        S.barrier(); A.top = pmark
        passB(DX, T)
        S.barrier(); A.top = pmark
        passF(DX, T, True)
        S.barrier(); A.top = base_mark
        wo = Tl(A.bf16(16 * 1024))
        wo3 = wo.ap.rearrange("p (k n) -> p k n", k=16)
        stage = Ring([Tl(A.f32(2048)) for _ in range(3)])
        self.load_w_bf16(W["ret_w_out"], 2048, 1024, wo3, wo, stage_ring=stage)
        self.out_proj(DX["Y"], 2048, wo3, wo, T, self.gt_bc, x_src, self.xs, self.x_res)
        if self.need_ctx:
            self.out_proj(DC["Y"], 2048, wo3, wo, TC, self.gtc_bc, c_src, self.cs, self.c_res)

    def mix_hy(self, x_src, c_src):
        S, A, W, T, TC = self.S, self.A, self.W, self.T, self.TC
        f32 = np.float32
        bf = ml_dtypes.bfloat16
        N = 16384
        TWO_PI = 2.0 * math.pi
        n_ = np.arange(128, dtype=np.float64)
        Fc = np.exp(-2j * np.pi * np.outer(n_, n_) / 128.0)
        Fre, Fim = Fc.real, Fc.imag
        twc = np.exp(-2j * np.pi * np.outer(n_, n_) / N)
        FS = np.concatenate([Fre, Fim], axis=1)
        cmat = np.concatenate([Fre, Fim, -Fim, Fre, -Fim, Fim, Fre, Fre / N, Fim / N], axis=1)
        cmat_d = self.const("c_hy_cmat", cmat.astype(bf))
        tw_d = self.const("c_hy_tw", np.concatenate([np.tile(twc.real, (1, 4)), np.tile(twc.imag, (1, 4))], axis=1).astype(f32))

        def fs_rows(L):
            M = L // 128
            rows = list(range(M)) + ([] if M == 64 else [])
            kr = list(range(M)) + list(range(128 - M, 128))
            return FS[:M].astype(bf), FS[kr].astype(bf)
        fss_x, fsk_x = fs_rows(T); fss_c, fsk_c = fs_rows(TC)
        fs_d = {"sx": self.const("c_hy_fssx", fss_x), "kx": self.const("c_hy_fskx", fsk_x),
                "sc": self.const("c_hy_fssc", fss_c), "kc": self.const("c_hy_fskc", fsk_c)}

        def zfeat(L):
            bands = 16
            t = np.linspace(0.0, 1.0, L, dtype=f32)[:, None]
            w = (f32(2.0 * math.pi) * np.arange(L, dtype=f32)[:, None] / f32(L)).astype(f32)
            fr = np.linspace(1e-4, bands - 1, bands, dtype=f32)[None, :]
            arg = (fr * w).astype(f32).astype(np.float64)
            z = np.concatenate([t, np.cos(arg).astype(f32), -np.sin(arg).astype(f32)], axis=-1)
            return np.ascontiguousarray(z.T.astype(f32)), np.ascontiguousarray(np.tile(t.T, (128, 1)).astype(f32))
        zT_x, tn_x = zfeat(T); zT_c, tn_c = zfeat(TC)
        zt_d = {"x": self.const("c_hy_zTx", zT_x), "c": self.const("c_hy_zTc", zT_c)}
        tn_d = {"x": self.const("c_hy_tnx", tn_x), "c": self.const("c_hy_tnc", tn_c)}
        deltas = np.abs(np.linspace(math.log(1e-2) / 1.5, math.log(1e-2) / 0.3, 1024, dtype=f32))
        nd_d = self.const("c_hy_ndelta", np.ascontiguousarray((-deltas).reshape(8, 128).T.astype(f32)))

        U = {"x": self.scratch("hy_Ux", [4096, T]), "c": self.scratch("hy_Uc", [4096, TC])}
        Z = {"x": self.scratch("hy_Zx", [3072, T]), "c": self.scratch("hy_Zc", [3072, TC])}
        KERN = {"x": self.scratch("hy_KERNx", [2, 1024, N]), "c": self.scratch("hy_KERNc", [2, 1024, N])}
        KF = {"x": self.scratch("hy_KFx", [2, 128, 1024, 256]), "c": self.scratch("hy_KFc", [2, 128, 1024, 256])}
        Y = {"x": self.scratch("hy_Yx", [1024, T], BF16), "c": self.scratch("hy_Yc", [1024, TC], BF16)}
        KC = self.scratch("hy_KC", [2, 1024, 2 * TC])
        identf_d = self.const("c_identf", np.eye(128, dtype=f32))
        jmat_d = self.const("c_jmat", np.eye(128, dtype=f32)[::-1].copy().astype(bf))
        Ures = {"x": FreeRes(), "c": FreeRes()}
        dres = FreeRes()
        self.Y_res = FreeRes()
        LEN = {"x": T, "c": TC}
        seqs = ["c", "x"] if self.need_ctx else ["c", "x"]

        w_tl = Tl(A.bf16(8 * 4096))
        w3 = w_tl.ap.rearrange("p (k n) -> p k n", k=8)
        stage = Ring([Tl(A.f32(2048)) for _ in range(3)])
        self.load_w_bf16(W["hy_w_in"], 1024, 4096, w3, w_tl, stage_ring=stage)
        mark = A.top
        self.proj_to_dram(c_src, self.c_res, TC, 1, w3, w_tl, 32, U["c"], Ures["c"], min(512, TC))
        A.top = mark
        S.barrier()
        self.proj_to_dram(x_src, self.x_res, T, 0, w3, w_tl, 32, U["x"], Ures["x"], min(512, T))
        S.barrier(); A.reset()

        def col(src_ap, n):
            t = Tl(A.f32(n))
            S.dma("sp", t.ap, src_ap.rearrange("(n p) -> p n", p=128), writes=[t], slow=True)
            return t
        cw = [col(W["hy_conv_w"][k], 24) for k in range(3)]
        cb = col(W["hy_conv_b"], 24)
        TM = max(T, TC)
        ur = Ring([Tl(A.f32(TM + 2)) for _ in range(2)])
        zr = Ring([Tl(A.f32(TM)) for _ in range(2)])
        for sq in ("c", "x"):
            Tn = LEN[sq]
            for n in range(24):
                ub = ur.next(); zb = zr.next()
                S.op("pool", lambda e, ub=ub: e.memset(ub.ap[:, 0:1], 0.0), writes=[ub])
                S.op("pool", lambda e, ub=ub, Tn=Tn: e.memset(ub.ap[:, Tn + 1:Tn + 2], 0.0), writes=[ub])
                S.dma("sp", ub.ap[:, 1:Tn + 1], U[sq][n * 128:(n + 1) * 128, :], reads=[Ures[sq]], writes=[ub])
                S.op("dve", lambda e, ub=ub, zb=zb, n=n, Tn=Tn: e.tensor_scalar(out=zb.ap[:, 0:Tn], in0=ub.ap[:, 0:Tn], scalar1=cw[0].ap[:, n:n + 1],
                                                                         scalar2=cb.ap[:, n:n + 1], op0=ALU.mult, op1=ALU.add),
                     reads=[ub, cw[0], cb], writes=[zb])
                for k in (1, 2):
                    S.op("dve", lambda e, ub=ub, zb=zb, n=n, k=k, Tn=Tn: e.scalar_tensor_tensor(out=zb.ap[:, 0:Tn], in0=ub.ap[:, k:k + Tn], scalar=cw[k].ap[:, n:n + 1],
                                                                                      in1=zb.ap[:, 0:Tn], op0=ALU.mult, op1=ALU.add),
                         reads=[ub, cw[k], zb], writes=[zb])
                S.dma("act", Z[sq][n * 128:(n + 1) * 128, :], zb.ap[:, 0:Tn], reads=[zb], writes=[dres])
        S.barrier(); A.reset()

        fw1 = Tl(A.f32(64)); fw2 = Tl(A.f32(64)); fw3 = Tl(A.f32(4096))
        S.dma("sp", fw1.ap[0:33, :], W["hy_fw1"], writes=[fw1])
        S.dma("sp", fw2.ap[0:64, :], W["hy_fw2"], writes=[fw2])
        S.dma("sp", fw3.ap[0:64, :], W["hy_fw3"], writes=[fw3])
        sm = Tl(A.f32(8))
        S.dma("sp", sm.ap[0:64, 0:1], W["hy_freq"].rearrange("(p o) -> p o", o=1), writes=[sm])
        S.dma("sp", sm.ap[0:64, 1:2], W["hy_fb1"].rearrange("(p o) -> p o", o=1), writes=[sm])
        S.dma("sp", sm.ap[0:64, 2:3], W["hy_fb2"].rearrange("(p o) -> p o", o=1), writes=[sm])
        OFF = math.pi + TWO_PI * 16
        for i in range(2):
            S.op("dve", lambda e, i=i: e.tensor_scalar(out=sm.ap[0:64, 3 + i:4 + i], in0=sm.ap[0:64, 1 + i:2 + i], scalar1=sm.ap[0:64, 0:1], scalar2=None, op0=ALU.mult),
                 reads=[sm], writes=[sm])
        nd = Tl(A.f32(8))
        S.dma("sp", nd.ap, nd_d, writes=[nd])
        skc = Tl(A.f32(16))
        S.dma("sp", skc.ap.rearrange("p (o n) -> p o n", o=2), W["hy_skip"].rearrange("o (n p) -> p o n", p=128), writes=[skc], slow=True)
        zero = Tl(A.f32(1))
        S.op("pool", lambda e: e.memset(zero.ap, 0.0), writes=[zero])
        MAGIC = 12582912.0
        hidT = Tl(A.f32(TM))
        f0 = Tl(A.f32(TM)); f1 = Tl(A.f32(TM)); rev = Tl(A.f32(TM))
        ztr = Ring([Tl(A.f32(512)) for _ in range(2)])
        h1r = Ring([{"a": Tl(A.f32(512)), "r": Tl(A.f32(512))} for _ in range(2)])
        tnr = Ring([Tl(A.f32(512)) for _ in range(2)])
        decr = Ring([Tl(A.f32(512)) for _ in range(2)])
        sums = Tl(A.f32(4))

        def sin_layer(pt, bias_col, dst_ap, TW, hb):
            a, r = hb["a"], hb["r"]
            S.op("dve", lambda e: e.tensor_scalar(out=a.ap[0:64, 0:TW], in0=pt, scalar1=sm.ap[0:64, 0:1], scalar2=sm.ap[0:64, bias_col:bias_col + 1],
                                                  op0=ALU.mult, op1=ALU.add), reads=[self.ps_res[0], sm], writes=[a])
            S.op("dve", lambda e: e.tensor_scalar(out=r.ap[0:64, 0:TW], in0=a.ap[0:64, 0:TW], scalar1=1.0 / TWO_PI, scalar2=MAGIC, op0=ALU.mult, op1=ALU.add),
                 reads=[a], writes=[r])
            S.op("dve", lambda e: e.tensor_scalar(out=r.ap[0:64, 0:TW], in0=r.ap[0:64, 0:TW], scalar1=-MAGIC, scalar2=None, op0=ALU.add), reads=[r], writes=[r])
            S.op("dve", lambda e: e.scalar_tensor_tensor(out=a.ap[0:64, 0:TW], in0=r.ap[0:64, 0:TW], scalar=-TWO_PI, in1=a.ap[0:64, 0:TW], op0=ALU.mult, op1=ALU.add),
                 reads=[r, a], writes=[a])
            S.op("act", lambda e: e.activation(out=dst_ap, in_=a.ap[0:64, 0:TW], func=AF.Sin), reads=[a], writes=[hidT, hb["r"]])

        for sq in ("c", "x"):
            L = LEN[sq]
            M = L // 128
            TW = min(512, L)
            for tt in range(L // TW):
                cs = slice(tt * TW, (tt + 1) * TW)
                zt = ztr.next(); hb = h1r.next()
                S.dma("sp", zt.ap[0:33, 0:TW], zt_d[sq][:, cs], writes=[zt])
                pt = self.psum(0)[0:64, 0:TW]
                S.op("pe", lambda e, pt=pt, zt=zt, TW=TW: e.matmul(pt, lhsT=fw1.ap[0:33, 0:64], rhs=zt.ap[0:33, 0:TW], start=True, stop=True),
                     reads=[fw1, zt], writes=[self.ps_res[0]])
                h1 = hb["r"]
                sin_layer(pt, 3, h1.ap[0:64, 0:TW], TW, hb)
                S.op("pe", lambda e, pt=pt, h1=h1, TW=TW: e.matmul(pt, lhsT=fw2.ap[0:64, 0:64], rhs=h1.ap[0:64, 0:TW], start=True, stop=True),
                     reads=[fw2, h1], writes=[self.ps_res[0]])
                hb2 = h1r.next()
                sin_layer(pt, 4, hidT.ap[0:64, cs], TW, hb2)
            for o in range(2):
                for cc in range(8):
                    fd = (f0, f1)
                    for dr in range(2):
                        j = o * 16 + dr * 8 + cc
                        for tt in range(L // TW):
                            cs = slice(tt * TW, (tt + 1) * TW)
                            b = 1 + (tt % 2)
                            pt = self.psum(b)[:, 0:TW]
                            S.op("pe", lambda e, pt=pt, j=j, cs=cs: e.matmul(pt, lhsT=fw3.ap[0:64, j * 128:(j + 1) * 128], rhs=hidT.ap[0:64, cs], start=True, stop=True),
                                 reads=[fw3, hidT], writes=[self.ps_res[b]])
                            tnt = tnr.next(); dc_ = decr.next()
                            S.dma("sp", tnt.ap[:, 0:TW], tn_d[sq][:, cs], writes=[tnt])
                            S.op("act", lambda e, tnt=tnt, dc_=dc_, cc=cc, TW=TW: e.activation(out=dc_.ap[:, 0:TW], in_=tnt.ap[:, 0:TW], func=AF.Exp, scale=nd.ap[:, cc:cc + 1]),
                                 reads=[tnt, nd], writes=[dc_])
                            S.op("dve", lambda e, pt=pt, dc_=dc_, dr=dr, cs=cs, TW=TW, fd=fd: e.tensor_tensor(out=fd[dr].ap[:, cs], in0=pt, in1=dc_.ap[:, 0:TW], op=ALU.mult),
                                 reads=[self.ps_res[b], dc_], writes=[fd[dr]])
                        s0 = dr
                        S.op("act", lambda e, dr=dr, s0=s0, L=L, fd=fd: e.activation(out=rev.ap[:, s0:L], in_=fd[dr].ap[:, s0:L], func=AF.Abs, accum_out=sums.ap[:, dr:dr + 1]),
                             reads=[fd[dr]], writes=[rev, sums])
                    S.op("dve", lambda e: e.tensor_tensor(out=sums.ap[:, 2:3], in0=sums.ap[:, 0:1], in1=sums.ap[:, 1:2], op=ALU.add), reads=[sums], writes=[sums])
                    S.op("dve", lambda e: e.reciprocal(out=sums.ap[:, 3:4], in_=sums.ap[:, 2:3]), reads=[sums], writes=[sums])
                    S.op("dve", lambda e, L=L: e.tensor_scalar(out=f0.ap[:, 0:L], in0=f0.ap[:, 0:L], scalar1=sums.ap[:, 3:4], scalar2=None, op0=ALU.mult),
                         reads=[f0, sums], writes=[f0])
                    S.op("dve", lambda e, o=o, cc=cc: e.tensor_scalar(out=f0.ap[:, 0:1], in0=f0.ap[:, 0:1], scalar1=skc.ap[:, o * 8 + cc:o * 8 + cc + 1], scalar2=None, op0=ALU.add),
                         reads=[f0, skc], writes=[f0])
                    rows = slice(cc * 128, (cc + 1) * 128)
                    S.op("dve", lambda e, L=L: e.tensor_scalar(out=rev.ap[:, 0:L - 1], in0=f1.ap[:, 1:L][:, ::-1], scalar1=sums.ap[:, 3:4], scalar2=None, op0=ALU.mult),
                         reads=[f1, sums], writes=[rev])
                    if sq == "c":
                        S.dma("sp", KC[o, rows, L - 1:2 * L - 1], f0.ap[:, 0:L], reads=[f0], writes=[dres])
                        S.dma("act", KC[o, rows, 0:L - 1], rev.ap[:, 0:L - 1], reads=[rev], writes=[dres])
                    else:
                        S.dma("sp", KERN[sq][o, rows, 0:L], f0.ap[:, 0:L], reads=[f0], writes=[dres])
                        S.dma("act", KERN[sq][o, rows, N - L + 1:N], rev.ap[:, 0:L - 1], reads=[rev], writes=[dres])
                        S.dma("sp", KERN[sq][o, rows, N - L:N - L + 1], zero.ap, reads=[zero], writes=[dres], slow=True)
        S.barrier(); A.reset()

        cm = Tl(A.bf16(9 * 128))
        S.dma("sp", cm.ap, cmat_d, writes=[cm])
        Fre_, Fim_, nFim_ = cm.ap[:, 0:128], cm.ap[:, 128:256], cm.ap[:, 256:384]
        GS1_, GS2_ = cm.ap[:, 384:640], cm.ap[:, 640:896]
        FreN_, FimN_ = cm.ap[:, 896:1024], cm.ap[:, 1024:1152]
        tw = Tl(A.f32(1024))
        S.dma("sp", tw.ap, tw_d, writes=[tw])
        twR = tw.ap[:, 0:512].rearrange("p (c k) -> p c k", c=4); twI = tw.ap[:, 512:1024].rearrange("p (c k) -> p c k", c=4)
        fs_t = {}
        for key, d in fs_d.items():
            t = Tl(A.bf16(256))
            R = d.shape[0]
            S.dma("sp", t.ap[0:R, :], d, writes=[t])
            fs_t[key] = (t, R)
        NCH = 2
        chs = []
        for ci in range(NCH):
            chs.append({"b": ci * 4,
                        "t": [Tl(A.f32(512)) for _ in range(4)],
                        "cre": Tl(A.bf16(512)), "cim": Tl(A.bf16(512)), "pre": Tl(A.bf16(512)), "pim": Tl(A.bf16(512)),
                        "zre": Tl(A.bf16(512)), "zim": Tl(A.bf16(512)),
                        "kf": Ring([Tl(A.f32(1024)) for _ in range(4)]),
                        "af": Ring([Tl(A.f32(512)) for _ in range(2)]), "ab": Ring([Tl(A.bf16(512)) for _ in range(3)]),
                        "x1": Ring([Tl(A.f32(512)) for _ in range(2)]), "x2": Ring([Tl(A.f32(512)) for _ in range(2)]),
                        "g": Ring([Tl(A.f32(512)) for _ in range(2)]), "y1": Ring([Tl(A.bf16(512)) for _ in range(2)]),
                        "y2": Tl(A.f32(512)), "yo": Ring([Tl(A.bf16(512)) for _ in range(2)]),
                        "kfo": Ring([Tl(A.f32(1024)) for _ in range(3)])})

        def v3(ap, c=4):
            return ap.rearrange("p (c k) -> p c k", c=c)

        def cmul(ch, src_re, src_im, rr, ri, are, aim, bre, bim, conj, out_re, out_im):
            t1, t2, t3, t4 = ch["t"]
            pr = [rr, ri]
            S.op("dve", lambda e: e.tensor_tensor(out=v3(t1.ap), in0=are, in1=bre, op=ALU.mult), reads=pr + src_re, writes=[t1])
            S.op("dve", lambda e: e.tensor_tensor(out=v3(t2.ap), in0=aim, in1=bim, op=ALU.mult), reads=pr + src_im, writes=[t2])
            S.op("pool", lambda e: e.tensor_tensor(out=out_re.ap, in0=t1.ap, in1=t2.ap, op=(ALU.add if conj else ALU.subtract)), reads=[t1, t2], writes=[out_re])
            S.op("dve", lambda e: e.tensor_tensor(out=v3(t3.ap), in0=aim, in1=bre, op=ALU.mult), reads=pr + src_re, writes=[t3])
            S.op("dve", lambda e: e.tensor_tensor(out=v3(t4.ap), in0=are, in1=bim, op=ALU.mult), reads=pr + src_im, writes=[t4])
            S.op("pool", lambda e: e.tensor_tensor(out=out_im.ap, in0=t3.ap, in1=t4.ap, op=(ALU.subtract if conj else ALU.add)), reads=[t3, t4], writes=[out_im])

        def fwd_fft(ch, a_tl, a3, R, fs):
            b = ch["b"]
            ps01 = self.psum(b, 2)
            for ci in range(4):
                S.op("pe", lambda e, ci=ci: e.matmul(ps01[:, ci * 256:(ci + 1) * 256], lhsT=a3[0:R, ci, :], rhs=fs.ap[0:R, :], start=True, stop=True),
                     reads=[a_tl, fs], writes=[self.ps_res[b + ci // 2]])
            yield
            p4 = ps01.rearrange("p (c r k) -> p c r k", c=4, r=2)
            rr = self.ps_res[b]; ri = self.ps_res[b + 1]
            cmul(ch, [tw], [tw], rr, ri, p4[:, :, 0, :], p4[:, :, 1, :], twR, twI, False, ch["cre"], ch["cim"])
            yield
            for bank, l1, l2 in ((b + 2, Fre_, nFim_), (b + 3, Fim_, Fre_)):
                pt = self.psum(bank)
                S.op("pe", lambda e, pt=pt, l1=l1: e.matmul(pt, lhsT=l1, rhs=ch["cre"].ap, start=True, stop=False), reads=[cm, ch["cre"]], writes=[self.ps_res[bank]])
                S.op("pe", lambda e, pt=pt, l2=l2: e.matmul(pt, lhsT=l2, rhs=ch["cim"].ap, start=False, stop=True), reads=[cm, ch["cim"]], writes=[self.ps_res[bank]])
            yield

        def inv_fft(ch, kf_tl, M):
            b = ch["b"]
            k4 = kf_tl.ap.rearrange("p (c r k) -> p c r k", c=4, r=2)
            cmul(ch, [kf_tl], [kf_tl], self.ps_res[b + 2], self.ps_res[b + 3], v3(self.psum(b + 2)), v3(self.psum(b + 3)), k4[:, :, 0, :], k4[:, :, 1, :],
                 False, ch["pre"], ch["pim"])
            yield
            ps01 = self.psum(b, 2)
            pre3, pim3 = v3(ch["pre"].ap), v3(ch["pim"].ap)
            for ci in range(4):
                S.op("pe", lambda e, ci=ci: e.matmul(ps01[:, ci * 256:(ci + 1) * 256], lhsT=pre3[:, ci, :], rhs=GS1_, start=True, stop=False),
                     reads=[ch["pre"], cm], writes=[self.ps_res[b + ci // 2]])
                S.op("pe", lambda e, ci=ci: e.matmul(ps01[:, ci * 256:(ci + 1) * 256], lhsT=pim3[:, ci, :], rhs=GS2_, start=False, stop=True),
                     reads=[ch["pim"], cm], writes=[self.ps_res[b + ci // 2]])
            yield
            p4 = ps01.rearrange("p (c r k) -> p c r k", c=4, r=2)
            cmul(ch, [tw], [tw], self.ps_res[b], self.ps_res[b + 1], p4[:, :, 0, :], p4[:, :, 1, :], twR, twI, True, ch["zre"], ch["zim"])
            yield
            pt = self.psum(b + 2)[0:M, :]
            S.op("pe", lambda e: e.matmul(pt, lhsT=FreN_[:, 0:M], rhs=ch["zre"].ap, start=True, stop=False), reads=[cm, ch["zre"]], writes=[self.ps_res[b + 2]])
            S.op("pe", lambda e: e.matmul(pt, lhsT=FimN_[:, 0:M], rhs=ch["zim"].ap, start=False, stop=True), reads=[cm, ch["zim"]], writes=[self.ps_res[b + 2]])
            yield

        def run_chains(gens):
            act = list(gens)
            while act:
                for g in list(act):
                    try:
                        next(g)
                    except StopIteration:
                        act.remove(g)

        def kern_chain(ch, sq, groups):
            L = LEN[sq]; M = L // 128
            fs, R = fs_t["k" + sq]
            b = ch["b"]
            def kload(og):
                o, g = og
                c0 = g * 4
                af = ch["kfo"].next()
                a3f = af.ap[:, 0:512].rearrange("p (c k) -> p c k", c=4)
                src = KERN[sq][o, c0:c0 + 4, :].rearrange("c (n1 n2) -> n1 c n2", n2=128)
                S.dma("sp", a3f[0:M], src[0:M], reads=[dres], writes=[af])
                if R > M:
                    S.dma("act", a3f[M:2 * M], src[128 - M:128], reads=[dres], writes=[af])
                ab = ch["ab"].next()
                S.op("act", lambda e, af=af, ab=ab: e.activation(out=ab.ap[0:R, :], in_=af.ap[0:R, 0:512], func=AF.Copy), reads=[af], writes=[ab])
                return ab
            nxt = kload(groups[0]) if groups else None
            for gi_, (o, g) in enumerate(groups):
                c0 = g * 4
                ab = nxt
                if gi_ + 1 < len(groups):
                    nxt = kload(groups[gi_ + 1])
                yield from fwd_fft(ch, ab, v3(ab.ap), R, fs)
                ko = ch["kf"].next()
                k4 = ko.ap.rearrange("p (c r k) -> p c r k", c=4, r=2)
                S.op("act", lambda e, k4=k4: e.activation(out=k4[:, :, 0, :], in_=v3(self.psum(b + 2)), func=AF.Copy), reads=[self.ps_res[b + 2]], writes=[ko])
                S.op("act", lambda e, k4=k4: e.activation(out=k4[:, :, 1, :], in_=v3(self.psum(b + 3)), func=AF.Copy), reads=[self.ps_res[b + 3]], writes=[ko])
                S.dma("sp", KF[sq][o, :, c0:c0 + 4, :], ko.ap.rearrange("p (c x) -> p c x", c=4), reads=[ko], writes=[dres])
                yield

        for sq in ("x",):
            allg = [(o, g) for o in range(2) for g in range(256)]
            run_chains([kern_chain(chs[i], sq, allg[i::NCH]) for i in range(NCH)])
        S.barrier()

        def conv_chain(ch, sq, groups):
            L = LEN[sq]; M = L // 128
            fs, R = fs_t["s" + sq]
            b = ch["b"]
            Zv = Z[sq].rearrange("c (n1 n2) -> n1 c n2", n2=128)
            Uv = U[sq].rearrange("c (n1 n2) -> n1 c n2", n2=128)
            Yv = Y[sq].rearrange("c (n1 n2) -> n1 c n2", n2=128)
            def cload(g):
                c0 = g * 4
                af = ch["af"].next(); x1 = ch["x1"].next(); x2 = ch["x2"].next(); gg = ch["g"].next(); ab = ch["ab"].next()
                S.dma("sp", v3(af.ap)[0:M], Zv[:, c0:c0 + 4, :], reads=[dres], writes=[af])
                S.dma("act", v3(x1.ap)[0:M], Zv[:, 1024 + c0:1024 + c0 + 4, :], reads=[dres], writes=[x1])
                S.dma("sp", v3(x2.ap)[0:M], Zv[:, 2048 + c0:2048 + c0 + 4, :], reads=[dres], writes=[x2])
                S.dma("act", v3(gg.ap)[0:M], Uv[:, 3072 + c0:3072 + c0 + 4, :], reads=[Ures[sq]], writes=[gg])
                S.op("act", lambda e, af=af, ab=ab: e.activation(out=ab.ap[0:M, :], in_=af.ap[0:M, :], func=AF.Copy), reads=[af], writes=[ab])
                S.op("act", lambda e, gg=gg: e.activation(out=gg.ap[0:M, :], in_=gg.ap[0:M, :], func=AF.Silu), reads=[gg], writes=[gg])
                kf0 = ch["kf"].next()
                S.dma("sp", kf0.ap.rearrange("p (c x) -> p c x", c=4), KF[sq][0, :, c0:c0 + 4, :], reads=[dres], writes=[kf0])
                kf1 = ch["kf"].next()
                S.dma("act", kf1.ap.rearrange("p (c x) -> p c x", c=4), KF[sq][1, :, c0:c0 + 4, :], reads=[dres], writes=[kf1])
                return x1, x2, gg, ab, kf0, kf1
            nxt = cload(groups[0]) if groups else None
            for gi_, g in enumerate(groups):
                c0 = g * 4
                x1, x2, gg, ab, kf0, kf1 = nxt
                if gi_ + 1 < len(groups):
                    nxt = cload(groups[gi_ + 1])
                yield from fwd_fft(ch, ab, v3(ab.ap), M, fs)
                yield from inv_fft(ch, kf0, M)
                y1 = ch["y1"].next()
                S.op("dve", lambda e, y1=y1, x1=x1: e.tensor_tensor(out=y1.ap[0:M, :], in0=self.psum(b + 2)[0:M, :], in1=x1.ap[0:M, :], op=ALU.mult),
                     reads=[self.ps_res[b + 2], x1], writes=[y1])
                yield
                yield from fwd_fft(ch, y1, v3(y1.ap), M, fs)
                yield from inv_fft(ch, kf1, M)
                y2 = ch["y2"]; yo = ch["yo"].next()
                S.op("dve", lambda e, y2=y2, x2=x2: e.tensor_tensor(out=y2.ap[0:M, :], in0=self.psum(b + 2)[0:M, :], in1=x2.ap[0:M, :], op=ALU.mult),
                     reads=[self.ps_res[b + 2], x2], writes=[y2])
                S.op("pool", lambda e, y2=y2, yo=yo, gg=gg: e.tensor_tensor(out=yo.ap[0:M, :], in0=y2.ap[0:M, :], in1=gg.ap[0:M, :], op=ALU.mult),
                     reads=[y2, gg], writes=[yo])
                S.dma("sp", Yv[:, c0:c0 + 4, :], v3(yo.ap)[0:M], reads=[yo], writes=[self.Y_res])
                yield

        for sq in ("x",):
            allg = list(range(256))
            run_chains([conv_chain(chs[i], sq, allg[i::NCH]) for i in range(NCH)])
        S.barrier(); A.reset()

        if self.need_ctx:
            Lc = TC; NH = Lc // 128
            identf = Tl(A.f32(128))
            S.dma("sp", identf.ap, identf_d, writes=[identf])
            tm = [Tl(A.f32(NH * 1024)) for _ in range(4)]
            t3 = lambda ap: ap.rearrange("p (h c) -> p h c", h=NH)
            ld = Ring([Tl(A.f32(Lc)) for _ in range(3)])
            tcnt = 0
            for n in range(32):
                src = Z["c"][n * 128:(n + 1) * 128, :] if n < 24 else U["c"][3072 + (n - 24) * 128:3072 + (n - 23) * 128, :]
                dstT = tm[n // 8]; cc = n % 8
                lt = ld.next()
                S.dma("sp" if n % 2 == 0 else "act", lt.ap, src, reads=[dres, Ures["c"]], writes=[lt])
                for h in range(NH):
                    bank = tcnt % 2
                    tcnt += 1
                    pt = self.psum(bank)[:, 0:128]
                    S.op("pe", lambda e, pt=pt, lt=lt, h=h: e.transpose(out=pt, in_=lt.ap[:, h * 128:(h + 1) * 128], identity=identf.ap),
                         reads=[lt, identf], writes=[self.ps_res[bank]])
                    if tcnt % 2 == 0:
                        S.op("act", lambda e, pt=pt, dstT=dstT, h=h, cc=cc: e.activation(out=t3(dstT.ap)[:, h, cc * 128:(cc + 1) * 128], in_=pt, func=AF.Copy),
                             reads=[self.ps_res[bank]], writes=[dstT])
                    else:
                        S.op("dve", lambda e, pt=pt, dstT=dstT, h=h, cc=cc: e.tensor_copy(out=t3(dstT.ap)[:, h, cc * 128:(cc + 1) * 128], in_=pt),
                             reads=[self.ps_res[bank]], writes=[dstT])
            vT, x1T, x2T, gT = tm
            S.op("act", lambda e: e.activation(out=gT.ap, in_=gT.ap, func=AF.Silu), reads=[gT], writes=[gT])
            nb = Tl(A.bf16(NH * 1024))
            ub = Tl(A.bf16(NH * 1024))
            ub3 = t3(ub.ap); nb3 = t3(nb.ap)
            jm = Tl(A.bf16(128))
            S.dma("sp", jm.ap, jmat_d, writes=[jm])
            S.op("act", lambda e: e.activation(out=nb.ap, in_=vT.ap, func=AF.Copy), reads=[vT], writes=[nb])

            def reverse_blocks():
                for sh in range(NH):
                    for hf in range(2):
                        bank = 2 * sh + hf
                        pt = self.psum(bank)
                        S.op("pe", lambda e, pt=pt, sh=sh, hf=hf: e.matmul(pt, lhsT=jm.ap, rhs=nb3[:, sh, hf * 512:(hf + 1) * 512], start=True, stop=True),
                             reads=[jm, nb], writes=[self.ps_res[bank]])
                        S.op("act", lambda e, pt=pt, sh=sh, hf=hf: e.activation(out=ub3[:, sh, hf * 512:(hf + 1) * 512], in_=pt, func=AF.Copy),
                             reads=[self.ps_res[bank]], writes=[ub])
            reverse_blocks()
            CG = 8
            trf = Ring([Tl(A.f32(CG * Lc)) for _ in range(4)])
            trb = Ring([Tl(A.bf16(CG * Lc)) for _ in range(6)])
            y2 = Tl(A.f32(1024)); yb = Tl(A.bf16(NH * 1024)); ycT = Tl(A.bf16(8 * Lc))
            ycT3 = ycT.ap.rearrange("p (k t) -> p k t", k=8)
            cvt = 0
            for o in range(2):
                for cg in range(1024 // CG):
                    tbs = []
                    for sh in range(NH):
                        tf = trf.next(); tb = trb.next()
                        off = (o * 1024 + cg * CG) * 2 * Lc + (Lc - 128) - sh * 128
                        src = bass.AP(KC.tensor, off, [[1, 128], [2 * Lc, CG], [1, Lc]])
                        S.dma("sp" if sh == 0 else "act", tf.ap.rearrange("p (c t) -> p c t", c=CG), src, reads=[dres], writes=[tf])
                        if cvt % 2 == 0:
                            S.op("act", lambda e, tf=tf, tb=tb: e.activation(out=tb.ap, in_=tf.ap, func=AF.Copy), reads=[tf], writes=[tb])
                        else:
                            S.op("dve", lambda e, tf=tf, tb=tb: e.tensor_copy(out=tb.ap, in_=tf.ap), reads=[tf], writes=[tb])
                        cvt += 1
                        tbs.append(tb)
                    for ci in range(CG):
                        c = cg * CG + ci
                        for th in range(NH):
                            bank = 4 + 2 * th + c // 512
                            pcol = self.psum(bank)[:, (c % 512):(c % 512) + 1]
                            for sh in range(NH):
                                tb = tbs[sh]
                                tb3 = tb.ap.rearrange("p (c t) -> p c t", c=CG)
                                S.op("pe", lambda e, pcol=pcol, tb3=tb3, ci=ci, th=th, sh=sh, c=c: e.matmul(pcol, lhsT=tb3[:, ci, th * 128:(th + 1) * 128], rhs=ub3[:, sh, c:c + 1],
                                                                                                  start=(sh == 0), stop=(sh == NH - 1)),
                                     reads=[tb, ub], writes=[self.ps_res[bank]])
                for th in range(NH):
                    pr = [self.ps_res[4 + 2 * th], self.ps_res[5 + 2 * th]]
                    pth = self.psum(4 + 2 * th, 2)
                    if o == 0:
                        S.op("dve", lambda e, pth=pth, th=th: e.tensor_tensor(out=nb3[:, th, :], in0=pth, in1=t3(x1T.ap)[:, th, :], op=ALU.mult),
                             reads=pr + [x1T], writes=[nb])
                        if th == NH - 1:
                            reverse_blocks()
                    else:
                        S.op("dve", lambda e, pth=pth, th=th: e.tensor_tensor(out=y2.ap, in0=pth, in1=t3(x2T.ap)[:, th, :], op=ALU.mult), reads=pr + [x2T], writes=[y2])
                        S.op("pool", lambda e, th=th: e.tensor_tensor(out=t3(yb.ap)[:, th, :], in0=y2.ap, in1=t3(gT.ap)[:, th, :], op=ALU.mult), reads=[y2, gT], writes=[yb])
                        pb = self.psum(th, 1, BF16)
                        for cc in range(8):
                            S.op("pe", lambda e, pb=pb, th=th, cc=cc: e.transpose(out=pb[:, cc * 128:(cc + 1) * 128], in_=t3(yb.ap)[:, th, cc * 128:(cc + 1) * 128], identity=self.ident.ap),
                                 reads=[yb, self.ident], writes=[self.ps_res[th]])
                        S.op("act", lambda e, pb=pb, th=th: e.activation(out=ycT3[:, :, th * 128:(th + 1) * 128], in_=pb.rearrange("p (k t) -> p k t", k=8), func=AF.Copy),
                             reads=[self.ps_res[th]], writes=[ycT])
            S.dma("sp", Y["c"].rearrange("(k p) t -> p k t", p=128), ycT3, reads=[ycT], writes=[self.Y_res])
            S.barrier(); A.reset()

        wo = Tl(A.bf16(8 * 1024))
        wo3 = wo.ap.rearrange("p (k n) -> p k n", k=8)
        stage = Ring([Tl(A.f32(2048)) for _ in range(3)])
        self.load_w_bf16(W["hy_w_out"], 1024, 1024, wo3, wo, stage_ring=stage)
        self.out_proj(Y["x"], 1024, wo3, wo, T, self.gt_bc, x_src, self.xs, self.x_res)
        if self.need_ctx:
            self.out_proj(Y["c"], 1024, wo3, wo, TC, self.gtc_bc, c_src, self.cs, self.c_res)

    def mix_att(self, x_src, c_src):
        S, A, W, T, TC = self.S, self.A, self.W, self.T, self.TC
        f32 = np.float32
        inv = (f32(10000.0) ** (-np.arange(32, dtype=f32) / f32(32))).astype(f32)
        rows = np.repeat(np.arange(T // GRID_W, dtype=f32), GRID_W); cols = np.tile(np.arange(GRID_W, dtype=f32), T // GRID_W)
        ang = np.concatenate([(rows[:, None] * inv[None, :]).astype(f32), (cols[:, None] * inv[None, :]).astype(f32)], axis=-1).astype(np.float64)
        cs_, sn_ = np.cos(ang).astype(f32).T, np.sin(ang).astype(f32).T
        tabs_x = (self.const("c_att_cos", np.concatenate([cs_, cs_], axis=0)), self.const("c_att_sin", np.concatenate([-sn_, sn_], axis=0)))
        tabs_c = (self.const("c_att_cosc", np.ones((128, TC), f32)), self.const("c_att_sinc", np.zeros((128, TC), f32)))
        QT = self.scratch("att_QT", [1024, T], BF16); GT = self.scratch("att_GT", [1024, T], BF16)
        Y = self.scratch("att_Y", [1024, T], BF16)
        dres = FreeRes()
        self.Y_res = FreeRes()
        NK = T + TC
        nkt = NK // 128
        KT = [Tl(A.bf16(NK)) for _ in range(2)]
        V = Tl(A.bf16(nkt * 256))
        V3 = V.ap.rearrange("p (k d) -> p k d", d=256)
        gq = Tl(A.f32(2))
        S.dma("sp", gq.ap[:, 0:1], W["att_q_norm_g"].rearrange("(p o) -> p o", o=1), writes=[gq])
        S.dma("sp", gq.ap[:, 1:2], W["att_k_norm_g"].rearrange("(p o) -> p o", o=1), writes=[gq])
        base_mark = A.top
        w_tl = Tl(A.bf16(8 * 2560))
        w3 = w_tl.ap.rearrange("p (k n) -> p k n", k=8)
        wmark = A.top
        stage = Ring([Tl(A.f32(2048)) for _ in range(2)])
        self.load_w_bf16(W["att_w_in"], 1024, 2560, w3, w_tl, stage_ring=stage)
        S.barrier(); A.top = wmark

        def att_proj(src, src_res, Tn, which, tabs, latent, key0):
            TT = min(512, Tn)
            xring, hring = self.make_norm_rings(TT, nh=2, nx=2)
            tab = Ring([{"c": Tl(A.f32(TT)), "s": Tl(A.f32(TT))} for _ in range(2)])
            tmp = Ring([{"sq": Tl(A.bf16(TT)), "rs": Tl(A.f32(TT)), "qn": Tl(A.f32(TT)), "sw": Tl(A.f32(TT)), "t1": Tl(A.f32(TT)), "t2": Tl(A.f32(TT))}
                        for _ in range(2)])
            stq = Ring([Tl(A.bf16(TT)) for _ in range(3)])
            cnt = [0, 0]
            for tt, hT, h3 in self.norm_tiles(src, src_res, Tn, which, xring, hring, None, TT):
                cols = slice(tt * TT, (tt + 1) * TT)
                tb = tab.next()
                S.dma("act", tb["c"].ap, tabs[0][:, cols], writes=[tb["c"]])
                S.dma("act", tb["s"].ap, tabs[1][:, cols], writes=[tb["s"]])
                heads = ([("q", i) for i in range(8)] if latent else []) + [("k", i) for i in range(2)]
                for kind, idx in heads:
                    oc = idx if kind == "q" else 8 + idx
                    gi = 0 if kind == "q" else 1
                    b = cnt[0] % 4
                    cnt[0] += 1
                    pt = self.psum(b)[:, 0:TT]
                    for kk in range(8):
                        S.op("pe", lambda e, kk=kk, oc=oc, pt=pt, h3=h3: e.matmul(pt, lhsT=w3[:, kk, oc * 128:(oc + 1) * 128], rhs=h3[:, kk, :],
                                                                              start=(kk == 0), stop=(kk == 7)), reads=[hT, w_tl], writes=[self.ps_res[b]])
                    tm = tmp.next()
                    sq, rs, qn, sw, t1, t2 = tm["sq"], tm["rs"], tm["qn"], tm["sw"], tm["t1"], tm["t2"]
                    S.op("act", lambda e, pt=pt, sq=sq: e.activation(out=sq.ap, in_=pt, func=AF.Square), reads=[self.ps_res[b]], writes=[sq])
                    b2 = 4 + cnt[1] % 2
                    cnt[1] += 1
                    p2 = self.psum(b2)[:, 0:TT]
                    S.op("pe", lambda e, p2=p2, sq=sq: e.matmul(p2, lhsT=self.ones_bf.ap, rhs=sq.ap, start=True, stop=True), reads=[sq, self.ones_bf], writes=[self.ps_res[b2]])
                    S.op("act", lambda e, p2=p2, rs=rs: e.activation(out=rs.ap, in_=p2, func=AF.Sqrt, scale=1.0 / 128.0, bias=self.eps_col.ap), reads=[self.ps_res[b2], self.eps_col], writes=[rs])
                    S.op("dve", lambda e, rs=rs: e.reciprocal(out=rs.ap, in_=rs.ap), reads=[rs], writes=[rs])
                    S.op("dve", lambda e, pt=pt, qn=qn, rs=rs, gi=gi: e.scalar_tensor_tensor(out=qn.ap, in0=pt, scalar=gq.ap[:, gi:gi + 1], in1=rs.ap, op0=ALU.mult, op1=ALU.mult),
                         reads=[self.ps_res[b], gq, rs], writes=[qn])
                    S.op("act", lambda e, qn=qn, sw=sw: e.activation(out=sw.ap[0:64, :], in_=qn.ap[64:128, :], func=AF.Copy), reads=[qn], writes=[sw])
                    S.op("pool", lambda e, qn=qn, sw=sw: e.tensor_copy(out=sw.ap[64:128, :], in_=qn.ap[0:64, :]), reads=[qn], writes=[sw])
                    S.op("pool", lambda e, qn=qn, t1=t1, tb=tb: e.tensor_tensor(out=t1.ap, in0=qn.ap, in1=tb["c"].ap, op=ALU.mult), reads=[qn, tb["c"]], writes=[t1])
                    S.op("dve", lambda e, sw=sw, t2=t2, tb=tb: e.tensor_tensor(out=t2.ap, in0=sw.ap, in1=tb["s"].ap, op=ALU.mult), reads=[sw, tb["s"]], writes=[t2])
                    if kind == "q":
                        st = stq.next()
                        S.op("pool", lambda e, t1=t1, t2=t2, st=st: e.tensor_tensor(out=st.ap, in0=t1.ap, in1=t2.ap, op=ALU.add), reads=[t1, t2], writes=[st])
                        S.dma("sp", QT[idx * 128:(idx + 1) * 128, cols], st.ap, reads=[st], writes=[dres])
                    else:
                        kdst = KT[idx].ap[:, key0 + tt * TT:key0 + (tt + 1) * TT]
                        S.op("pool", lambda e, t1=t1, t2=t2, kdst=kdst: e.tensor_tensor(out=kdst, in0=t1.ap, in1=t2.ap, op=ALU.add), reads=[t1, t2], writes=[KT[idx]])
                if latent:
                    for j in range(8):
                        b = cnt[0] % 4
                        cnt[0] += 1
                        pt = self.psum(b)[:, 0:TT]
                        for kk in range(8):
                            S.op("pe", lambda e, kk=kk, j=j, pt=pt, h3=h3: e.matmul(pt, lhsT=w3[:, kk, 1536 + j * 128:1536 + (j + 1) * 128], rhs=h3[:, kk, :],
                                                                                 start=(kk == 0), stop=(kk == 7)), reads=[hT, w_tl], writes=[self.ps_res[b]])
                        st = stq.next()
                        S.op("act", lambda e, pt=pt, st=st: e.activation(out=st.ap, in_=pt, func=AF.Silu), reads=[self.ps_res[b]], writes=[st])
                        S.dma("sp", GT[j * 128:(j + 1) * 128, cols], st.ap, reads=[st], writes=[dres])
                for s_ in range(TT // 128):
                    kti = (key0 + tt * TT) // 128 + s_
                    b2 = 6 + cnt[1] % 2
                    cnt[1] += 1
                    pv = self.psum(b2)[:, 0:256]
                    for kk in range(8):
                        S.op("pe", lambda e, kk=kk, pv=pv, h3=h3, s_=s_: e.matmul(pv, lhsT=h3[:, kk, s_ * 128:(s_ + 1) * 128], rhs=w3[:, kk, 1280:1536],
                                                                              start=(kk == 0), stop=(kk == 7)), reads=[hT, w_tl], writes=[self.ps_res[b2]])
                    S.op("act", lambda e, pv=pv, kti=kti: e.activation(out=V3[:, kti, :], in_=pv, func=AF.Copy), reads=[self.ps_res[b2]], writes=[V])

        att_proj(c_src, self.c_res, TC, 1, tabs_c, False, T)
        S.barrier(); A.top = wmark
        att_proj(x_src, self.x_res, T, 0, tabs_x, True, 0)
        S.barrier(); A.top = base_mark

        scale = 1.0 / math.sqrt(128.0)
        qr = Ring([{"q": Tl(A.bf16(1024)), "g": Tl(A.bf16(1024)), "y": Tl(A.bf16(1024))} for _ in range(2)])
        pTr = Ring([Tl(A.bf16(512)) for _ in range(7)])
        accs = Ring([{"d": Tl(A.f32(512)), "p": Tl(A.f32(512)), "rs": Tl(A.f32(512)), "o": Tl(A.f32(512))} for _ in range(2)])
        QTv = QT.rearrange("(h p) t -> p h t", p=128); GTv = GT.rearrange("(h p) t -> p h t", p=128); Yv = Y.rearrange("(h p) t -> p h t", p=128)
        gcnt = 0
        for qt in range(T // 128):
            cs = slice(qt * 128, (qt + 1) * 128)
            qb = qr.next()
            q, gt_, y = qb["q"], qb["g"], qb["y"]
            q3 = q.ap.rearrange("p (h t) -> p h t", h=8); g3 = gt_.ap.rearrange("p (h t) -> p h t", h=8)
            S.dma("sp", q3, QTv[:, :, cs], reads=[dres], writes=[q])
            S.dma("act", g3, GTv[:, :, cs], reads=[dres], writes=[gt_])
            for g in range(2):
                q2 = q.ap[:, g * 512:(g + 1) * 512]
                bO = 4 + gcnt % 2
                bSum = 6 + gcnt % 2
                gcnt += 1
                pO = self.psum(bO)
                ac = accs.next()
                LA = 3
                pTs = {}
                for kk_ in range(nkt + LA):
                    if kk_ < nkt:
                        kt = kk_
                        bS = kt % 4
                        pS = self.psum(bS)
                        S.op("pe", lambda e, pS=pS, g=g, kt=kt, q2=q2: e.matmul(pS, lhsT=KT[g].ap[:, kt * 128:(kt + 1) * 128], rhs=q2, start=True, stop=True),
                             reads=[KT[g], q], writes=[self.ps_res[bS]])
                        pT = pTr.next()
                        pTs[kt] = pT
                        S.op("act", lambda e, pS=pS, pT=pT: e.activation(out=pT.ap, in_=pS, func=AF.Exp, scale=scale), reads=[self.ps_res[bS]], writes=[pT])
                    kt = kk_ - LA
                    if kt >= 0:
                        pT = pTs.pop(kt)
                        S.op("pe", lambda e, pO=pO, g=g, kt=kt, pT=pT: e.matmul(pO, lhsT=V3[:, kt, g * 128:(g + 1) * 128], rhs=pT.ap, start=(kt == 0), stop=(kt == nkt - 1)),
                             reads=[V, pT], writes=[self.ps_res[bO]])
                        eng, at = ("dve", ac["d"]) if kt % 2 == 0 else ("pool", ac["p"])
                        if kt < 2:
                            S.op(eng, lambda e, at=at, pT=pT: e.tensor_copy(out=at.ap, in_=pT.ap), reads=[pT], writes=[at])
                        else:
                            S.op(eng, lambda e, at=at, pT=pT: e.tensor_tensor(out=at.ap, in0=at.ap, in1=pT.ap, op=ALU.add), reads=[at, pT], writes=[at])
                S.op("dve", lambda e, ac=ac: e.tensor_tensor(out=ac["d"].ap, in0=ac["d"].ap, in1=ac["p"].ap, op=ALU.add), reads=[ac["d"], ac["p"]], writes=[ac["d"]])
                pSum = self.psum(bSum)
                S.op("pe", lambda e, pSum=pSum, ac=ac: e.matmul(pSum, lhsT=self.ones_row.ap, rhs=ac["d"].ap, start=True, stop=True),
                     reads=[self.ones_row, ac["d"]], writes=[self.ps_res[bSum]])
                S.op("dve", lambda e, pSum=pSum, ac=ac: e.reciprocal(out=ac["rs"].ap, in_=pSum), reads=[self.ps_res[bSum]], writes=[ac["rs"]])
                S.op("dve", lambda e, pO=pO, ac=ac: e.tensor_tensor(out=ac["o"].ap, in0=pO, in1=ac["rs"].ap, op=ALU.mult), reads=[self.ps_res[bO], ac["rs"]], writes=[ac["o"]])
                S.op("pool", lambda e, ac=ac, y=y, gt_=gt_, g=g: e.tensor_tensor(out=y.ap[:, g * 512:(g + 1) * 512], in0=ac["o"].ap, in1=gt_.ap[:, g * 512:(g + 1) * 512], op=ALU.mult),
                     reads=[ac["o"], gt_], writes=[y])
            S.dma("sp", Yv[:, :, cs], y.ap.rearrange("p (h t) -> p h t", h=8), reads=[y], writes=[self.Y_res])
        S.barrier(); A.top = base_mark
        wo = Tl(A.bf16(8 * 1024))
        wo3 = wo.ap.rearrange("p (k n) -> p k n", k=8)
        stage = Ring([Tl(A.f32(2048)) for _ in range(3)])
        self.load_w_bf16(W["att_w_out"], 1024, 1024, wo3, wo, stage_ring=stage)
        self.out_proj(Y, 1024, wo3, wo, T, self.gt_bc, x_src, self.xs, self.x_res)


_CACHE = {}


def get_program(T, TC, nlayers=4, dbg=False):
    key = (T, TC, nlayers, dbg)
    if key not in _CACHE:
        b = Builder(T, TC, nlayers, dbg)
        nc = b.build()
        _CACHE[key] = (nc, b)
    return _CACHE[key]


def kernel(**inputs):
    x = np.asarray(inputs["x"], dtype=np.float32)
    B, T, _ = x.shape
    TC = inputs["ctx"].shape[1]
    nc, b = get_program(T, TC)
    in_maps = []
    for i in range(B):
        m = {}
        for name in b.ins:
            if name in b.host_consts:
                m[name] = b.host_consts[name]
            elif name == "x":
                m[name] = np.ascontiguousarray(x[i])
            elif name == "c":
                m[name] = np.ascontiguousarray(np.asarray(inputs["c"], dtype=np.float32)[i])
            elif name == "ctx":
                m[name] = np.ascontiguousarray(np.asarray(inputs["ctx"], dtype=np.float32)[i])
            else:
                m[name] = np.ascontiguousarray(np.asarray(inputs[name], dtype=np.float32))
        in_maps.append(m)
    res = run_bass_kernel_spmd(nc, in_maps, core_ids=list(range(B)))
    return np.stack([np.asarray(r["out"]) for r in res.results], axis=0).astype(np.float32)
```

```python
import math
import numpy as np
import ml_dtypes
from contextlib import ExitStack
import concourse.bass as bass
import concourse.mybir as mybir
from concourse.bass_utils import run_bass_kernel_spmd

F32 = mybir.dt.float32
BF16 = mybir.dt.bfloat16
AF = mybir.ActivationFunctionType
ALU = mybir.AluOpType
AX = mybir.AxisListType

P = 128
D = 1024
KD = 8
EPS = 1e-6


class Res:
    __slots__ = ("lw", "rd")

    def __init__(self):
        self.lw = None
        self.rd = {}


class FreeRes(Res):
    __slots__ = ()


class Tl:
    __slots__ = ("ap", "res")

    def __init__(self, ap, res=None):
        self.ap = ap
        self.res = res if res is not None else Res()


class Ring:
    def __init__(self, tiles):
        self.t = tiles
        self.i = 0

    def next(self):
        t = self.t[self.i % len(self.t)]
        self.i += 1
        return t


class Sched:
    ENGS = ("pe", "act", "dve", "pool", "sp")
    NS = {"sp": 14, "act": 8}

    def __init__(self, nc, es):
        self.nc = nc
        self.ops = {e: [] for e in self.ENGS}
        self.cnt = {e: 0 for e in self.ENGS}
        self.known = {e: {} for e in self.ENGS}
        self.sems = {}
        for e in ("pe", "act", "dve", "pool"):
            self.sems[("c", e)] = es.enter_context(nc.semaphore("c_" + e))
        self.dq = {}
        for q, n in self.NS.items():
            for i in range(n):
                self.sems[("d", q, i)] = es.enter_context(nc.semaphore("d_%s_%d" % (q, i)))
            self.dq[q] = {"next": 0, "val": [0] * n}
        self.nops = 0

    def _collect(self, eng, reads, writes):
        deps = {}

        def add(tok, raw):
            if tok is None:
                return
            semkey, val, teng, seq = tok
            if semkey[0] == "c" and teng == eng:
                if eng == "pe" or not raw:
                    return
                if self.cnt[eng] - seq > 3:
                    return
            if deps.get(semkey, 0) < val:
                deps[semkey] = val

        for r in reads:
            add(r.lw, True)
        for w in writes:
            add(w.lw, False)
            for t in w.rd.values():
                add(t, False)
        return deps

    def _waits(self, eng, deps):
        out = []
        kn = self.known[eng]
        for semkey, val in deps.items():
            if kn.get(semkey, 0) >= val:
                continue
            kn[semkey] = val
            out.append((semkey, val))
        return out

    def _commit(self, tok, reads, writes):
        semkey = tok[0]
        for r in reads:
            old = r.rd.get(semkey)
            if old is None or old[1] < tok[1]:
                r.rd[semkey] = tok
        for w in writes:
            w.lw = tok
            w.rd = {}

    @staticmethod
    def _res(lst):
        return [r for r in (x.res if isinstance(x, Tl) else x for x in lst) if not isinstance(r, FreeRes)]

    def op(self, eng, emit, reads=(), writes=()):
        reads = self._res(reads)
        writes = self._res(writes)
        waits = self._waits(eng, self._collect(eng, reads, writes))
        self.cnt[eng] += 1
        semkey = ("c", eng)
        tok = (semkey, self.cnt[eng], eng, self.cnt[eng])
        self.ops[eng].append((waits, emit, semkey, 1))
        self._commit(tok, reads, writes)
        self.nops += 1

    def dma(self, q, out, in_, reads=(), writes=(), slow=False):
        reads = self._res(reads)
        writes = self._res(writes)
        deps = self._collect(q, reads, writes)
        st = self.dq[q]
        i = st["next"] % self.NS[q]
        st["next"] += 1
        semkey = ("d", q, i)
        prev = st["val"][i]
        if prev > 0 and deps.get(semkey, 0) < prev:
            deps[semkey] = prev
        waits = self._waits(q, deps)
        st["val"][i] = prev + 16
        tok = (semkey, prev + 16, q, 0)
        if slow:
            emit = lambda e: e.dma_start(out=out, in_=in_, allow_slow_non_contiguous=True)
        else:
            emit = lambda e: e.dma_start(out=out, in_=in_)
        self.ops[q].append((waits, emit, semkey, 16))
        self._commit(tok, reads, writes)
        self.nops += 1

    def barrier(self):
        deps = {}
        for e in ("pe", "act", "dve", "pool"):
            if self.cnt[e] > 0:
                deps[("c", e)] = self.cnt[e]
        for q, st in self.dq.items():
            for i, v in enumerate(st["val"]):
                if v > 0:
                    deps[("d", q, i)] = v
        for e in self.ENGS:
            d = {k: v for k, v in deps.items() if not (k[0] == "c" and k[1] == e)}
            waits = self._waits(e, d)
            if waits:
                self.ops[e].append((waits, None, None, 0))

    def emit(self):
        self.barrier()
        sems = self.sems
        ops = self.ops

        def run(name):
            def f(eng):
                for waits, emit, semkey, inc in ops[name]:
                    for sk, val in waits:
                        eng.wait_ge(sems[sk], val)
                    if emit is not None:
                        emit(eng).then_inc(sems[semkey], inc)
            return f

        with self.nc.Block() as block:
            block.tensor(run("pe"))
            block.scalar(run("act"))
            block.vector(run("dve"))
            block.gpsimd(run("pool"))
            block.sync(run("sp"))


class Arena:
    def __init__(self, ap, ncols):
        self.ap = ap
        self.n = ncols
        self.top = 0
        self.ptop = ncols

    def reset(self):
        self.top = 0

    def f32(self, cols):
        a = self.ap[:, self.top:self.top + cols]
        self.top += cols
        assert self.top <= self.ptop, "SBUF arena overflow %d > %d" % (self.top, self.ptop)
        return a

    def bf16(self, cols):
        c = (cols + 1) // 2
        return self.f32(c).bitcast(BF16)

    def pf32(self, cols):
        self.ptop -= cols
        assert self.top <= self.ptop
        return self.ap[:, self.ptop:self.ptop + cols]

    def pbf16(self, cols):
        c = (cols + 1) // 2
        return self.pf32(c).bitcast(BF16)


ARENA_COLS = 48 * 1024

LRU_W = 2048
RET_H, RET_DK, RET_DV = 4, 256, 512
ATT_HQ, ATT_G, ATT_D = 8, 2, 128
GRID_W = 64


class Builder:
    def __init__(self, T, TC, nlayers=4, dbg=False):
        self.T, self.TC, self.nlayers, self.dbg = T, TC, nlayers, dbg
        self.nc = bass.Bass("TRN2", target_bir_lowering=False)
        self.ins = {}
        self.host_consts = {}

    def inp(self, name, shape, dt=F32):
        t = self.nc.dram_tensor(name, list(shape), dt, kind="ExternalInput").ap()
        self.ins[name] = t
        return t

    def scratch(self, name, shape, dt=F32):
        return self.nc.dram_tensor(name, list(shape), dt, kind="Internal").ap()

    def const(self, name, arr):
        arr = np.ascontiguousarray(arr)
        dt = BF16 if arr.dtype == ml_dtypes.bfloat16 else F32
        self.host_consts[name] = arr
        return self.inp(name, arr.shape, dt)

    def psum(self, b0, nb=1, dt=F32):
        a = self.ps[:, b0 * 512:(b0 + nb) * 512]
        return a.bitcast(BF16) if dt == BF16 else a

    def build(self):
        nc = self.nc
        T, TC = self.T, self.TC
        I = self.inp
        x = I("x", [T, D]); c = I("c", [D]); ctx = I("ctx", [TC, D]); c_ctx = I("c_ctx", [D])
        W = {}
        for p in ("lru", "ret", "hy", "att"):
            W[p + "_mod_w"] = I(p + "_mod_w", [D, 3 * D]); W[p + "_mod_b"] = I(p + "_mod_b", [3 * D])
            W[p + "_norm_g"] = I(p + "_norm_g", [D])
        W["lru_w_in"] = I("lru_w_in", [D, 4096]); W["lru_conv_w"] = I("lru_conv_w", [4, 2048])
        W["lru_conv_b"] = I("lru_conv_b", [2048]); W["lru_w_r"] = I("lru_w_r", [2, 16, 128, 128])
        W["lru_b_r"] = I("lru_b_r", [2, 2048]); W["lru_w_i"] = I("lru_w_i", [2, 16, 128, 128])
        W["lru_b_i"] = I("lru_b_i", [2, 2048]); W["lru_lambda"] = I("lru_lambda", [2, 2048])
        W["lru_w_out"] = I("lru_w_out", [2048, D])
        W["ret_w_in"] = I("ret_w_in", [D, 6144]); W["ret_decay_logit"] = I("ret_decay_logit", [2, 4])
        W["ret_w_out"] = I("ret_w_out", [2048, D])
        W["hy_w_in"] = I("hy_w_in", [D, 4096]); W["hy_conv_w"] = I("hy_conv_w", [3, 3072])
        W["hy_conv_b"] = I("hy_conv_b", [3072]); W["hy_fw1"] = I("hy_fw1", [33, 64]); W["hy_fb1"] = I("hy_fb1", [64])
        W["hy_fw2"] = I("hy_fw2", [64, 64]); W["hy_fb2"] = I("hy_fb2", [64]); W["hy_fw3"] = I("hy_fw3", [64, 4096])
        W["hy_freq"] = I("hy_freq", [64]); W["hy_skip"] = I("hy_skip", [2, 1024]); W["hy_w_out"] = I("hy_w_out", [1024, D])
        W["att_w_in"] = I("att_w_in", [D, 2560]); W["att_q_norm_g"] = I("att_q_norm_g", [128])
        W["att_k_norm_g"] = I("att_k_norm_g", [128]); W["att_w_out"] = I("att_w_out", [1024, D])
        W["final_norm_g"] = I("final_norm_g", [D])
        self.W = W
        self.x_in, self.ctx_in, self.c_in, self.cctx_in = x, ctx, c, c_ctx
        ident_d = self.const("c_ident", np.eye(128, dtype=np.float32).astype(ml_dtypes.bfloat16))
        self.out = nc.dram_tensor("out", [T, D], F32, kind="ExternalOutput").ap()
        if self.dbg:
            self.ctx_out = nc.dram_tensor("ctx_out", [TC, D], F32, kind="ExternalOutput").ap()
        self.xs = self.scratch("xs", [T, D]); self.cs = self.scratch("cs", [TC, D])
        self.x_res = [Res() for _ in range(T // 128)]
        self.c_res = [Res() for _ in range(TC // 128)]

        with ExitStack() as es:
            self.S = S = Sched(nc, es)
            arena_t = es.enter_context(nc.sbuf_tensor("arena", [P, ARENA_COLS], F32))
            self.A = A = Arena(arena_t, ARENA_COLS)
            self.ps = es.enter_context(nc.psum_tensor("ps", [P, 4096], F32))
            self.ps_res = [Res() for _ in range(8)]
            self.ident = Tl(A.pbf16(128))
            S.dma("sp", self.ident.ap, ident_d, writes=[self.ident])
            self.ones_row = Tl(A.pf32(128))
            S.op("pool", lambda e: e.memset(self.ones_row.ap, 1.0), writes=[self.ones_row])
            self.ones_bf = Tl(A.pbf16(128))
            self.eps_col = Tl(A.pf32(1))
            S.op("pool", lambda e: e.memset(self.eps_col.ap, EPS), writes=[self.eps_col])
            S.op("pool", lambda e: e.memset(self.ones_bf.ap, 1.0), writes=[self.ones_bf])
            self.modcol = Tl(A.pf32(48))
            self.gs = Tl(A.pf32(16)); self.sh = Tl(A.pf32(16))
            self.gt_bc = Tl(A.pf32(1024)); self.gtc_bc = Tl(A.pf32(1024))
            self.scol = Tl(A.pf32(16))
            self.prep_cond()

            x_src, c_src = self.x_in, self.ctx_in
            layers = [("lru", self.mix_lru), ("ret", self.mix_ret), ("hy", self.mix_hy), ("att", self.mix_att)]
            for li in range(self.nlayers):
                pfx, fn = layers[li]
                self.layer_idx = li
                self.need_ctx = li < 3
                S.barrier(); A.reset()
                self.modulation(pfx)
                S.barrier(); A.reset()
                fn(x_src, c_src)
                x_src, c_src = self.xs, self.cs
            S.barrier(); A.reset()
            self.final_norm(x_src)
            if self.dbg:
                S.barrier(); A.reset()
                t = Tl(A.f32(1024))
                for i in range(TC // 128):
                    S.dma("sp", t.ap, c_src[i * 128:(i + 1) * 128, :], reads=[self.c_res[i]], writes=[t])
                    S.dma("sp", self.ctx_out[i * 128:(i + 1) * 128, :], t.ap, reads=[t])
            S.emit()
        return nc

    def prep_cond(self):
        S, A = self.S, self.A
        raw = Tl(A.f32(16))
        r3 = raw.ap.rearrange("p (k n) -> p k n", n=2)
        S.dma("sp", r3[:, :, 0], self.c_in.rearrange("(k p) -> p k", p=128), writes=[raw], slow=True)
        S.dma("sp", r3[:, :, 1], self.cctx_in.rearrange("(k p) -> p k", p=128), writes=[raw], slow=True)
        S.op("act", lambda e: e.activation(out=self.scol.ap, in_=raw.ap, func=AF.Silu), reads=[raw], writes=[self.scol])

    def modulation(self, pfx):
        S, A, W = self.S, self.A, self.W
        mw = Tl(A.f32(8 * 3072))
        mw3 = mw.ap.rearrange("p (k n) -> p k n", k=8)
        src = W[pfx + "_mod_w"].rearrange("(k p) n -> p k n", p=128)
        for k in range(8):
            S.dma("sp" if k % 2 == 0 else "act", mw3[:, k, :], src[:, k, :], writes=[mw])
        mb = Tl(A.f32(24))
        S.dma("sp", mb.ap, W[pfx + "_mod_b"].rearrange("(j p) -> p j", p=128), writes=[mb], slow=True)
        g = Tl(A.f32(8))
        S.dma("sp", g.ap, W[pfx + "_norm_g"].rearrange("(j p) -> p j", p=128), writes=[g], slow=True)
        mbrow = Tl(A.f32(1024))
        S.dma("sp", mbrow.ap[0:1, :], W[pfx + "_mod_b"][2048:3072].rearrange("(o n) -> o n", o=1), writes=[mbrow])
        sc3 = self.scol.ap.rearrange("p (k n) -> p k n", n=2)
        mc3 = self.modcol.ap.rearrange("p (j n) -> p j n", n=2)
        for j in range(24):
            b = j % 4
            pt = self.psum(b)
            for k in range(8):
                S.op("pe", lambda e, k=k, j=j, pt=pt: e.matmul(pt[:, 0:2], lhsT=mw3[:, k, j * 128:(j + 1) * 128], rhs=sc3[:, k, :],
                                                            start=(k == 0), stop=(k == 7)),
                     reads=[mw, self.scol], writes=[self.ps_res[b]])
            S.op("dve", lambda e, j=j, pt=pt: e.tensor_scalar(out=mc3[:, j, :], in0=pt[:, 0:2], scalar1=mb.ap[:, j:j + 1], scalar2=None,
                                                           op0=ALU.add), reads=[self.ps_res[b], mb], writes=[self.modcol])
        gs3 = self.gs.ap.rearrange("p (k n) -> p k n", n=2)
        sh3 = self.sh.ap.rearrange("p (k n) -> p k n", n=2)
        for n in range(2):
            S.op("dve", lambda e, n=n: e.tensor_scalar(out=gs3[:, :, n], in0=mc3[:, 8:16, n], scalar1=1.0, scalar2=None, op0=ALU.add),
                 reads=[self.modcol], writes=[self.gs])
            S.op("dve", lambda e, n=n: e.tensor_tensor(out=gs3[:, :, n], in0=gs3[:, :, n], in1=g.ap, op=ALU.mult),
                 reads=[self.gs, g], writes=[self.gs])
            S.op("dve", lambda e, n=n: e.tensor_copy(out=sh3[:, :, n], in_=mc3[:, 0:8, n]), reads=[self.modcol], writes=[self.sh])
        for n, dst in ((0, self.gt_bc), (1, self.gtc_bc)):
            row = Tl(A.f32(1024))
            for hh in range(2):
                b = 4 + hh
                pt = self.psum(b)
                for k in range(8):
                    S.op("pe", lambda e, k=k, hh=hh, n=n, pt=pt: e.matmul(pt[0:1, :], lhsT=sc3[:, k, n:n + 1],
                                                                       rhs=mw3[:, k, 2048 + hh * 512:2048 + (hh + 1) * 512],
                                                                       start=(k == 0), stop=(k == 7)),
                         reads=[mw, self.scol], writes=[self.ps_res[b]])
                S.op("dve", lambda e, hh=hh, pt=pt, row=row: e.tensor_tensor(out=row.ap[0:1, hh * 512:(hh + 1) * 512], in0=pt[0:1, :],
                                                                           in1=mbrow.ap[0:1, hh * 512:(hh + 1) * 512], op=ALU.add),
                     reads=[self.ps_res[b], mbrow], writes=[row])
            for hh in range(2):
                b = 6 + hh
                pt = self.psum(b)
                S.op("pe", lambda e, hh=hh, pt=pt, row=row: e.matmul(pt, lhsT=self.ones_row.ap[0:1, :], rhs=row.ap[0:1, hh * 512:(hh + 1) * 512],
                                                                   start=True, stop=True),
                     reads=[row, self.ones_row], writes=[self.ps_res[b]])
                S.op("act", lambda e, hh=hh, pt=pt, dst=dst: e.activation(out=dst.ap[:, hh * 512:(hh + 1) * 512], in_=pt, func=AF.Copy),
                     reads=[self.ps_res[b]], writes=[dst])

    def load_w_bf16(self, w_dram, K, N, dst3, dst_tl, col0=0, ncols=None, stage_ring=None):
        S = self.S
        ncols = N if ncols is None else ncols
        src = w_dram.rearrange("(k p) n -> p k n", p=128)
        CH = 2048
        i = 0
        for k in range(K // 128):
            for c0 in range(0, ncols, CH):
                cw = min(CH, ncols - c0)
                st = stage_ring.next()
                S.dma("sp" if i % 2 == 0 else "act", st.ap[:, 0:cw], src[:, k, col0 + c0:col0 + c0 + cw], writes=[st])
                eng = ("act", "dve", "pool")[i % 3]
                if eng == "act":
                    S.op("act", lambda e, st=st, k=k, c0=c0, cw=cw: e.activation(out=dst3[:, k, c0:c0 + cw], in_=st.ap[:, 0:cw], func=AF.Copy),
                         reads=[st], writes=[dst_tl])
                else:
                    S.op(eng, lambda e, st=st, k=k, c0=c0, cw=cw: e.tensor_copy(out=dst3[:, k, c0:c0 + cw], in_=st.ap[:, 0:cw]),
                         reads=[st], writes=[dst_tl])
                i += 1

    def norm_tiles(self, src, src_res, Tn, which, xring, hring, small, TT):
        S = self.S
        gs3 = self.gs.ap.rearrange("p (k n) -> p k n", n=2)
        sh3 = self.sh.ap.rearrange("p (k n) -> p k n", n=2)
        for tt in range(Tn // TT):
            hT = hring.next()
            h3 = hT.ap.rearrange("p (k t) -> p k t", k=8)
            for s in range(TT // 128):
                ti = (tt * TT) // 128 + s
                xt = xring.next()
                xf, xb, ss = xt["xf"], xt["xb"], xt["ss"]
                S.dma("sp", xf.ap, src[ti * 128:(ti + 1) * 128, :], reads=[src_res[ti]], writes=[xf])
                S.op("act", lambda e, xf=xf, xb=xb, ss=ss: e.activation(out=xb.ap, in_=xf.ap, func=AF.Square, accum_out=ss.ap),
                     reads=[xf], writes=[xb, ss])
                S.op("dve", lambda e, ss=ss: e.tensor_scalar(out=ss.ap, in0=ss.ap, scalar1=1.0 / D, scalar2=EPS, op0=ALU.mult, op1=ALU.add),
                     reads=[ss], writes=[ss])
                S.op("act", lambda e, ss=ss: e.activation(out=ss.ap, in_=ss.ap, func=AF.Sqrt), reads=[ss], writes=[ss])
                S.op("dve", lambda e, ss=ss: e.reciprocal(out=ss.ap, in_=ss.ap), reads=[ss], writes=[ss])
                S.op("act", lambda e, xf=xf, xb=xb, ss=ss: e.activation(out=xb.ap, in_=xf.ap, func=AF.Copy, scale=ss.ap),
                     reads=[xf, ss], writes=[xb])
                b = self.tp_bank
                self.tp_bank = 6 + (self.tp_bank - 6 + 1) % 2
                pb = self.psum(b, 1, BF16)
                for j in range(8):
                    S.op("pe", lambda e, j=j, pb=pb, xb=xb: e.transpose(out=pb[:, j * 128:(j + 1) * 128], in_=xb.ap[:, j * 128:(j + 1) * 128],
                                                                        identity=self.ident.ap),
                         reads=[xb, self.ident], writes=[self.ps_res[b]])
                for j in range(8):
                    eng = "dve" if j % 2 == 0 else "pool"
                    eng = "dve"
                    S.op(eng, lambda e, j=j, pb=pb, s=s, h3=h3: e.tensor_scalar(out=h3[:, j, s * 128:(s + 1) * 128], in0=pb[:, j * 128:(j + 1) * 128],
                                                                         scalar1=gs3[:, j, which:which + 1], scalar2=sh3[:, j, which:which + 1],
                                                                         op0=ALU.mult, op1=ALU.add),
                         reads=[self.ps_res[b], self.gs, self.sh], writes=[hT])
            yield tt, hT, h3

    def make_norm_rings(self, TT, nh=2, nx=3):
        A = self.A
        xring = Ring([{"xf": Tl(A.f32(1024)), "xb": Tl(A.bf16(1024)), "ss": Tl(A.f32(1))} for _ in range(nx)])
        hring = Ring([Tl(A.bf16(8 * TT)) for _ in range(nh)])
        self.tp_bank = 6
        return xring, hring

    def out_proj(self, Y, C, w_bf3, w_tl, Tn, gt, src, dst, res_list, TT=512):
        S, A = self.S, self.A
        kc = C // 128
        TT = min(TT, Tn)
        yring = Ring([Tl(A.bf16(kc * TT)) for _ in range(2)])
        xring = Ring([Tl(A.f32(1024)) for _ in range(3)])
        Yv = Y.rearrange("(k p) t -> p k t", p=128)
        pbi = 0
        for tt in range(Tn // TT):
            yt = yring.next()
            y3 = yt.ap.rearrange("p (k t) -> p k t", k=kc)
            S.dma("sp", y3, Yv[:, :, tt * TT:(tt + 1) * TT], writes=[yt], reads=[self.Y_res])
            for s in range(TT // 128):
                ti = (tt * TT) // 128 + s
                xt = xring.next()
                S.dma("act", xt.ap, src[ti * 128:(ti + 1) * 128, :], reads=[res_list[ti]], writes=[xt])
                b0 = (pbi % 3) * 2
                pbi += 1
                for hh in range(2):
                    pt = self.psum(b0 + hh)
                    for k in range(kc):
                        S.op("pe", lambda e, k=k, hh=hh, pt=pt, s=s, y3=y3: e.matmul(pt, lhsT=y3[:, k, s * 128:(s + 1) * 128],
                                                                                 rhs=w_bf3[:, k, hh * 512:(hh + 1) * 512],
                                                                                 start=(k == 0), stop=(k == kc - 1)),
                             reads=[yt, w_tl], writes=[self.ps_res[b0 + hh]])
                pt2 = self.psum(b0, 2)
                S.op("dve", lambda e, pt2=pt2: e.tensor_tensor(out=pt2, in0=pt2, in1=gt.ap, op=ALU.mult),
                     reads=[self.ps_res[b0], self.ps_res[b0 + 1], gt], writes=[self.ps_res[b0], self.ps_res[b0 + 1]])
                S.op("dve", lambda e, pt2=pt2, xt=xt: e.tensor_tensor(out=xt.ap, in0=pt2, in1=xt.ap, op=ALU.add),
                     reads=[self.ps_res[b0], self.ps_res[b0 + 1], xt], writes=[xt])
                S.dma("sp", dst[ti * 128:(ti + 1) * 128, :], xt.ap, reads=[xt], writes=[res_list[ti]])

    def final_norm(self, src):
        S, A, T = self.S, self.A, self.T
        g = Tl(A.f32(1024))
        S.dma("sp", g.ap, self.W["final_norm_g"].partition_broadcast(128), writes=[g])
        ring = Ring([{"xf": Tl(A.f32(1024)), "sq": Tl(A.f32(1024)), "ss": Tl(A.f32(1))} for _ in range(3)])
        ores = Res()
        for ti in range(T // 128):
            r = ring.next()
            xf, sq, ss = r["xf"], r["sq"], r["ss"]
            S.dma("sp", xf.ap, src[ti * 128:(ti + 1) * 128, :], reads=[self.x_res[ti]], writes=[xf])
            S.op("act", lambda e, xf=xf, sq=sq, ss=ss: e.activation(out=sq.ap, in_=xf.ap, func=AF.Square, accum_out=ss.ap),
                 reads=[xf], writes=[sq, ss])
            S.op("dve", lambda e, ss=ss: e.tensor_scalar(out=ss.ap, in0=ss.ap, scalar1=1.0 / D, scalar2=EPS, op0=ALU.mult, op1=ALU.add),
                 reads=[ss], writes=[ss])
            S.op("act", lambda e, ss=ss: e.activation(out=ss.ap, in_=ss.ap, func=AF.Sqrt), reads=[ss], writes=[ss])
            S.op("dve", lambda e, ss=ss: e.reciprocal(out=ss.ap, in_=ss.ap), reads=[ss], writes=[ss])
            S.op("dve", lambda e, xf=xf, sq=sq, ss=ss: e.scalar_tensor_tensor(out=sq.ap, in0=xf.ap, scalar=ss.ap, in1=g.ap,
                                                                             op0=ALU.mult, op1=ALU.mult),
                 reads=[xf, ss, g], writes=[sq])
            S.dma("act", self.out[ti * 128:(ti + 1) * 128, :], sq.ap, reads=[sq], writes=[ores])

    def proj_to_dram(self, src, src_res, Tn, which, w3, w_tl, nchunks, U, U_res, TT):
        S, A = self.S, self.A
        xring, hring = self.make_norm_rings(TT)
        stg = Ring([Tl(A.f32(TT)) for _ in range(4)])
        bi = 0
        for tt, hT, h3 in self.norm_tiles(src, src_res, Tn, which, xring, hring, None, TT):
            for oc in range(nchunks):
                b = bi % 6
                bi += 1
                pt = self.psum(b)[:, 0:TT]
                for k in range(8):
                    S.op("pe", lambda e, k=k, oc=oc, pt=pt, h3=h3: e.matmul(pt, lhsT=w3[:, k, oc * 128:(oc + 1) * 128], rhs=h3[:, k, :],
                                                                         start=(k == 0), stop=(k == 7)),
                         reads=[hT, w_tl], writes=[self.ps_res[b]])
                st = stg.next()
                if oc % 2 == 0:
                    S.op("act", lambda e, pt=pt, st=st: e.activation(out=st.ap, in_=pt, func=AF.Copy), reads=[self.ps_res[b]], writes=[st])
                else:
                    S.op("dve", lambda e, pt=pt, st=st: e.tensor_copy(out=st.ap, in_=pt), reads=[self.ps_res[b]], writes=[st])
                S.dma("sp" if oc % 2 == 0 else "act", U[oc * 128:(oc + 1) * 128, tt * TT:(tt + 1) * TT], st.ap, reads=[st], writes=[U_res])

    def mix_lru(self, x_src, c_src):
        S, A, W, T, TC = self.S, self.A, self.W, self.T, self.TC
        U = self.scratch("lru_U", [4096, T]); UC = self.scratch("lru_UC", [4096, TC])
        Y = self.scratch("lru_Y", [2048, T], BF16); YC = self.scratch("lru_YC", [2048, TC], BF16)
        U_res, UC_res = FreeRes(), FreeRes()
        self.Y_res = FreeRes()
        w_tl = Tl(A.bf16(8 * 4096))
        w3 = w_tl.ap.rearrange("p (k n) -> p k n", k=8)
        stage = Ring([Tl(A.f32(2048)) for _ in range(3)])
        self.load_w_bf16(W["lru_w_in"], 1024, 4096, w3, w_tl, stage_ring=stage)
        mark = A.top
        self.proj_to_dram(c_src, self.c_res, TC, 1, w3, w_tl, 32, UC, UC_res, min(512, TC))
        A.top = mark
        S.barrier()
        self.proj_to_dram(x_src, self.x_res, T, 0, w3, w_tl, 32, U, U_res, min(512, T))
        S.barrier(); A.reset()
        def col(src_ap, n, pattern="(n p) -> p n"):
            t = Tl(A.f32(n))
            S.dma("sp", t.ap, src_ap.rearrange(pattern, p=128), writes=[t], slow=True)
            return t
        cw = [col(W["lru_conv_w"][k], 16) for k in range(4)]
        cb = col(W["lru_conv_b"], 16)
        br = [col(W["lru_b_r"][d], 16) for d in range(2)]
        bi_ = [col(W["lru_b_i"][d], 16) for d in range(2)]
        lam = [col(W["lru_lambda"][d], 16) for d in range(2)]
        cdec, cdec2 = [], []
        for d in range(2):
            t = Tl(A.f32(16)); t2 = Tl(A.f32(16))
            S.op("act", lambda e, t=t, d=d: e.activation(out=t.ap, in_=lam[d].ap, func=AF.Exp, scale=-1.0), reads=[lam[d]], writes=[t])
            S.op("act", lambda e, t=t: e.activation(out=t.ap, in_=t.ap, func=AF.Ln, bias=1.0), reads=[t], writes=[t])
            S.op("dve", lambda e, t=t, t2=t2: e.tensor_scalar(out=t2.ap, in0=t.ap, scalar1=-16.0, scalar2=None, op0=ALU.mult), reads=[t], writes=[t2])
            S.op("dve", lambda e, t=t: e.tensor_scalar(out=t.ap, in0=t.ap, scalar1=-8.0, scalar2=None, op0=ALU.mult), reads=[t], writes=[t])
            cdec.append(t); cdec2.append(t2)
        gw = Tl(A.bf16(2 * 2 * 16 * 128))
        gw5 = gw.ap.rearrange("p (d g n k) -> p d g n k", d=2, g=2, n=16)
        gst = Ring([Tl(A.f32(16 * 128)) for _ in range(2)])
        for d in range(2):
            for g, nm in enumerate(("lru_w_r", "lru_w_i")):
                st = gst.next()
                st3 = st.ap.rearrange("p (n k) -> p n k", n=16)
                S.dma("sp", st3, W[nm][d].rearrange("n j k -> j n k"), writes=[st])
                S.op("dve", lambda e, st3=st3, d=d, g=g: e.tensor_copy(out=gw5[:, d, g, :, :], in_=st3), reads=[st], writes=[gw])
        TM = max(T, TC)
        ubuf = Tl(A.f32(TM + 3))
        xl = Tl(A.f32(TM)); xlb = Tl(A.bf16(TM))
        TTS = min(1024, TM)
        tring = Ring([{k: Tl(A.f32(TTS)) for k in ("r", "i", "a", "hb", "g")} for _ in range(2)])
        yring = Ring([Tl(A.bf16(TTS)) for _ in range(2)])
        fin = [Tl(A.f32(1)), Tl(A.f32(1))]
        pb = [0]

        def seq(n, Useq, Useq_res, Tn, init, Yseq):
            tts = min(TTS, Tn)
            nt = Tn // tts
            S.op("pool", lambda e: e.memset(ubuf.ap[:, 0:2], 0.0), writes=[ubuf])
            S.op("pool", lambda e: e.memset(ubuf.ap[:, Tn + 2:Tn + 3], 0.0), writes=[ubuf])
            S.dma("sp", ubuf.ap[:, 2:Tn + 2], Useq[n * 128:(n + 1) * 128, :], reads=[Useq_res], writes=[ubuf])
            S.op("dve", lambda e: e.tensor_scalar(out=xl.ap[:, 0:Tn], in0=ubuf.ap[:, 0:Tn], scalar1=cw[0].ap[:, n:n + 1],
                                                  scalar2=cb.ap[:, n:n + 1], op0=ALU.mult, op1=ALU.add),
                 reads=[ubuf, cw[0], cb], writes=[xl])
            for k in range(1, 4):
                S.op("dve", lambda e, k=k: e.scalar_tensor_tensor(out=xl.ap[:, 0:Tn], in0=ubuf.ap[:, k:k + Tn], scalar=cw[k].ap[:, n:n + 1],
                                                                  in1=xl.ap[:, 0:Tn], op0=ALU.mult, op1=ALU.add),
                     reads=[ubuf, cw[k], xl], writes=[xl])
            S.op("act", lambda e: e.activation(out=xlb.ap[:, 0:Tn], in_=xl.ap[:, 0:Tn], func=AF.Copy), reads=[xl], writes=[xlb])
            hf = ubuf
            for d in range(2):
                order = list(range(nt)) if d == 0 else list(range(nt - 1, -1, -1))
                state = init[d]
                for tt in order:
                    sl = slice(tt * tts, (tt + 1) * tts)
                    tb = tring.next()
                    r, i_, a, hb, g = tb["r"], tb["i"], tb["a"], tb["hb"], tb["g"]
                    pb[0] += 1
                    nb = (tts + 511) // 512
                    bank0 = (pb[0] % 2) * 4
                    for gi in range(2):
                        for h in range(nb):
                            c0 = h * 512
                            cwid = min(512, tts - c0)
                            bank = bank0 + gi * 2 + h
                            pt = self.psum(bank)[:, 0:cwid]
                            S.op("pe", lambda e, pt=pt, gi=gi, c0=c0, cwid=cwid, d=d, tt=tt: e.matmul(
                                pt, lhsT=gw5[:, d, gi, n, :], rhs=xlb.ap[:, tt * tts + c0:tt * tts + c0 + cwid], start=True, stop=True),
                                 reads=[gw, xlb], writes=[self.ps_res[bank]])
                            dst = r if gi == 0 else i_
                            bias = (br if gi == 0 else bi_)[d]
                            S.op("act", lambda e, pt=pt, dst=dst, bias=bias, c0=c0, cwid=cwid: e.activation(
                                out=dst.ap[:, c0:c0 + cwid], in_=pt, func=AF.Sigmoid, bias=bias.ap[:, n:n + 1]),
                                 reads=[self.ps_res[bank], bias], writes=[dst])
                    S.op("act", lambda e, r=r, a=a, d=d: e.activation(out=a.ap[:, 0:tts], in_=r.ap[:, 0:tts], func=AF.Exp, scale=cdec[d].ap[:, n:n + 1]),
                         reads=[r, cdec[d]], writes=[a])
                    S.op("pool", lambda e, r=r, a=a: e.tensor_tensor(out=r.ap[:, 0:tts], in0=a.ap[:, 0:tts], in1=a.ap[:, 0:tts], op=ALU.mult),
                         reads=[a], writes=[r])
                    S.op("act", lambda e, r=r: e.activation(out=r.ap[:, 0:tts], in_=r.ap[:, 0:tts], func=AF.Sqrt, scale=-1.0, bias=1.0),
                         reads=[r], writes=[r])
                    S.op("pool", lambda e, i_=i_, sl=sl: e.tensor_tensor(out=i_.ap[:, 0:tts], in0=i_.ap[:, 0:tts], in1=xl.ap[:, sl], op=ALU.mult),
                         reads=[i_, xl], writes=[i_])
                    S.op("dve", lambda e, i_=i_, r=r: e.tensor_tensor(out=i_.ap[:, 0:tts], in0=i_.ap[:, 0:tts], in1=r.ap[:, 0:tts], op=ALU.mult),
                         reads=[i_, r], writes=[i_])
                    ini = state if isinstance(state, float) else state.ap
                    rds = [a, i_] + ([] if isinstance(state, float) else [state])
                    if d == 0:
                        S.op("dve", lambda e, a=a, i_=i_, ini=ini, sl=sl: e.tensor_tensor_scan(out=hf.ap[:, sl], data0=a.ap[:, 0:tts], data1=i_.ap[:, 0:tts],
                                                                                         initial=ini, op0=ALU.mult, op1=ALU.add),
                             reads=rds, writes=[hf])
                        state = Tl(hf.ap[:, (tt + 1) * tts - 1:(tt + 1) * tts], hf.res)
                    else:
                        S.op("dve", lambda e, a=a, i_=i_, ini=ini, hb=hb: e.tensor_tensor_scan(out=hb.ap[:, 0:tts][:, ::-1], data0=a.ap[:, 0:tts][:, ::-1],
                                                                                         data1=i_.ap[:, 0:tts][:, ::-1], initial=ini,
                                                                                         op0=ALU.mult, op1=ALU.add),
                             reads=rds, writes=[hb])
                        state = Tl(hb.ap[:, 0:1], hb.res)
                        if tt == 0:
                            S.op("act", lambda e, hb=hb: e.activation(out=fin[1].ap, in_=hb.ap[:, 0:1], func=AF.Copy), reads=[hb], writes=[fin[1]])
                        if Yseq is not None:
                            S.dma("act", g.ap[:, 0:tts], Useq[2048 + n * 128:2048 + (n + 1) * 128, sl], reads=[Useq_res], writes=[g])
                            S.op("act", lambda e, g=g: e.activation(out=g.ap[:, 0:tts], in_=g.ap[:, 0:tts], func=AF.Silu), reads=[g], writes=[g])
                            S.op("pool", lambda e, hb=hb, sl=sl, a=a: e.tensor_tensor(out=a.ap[:, 0:tts], in0=hb.ap[:, 0:tts], in1=hf.ap[:, sl], op=ALU.add),
                                 reads=[hb, hf], writes=[a])
                            yt = yring.next()
                            S.op("dve", lambda e, a=a, g=g, yt=yt: e.tensor_tensor(out=yt.ap[:, 0:tts], in0=a.ap[:, 0:tts], in1=g.ap[:, 0:tts], op=ALU.mult),
                                 reads=[a, g], writes=[yt])
                            S.dma("sp", Yseq[n * 128:(n + 1) * 128, sl], yt.ap[:, 0:tts], reads=[yt], writes=[self.Y_res])
                if d == 0:
                    S.op("act", lambda e: e.activation(out=fin[0].ap, in_=hf.ap[:, Tn - 1:Tn], func=AF.Copy), reads=[hf], writes=[fin[0]])

        for n in range(16):
            seq(n, UC, UC_res, TC, [0.0, 0.0], YC if self.need_ctx else None)
            seq(n, U, U_res, T, [fin[0], fin[1]], Y)
        S.barrier(); A.reset()
        wo = Tl(A.bf16(16 * 1024))
        wo3 = wo.ap.rearrange("p (k n) -> p k n", k=16)
        stage = Ring([Tl(A.f32(2048)) for _ in range(3)])
        self.load_w_bf16(W["lru_w_out"], 2048, 1024, wo3, wo, stage_ring=stage)
        self.out_proj(Y, 2048, wo3, wo, T, self.gt_bc, x_src, self.xs, self.x_res)
        if self.need_ctx:
            self.out_proj(YC, 2048, wo3, wo, TC, self.gtc_bc, c_src, self.cs, self.c_res)

    def mix_ret(self, x_src, c_src):
        S, A, W, T, TC = self.S, self.A, self.W, self.T, self.TC
        C = 128
        f32 = np.float32
        p_ = np.arange(128, dtype=f32)
        cst = np.zeros((128, 4 + 4 * 128), f32)
        cst[:, 0] = C - 1 - p_; cst[:, 1] = p_; cst[:, 2] = C
        ii = np.arange(128, dtype=f32)[None, :]; jj = np.arange(128, dtype=f32)[:, None]
        cst[:, 4:132] = ii + 1.0
        cst[:, 132:260] = C - ii
        cst[:, 260:388] = np.maximum(ii - jj, 0.0)
        cst[:, 388:516] = np.maximum(jj - ii, 0.0)
        cst_d = self.const("c_ret_cst", cst)
        inv = (f32(10000.0) ** (-np.arange(128, dtype=f32) / f32(128))).astype(f32)
        ang = (np.arange(T, dtype=f32)[:, None] * inv[None, :]).astype(f32)
        cs_, sn_ = np.cos(ang.astype(np.float64)).astype(f32), np.sin(ang.astype(np.float64)).astype(f32)
        tabs_x = dict(cF=self.const("c_ret_cF", cs_.T), sF=self.const("c_ret_sF", sn_.T),
                      cT=self.const("c_ret_cT", np.tile(cs_, (1, 2))), sT=self.const("c_ret_sT", np.tile(sn_, (1, 2))))
        tabs_c = dict(cF=self.const("c_ret_cFc", np.ones((128, TC), f32)), sF=self.const("c_ret_sFc", np.zeros((128, TC), f32)),
                      cT=self.const("c_ret_cTc", np.ones((TC, 256), f32)), sT=self.const("c_ret_sTc", np.zeros((TC, 256), f32)))

        def mk(T_, sfx):
            d = {}
            for nm in ("Q", "QF", "QB", "KT"):
                d[nm] = self.scratch("ret_%s%s" % (nm, sfx), [1024, T_], BF16)
            for nm in ("KF", "KB"):
                d[nm] = self.scratch("ret_%s%s" % (nm, sfx), [T_, 1024], BF16)
            d["V"] = self.scratch("ret_V" + sfx, [T_, 2048], BF16); d["G"] = self.scratch("ret_G" + sfx, [T_, 2048], BF16)
            d["OB"] = self.scratch("ret_OB" + sfx, [T_, 2048], F32); d["Y"] = self.scratch("ret_Y" + sfx, [2048, T_], BF16)
            d["res"] = FreeRes()
            return d
        DX, DC = mk(T, "x"), mk(TC, "c")
        self.Y_res = FreeRes()

        cstt = Tl(A.f32(516))
        S.dma("sp", cstt.ap, cst_d, writes=[cstt])
        lg = Tl(A.f32(8))
        S.dma("sp", lg.ap, W["ret_decay_logit"].rearrange("d h -> (d h)").partition_broadcast(128), writes=[lg])
        S.op("act", lambda e: e.activation(out=lg.ap, in_=lg.ap, func=AF.Exp, scale=-1.0), reads=[lg], writes=[lg])
        S.op("act", lambda e: e.activation(out=lg.ap, in_=lg.ap, func=AF.Ln, bias=1.0), reads=[lg], writes=[lg])
        S.op("dve", lambda e: e.tensor_scalar(out=lg.ap, in0=lg.ap, scalar1=-1.0, scalar2=None, op0=ALU.mult), reads=[lg], writes=[lg])
        kd = Tl(A.f32(8)); blk = Tl(A.f32(8))
        for d in range(2):
            for h in range(4):
                ix = d * 4 + h
                S.op("act", lambda e, d=d, ix=ix: e.activation(out=kd.ap[:, ix:ix + 1], in_=cstt.ap[:, d:d + 1], func=AF.Exp, scale=lg.ap[:, ix:ix + 1]),
                     reads=[cstt, lg], writes=[kd])
                S.op("act", lambda e, ix=ix: e.activation(out=blk.ap[:, ix:ix + 1], in_=cstt.ap[:, 2:3], func=AF.Exp, scale=lg.ap[:, ix:ix + 1]),
                     reads=[cstt, lg], writes=[blk])
        S.op("dve", lambda e: e.tensor_scalar(out=kd.ap, in0=kd.ap, scalar1=1.0 / 16.0, scalar2=None, op0=ALU.mult), reads=[kd], writes=[kd])
        onesf = Tl(A.f32(256))
        S.op("pool", lambda e: e.memset(onesf.ap, 1.0), writes=[onesf])
        dect = [Tl(A.f32(1024)), Tl(A.f32(1024))]
        qdt = [Tl(A.f32(4 * 512)), Tl(A.f32(4 * 512))]
        maskT = Tl(A.f32(4 * 128))
        mtmp = Tl(A.f32(128))
        for d in range(2):
            for h in range(4):
                ix = d * 4 + h
                S.op("dve", lambda e, d=d, h=h, ix=ix: e.tensor_scalar(out=dect[d].ap[:, h * 256:(h + 1) * 256], in0=onesf.ap, scalar1=kd.ap[:, ix:ix + 1],
                                                                   scalar2=None, op0=ALU.mult), reads=[onesf, kd], writes=[dect[d]])
                for r in range(4):
                    S.op("act", lambda e, d=d, h=h, ix=ix, r=r: e.activation(out=qdt[d].ap[:, h * 512 + r * 128:h * 512 + (r + 1) * 128],
                                                                          in_=cstt.ap[:, 4 + d * 128:4 + (d + 1) * 128], func=AF.Exp,
                                                                          scale=lg.ap[:, ix:ix + 1]), reads=[cstt, lg], writes=[qdt[d]])
        for h in range(4):
            S.op("act", lambda e, h=h: e.activation(out=maskT.ap[:, h * 128:(h + 1) * 128], in_=cstt.ap[:, 260:388], func=AF.Exp, scale=lg.ap[:, h:h + 1]),
                 reads=[cstt, lg], writes=[maskT])
            S.op("act", lambda e, h=h: e.activation(out=mtmp.ap, in_=cstt.ap[:, 388:516], func=AF.Exp, scale=lg.ap[:, 4 + h:5 + h]),
                 reads=[cstt, lg], writes=[mtmp])
            S.op("dve", lambda e, h=h: e.scalar_tensor_tensor(out=maskT.ap[:, h * 128:(h + 1) * 128], in0=maskT.ap[:, h * 128:(h + 1) * 128],
                                                              scalar=1.0 / 16.0, in1=mtmp.ap, op0=ALU.mult, op1=ALU.mult),
                 reads=[maskT, mtmp], writes=[maskT])
        base_mark = A.top

        def ret_proj(src, src_res, Tn, which, tabs, Dd, part, w3, w_tl):
            TT = min(512, Tn)
            xring, hring = self.make_norm_rings(TT, nh=2, nx=2)
            if part == 0:
                tabF = Ring([{"c": Tl(A.f32(TT)), "s": Tl(A.f32(TT))} for _ in range(2)])
                tmp = Ring([{k: Tl(A.f32(TT)) for k in ("t1", "t2", "o1", "o2")} for _ in range(2)])
                stb = Ring([Tl(A.bf16(TT)) for _ in range(6)])
            else:
                tabT = Ring([{"c": Tl(A.f32(256)), "s": Tl(A.f32(256))} for _ in range(2)])
                tk = Ring([{"u1": Tl(A.f32(256)), "u2": Tl(A.f32(256)), "ko": Tl(A.f32(512))} for _ in range(2)])
                kst = Ring([{"f": Tl(A.bf16(1024)), "b": Tl(A.bf16(1024))} for _ in range(2)])
                vst = Ring([Tl(A.bf16(2048)) for _ in range(2)])
                gst = Ring([Tl(A.bf16(2048)) for _ in range(2)])
            dres = Dd["res"]
            cnt = [0, 0]
            for tt, hT, h3 in self.norm_tiles(src, src_res, Tn, which, xring, hring, None, TT):
                cols = slice(tt * TT, (tt + 1) * TT)
                if part == 0:
                    tf = tabF.next()
                    S.dma("act", tf["c"].ap, tabs["cF"][:, cols], writes=[tf["c"]])
                    S.dma("act", tf["s"].ap, tabs["sF"][:, cols], writes=[tf["s"]])
                for qk in (range(2) if part == 0 else ()):
                    for h in range(4):
                        b0 = (cnt[0] % 2) * 2
                        cnt[0] += 1
                        for half in range(2):
                            oc = qk * 8 + h * 2 + half
                            pt = self.psum(b0 + half)[:, 0:TT]
                            for kk in range(8):
                                S.op("pe", lambda e, kk=kk, oc=oc, pt=pt, h3=h3: e.matmul(pt, lhsT=w3[:, kk, oc * 128:(oc + 1) * 128], rhs=h3[:, kk, :],
                                                                                      start=(kk == 0), stop=(kk == 7)),
                                     reads=[hT, w_tl], writes=[self.ps_res[b0 + half]])
                        x1 = self.psum(b0)[:, 0:TT]; x2 = self.psum(b0 + 1)[:, 0:TT]
                        r1, r2 = self.ps_res[b0], self.ps_res[b0 + 1]
                        tm = tmp.next()
                        t1, t2, o1, o2 = tm["t1"], tm["t2"], tm["o1"], tm["o2"]
                        S.op("dve", lambda e, x1=x1, t1=t1, tf=tf: e.tensor_tensor(out=t1.ap, in0=x1, in1=tf["c"].ap, op=ALU.mult), reads=[r1, tf["c"]], writes=[t1])
                        S.op("dve", lambda e, x2=x2, t2=t2, tf=tf: e.tensor_tensor(out=t2.ap, in0=x2, in1=tf["s"].ap, op=ALU.mult), reads=[r2, tf["s"]], writes=[t2])
                        S.op("pool", lambda e, t1=t1, t2=t2, o1=o1: e.tensor_tensor(out=o1.ap, in0=t1.ap, in1=t2.ap, op=ALU.subtract), reads=[t1, t2], writes=[o1])
                        S.op("dve", lambda e, x1=x1, t1=t1, tf=tf: e.tensor_tensor(out=t1.ap, in0=x1, in1=tf["s"].ap, op=ALU.mult), reads=[r1, tf["s"], o1], writes=[t1])
                        S.op("dve", lambda e, x2=x2, t2=t2, tf=tf: e.tensor_tensor(out=t2.ap, in0=x2, in1=tf["c"].ap, op=ALU.mult), reads=[r2, tf["c"], o1], writes=[t2])
                        S.op("pool", lambda e, t1=t1, t2=t2, o2=o2: e.tensor_tensor(out=o2.ap, in0=t1.ap, in1=t2.ap, op=ALU.add), reads=[t1, t2], writes=[o2])
                        for half, o in ((0, o1), (1, o2)):
                            row0 = h * 256 + half * 128
                            if qk == 0:
                                st = stb.next()
                                S.op("act", lambda e, st=st, o=o: e.activation(out=st.ap, in_=o.ap, func=AF.Copy), reads=[o], writes=[st])
                                S.dma("sp", Dd["Q"][row0:row0 + 128, cols], st.ap, reads=[st], writes=[dres])
                                for d, nm in ((0, "QF"), (1, "QB")):
                                    st = stb.next()
                                    S.op("pool" if d == 0 else "dve", lambda e, st=st, o=o, d=d, h=h: e.tensor_tensor(out=st.ap, in0=o.ap, in1=qdt[d].ap[:, h * 512:h * 512 + TT],
                                                                                                              op=ALU.mult), reads=[o, qdt[d]], writes=[st])
                                    S.dma("sp", Dd[nm][row0:row0 + 128, cols], st.ap, reads=[st], writes=[dres])
                            else:
                                st = stb.next()
                                S.op("act", lambda e, st=st, o=o: e.activation(out=st.ap, in_=o.ap, func=AF.Copy), reads=[o], writes=[st])
                                S.dma("sp", Dd["KT"][row0:row0 + 128, cols], st.ap, reads=[st], writes=[dres])
                for s in (range(TT // 128) if part == 1 else ()):
                    ti = tt * (TT // 128) + s
                    rows = slice(ti * 128, (ti + 1) * 128)
                    tT = tabT.next()
                    S.dma("act", tT["c"].ap, tabs["cT"][rows, :], writes=[tT["c"]])
                    S.dma("act", tT["s"].ap, tabs["sT"][rows, :], writes=[tT["s"]])
                    cT3 = tT["c"].ap.rearrange("p (a c) -> p a c", a=2); sT3 = tT["s"].ap.rearrange("p (a c) -> p a c", a=2)
                    ks = kst.next()

                    def tok_mm(col0, bank):
                        pt = self.psum(bank)
                        for kk in range(8):
                            S.op("pe", lambda e, kk=kk, pt=pt, s=s, h3=h3, col0=col0: e.matmul(pt, lhsT=h3[:, kk, s * 128:(s + 1) * 128],
                                                                                          rhs=w3[:, kk, col0:col0 + 512], start=(kk == 0), stop=(kk == 7)),
                                 reads=[hT, w_tl], writes=[self.ps_res[bank]])
                        return pt
                    for g in range(2):
                        bank = 4 + cnt[1] % 2
                        cnt[1] += 1
                        pt = tok_mm(g * 512, bank)
                        pr = self.ps_res[bank]
                        pv = pt.rearrange("p (a b c) -> p a b c", a=2, b=2)
                        x1, x2 = pv[:, :, 0, :], pv[:, :, 1, :]
                        t = tk.next()
                        u1, u2, ko = t["u1"], t["u2"], t["ko"]
                        u13 = u1.ap.rearrange("p (a c) -> p a c", a=2); u23 = u2.ap.rearrange("p (a c) -> p a c", a=2)
                        ko4 = ko.ap.rearrange("p (a b c) -> p a b c", a=2, b=2)
                        S.op("dve", lambda e, x1=x1, u13=u13, cT3=cT3: e.tensor_tensor(out=u13, in0=x1, in1=cT3, op=ALU.mult), reads=[pr, tT["c"]], writes=[u1])
                        S.op("dve", lambda e, x2=x2, u23=u23, sT3=sT3: e.tensor_tensor(out=u23, in0=x2, in1=sT3, op=ALU.mult), reads=[pr, tT["s"]], writes=[u2])
                        S.op("pool", lambda e, u13=u13, u23=u23, ko4=ko4: e.tensor_tensor(out=ko4[:, :, 0, :], in0=u13, in1=u23, op=ALU.subtract), reads=[u1, u2], writes=[ko])
                        S.op("dve", lambda e, x1=x1, u13=u13, sT3=sT3: e.tensor_tensor(out=u13, in0=x1, in1=sT3, op=ALU.mult), reads=[pr, tT["s"], ko], writes=[u1])
                        S.op("dve", lambda e, x2=x2, u23=u23, cT3=cT3: e.tensor_tensor(out=u23, in0=x2, in1=cT3, op=ALU.mult), reads=[pr, tT["c"], ko], writes=[u2])
                        S.op("pool", lambda e, u13=u13, u23=u23, ko4=ko4: e.tensor_tensor(out=ko4[:, :, 1, :], in0=u13, in1=u23, op=ALU.add), reads=[u1, u2], writes=[ko])
                        S.op("pool", lambda e, ko=ko, ks=ks, g=g: e.tensor_tensor(out=ks["f"].ap[:, g * 512:(g + 1) * 512], in0=ko.ap, in1=dect[0].ap[:, g * 512:(g + 1) * 512], op=ALU.mult),
                             reads=[ko, dect[0]], writes=[ks["f"]])
                        S.op("dve", lambda e, ko=ko, ks=ks, g=g: e.tensor_tensor(out=ks["b"].ap[:, g * 512:(g + 1) * 512], in0=ko.ap, in1=dect[1].ap[:, g * 512:(g + 1) * 512], op=ALU.mult),
                             reads=[ko, dect[1]], writes=[ks["b"]])
                    S.dma("sp", Dd["KF"][rows, :], ks["f"].ap, reads=[ks["f"]], writes=[dres])
                    S.dma("sp", Dd["KB"][rows, :], ks["b"].ap, reads=[ks["b"]], writes=[dres])
                    vs = vst.next(); gs_ = gst.next()
                    for g in range(4):
                        bank = 4 + cnt[1] % 2
                        cnt[1] += 1
                        pt = tok_mm(1024 + g * 512, bank)
                        S.op("act", lambda e, pt=pt, vs=vs, g=g: e.activation(out=vs.ap[:, g * 512:(g + 1) * 512], in_=pt, func=AF.Copy), reads=[self.ps_res[bank]], writes=[vs])
                    S.dma("sp", Dd["V"][rows, :], vs.ap, reads=[vs], writes=[dres])
                    for g in range(4):
                        bank = 4 + cnt[1] % 2
                        cnt[1] += 1
                        pt = tok_mm(3072 + g * 512, bank)
                        S.op("act", lambda e, pt=pt, gs_=gs_, g=g: e.activation(out=gs_.ap[:, g * 512:(g + 1) * 512], in_=pt, func=AF.Silu), reads=[self.ps_res[bank]], writes=[gs_])
                    S.dma("sp", Dd["G"][rows, :], gs_.ap, reads=[gs_], writes=[dres])

        for part, (c0, ncol) in enumerate(((0, 2048), (1024, 5120))):
            A.top = base_mark
            w_tl = Tl(A.bf16(8 * ncol))
            w3 = w_tl.ap.rearrange("p (k n) -> p k n", k=8)
            wmark = A.top
            stage = Ring([Tl(A.f32(2048)) for _ in range(2)])
            self.load_w_bf16(W["ret_w_in"], 1024, 6144, w3, w_tl, col0=c0, ncols=ncol, stage_ring=stage)
            S.barrier(); A.top = wmark
            ret_proj(c_src, self.c_res, TC, 1, tabs_c, DC, part, w3, w_tl)
            S.barrier(); A.top = wmark
            ret_proj(x_src, self.x_res, T, 0, tabs_x, DX, part, w3, w_tl)
            S.barrier()
        A.top = base_mark

        Sst = [[Tl(A.f32(512)) for _ in range(8)] for _ in range(2)]
        Sbf = [[Tl(A.bf16(512)) for _ in range(8)] for _ in range(2)]
        for d in range(2):
            for hd in range(8):
                S.op("pool", lambda e, d=d, hd=hd: e.memset(Sst[d][hd].ap, 0.0), writes=[Sst[d][hd]])
                S.op("pool", lambda e, d=d, hd=hd: e.memset(Sbf[d][hd].ap, 0.0), writes=[Sbf[d][hd]])
        pmark = A.top
        ucnt = [0]

        def state_update(d, kt_, vt_):
            for h in range(4):
                for dc in range(2):
                    hd = h * 2 + dc
                    bank = 6 + ucnt[0] % 2
                    ucnt[0] += 1
                    pt = self.psum(bank)
                    S.op("pe", lambda e, pt=pt, h=h, dc=dc, kt_=kt_, vt_=vt_: e.matmul(pt, lhsT=kt_.ap[:, h * 256 + dc * 128:h * 256 + (dc + 1) * 128],
                                                                                   rhs=vt_.ap[:, h * 512:(h + 1) * 512], start=True, stop=True),
                         reads=[kt_, vt_], writes=[self.ps_res[bank]])
                    st = Sst[d][hd]
                    S.op("dve", lambda e, pt=pt, st=st, d=d, h=h: e.scalar_tensor_tensor(out=st.ap, in0=st.ap, scalar=blk.ap[:, d * 4 + h:d * 4 + h + 1], in1=pt,
                                                                                    op0=ALU.mult, op1=ALU.add), reads=[st, blk, self.ps_res[bank]], writes=[st])
                    S.op("act", lambda e, st=st, d=d, hd=hd: e.activation(out=Sbf[d][hd].ap, in_=st.ap, func=AF.Copy), reads=[st], writes=[Sbf[d][hd]])

        def passB(Dd, Tn):
            nch = Tn // 128
            ring = Ring([{"q": Tl(A.bf16(1024)), "k": Tl(A.bf16(1024)), "v": Tl(A.bf16(2048)), "ob": Tl(A.f32(2048))} for _ in range(3)])
            bc = 0

            def bload(c):
                r = ring.next()
                cs = slice(c * 128, (c + 1) * 128)
                q3 = r["q"].ap.rearrange("p (a t) -> p a t", a=8)
                S.dma("sp", q3, Dd["QB"].rearrange("(a p) t -> p a t", p=128)[:, :, cs], reads=[Dd["res"]], writes=[r["q"]])
                S.dma("act", r["k"].ap, Dd["KB"][cs, :], reads=[Dd["res"]], writes=[r["k"]])
                S.dma("sp", r["v"].ap, Dd["V"][cs, :], reads=[Dd["res"]], writes=[r["v"]])
                return r, q3
            order = list(range(nch - 1, -1, -1))
            nxt = bload(order[0])
            for oi, c in enumerate(order):
                r, q3 = nxt
                if oi + 1 < len(order):
                    nxt = bload(order[oi + 1])
                q, k, v, ob = r["q"], r["k"], r["v"], r["ob"]
                cs = slice(c * 128, (c + 1) * 128)
                for h in range(4):
                    bank = bc % 4
                    bc += 1
                    pt = self.psum(bank)
                    for dc in range(2):
                        hd = h * 2 + dc
                        S.op("pe", lambda e, pt=pt, hd=hd, dc=dc, q3=q3: e.matmul(pt, lhsT=q3[:, hd, :], rhs=Sbf[1][hd].ap, start=(dc == 0), stop=(dc == 1)),
                             reads=[q, Sbf[1][hd]], writes=[self.ps_res[bank]])
                    if h % 2 == 0:
                        S.op("act", lambda e, pt=pt, ob=ob, h=h: e.activation(out=ob.ap[:, h * 512:(h + 1) * 512], in_=pt, func=AF.Copy), reads=[self.ps_res[bank]], writes=[ob])
                    else:
                        S.op("dve", lambda e, pt=pt, ob=ob, h=h: e.tensor_copy(out=ob.ap[:, h * 512:(h + 1) * 512], in_=pt), reads=[self.ps_res[bank]], writes=[ob])
                S.dma("sp", Dd["OB"][cs, :], ob.ap, reads=[ob], writes=[Dd["res"]])
                state_update(1, k, v)

        def passF(Dd, Tn, want_y):
            nch = Tn // 128
            ring = Ring([{"q": Tl(A.bf16(1024)), "qf": Tl(A.bf16(1024)), "kt": Tl(A.bf16(1024)), "k": Tl(A.bf16(1024)), "v": Tl(A.bf16(2048)),
                          "g": Tl(A.bf16(2048)), "ob": Tl(A.f32(2048)), "o": Tl(A.f32(2048)), "y": Tl(A.bf16(2048)), "yT": Tl(A.bf16(2048)),
                          "ss": Tl(A.f32(4)), "junk": Tl(A.bf16(512))} for _ in range(2)])
            pTr = Ring([Tl(A.bf16(128)) for _ in range(3)])
            bs = bo = 0

            def fload(c):
                r = ring.next()
                cs = slice(c * 128, (c + 1) * 128)
                q3 = r["q"].ap.rearrange("p (a t) -> p a t", a=8); qf3 = r["qf"].ap.rearrange("p (a t) -> p a t", a=8)
                kt3 = r["kt"].ap.rearrange("p (a t) -> p a t", a=8)
                fm = lambda nm: Dd[nm].rearrange("(a p) t -> p a t", p=128)[:, :, cs]
                S.dma("sp", q3, fm("Q"), reads=[Dd["res"]], writes=[r["q"]])
                S.dma("act", qf3, fm("QF"), reads=[Dd["res"]], writes=[r["qf"]])
                S.dma("sp", kt3, fm("KT"), reads=[Dd["res"]], writes=[r["kt"]])
                S.dma("act", r["k"].ap, Dd["KF"][cs, :], reads=[Dd["res"]], writes=[r["k"]])
                S.dma("sp", r["v"].ap, Dd["V"][cs, :], reads=[Dd["res"]], writes=[r["v"]])
                if want_y:
                    S.dma("act", r["g"].ap, Dd["G"][cs, :], reads=[Dd["res"]], writes=[r["g"]])
                    S.dma("sp", r["ob"].ap, Dd["OB"][cs, :], reads=[Dd["res"]], writes=[r["ob"]])
                return r, q3, qf3, kt3
            nxt = fload(0)
            for c in range(nch):
                r, q3, qf3, kt3 = nxt
                if c + 1 < nch:
                    nxt = fload(c + 1)
                q, qf, kt, k, v, g, ob, o, y, yT, ss, junk = (r[n_] for n_ in ("q", "qf", "kt", "k", "v", "g", "ob", "o", "y", "yT", "ss", "junk"))
                cs = slice(c * 128, (c + 1) * 128)
                if want_y:
                    def emit_S(h):
                        nonlocal bs
                        bS = bs % 2
                        bs += 1
                        pS = self.psum(bS)[:, 0:128]
                        for dc in range(2):
                            hd = h * 2 + dc
                            S.op("pe", lambda e, pS=pS, hd=hd, dc=dc, kt3=kt3, q3=q3: e.matmul(pS, lhsT=kt3[:, hd, :], rhs=q3[:, hd, :], start=(dc == 0), stop=(dc == 1)),
                                 reads=[kt, q], writes=[self.ps_res[bS]])
                        pT = pTr.next()
                        S.op("dve", lambda e, pS=pS, pT=pT, h=h: e.tensor_tensor(out=pT.ap, in0=pS, in1=maskT.ap[:, h * 128:(h + 1) * 128], op=ALU.mult),
                             reads=[self.ps_res[bS], maskT], writes=[pT])
                        return pT
                    pT_next = emit_S(0)
                    for h in range(4):
                        pT = pT_next
                        if h + 1 < 4:
                            pT_next = emit_S(h + 1)
                        bO = 2 + bo % 2
                        bo += 1
                        pO = self.psum(bO)
                        S.op("pe", lambda e, pO=pO, pT=pT, v=v, h=h: e.matmul(pO, lhsT=pT.ap, rhs=v.ap[:, h * 512:(h + 1) * 512], start=True, stop=False),
                             reads=[pT, v], writes=[self.ps_res[bO]])
                        for dc in range(2):
                            hd = h * 2 + dc
                            S.op("pe", lambda e, pO=pO, hd=hd, dc=dc, qf3=qf3: e.matmul(pO, lhsT=qf3[:, hd, :], rhs=Sbf[0][hd].ap, start=False, stop=(dc == 1)),
                                 reads=[qf, Sbf[0][hd]], writes=[self.ps_res[bO]])
                        S.op("dve", lambda e, pO=pO, o=o, ob=ob, h=h: e.tensor_tensor(out=o.ap[:, h * 512:(h + 1) * 512], in0=pO, in1=ob.ap[:, h * 512:(h + 1) * 512], op=ALU.add),
                             reads=[self.ps_res[bO], ob], writes=[o])
                        S.op("act", lambda e, o=o, junk=junk, ss=ss, h=h: e.activation(out=junk.ap, in_=o.ap[:, h * 512:(h + 1) * 512], func=AF.Square, accum_out=ss.ap[:, h:h + 1]),
                             reads=[o], writes=[junk, ss])
                    S.op("dve", lambda e, ss=ss: e.tensor_scalar(out=ss.ap, in0=ss.ap, scalar1=1.0 / 512.0, scalar2=EPS, op0=ALU.mult, op1=ALU.add), reads=[ss], writes=[ss])
                    S.op("act", lambda e, ss=ss: e.activation(out=ss.ap, in_=ss.ap, func=AF.Sqrt), reads=[ss], writes=[ss])
                    S.op("dve", lambda e, ss=ss: e.reciprocal(out=ss.ap, in_=ss.ap), reads=[ss], writes=[ss])
                    for h in range(4):
                        S.op("dve", lambda e, o=o, y=y, g=g, ss=ss, h=h: e.scalar_tensor_tensor(
                            out=y.ap[:, h * 512:(h + 1) * 512], in0=o.ap[:, h * 512:(h + 1) * 512], scalar=ss.ap[:, h:h + 1], in1=g.ap[:, h * 512:(h + 1) * 512],
                            op0=ALU.mult, op1=ALU.mult), reads=[o, ss, g], writes=[y])
                    for half in range(2):
                        bank = 4 + half
                        pb = self.psum(bank, 1, BF16)
                        for j in range(8):
                            jj_ = half * 8 + j
                            S.op("pe", lambda e, pb=pb, j=j, jj_=jj_, y=y: e.transpose(out=pb[:, j * 128:(j + 1) * 128], in_=y.ap[:, jj_ * 128:(jj_ + 1) * 128], identity=self.ident.ap),
                                 reads=[y, self.ident], writes=[self.ps_res[bank]])
                        if half == 0:
                            S.op("act", lambda e, pb=pb, yT=yT: e.activation(out=yT.ap[:, 0:1024], in_=pb, func=AF.Copy), reads=[self.ps_res[bank]], writes=[yT])
                        else:
                            S.op("dve", lambda e, pb=pb, yT=yT: e.tensor_copy(out=yT.ap[:, 1024:2048], in_=pb), reads=[self.ps_res[bank]], writes=[yT])
                    S.dma("sp", Dd["Y"].rearrange("(a p) t -> p a t", p=128)[:, :, cs], yT.ap.rearrange("p (a t) -> p a t", a=16), reads=[yT], writes=[self.Y_res])
                state_update(0, k, v)

        passB(DC, TC)
        S.barrier(); A.top = pmark
        passF(DC, TC, self.need_ctx)
        S.barrier(); A.top = pmark
        passB(DX, T)
        S.barrier(); A.top = pmark
        passF(DX, T, True)
        S.barrier(); A.top = base_mark
        wo = Tl(A.bf16(16 * 1024))
        wo3 = wo.ap.rearrange("p (k n) -> p k n", k=16)
        stage = Ring([Tl(A.f32(2048)) for _ in range(3)])
        self.load_w_bf16(W["ret_w_out"], 2048, 1024, wo3, wo, stage_ring=stage)
        self.out_proj(DX["Y"], 2048, wo3, wo, T, self.gt_bc, x_src, self.xs, self.x_res)
        if self.need_ctx:
            self.out_proj(DC["Y"], 2048, wo3, wo, TC, self.gtc_bc, c_src, self.cs, self.c_res)

    def mix_hy(self, x_src, c_src):
        S, A, W, T, TC = self.S, self.A, self.W, self.T, self.TC
        f32 = np.float32
        bf = ml_dtypes.bfloat16
        N = 16384
        TWO_PI = 2.0 * math.pi
        n_ = np.arange(128, dtype=np.float64)
        Fc = np.exp(-2j * np.pi * np.outer(n_, n_) / 128.0)
        Fre, Fim = Fc.real, Fc.imag
        twc = np.exp(-2j * np.pi * np.outer(n_, n_) / N)
        FS = np.concatenate([Fre, Fim], axis=1)
        cmat = np.concatenate([Fre, Fim, -Fim, Fre, -Fim, Fim, Fre, Fre / N, Fim / N], axis=1)
        cmat_d = self.const("c_hy_cmat", cmat.astype(bf))
        tw_d = self.const("c_hy_tw", np.concatenate([np.tile(twc.real, (1, 4)), np.tile(twc.imag, (1, 4))], axis=1).astype(f32))

        def fs_rows(L):
            M = L // 128
            rows = list(range(M)) + ([] if M == 64 else [])
            kr = list(range(M)) + list(range(128 - M, 128))
            return FS[:M].astype(bf), FS[kr].astype(bf)
        fss_x, fsk_x = fs_rows(T); fss_c, fsk_c = fs_rows(TC)
        fs_d = {"sx": self.const("c_hy_fssx", fss_x), "kx": self.const("c_hy_fskx", fsk_x),
                "sc": self.const("c_hy_fssc", fss_c), "kc": self.const("c_hy_fskc", fsk_c)}

        def zfeat(L):
            bands = 16
            t = np.linspace(0.0, 1.0, L, dtype=f32)[:, None]
            w = (f32(2.0 * math.pi) * np.arange(L, dtype=f32)[:, None] / f32(L)).astype(f32)
            fr = np.linspace(1e-4, bands - 1, bands, dtype=f32)[None, :]
            arg = (fr * w).astype(f32).astype(np.float64)
            z = np.concatenate([t, np.cos(arg).astype(f32), -np.sin(arg).astype(f32)], axis=-1)
            return np.ascontiguousarray(z.T.astype(f32)), np.ascontiguousarray(np.tile(t.T, (128, 1)).astype(f32))
        zT_x, tn_x = zfeat(T); zT_c, tn_c = zfeat(TC)
        zt_d = {"x": self.const("c_hy_zTx", zT_x), "c": self.const("c_hy_zTc", zT_c)}
        tn_d = {"x": self.const("c_hy_tnx", tn_x), "c": self.const("c_hy_tnc", tn_c)}
        deltas = np.abs(np.linspace(math.log(1e-2) / 1.5, math.log(1e-2) / 0.3, 1024, dtype=f32))
        nd_d = self.const("c_hy_ndelta", np.ascontiguousarray((-deltas).reshape(8, 128).T.astype(f32)))

        U = {"x": self.scratch("hy_Ux", [4096, T]), "c": self.scratch("hy_Uc", [4096, TC])}
        Z = {"x": self.scratch("hy_Zx", [3072, T]), "c": self.scratch("hy_Zc", [3072, TC])}
        KERN = {"x": self.scratch("hy_KERNx", [2, 1024, N]), "c": self.scratch("hy_KERNc", [2, 1024, N])}
        KF = {"x": self.scratch("hy_KFx", [2, 128, 1024, 256]), "c": self.scratch("hy_KFc", [2, 128, 1024, 256])}
        Y = {"x": self.scratch("hy_Yx", [1024, T], BF16), "c": self.scratch("hy_Yc", [1024, TC], BF16)}
        KC = self.scratch("hy_KC", [2, 1024, 2 * TC])
        identf_d = self.const("c_identf", np.eye(128, dtype=f32))
        jmat_d = self.const("c_jmat", np.eye(128, dtype=f32)[::-1].copy().astype(bf))
        Ures = {"x": FreeRes(), "c": FreeRes()}
        dres = FreeRes()
        self.Y_res = FreeRes()
        LEN = {"x": T, "c": TC}
        seqs = ["c", "x"] if self.need_ctx else ["c", "x"]

        w_tl = Tl(A.bf16(8 * 4096))
        w3 = w_tl.ap.rearrange("p (k n) -> p k n", k=8)
        stage = Ring([Tl(A.f32(2048)) for _ in range(3)])
        self.load_w_bf16(W["hy_w_in"], 1024, 4096, w3, w_tl, stage_ring=stage)
        mark = A.top
        self.proj_to_dram(c_src, self.c_res, TC, 1, w3, w_tl, 32, U["c"], Ures["c"], min(512, TC))
        A.top = mark
        S.barrier()
        self.proj_to_dram(x_src, self.x_res, T, 0, w3, w_tl, 32, U["x"], Ures["x"], min(512, T))
        S.barrier(); A.reset()

        def col(src_ap, n):
            t = Tl(A.f32(n))
            S.dma("sp", t.ap, src_ap.rearrange("(n p) -> p n", p=128), writes=[t], slow=True)
            return t
        cw = [col(W["hy_conv_w"][k], 24) for k in range(3)]
        cb = col(W["hy_conv_b"], 24)
        TM = max(T, TC)
        ur = Ring([Tl(A.f32(TM + 2)) for _ in range(2)])
        zr = Ring([Tl(A.f32(TM)) for _ in range(2)])
        for sq in ("c", "x"):
            Tn = LEN[sq]
            for n in range(24):
                ub = ur.next(); zb = zr.next()
                S.op("pool", lambda e, ub=ub: e.memset(ub.ap[:, 0:1], 0.0), writes=[ub])
                S.op("pool", lambda e, ub=ub, Tn=Tn: e.memset(ub.ap[:, Tn + 1:Tn + 2], 0.0), writes=[ub])
                S.dma("sp", ub.ap[:, 1:Tn + 1], U[sq][n * 128:(n + 1) * 128, :], reads=[Ures[sq]], writes=[ub])
                S.op("dve", lambda e, ub=ub, zb=zb, n=n, Tn=Tn: e.tensor_scalar(out=zb.ap[:, 0:Tn], in0=ub.ap[:, 0:Tn], scalar1=cw[0].ap[:, n:n + 1],
                                                                         scalar2=cb.ap[:, n:n + 1], op0=ALU.mult, op1=ALU.add),
                     reads=[ub, cw[0], cb], writes=[zb])
                for k in (1, 2):
                    S.op("dve", lambda e, ub=ub, zb=zb, n=n, k=k, Tn=Tn: e.scalar_tensor_tensor(out=zb.ap[:, 0:Tn], in0=ub.ap[:, k:k + Tn], scalar=cw[k].ap[:, n:n + 1],
                                                                                      in1=zb.ap[:, 0:Tn], op0=ALU.mult, op1=ALU.add),
                         reads=[ub, cw[k], zb], writes=[zb])
                S.dma("act", Z[sq][n * 128:(n + 1) * 128, :], zb.ap[:, 0:Tn], reads=[zb], writes=[dres])
        S.barrier(); A.reset()

        fw1 = Tl(A.f32(64)); fw2 = Tl(A.f32(64)); fw3 = Tl(A.f32(4096))
        S.dma("sp", fw1.ap[0:33, :], W["hy_fw1"], writes=[fw1])
        S.dma("sp", fw2.ap[0:64, :], W["hy_fw2"], writes=[fw2])
        S.dma("sp", fw3.ap[0:64, :], W["hy_fw3"], writes=[fw3])
        sm = Tl(A.f32(8))
        S.dma("sp", sm.ap[0:64, 0:1], W["hy_freq"].rearrange("(p o) -> p o", o=1), writes=[sm])
        S.dma("sp", sm.ap[0:64, 1:2], W["hy_fb1"].rearrange("(p o) -> p o", o=1), writes=[sm])
        S.dma("sp", sm.ap[0:64, 2:3], W["hy_fb2"].rearrange("(p o) -> p o", o=1), writes=[sm])
        OFF = math.pi + TWO_PI * 16
        for i in range(2):
            S.op("dve", lambda e, i=i: e.tensor_scalar(out=sm.ap[0:64, 3 + i:4 + i], in0=sm.ap[0:64, 1 + i:2 + i], scalar1=sm.ap[0:64, 0:1], scalar2=None, op0=ALU.mult),
                 reads=[sm], writes=[sm])
        nd = Tl(A.f32(8))
        S.dma("sp", nd.ap, nd_d, writes=[nd])
        skc = Tl(A.f32(16))
        S.dma("sp", skc.ap.rearrange("p (o n) -> p o n", o=2), W["hy_skip"].rearrange("o (n p) -> p o n", p=128), writes=[skc], slow=True)
        zero = Tl(A.f32(1))
        S.op("pool", lambda e: e.memset(zero.ap, 0.0), writes=[zero])
        MAGIC = 12582912.0
        hidT = Tl(A.f32(TM))
        f0 = Tl(A.f32(TM)); f1 = Tl(A.f32(TM)); rev = Tl(A.f32(TM))
        ztr = Ring([Tl(A.f32(512)) for _ in range(2)])
        h1r = Ring([{"a": Tl(A.f32(512)), "r": Tl(A.f32(512))} for _ in range(2)])
        tnr = Ring([Tl(A.f32(512)) for _ in range(2)])
        decr = Ring([Tl(A.f32(512)) for _ in range(2)])
        sums = Tl(A.f32(4))

        def sin_layer(pt, bias_col, dst_ap, TW, hb):
            a, r = hb["a"], hb["r"]
            S.op("dve", lambda e: e.tensor_scalar(out=a.ap[0:64, 0:TW], in0=pt, scalar1=sm.ap[0:64, 0:1], scalar2=sm.ap[0:64, bias_col:bias_col + 1],
                                                  op0=ALU.mult, op1=ALU.add), reads=[self.ps_res[0], sm], writes=[a])
            S.op("dve", lambda e: e.tensor_scalar(out=r.ap[0:64, 0:TW], in0=a.ap[0:64, 0:TW], scalar1=1.0 / TWO_PI, scalar2=MAGIC, op0=ALU.mult, op1=ALU.add),
                 reads=[a], writes=[r])
            S.op("dve", lambda e: e.tensor_scalar(out=r.ap[0:64, 0:TW], in0=r.ap[0:64, 0:TW], scalar1=-MAGIC, scalar2=None, op0=ALU.add), reads=[r], writes=[r])
            S.op("dve", lambda e: e.scalar_tensor_tensor(out=a.ap[0:64, 0:TW], in0=r.ap[0:64, 0:TW], scalar=-TWO_PI, in1=a.ap[0:64, 0:TW], op0=ALU.mult, op1=ALU.add),
                 reads=[r, a], writes=[a])
            S.op("act", lambda e: e.activation(out=dst_ap, in_=a.ap[0:64, 0:TW], func=AF.Sin), reads=[a], writes=[hidT, hb["r"]])

        for sq in ("c", "x"):
            L = LEN[sq]
            M = L // 128
            TW = min(512, L)
            for tt in range(L // TW):
                cs = slice(tt * TW, (tt + 1) * TW)
                zt = ztr.next(); hb = h1r.next()
                S.dma("sp", zt.ap[0:33, 0:TW], zt_d[sq][:, cs], writes=[zt])
                pt = self.psum(0)[0:64, 0:TW]
                S.op("pe", lambda e, pt=pt, zt=zt, TW=TW: e.matmul(pt, lhsT=fw1.ap[0:33, 0:64], rhs=zt.ap[0:33, 0:TW], start=True, stop=True),
                     reads=[fw1, zt], writes=[self.ps_res[0]])
                h1 = hb["r"]
                sin_layer(pt, 3, h1.ap[0:64, 0:TW], TW, hb)
                S.op("pe", lambda e, pt=pt, h1=h1, TW=TW: e.matmul(pt, lhsT=fw2.ap[0:64, 0:64], rhs=h1.ap[0:64, 0:TW], start=True, stop=True),
                     reads=[fw2, h1], writes=[self.ps_res[0]])
                hb2 = h1r.next()
                sin_layer(pt, 4, hidT.ap[0:64, cs], TW, hb2)
            for o in range(2):
                for cc in range(8):
                    fd = (f0, f1)
                    for dr in range(2):
                        j = o * 16 + dr * 8 + cc
                        for tt in range(L // TW):
                            cs = slice(tt * TW, (tt + 1) * TW)
                            b = 1 + (tt % 2)
                            pt = self.psum(b)[:, 0:TW]
                            S.op("pe", lambda e, pt=pt, j=j, cs=cs: e.matmul(pt, lhsT=fw3.ap[0:64, j * 128:(j + 1) * 128], rhs=hidT.ap[0:64, cs], start=True, stop=True),
                                 reads=[fw3, hidT], writes=[self.ps_res[b]])
                            tnt = tnr.next(); dc_ = decr.next()
                            S.dma("sp", tnt.ap[:, 0:TW], tn_d[sq][:, cs], writes=[tnt])
                            S.op("act", lambda e, tnt=tnt, dc_=dc_, cc=cc, TW=TW: e.activation(out=dc_.ap[:, 0:TW], in_=tnt.ap[:, 0:TW], func=AF.Exp, scale=nd.ap[:, cc:cc + 1]),
                                 reads=[tnt, nd], writes=[dc_])
                            S.op("dve", lambda e, pt=pt, dc_=dc_, dr=dr, cs=cs, TW=TW, fd=fd: e.tensor_tensor(out=fd[dr].ap[:, cs], in0=pt, in1=dc_.ap[:, 0:TW], op=ALU.mult),
                                 reads=[self.ps_res[b], dc_], writes=[fd[dr]])
                        s0 = dr
                        S.op("act", lambda e, dr=dr, s0=s0, L=L, fd=fd: e.activation(out=rev.ap[:, s0:L], in_=fd[dr].ap[:, s0:L], func=AF.Abs, accum_out=sums.ap[:, dr:dr + 1]),
                             reads=[fd[dr]], writes=[rev, sums])
                    S.op("dve", lambda e: e.tensor_tensor(out=sums.ap[:, 2:3], in0=sums.ap[:, 0:1], in1=sums.ap[:, 1:2], op=ALU.add), reads=[sums], writes=[sums])
                    S.op("dve", lambda e: e.reciprocal(out=sums.ap[:, 3:4], in_=sums.ap[:, 2:3]), reads=[sums], writes=[sums])
                    S.op("dve", lambda e, L=L: e.tensor_scalar(out=f0.ap[:, 0:L], in0=f0.ap[:, 0:L], scalar1=sums.ap[:, 3:4], scalar2=None, op0=ALU.mult),
                         reads=[f0, sums], writes=[f0])
                    S.op("dve", lambda e, o=o, cc=cc: e.tensor_scalar(out=f0.ap[:, 0:1], in0=f0.ap[:, 0:1], scalar1=skc.ap[:, o * 8 + cc:o * 8 + cc + 1], scalar2=None, op0=ALU.add),
                         reads=[f0, skc], writes=[f0])
                    rows = slice(cc * 128, (cc + 1) * 128)
                    S.op("dve", lambda e, L=L: e.tensor_scalar(out=rev.ap[:, 0:L - 1], in0=f1.ap[:, 1:L][:, ::-1], scalar1=sums.ap[:, 3:4], scalar2=None, op0=ALU.mult),
                         reads=[f1, sums], writes=[rev])
                    if sq == "c":
                        S.dma("sp", KC[o, rows, L - 1:2 * L - 1], f0.ap[:, 0:L], reads=[f0], writes=[dres])
                        S.dma("act", KC[o, rows, 0:L - 1], rev.ap[:, 0:L - 1], reads=[rev], writes=[dres])
                    else:
                        S.dma("sp", KERN[sq][o, rows, 0:L], f0.ap[:, 0:L], reads=[f0], writes=[dres])
                        S.dma("act", KERN[sq][o, rows, N - L + 1:N], rev.ap[:, 0:L - 1], reads=[rev], writes=[dres])
                        S.dma("sp", KERN[sq][o, rows, N - L:N - L + 1], zero.ap, reads=[zero], writes=[dres], slow=True)
        S.barrier(); A.reset()

        cm = Tl(A.bf16(9 * 128))
        S.dma("sp", cm.ap, cmat_d, writes=[cm])
        Fre_, Fim_, nFim_ = cm.ap[:, 0:128], cm.ap[:, 128:256], cm.ap[:, 256:384]
        GS1_, GS2_ = cm.ap[:, 384:640], cm.ap[:, 640:896]
        FreN_, FimN_ = cm.ap[:, 896:1024], cm.ap[:, 1024:1152]
        tw = Tl(A.f32(1024))
        S.dma("sp", tw.ap, tw_d, writes=[tw])
        twR = tw.ap[:, 0:512].rearrange("p (c k) -> p c k", c=4); twI = tw.ap[:, 512:1024].rearrange("p (c k) -> p c k", c=4)
        fs_t = {}
        for key, d in fs_d.items():
            t = Tl(A.bf16(256))
            R = d.shape[0]
            S.dma("sp", t.ap[0:R, :], d, writes=[t])
            fs_t[key] = (t, R)
        NCH = 2
        chs = []
        for ci in range(NCH):
            chs.append({"b": ci * 4,
                        "t": [Tl(A.f32(512)) for _ in range(4)],
                        "cre": Tl(A.bf16(512)), "cim": Tl(A.bf16(512)), "pre": Tl(A.bf16(512)), "pim": Tl(A.bf16(512)),
                        "zre": Tl(A.bf16(512)), "zim": Tl(A.bf16(512)),
                        "kf": Ring([Tl(A.f32(1024)) for _ in range(4)]),
                        "af": Ring([Tl(A.f32(512)) for _ in range(2)]), "ab": Ring([Tl(A.bf16(512)) for _ in range(3)]),
                        "x1": Ring([Tl(A.f32(512)) for _ in range(2)]), "x2": Ring([Tl(A.f32(512)) for _ in range(2)]),
                        "g": Ring([Tl(A.f32(512)) for _ in range(2)]), "y1": Ring([Tl(A.bf16(512)) for _ in range(2)]),
                        "y2": Tl(A.f32(512)), "yo": Ring([Tl(A.bf16(512)) for _ in range(2)]),
                        "kfo": Ring([Tl(A.f32(1024)) for _ in range(3)])})

        def v3(ap, c=4):
            return ap.rearrange("p (c k) -> p c k", c=c)

        def cmul(ch, src_re, src_im, rr, ri, are, aim, bre, bim, conj, out_re, out_im):
            t1, t2, t3, t4 = ch["t"]
            pr = [rr, ri]
            S.op("dve", lambda e: e.tensor_tensor(out=v3(t1.ap), in0=are, in1=bre, op=ALU.mult), reads=pr + src_re, writes=[t1])
            S.op("dve", lambda e: e.tensor_tensor(out=v3(t2.ap), in0=aim, in1=bim, op=ALU.mult), reads=pr + src_im, writes=[t2])
            S.op("pool", lambda e: e.tensor_tensor(out=out_re.ap, in0=t1.ap, in1=t2.ap, op=(ALU.add if conj else ALU.subtract)), reads=[t1, t2], writes=[out_re])
            S.op("dve", lambda e: e.tensor_tensor(out=v3(t3.ap), in0=aim, in1=bre, op=ALU.mult), reads=pr + src_re, writes=[t3])
            S.op("dve", lambda e: e.tensor_tensor(out=v3(t4.ap), in0=are, in1=bim, op=ALU.mult), reads=pr + src_im, writes=[t4])
            S.op("pool", lambda e: e.tensor_tensor(out=out_im.ap, in0=t3.ap, in1=t4.ap, op=(ALU.subtract if conj else ALU.add)), reads=[t3, t4], writes=[out_im])

        def fwd_fft(ch, a_tl, a3, R, fs):
            b = ch["b"]
            ps01 = self.psum(b, 2)
            for ci in range(4):
                S.op("pe", lambda e, ci=ci: e.matmul(ps01[:, ci * 256:(ci + 1) * 256], lhsT=a3[0:R, ci, :], rhs=fs.ap[0:R, :], start=True, stop=True),
                     reads=[a_tl, fs], writes=[self.ps_res[b + ci // 2]])
            yield
            p4 = ps01.rearrange("p (c r k) -> p c r k", c=4, r=2)
            rr = self.ps_res[b]; ri = self.ps_res[b + 1]
            cmul(ch, [tw], [tw], rr, ri, p4[:, :, 0, :], p4[:, :, 1, :], twR, twI, False, ch["cre"], ch["cim"])
            yield
            for bank, l1, l2 in ((b + 2, Fre_, nFim_), (b + 3, Fim_, Fre_)):
                pt = self.psum(bank)
                S.op("pe", lambda e, pt=pt, l1=l1: e.matmul(pt, lhsT=l1, rhs=ch["cre"].ap, start=True, stop=False), reads=[cm, ch["cre"]], writes=[self.ps_res[bank]])
                S.op("pe", lambda e, pt=pt, l2=l2: e.matmul(pt, lhsT=l2, rhs=ch["cim"].ap, start=False, stop=True), reads=[cm, ch["cim"]], writes=[self.ps_res[bank]])
            yield

        def inv_fft(ch, kf_tl, M):
            b = ch["b"]
            k4 = kf_tl.ap.rearrange("p (c r k) -> p c r k", c=4, r=2)
            cmul(ch, [kf_tl], [kf_tl], self.ps_res[b + 2], self.ps_res[b + 3], v3(self.psum(b + 2)), v3(self.psum(b + 3)), k4[:, :, 0, :], k4[:, :, 1, :],
                 False, ch["pre"], ch["pim"])
            yield
            ps01 = self.psum(b, 2)
            pre3, pim3 = v3(ch["pre"].ap), v3(ch["pim"].ap)
            for ci in range(4):
                S.op("pe", lambda e, ci=ci: e.matmul(ps01[:, ci * 256:(ci + 1) * 256], lhsT=pre3[:, ci, :], rhs=GS1_, start=True, stop=False),
                     reads=[ch["pre"], cm], writes=[self.ps_res[b + ci // 2]])
                S.op("pe", lambda e, ci=ci: e.matmul(ps01[:, ci * 256:(ci + 1) * 256], lhsT=pim3[:, ci, :], rhs=GS2_, start=False, stop=True),
                     reads=[ch["pim"], cm], writes=[self.ps_res[b + ci // 2]])
            yield
            p4 = ps01.rearrange("p (c r k) -> p c r k", c=4, r=2)
            cmul(ch, [tw], [tw], self.ps_res[b], self.ps_res[b + 1], p4[:, :, 0, :], p4[:, :, 1, :], twR, twI, True, ch["zre"], ch["zim"])
            yield
            pt = self.psum(b + 2)[0:M, :]
            S.op("pe", lambda e: e.matmul(pt, lhsT=FreN_[:, 0:M], rhs=ch["zre"].ap, start=True, stop=False), reads=[cm, ch["zre"]], writes=[self.ps_res[b + 2]])
            S.op("pe", lambda e: e.matmul(pt, lhsT=FimN_[:, 0:M], rhs=ch["zim"].ap, start=False, stop=True), reads=[cm, ch["zim"]], writes=[self.ps_res[b + 2]])
            yield

        def run_chains(gens):
            act = list(gens)
            while act:
                for g in list(act):
                    try:
                        next(g)
                    except StopIteration:
                        act.remove(g)

        def kern_chain(ch, sq, groups):
            L = LEN[sq]; M = L // 128
            fs, R = fs_t["k" + sq]
            b = ch["b"]
            def kload(og):
                o, g = og
                c0 = g * 4
                af = ch["kfo"].next()
                a3f = af.ap[:, 0:512].rearrange("p (c k) -> p c k", c=4)
                src = KERN[sq][o, c0:c0 + 4, :].rearrange("c (n1 n2) -> n1 c n2", n2=128)
                S.dma("sp", a3f[0:M], src[0:M], reads=[dres], writes=[af])
                if R > M:
                    S.dma("act", a3f[M:2 * M], src[128 - M:128], reads=[dres], writes=[af])
                ab = ch["ab"].next()
                S.op("act", lambda e, af=af, ab=ab: e.activation(out=ab.ap[0:R, :], in_=af.ap[0:R, 0:512], func=AF.Copy), reads=[af], writes=[ab])
                return ab
            nxt = kload(groups[0]) if groups else None
            for gi_, (o, g) in enumerate(groups):
                c0 = g * 4
                ab = nxt
                if gi_ + 1 < len(groups):
                    nxt = kload(groups[gi_ + 1])
                yield from fwd_fft(ch, ab, v3(ab.ap), R, fs)
                ko = ch["kf"].next()
                k4 = ko.ap.rearrange("p (c r k) -> p c r k", c=4, r=2)
                S.op("act", lambda e, k4=k4: e.activation(out=k4[:, :, 0, :], in_=v3(self.psum(b + 2)), func=AF.Copy), reads=[self.ps_res[b + 2]], writes=[ko])
                S.op("act", lambda e, k4=k4: e.activation(out=k4[:, :, 1, :], in_=v3(self.psum(b + 3)), func=AF.Copy), reads=[self.ps_res[b + 3]], writes=[ko])
                S.dma("sp", KF[sq][o, :, c0:c0 + 4, :], ko.ap.rearrange("p (c x) -> p c x", c=4), reads=[ko], writes=[dres])
                yield

        for sq in ("x",):
            allg = [(o, g) for o in range(2) for g in range(256)]
            run_chains([kern_chain(chs[i], sq, allg[i::NCH]) for i in range(NCH)])
        S.barrier()

        def conv_chain(ch, sq, groups):
            L = LEN[sq]; M = L // 128
            fs, R = fs_t["s" + sq]
            b = ch["b"]
            Zv = Z[sq].rearrange("c (n1 n2) -> n1 c n2", n2=128)
            Uv = U[sq].rearrange("c (n1 n2) -> n1 c n2", n2=128)
            Yv = Y[sq].rearrange("c (n1 n2) -> n1 c n2", n2=128)
            def cload(g):
                c0 = g * 4
                af = ch["af"].next(); x1 = ch["x1"].next(); x2 = ch["x2"].next(); gg = ch["g"].next(); ab = ch["ab"].next()
                S.dma("sp", v3(af.ap)[0:M], Zv[:, c0:c0 + 4, :], reads=[dres], writes=[af])
                S.dma("act", v3(x1.ap)[0:M], Zv[:, 1024 + c0:1024 + c0 + 4, :], reads=[dres], writes=[x1])
                S.dma("sp", v3(x2.ap)[0:M], Zv[:, 2048 + c0:2048 + c0 + 4, :], reads=[dres], writes=[x2])
                S.dma("act", v3(gg.ap)[0:M], Uv[:, 3072 + c0:3072 + c0 + 4, :], reads=[Ures[sq]], writes=[gg])
                S.op("act", lambda e, af=af, ab=ab: e.activation(out=ab.ap[0:M, :], in_=af.ap[0:M, :], func=AF.Copy), reads=[af], writes=[ab])
                S.op("act", lambda e, gg=gg: e.activation(out=gg.ap[0:M, :], in_=gg.ap[0:M, :], func=AF.Silu), reads=[gg], writes=[gg])
                kf0 = ch["kf"].next()
                S.dma("sp", kf0.ap.rearrange("p (c x) -> p c x", c=4), KF[sq][0, :, c0:c0 + 4, :], reads=[dres], writes=[kf0])
                kf1 = ch["kf"].next()
                S.dma("act", kf1.ap.rearrange("p (c x) -> p c x", c=4), KF[sq][1, :, c0:c0 + 4, :], reads=[dres], writes=[kf1])
                return x1, x2, gg, ab, kf0, kf1
            nxt = cload(groups[0]) if groups else None
            for gi_, g in enumerate(groups):
                c0 = g * 4
                x1, x2, gg, ab, kf0, kf1 = nxt
                if gi_ + 1 < len(groups):
                    nxt = cload(groups[gi_ + 1])
                yield from fwd_fft(ch, ab, v3(ab.ap), M, fs)
                yield from inv_fft(ch, kf0, M)
                y1 = ch["y1"].next()
                S.op("dve", lambda e, y1=y1, x1=x1: e.tensor_tensor(out=y1.ap[0:M, :], in0=self.psum(b + 2)[0:M, :], in1=x1.ap[0:M, :], op=ALU.mult),
                     reads=[self.ps_res[b + 2], x1], writes=[y1])
                yield
                yield from fwd_fft(ch, y1, v3(y1.ap), M, fs)
                yield from inv_fft(ch, kf1, M)
                y2 = ch["y2"]; yo = ch["yo"].next()
                S.op("dve", lambda e, y2=y2, x2=x2: e.tensor_tensor(out=y2.ap[0:M, :], in0=self.psum(b + 2)[0:M, :], in1=x2.ap[0:M, :], op=ALU.mult),
                     reads=[self.ps_res[b + 2], x2], writes=[y2])
                S.op("pool", lambda e, y2=y2, yo=yo, gg=gg: e.tensor_tensor(out=yo.ap[0:M, :], in0=y2.ap[0:M, :], in1=gg.ap[0:M, :], op=ALU.mult),
                     reads=[y2, gg], writes=[yo])
                S.dma("sp", Yv[:, c0:c0 + 4, :], v3(yo.ap)[0:M], reads=[yo], writes=[self.Y_res])
                yield

        for sq in ("x",):
            allg = list(range(256))
            run_chains([conv_chain(chs[i], sq, allg[i::NCH]) for i in range(NCH)])
        S.barrier(); A.reset()

        if self.need_ctx:
            Lc = TC; NH = Lc // 128
            identf = Tl(A.f32(128))
            S.dma("sp", identf.ap, identf_d, writes=[identf])
            tm = [Tl(A.f32(NH * 1024)) for _ in range(4)]
            t3 = lambda ap: ap.rearrange("p (h c) -> p h c", h=NH)
            ld = Ring([Tl(A.f32(Lc)) for _ in range(3)])
            tcnt = 0
            for n in range(32):
                src = Z["c"][n * 128:(n + 1) * 128, :] if n < 24 else U["c"][3072 + (n - 24) * 128:3072 + (n - 23) * 128, :]
                dstT = tm[n // 8]; cc = n % 8
                lt = ld.next()
                S.dma("sp" if n % 2 == 0 else "act", lt.ap, src, reads=[dres, Ures["c"]], writes=[lt])
                for h in range(NH):
                    bank = tcnt % 2
                    tcnt += 1
                    pt = self.psum(bank)[:, 0:128]
                    S.op("pe", lambda e, pt=pt, lt=lt, h=h: e.transpose(out=pt, in_=lt.ap[:, h * 128:(h + 1) * 128], identity=identf.ap),
                         reads=[lt, identf], writes=[self.ps_res[bank]])
                    if tcnt % 2 == 0:
                        S.op("act", lambda e, pt=pt, dstT=dstT, h=h, cc=cc: e.activation(out=t3(dstT.ap)[:, h, cc * 128:(cc + 1) * 128], in_=pt, func=AF.Copy),
                             reads=[self.ps_res[bank]], writes=[dstT])
                    else:
                        S.op("dve", lambda e, pt=pt, dstT=dstT, h=h, cc=cc: e.tensor_copy(out=t3(dstT.ap)[:, h, cc * 128:(cc + 1) * 128], in_=pt),
                             reads=[self.ps_res[bank]], writes=[dstT])
            vT, x1T, x2T, gT = tm
            S.op("act", lambda e: e.activation(out=gT.ap, in_=gT.ap, func=AF.Silu), reads=[gT], writes=[gT])
            nb = Tl(A.bf16(NH * 1024))
            ub = Tl(A.bf16(NH * 1024))
            ub3 = t3(ub.ap); nb3 = t3(nb.ap)
            jm = Tl(A.bf16(128))
            S.dma("sp", jm.ap, jmat_d, writes=[jm])
            S.op("act", lambda e: e.activation(out=nb.ap, in_=vT.ap, func=AF.Copy), reads=[vT], writes=[nb])

            def reverse_blocks():
                for sh in range(NH):
                    for hf in range(2):
                        bank = 2 * sh + hf
                        pt = self.psum(bank)
                        S.op("pe", lambda e, pt=pt, sh=sh, hf=hf: e.matmul(pt, lhsT=jm.ap, rhs=nb3[:, sh, hf * 512:(hf + 1) * 512], start=True, stop=True),
                             reads=[jm, nb], writes=[self.ps_res[bank]])
                        S.op("act", lambda e, pt=pt, sh=sh, hf=hf: e.activation(out=ub3[:, sh, hf * 512:(hf + 1) * 512], in_=pt, func=AF.Copy),
                             reads=[self.ps_res[bank]], writes=[ub])
            reverse_blocks()
            CG = 8
            trf = Ring([Tl(A.f32(CG * Lc)) for _ in range(4)])
            trb = Ring([Tl(A.bf16(CG * Lc)) for _ in range(6)])
            y2 = Tl(A.f32(1024)); yb = Tl(A.bf16(NH * 1024)); ycT = Tl(A.bf16(8 * Lc))
            ycT3 = ycT.ap.rearrange("p (k t) -> p k t", k=8)
            cvt = 0
            for o in range(2):
                for cg in range(1024 // CG):
                    tbs = []
                    for sh in range(NH):
                        tf = trf.next(); tb = trb.next()
                        off = (o * 1024 + cg * CG) * 2 * Lc + (Lc - 128) - sh * 128
                        src = bass.AP(KC.tensor, off, [[1, 128], [2 * Lc, CG], [1, Lc]])
                        S.dma("sp" if sh == 0 else "act", tf.ap.rearrange("p (c t) -> p c t", c=CG), src, reads=[dres], writes=[tf])
                        if cvt % 2 == 0:
                            S.op("act", lambda e, tf=tf, tb=tb: e.activation(out=tb.ap, in_=tf.ap, func=AF.Copy), reads=[tf], writes=[tb])
                        else:
                            S.op("dve", lambda e, tf=tf, tb=tb: e.tensor_copy(out=tb.ap, in_=tf.ap), reads=[tf], writes=[tb])
                        cvt += 1
                        tbs.append(tb)
                    for ci in range(CG):
                        c = cg * CG + ci
                        for th in range(NH):
                            bank = 4 + 2 * th + c // 512
                            pcol = self.psum(bank)[:, (c % 512):(c % 512) + 1]
                            for sh in range(NH):
                                tb = tbs[sh]
                                tb3 = tb.ap.rearrange("p (c t) -> p c t", c=CG)
                                S.op("pe", lambda e, pcol=pcol, tb3=tb3, ci=ci, th=th, sh=sh, c=c: e.matmul(pcol, lhsT=tb3[:, ci, th * 128:(th + 1) * 128], rhs=ub3[:, sh, c:c + 1],
                                                                                                  start=(sh == 0), stop=(sh == NH - 1)),
                                     reads=[tb, ub], writes=[self.ps_res[bank]])
                for th in range(NH):
                    pr = [self.ps_res[4 + 2 * th], self.ps_res[5 + 2 * th]]
                    pth = self.psum(4 + 2 * th, 2)
                    if o == 0:
                        S.op("dve", lambda e, pth=pth, th=th: e.tensor_tensor(out=nb3[:, th, :], in0=pth, in1=t3(x1T.ap)[:, th, :], op=ALU.mult),
                             reads=pr + [x1T], writes=[nb])
                        if th == NH - 1:
                            reverse_blocks()
                    else:
                        S.op("dve", lambda e, pth=pth, th=th: e.tensor_tensor(out=y2.ap, in0=pth, in1=t3(x2T.ap)[:, th, :], op=ALU.mult), reads=pr + [x2T], writes=[y2])
                        S.op("pool", lambda e, th=th: e.tensor_tensor(out=t3(yb.ap)[:, th, :], in0=y2.ap, in1=t3(gT.ap)[:, th, :], op=ALU.mult), reads=[y2, gT], writes=[yb])
                        pb = self.psum(th, 1, BF16)
                        for cc in range(8):
                            S.op("pe", lambda e, pb=pb, th=th, cc=cc: e.transpose(out=pb[:, cc * 128:(cc + 1) * 128], in_=t3(yb.ap)[:, th, cc * 128:(cc + 1) * 128], identity=self.ident.ap),
                                 reads=[yb, self.ident], writes=[self.ps_res[th]])
                        S.op("act", lambda e, pb=pb, th=th: e.activation(out=ycT3[:, :, th * 128:(th + 1) * 128], in_=pb.rearrange("p (k t) -> p k t", k=8), func=AF.Copy),
                             reads=[self.ps_res[th]], writes=[ycT])
            S.dma("sp", Y["c"].rearrange("(k p) t -> p k t", p=128), ycT3, reads=[ycT], writes=[self.Y_res])
            S.barrier(); A.reset()

        wo = Tl(A.bf16(8 * 1024))
        wo3 = wo.ap.rearrange("p (k n) -> p k n", k=8)
        stage = Ring([Tl(A.f32(2048)) for _ in range(3)])
        self.load_w_bf16(W["hy_w_out"], 1024, 1024, wo3, wo, stage_ring=stage)
        self.out_proj(Y["x"], 1024, wo3, wo, T, self.gt_bc, x_src, self.xs, self.x_res)
        if self.need_ctx:
            self.out_proj(Y["c"], 1024, wo3, wo, TC, self.gtc_bc, c_src, self.cs, self.c_res)

    def mix_att(self, x_src, c_src):
        S, A, W, T, TC = self.S, self.A, self.W, self.T, self.TC
        f32 = np.float32
        inv = (f32(10000.0) ** (-np.arange(32, dtype=f32) / f32(32))).astype(f32)
        rows = np.repeat(np.arange(T // GRID_W, dtype=f32), GRID_W); cols = np.tile(np.arange(GRID_W, dtype=f32), T // GRID_W)
        ang = np.concatenate([(rows[:, None] * inv[None, :]).astype(f32), (cols[:, None] * inv[None, :]).astype(f32)], axis=-1).astype(np.float64)
        cs_, sn_ = np.cos(ang).astype(f32).T, np.sin(ang).astype(f32).T
        tabs_x = (self.const("c_att_cos", np.concatenate([cs_, cs_], axis=0)), self.const("c_att_sin", np.concatenate([-sn_, sn_], axis=0)))
        tabs_c = (self.const("c_att_cosc", np.ones((128, TC), f32)), self.const("c_att_sinc", np.zeros((128, TC), f32)))
        QT = self.scratch("att_QT", [1024, T], BF16); GT = self.scratch("att_GT", [1024, T], BF16)
        Y = self.scratch("att_Y", [1024, T], BF16)
        dres = FreeRes()
        self.Y_res = FreeRes()
        NK = T + TC
        nkt = NK // 128
        KT = [Tl(A.bf16(NK)) for _ in range(2)]
        V = Tl(A.bf16(nkt * 256))
        V3 = V.ap.rearrange("p (k d) -> p k d", d=256)
        gq = Tl(A.f32(2))
        S.dma("sp", gq.ap[:, 0:1], W["att_q_norm_g"].rearrange("(p o) -> p o", o=1), writes=[gq])
        S.dma("sp", gq.ap[:, 1:2], W["att_k_norm_g"].rearrange("(p o) -> p o", o=1), writes=[gq])
        base_mark = A.top
        w_tl = Tl(A.bf16(8 * 2560))
        w3 = w_tl.ap.rearrange("p (k n) -> p k n", k=8)
        wmark = A.top
        stage = Ring([Tl(A.f32(2048)) for _ in range(2)])
        self.load_w_bf16(W["att_w_in"], 1024, 2560, w3, w_tl, stage_ring=stage)
        S.barrier(); A.top = wmark

        def att_proj(src, src_res, Tn, which, tabs, latent, key0):
            TT = min(512, Tn)
            xring, hring = self.make_norm_rings(TT, nh=2, nx=2)
            tab = Ring([{"c": Tl(A.f32(TT)), "s": Tl(A.f32(TT))} for _ in range(2)])
            tmp = Ring([{"sq": Tl(A.bf16(TT)), "rs": Tl(A.f32(TT)), "qn": Tl(A.f32(TT)), "sw": Tl(A.f32(TT)), "t1": Tl(A.f32(TT)), "t2": Tl(A.f32(TT))}
                        for _ in range(2)])
            stq = Ring([Tl(A.bf16(TT)) for _ in range(3)])
            cnt = [0, 0]
            for tt, hT, h3 in self.norm_tiles(src, src_res, Tn, which, xring, hring, None, TT):
                cols = slice(tt * TT, (tt + 1) * TT)
                tb = tab.next()
                S.dma("act", tb["c"].ap, tabs[0][:, cols], writes=[tb["c"]])
                S.dma("act", tb["s"].ap, tabs[1][:, cols], writes=[tb["s"]])
                heads = ([("q", i) for i in range(8)] if latent else []) + [("k", i) for i in range(2)]
                for kind, idx in heads:
                    oc = idx if kind == "q" else 8 + idx
                    gi = 0 if kind == "q" else 1
                    b = cnt[0] % 4
                    cnt[0] += 1
                    pt = self.psum(b)[:, 0:TT]
                    for kk in range(8):
                        S.op("pe", lambda e, kk=kk, oc=oc, pt=pt, h3=h3: e.matmul(pt, lhsT=w3[:, kk, oc * 128:(oc + 1) * 128], rhs=h3[:, kk, :],
                                                                              start=(kk == 0), stop=(kk == 7)), reads=[hT, w_tl], writes=[self.ps_res[b]])
                    tm = tmp.next()
                    sq, rs, qn, sw, t1, t2 = tm["sq"], tm["rs"], tm["qn"], tm["sw"], tm["t1"], tm["t2"]
                    S.op("act", lambda e, pt=pt, sq=sq: e.activation(out=sq.ap, in_=pt, func=AF.Square), reads=[self.ps_res[b]], writes=[sq])
                    b2 = 4 + cnt[1] % 2
                    cnt[1] += 1
                    p2 = self.psum(b2)[:, 0:TT]
                    S.op("pe", lambda e, p2=p2, sq=sq: e.matmul(p2, lhsT=self.ones_bf.ap, rhs=sq.ap, start=True, stop=True), reads=[sq, self.ones_bf], writes=[self.ps_res[b2]])
                    S.op("act", lambda e, p2=p2, rs=rs: e.activation(out=rs.ap, in_=p2, func=AF.Sqrt, scale=1.0 / 128.0, bias=self.eps_col.ap), reads=[self.ps_res[b2], self.eps_col], writes=[rs])
                    S.op("dve", lambda e, rs=rs: e.reciprocal(out=rs.ap, in_=rs.ap), reads=[rs], writes=[rs])
                    S.op("dve", lambda e, pt=pt, qn=qn, rs=rs, gi=gi: e.scalar_tensor_tensor(out=qn.ap, in0=pt, scalar=gq.ap[:, gi:gi + 1], in1=rs.ap, op0=ALU.mult, op1=ALU.mult),
                         reads=[self.ps_res[b], gq, rs], writes=[qn])
                    S.op("act", lambda e, qn=qn, sw=sw: e.activation(out=sw.ap[0:64, :], in_=qn.ap[64:128, :], func=AF.Copy), reads=[qn], writes=[sw])
                    S.op("pool", lambda e, qn=qn, sw=sw: e.tensor_copy(out=sw.ap[64:128, :], in_=qn.ap[0:64, :]), reads=[qn], writes=[sw])
                    S.op("pool", lambda e, qn=qn, t1=t1, tb=tb: e.tensor_tensor(out=t1.ap, in0=qn.ap, in1=tb["c"].ap, op=ALU.mult), reads=[qn, tb["c"]], writes=[t1])
                    S.op("dve", lambda e, sw=sw, t2=t2, tb=tb: e.tensor_tensor(out=t2.ap, in0=sw.ap, in1=tb["s"].ap, op=ALU.mult), reads=[sw, tb["s"]], writes=[t2])
                    if kind == "q":
                        st = stq.next()
                        S.op("pool", lambda e, t1=t1, t2=t2, st=st: e.tensor_tensor(out=st.ap, in0=t1.ap, in1=t2.ap, op=ALU.add), reads=[t1, t2], writes=[st])
                        S.dma("sp", QT[idx * 128:(idx + 1) * 128, cols], st.ap, reads=[st], writes=[dres])
                    else:
                        kdst = KT[idx].ap[:, key0 + tt * TT:key0 + (tt + 1) * TT]
                        S.op("pool", lambda e, t1=t1, t2=t2, kdst=kdst: e.tensor_tensor(out=kdst, in0=t1.ap, in1=t2.ap, op=ALU.add), reads=[t1, t2], writes=[KT[idx]])
                if latent:
                    for j in range(8):
                        b = cnt[0] % 4
                        cnt[0] += 1
                        pt = self.psum(b)[:, 0:TT]
                        for kk in range(8):
                            S.op("pe", lambda e, kk=kk, j=j, pt=pt, h3=h3: e.matmul(pt, lhsT=w3[:, kk, 1536 + j * 128:1536 + (j + 1) * 128], rhs=h3[:, kk, :],
                                                                                 start=(kk == 0), stop=(kk == 7)), reads=[hT, w_tl], writes=[self.ps_res[b]])
                        st = stq.next()
                        S.op("act", lambda e, pt=pt, st=st: e.activation(out=st.ap, in_=pt, func=AF.Silu), reads=[self.ps_res[b]], writes=[st])
                        S.dma("sp", GT[j * 128:(j + 1) * 128, cols], st.ap, reads=[st], writes=[dres])
                for s_ in range(TT // 128):
                    kti = (key0 + tt * TT) // 128 + s_
                    b2 = 6 + cnt[1] % 2
                    cnt[1] += 1
                    pv = self.psum(b2)[:, 0:256]
                    for kk in range(8):
                        S.op("pe", lambda e, kk=kk, pv=pv, h3=h3, s_=s_: e.matmul(pv, lhsT=h3[:, kk, s_ * 128:(s_ + 1) * 128], rhs=w3[:, kk, 1280:1536],
                                                                              start=(kk == 0), stop=(kk == 7)), reads=[hT, w_tl], writes=[self.ps_res[b2]])
                    S.op("act", lambda e, pv=pv, kti=kti: e.activation(out=V3[:, kti, :], in_=pv, func=AF.Copy), reads=[self.ps_res[b2]], writes=[V])

        att_proj(c_src, self.c_res, TC, 1, tabs_c, False, T)
        S.barrier(); A.top = wmark
        att_proj(x_src, self.x_res, T, 0, tabs_x, True, 0)
        S.barrier(); A.top = base_mark

        scale = 1.0 / math.sqrt(128.0)
        qr = Ring([{"q": Tl(A.bf16(1024)), "g": Tl(A.bf16(1024)), "y": Tl(A.bf16(1024))} for _ in range(2)])
        pTr = Ring([Tl(A.bf16(512)) for _ in range(7)])
        accs = Ring([{"d": Tl(A.f32(512)), "p": Tl(A.f32(512)), "rs": Tl(A.f32(512)), "o": Tl(A.f32(512))} for _ in range(2)])
        QTv = QT.rearrange("(h p) t -> p h t", p=128); GTv = GT.rearrange("(h p) t -> p h t", p=128); Yv = Y.rearrange("(h p) t -> p h t", p=128)
        gcnt = 0
        for qt in range(T // 128):
            cs = slice(qt * 128, (qt + 1) * 128)
            qb = qr.next()
            q, gt_, y = qb["q"], qb["g"], qb["y"]
            q3 = q.ap.rearrange("p (h t) -> p h t", h=8); g3 = gt_.ap.rearrange("p (h t) -> p h t", h=8)
            S.dma("sp", q3, QTv[:, :, cs], reads=[dres], writes=[q])
            S.dma("act", g3, GTv[:, :, cs], reads=[dres], writes=[gt_])
            for g in range(2):
                q2 = q.ap[:, g * 512:(g + 1) * 512]
                bO = 4 + gcnt % 2
                bSum = 6 + gcnt % 2
                gcnt += 1
                pO = self.psum(bO)
                ac = accs.next()
                LA = 3
                pTs = {}
                for kk_ in range(nkt + LA):
                    if kk_ < nkt:
                        kt = kk_
                        bS = kt % 4
                        pS = self.psum(bS)
                        S.op("pe", lambda e, pS=pS, g=g, kt=kt, q2=q2: e.matmul(pS, lhsT=KT[g].ap[:, kt * 128:(kt + 1) * 128], rhs=q2, start=True, stop=True),
                             reads=[KT[g], q], writes=[self.ps_res[bS]])
                        pT = pTr.next()
                        pTs[kt] = pT
                        S.op("act", lambda e, pS=pS, pT=pT: e.activation(out=pT.ap, in_=pS, func=AF.Exp, scale=scale), reads=[self.ps_res[bS]], writes=[pT])
                    kt = kk_ - LA
                    if kt >= 0:
                        pT = pTs.pop(kt)
                        S.op("pe", lambda e, pO=pO, g=g, kt=kt, pT=pT: e.matmul(pO, lhsT=V3[:, kt, g * 128:(g + 1) * 128], rhs=pT.ap, start=(kt == 0), stop=(kt == nkt - 1)),
                             reads=[V, pT], writes=[self.ps_res[bO]])
                        eng, at = ("dve", ac["d"]) if kt % 2 == 0 else ("pool", ac["p"])
                        if kt < 2:
                            S.op(eng, lambda e, at=at, pT=pT: e.tensor_copy(out=at.ap, in_=pT.ap), reads=[pT], writes=[at])
                        else:
                            S.op(eng, lambda e, at=at, pT=pT: e.tensor_tensor(out=at.ap, in0=at.ap, in1=pT.ap, op=ALU.add), reads=[at, pT], writes=[at])
                S.op("dve", lambda e, ac=ac: e.tensor_tensor(out=ac["d"].ap, in0=ac["d"].ap, in1=ac["p"].ap, op=ALU.add), reads=[ac["d"], ac["p"]], writes=[ac["d"]])
                pSum = self.psum(bSum)
                S.op("pe", lambda e, pSum=pSum, ac=ac: e.matmul(pSum, lhsT=self.ones_row.ap, rhs=ac["d"].ap, start=True, stop=True),
                     reads=[self.ones_row, ac["d"]], writes=[self.ps_res[bSum]])
                S.op("dve", lambda e, pSum=pSum, ac=ac: e.reciprocal(out=ac["rs"].ap, in_=pSum), reads=[self.ps_res[bSum]], writes=[ac["rs"]])
                S.op("dve", lambda e, pO=pO, ac=ac: e.tensor_tensor(out=ac["o"].ap, in0=pO, in1=ac["rs"].ap, op=ALU.mult), reads=[self.ps_res[bO], ac["rs"]], writes=[ac["o"]])
                S.op("pool", lambda e, ac=ac, y=y, gt_=gt_, g=g: e.tensor_tensor(out=y.ap[:, g * 512:(g + 1) * 512], in0=ac["o"].ap, in1=gt_.ap[:, g * 512:(g + 1) * 512], op=ALU.mult),
                     reads=[ac["o"], gt_], writes=[y])
            S.dma("sp", Yv[:, :, cs], y.ap.rearrange("p (h t) -> p h t", h=8), reads=[y], writes=[self.Y_res])
        S.barrier(); A.top = base_mark
        wo = Tl(A.bf16(8 * 1024))
        wo3 = wo.ap.rearrange("p (k n) -> p k n", k=8)
        stage = Ring([Tl(A.f32(2048)) for _ in range(3)])
        self.load_w_bf16(W["att_w_out"], 1024, 1024, wo3, wo, stage_ring=stage)
        self.out_proj(Y, 1024, wo3, wo, T, self.gt_bc, x_src, self.xs, self.x_res)


_CACHE = {}


def get_program(T, TC, nlayers=4, dbg=False):
    key = (T, TC, nlayers, dbg)
    if key not in _CACHE:
        b = Builder(T, TC, nlayers, dbg)
        nc = b.build()
        _CACHE[key] = (nc, b)
    return _CACHE[key]


def kernel(**inputs):
    x = np.asarray(inputs["x"], dtype=np.float32)
    B, T, _ = x.shape
    TC = inputs["ctx"].shape[1]
    nc, b = get_program(T, TC)
    in_maps = []
    for i in range(B):
        m = {}
        for name in b.ins:
            if name in b.host_consts:
                m[name] = b.host_consts[name]
            elif name == "x":
                m[name] = np.ascontiguousarray(x[i])
            elif name == "c":
                m[name] = np.ascontiguousarray(np.asarray(inputs["c"], dtype=np.float32)[i])
            elif name == "ctx":
                m[name] = np.ascontiguousarray(np.asarray(inputs["ctx"], dtype=np.float32)[i])
            else:
                m[name] = np.ascontiguousarray(np.asarray(inputs[name], dtype=np.float32))
        in_maps.append(m)
    res = run_bass_kernel_spmd(nc, in_maps, core_ids=list(range(B)))
    return np.stack([np.asarray(r["out"]) for r in res.results], axis=0).astype(np.float32)
```

```python
import math
import numpy as np
import ml_dtypes
from contextlib import ExitStack
import concourse.bass as bass
import concourse.mybir as mybir
from concourse.bass_utils import run_bass_kernel_spmd

F32 = mybir.dt.float32
BF16 = mybir.dt.bfloat16
AF = mybir.ActivationFunctionType
ALU = mybir.AluOpType
AX = mybir.AxisListType

P = 128
D = 1024
KD = 8
EPS = 1e-6


class Res:
    __slots__ = ("lw", "rd")

    def __init__(self):
        self.lw = None
        self.rd = {}


class FreeRes(Res):
    __slots__ = ()


class Tl:
    __slots__ = ("ap", "res")

    def __init__(self, ap, res=None):
        self.ap = ap
        self.res = res if res is not None else Res()


class Ring:
    def __init__(self, tiles):
        self.t = tiles
        self.i = 0

    def next(self):
        t = self.t[self.i % len(self.t)]
        self.i += 1
        return t


class Sched:
    ENGS = ("pe", "act", "dve", "pool", "sp")
    NS = {"sp": 32, "act": 16}

    def __init__(self, nc, es):
        self.nc = nc
        self.ops = {e: [] for e in self.ENGS}
        self.cnt = {e: 0 for e in self.ENGS}
        self.known = {e: {} for e in self.ENGS}
        self.sems = {}
        for e in ("pe", "act", "dve", "pool"):
            self.sems[("c", e)] = es.enter_context(nc.semaphore("c_" + e))
        self.dq = {}
        for q, n in self.NS.items():
            for i in range(n):
                self.sems[("d", q, i)] = es.enter_context(nc.semaphore("d_%s_%d" % (q, i)))
            self.dq[q] = {"next": 0, "val": [0] * n}
        self.nops = 0

    def _collect(self, eng, reads, writes):
        deps = {}

        def add(tok, raw):
            if tok is None:
                return
            semkey, val, teng, seq = tok
            if semkey[0] == "c" and teng == eng:
                if eng == "pe" or not raw:
                    return
                if self.cnt[eng] - seq > 3:
                    return
            if deps.get(semkey, 0) < val:
                deps[semkey] = val

        for r in reads:
            add(r.lw, True)
        for w in writes:
            add(w.lw, False)
            for t in w.rd.values():
                add(t, False)
        return deps

    def _waits(self, eng, deps):
        out = []
        kn = self.known[eng]
        for semkey, val in deps.items():
            if kn.get(semkey, 0) >= val:
                continue
            kn[semkey] = val
            out.append((semkey, val))
        return out

    def _commit(self, tok, reads, writes):
        semkey = tok[0]
        for r in reads:
            old = r.rd.get(semkey)
            if old is None or old[1] < tok[1]:
                r.rd[semkey] = tok
        for w in writes:
            w.lw = tok
            w.rd = {}

    @staticmethod
    def _res(lst):
        return [r for r in (x.res if isinstance(x, Tl) else x for x in lst) if not isinstance(r, FreeRes)]

    def op(self, eng, emit, reads=(), writes=()):
        reads = self._res(reads)
        writes = self._res(writes)
        waits = self._waits(eng, self._collect(eng, reads, writes))
        self.cnt[eng] += 1
        semkey = ("c", eng)
        tok = (semkey, self.cnt[eng], eng, self.cnt[eng])
        self.ops[eng].append((waits, emit, semkey, 1))
        self._commit(tok, reads, writes)
        self.nops += 1

    def dma(self, q, out, in_, reads=(), writes=(), slow=False):
        reads = self._res(reads)
        writes = self._res(writes)
        deps = self._collect(q, reads, writes)
        st = self.dq[q]
        i = st["next"] % self.NS[q]
        st["next"] += 1
        semkey = ("d", q, i)
        prev = st["val"][i]
        if prev > 0 and deps.get(semkey, 0) < prev:
            deps[semkey] = prev
        waits = self._waits(q, deps)
        st["val"][i] = prev + 16
        tok = (semkey, prev + 16, q, 0)
        if slow:
            emit = lambda e: e.dma_start(out=out, in_=in_, allow_slow_non_contiguous=True)
        else:
            emit = lambda e: e.dma_start(out=out, in_=in_)
        self.ops[q].append((waits, emit, semkey, 16))
        self._commit(tok, reads, writes)
        self.nops += 1

    def barrier(self):
        deps = {}
        for e in ("pe", "act", "dve", "pool"):
            if self.cnt[e] > 0:
                deps[("c", e)] = self.cnt[e]
        for q, st in self.dq.items():
            for i, v in enumerate(st["val"]):
                if v > 0:
                    deps[("d", q, i)] = v
        for e in self.ENGS:
            d = {k: v for k, v in deps.items() if not (k[0] == "c" and k[1] == e)}
            waits = self._waits(e, d)
            if waits:
                self.ops[e].append((waits, None, None, 0))

    def emit(self):
        self.barrier()
        sems = self.sems
        ops = self.ops

        def run(name):
            def f(eng):
                for waits, emit, semkey, inc in ops[name]:
                    for sk, val in waits:
                        eng.wait_ge(sems[sk], val)
                    if emit is not None:
                        emit(eng).then_inc(sems[semkey], inc)
            return f

        with self.nc.Block() as block:
            block.tensor(run("pe"))
            block.scalar(run("act"))
            block.vector(run("dve"))
            block.gpsimd(run("pool"))
            block.sync(run("sp"))


class Arena:
    def __init__(self, ap, ncols):
        self.ap = ap
        self.n = ncols
        self.top = 0
        self.ptop = ncols

    def reset(self):
        self.top = 0

    def f32(self, cols):
        a = self.ap[:, self.top:self.top + cols]
        self.top += cols
        assert self.top <= self.ptop, "SBUF arena overflow %d > %d" % (self.top, self.ptop)
        return a

    def bf16(self, cols):
        c = (cols + 1) // 2
        return self.f32(c).bitcast(BF16)

    def pf32(self, cols):
        self.ptop -= cols
        assert self.top <= self.ptop
        return self.ap[:, self.ptop:self.ptop + cols]

    def pbf16(self, cols):
        c = (cols + 1) // 2
        return self.pf32(c).bitcast(BF16)


ARENA_COLS = 48 * 1024

LRU_W = 2048
RET_H, RET_DK, RET_DV = 4, 256, 512
ATT_HQ, ATT_G, ATT_D = 8, 2, 128
GRID_W = 64


class Builder:
    def __init__(self, T, TC, nlayers=4, dbg=False):
        self.T, self.TC, self.nlayers, self.dbg = T, TC, nlayers, dbg
        self.nc = bass.Bass("TRN2", target_bir_lowering=False)
        self.ins = {}
        self.host_consts = {}

    def inp(self, name, shape, dt=F32):
        t = self.nc.dram_tensor(name, list(shape), dt, kind="ExternalInput").ap()
        self.ins[name] = t
        return t

    def scratch(self, name, shape, dt=F32):
        return self.nc.dram_tensor(name, list(shape), dt, kind="Internal").ap()

    def const(self, name, arr):
        arr = np.ascontiguousarray(arr)
        dt = BF16 if arr.dtype == ml_dtypes.bfloat16 else F32
        self.host_consts[name] = arr
        return self.inp(name, arr.shape, dt)

    def psum(self, b0, nb=1, dt=F32):
        a = self.ps[:, b0 * 512:(b0 + nb) * 512]
        return a.bitcast(BF16) if dt == BF16 else a

    def build(self):
        nc = self.nc
        T, TC = self.T, self.TC
        I = self.inp
        x = I("x", [T, D]); c = I("c", [D]); ctx = I("ctx", [TC, D]); c_ctx = I("c_ctx", [D])
        W = {}
        for p in ("lru", "ret", "hy", "att"):
            W[p + "_mod_w"] = I(p + "_mod_w", [D, 3 * D]); W[p + "_mod_b"] = I(p + "_mod_b", [3 * D])
            W[p + "_norm_g"] = I(p + "_norm_g", [D])
        W["lru_w_in"] = I("lru_w_in", [D, 4096]); W["lru_conv_w"] = I("lru_conv_w", [4, 2048])
        W["lru_conv_b"] = I("lru_conv_b", [2048]); W["lru_w_r"] = I("lru_w_r", [2, 16, 128, 128])
        W["lru_b_r"] = I("lru_b_r", [2, 2048]); W["lru_w_i"] = I("lru_w_i", [2, 16, 128, 128])
        W["lru_b_i"] = I("lru_b_i", [2, 2048]); W["lru_lambda"] = I("lru_lambda", [2, 2048])
        W["lru_w_out"] = I("lru_w_out", [2048, D])
        W["ret_w_in"] = I("ret_w_in", [D, 6144]); W["ret_decay_logit"] = I("ret_decay_logit", [2, 4])
        W["ret_w_out"] = I("ret_w_out", [2048, D])
        W["hy_w_in"] = I("hy_w_in", [D, 4096]); W["hy_conv_w"] = I("hy_conv_w", [3, 3072])
        W["hy_conv_b"] = I("hy_conv_b", [3072]); W["hy_fw1"] = I("hy_fw1", [33, 64]); W["hy_fb1"] = I("hy_fb1", [64])
        W["hy_fw2"] = I("hy_fw2", [64, 64]); W["hy_fb2"] = I("hy_fb2", [64]); W["hy_fw3"] = I("hy_fw3", [64, 4096])
        W["hy_freq"] = I("hy_freq", [64]); W["hy_skip"] = I("hy_skip", [2, 1024]); W["hy_w_out"] = I("hy_w_out", [1024, D])
        W["att_w_in"] = I("att_w_in", [D, 2560]); W["att_q_norm_g"] = I("att_q_norm_g", [128])
        W["att_k_norm_g"] = I("att_k_norm_g", [128]); W["att_w_out"] = I("att_w_out", [1024, D])
        W["final_norm_g"] = I("final_norm_g", [D])
        self.W = W
        self.x_in, self.ctx_in, self.c_in, self.cctx_in = x, ctx, c, c_ctx
        ident_d = self.const("c_ident", np.eye(128, dtype=np.float32).astype(ml_dtypes.bfloat16))
        self.out = nc.dram_tensor("out", [T, D], F32, kind="ExternalOutput").ap()
        if self.dbg:
            self.ctx_out = nc.dram_tensor("ctx_out", [TC, D], F32, kind="ExternalOutput").ap()
        self.xs = self.scratch("xs", [T, D]); self.cs = self.scratch("cs", [TC, D])
        self.x_res = [Res() for _ in range(T // 128)]
        self.c_res = [Res() for _ in range(TC // 128)]

        with ExitStack() as es:
            self.S = S = Sched(nc, es)
            arena_t = es.enter_context(nc.sbuf_tensor("arena", [P, ARENA_COLS], F32))
            self.A = A = Arena(arena_t, ARENA_COLS)
            self.ps = es.enter_context(nc.psum_tensor("ps", [P, 4096], F32))
            self.ps_res = [Res() for _ in range(8)]
            self.ident = Tl(A.pbf16(128))
            S.dma("sp", self.ident.ap, ident_d, writes=[self.ident])
            self.ones_row = Tl(A.pf32(128))
            S.op("pool", lambda e: e.memset(self.ones_row.ap, 1.0), writes=[self.ones_row])
            self.ones_bf = Tl(A.pbf16(128))
            self.eps_col = Tl(A.pf32(1))
            S.op("pool", lambda e: e.memset(self.eps_col.ap, EPS), writes=[self.eps_col])
            S.op("pool", lambda e: e.memset(self.ones_bf.ap, 1.0), writes=[self.ones_bf])
            self.modcol = Tl(A.pf32(48))
            self.gs = Tl(A.pf32(16)); self.sh = Tl(A.pf32(16))
            self.gt_bc = Tl(A.pf32(1024)); self.gtc_bc = Tl(A.pf32(1024))
            self.scol = Tl(A.pf32(16))
            self.prep_cond()

            x_src, c_src = self.x_in, self.ctx_in
            layers = [("lru", self.mix_lru), ("ret", self.mix_ret), ("hy", self.mix_hy), ("att", self.mix_att)]
            for li in range(self.nlayers):
                pfx, fn = layers[li]
                self.layer_idx = li
                self.need_ctx = li < 3
                S.barrier(); A.reset()
                self.modulation(pfx)
                S.barrier(); A.reset()
                fn(x_src, c_src)
                x_src, c_src = self.xs, self.cs
            S.barrier(); A.reset()
            self.final_norm(x_src)
            if self.dbg:
                S.barrier(); A.reset()
                t = Tl(A.f32(1024))
                for i in range(TC // 128):
                    S.dma("sp", t.ap, c_src[i * 128:(i + 1) * 128, :], reads=[self.c_res[i]], writes=[t])
                    S.dma("sp", self.ctx_out[i * 128:(i + 1) * 128, :], t.ap, reads=[t])
            S.emit()
        return nc

    def prep_cond(self):
        S, A = self.S, self.A
        raw = Tl(A.f32(16))
        r3 = raw.ap.rearrange("p (k n) -> p k n", n=2)
        S.dma("sp", r3[:, :, 0], self.c_in.rearrange("(k p) -> p k", p=128), writes=[raw], slow=True)
        S.dma("sp", r3[:, :, 1], self.cctx_in.rearrange("(k p) -> p k", p=128), writes=[raw], slow=True)
        S.op("act", lambda e: e.activation(out=self.scol.ap, in_=raw.ap, func=AF.Silu), reads=[raw], writes=[self.scol])

    def modulation(self, pfx):
        S, A, W = self.S, self.A, self.W
        mw = Tl(A.f32(8 * 3072))
        mw3 = mw.ap.rearrange("p (k n) -> p k n", k=8)
        src = W[pfx + "_mod_w"].rearrange("(k p) n -> p k n", p=128)
        for k in range(8):
            S.dma("sp" if k % 2 == 0 else "act", mw3[:, k, :], src[:, k, :], writes=[mw])
        mb = Tl(A.f32(24))
        S.dma("sp", mb.ap, W[pfx + "_mod_b"].rearrange("(j p) -> p j", p=128), writes=[mb], slow=True)
        g = Tl(A.f32(8))
        S.dma("sp", g.ap, W[pfx + "_norm_g"].rearrange("(j p) -> p j", p=128), writes=[g], slow=True)
        mbrow = Tl(A.f32(1024))
        S.dma("sp", mbrow.ap[0:1, :], W[pfx + "_mod_b"][2048:3072].rearrange("(o n) -> o n", o=1), writes=[mbrow])
        sc3 = self.scol.ap.rearrange("p (k n) -> p k n", n=2)
        mc3 = self.modcol.ap.rearrange("p (j n) -> p j n", n=2)
        for j in range(24):
            b = j % 4
            pt = self.psum(b)
            for k in range(8):
                S.op("pe", lambda e, k=k, j=j, pt=pt: e.matmul(pt[:, 0:2], lhsT=mw3[:, k, j * 128:(j + 1) * 128], rhs=sc3[:, k, :],
                                                            start=(k == 0), stop=(k == 7)),
                     reads=[mw, self.scol], writes=[self.ps_res[b]])
            S.op("dve", lambda e, j=j, pt=pt: e.tensor_scalar(out=mc3[:, j, :], in0=pt[:, 0:2], scalar1=mb.ap[:, j:j + 1], scalar2=None,
                                                           op0=ALU.add), reads=[self.ps_res[b], mb], writes=[self.modcol])
        gs3 = self.gs.ap.rearrange("p (k n) -> p k n", n=2)
        sh3 = self.sh.ap.rearrange("p (k n) -> p k n", n=2)
        for n in range(2):
            S.op("dve", lambda e, n=n: e.tensor_scalar(out=gs3[:, :, n], in0=mc3[:, 8:16, n], scalar1=1.0, scalar2=None, op0=ALU.add),
                 reads=[self.modcol], writes=[self.gs])
            S.op("dve", lambda e, n=n: e.tensor_tensor(out=gs3[:, :, n], in0=gs3[:, :, n], in1=g.ap, op=ALU.mult),
                 reads=[self.gs, g], writes=[self.gs])
            S.op("dve", lambda e, n=n: e.tensor_copy(out=sh3[:, :, n], in_=mc3[:, 0:8, n]), reads=[self.modcol], writes=[self.sh])
        for n, dst in ((0, self.gt_bc), (1, self.gtc_bc)):
            row = Tl(A.f32(1024))
            for hh in range(2):
                b = 4 + hh
                pt = self.psum(b)
                for k in range(8):
                    S.op("pe", lambda e, k=k, hh=hh, n=n, pt=pt: e.matmul(pt[0:1, :], lhsT=sc3[:, k, n:n + 1],
                                                                       rhs=mw3[:, k, 2048 + hh * 512:2048 + (hh + 1) * 512],
                                                                       start=(k == 0), stop=(k == 7)),
                         reads=[mw, self.scol], writes=[self.ps_res[b]])
                S.op("dve", lambda e, hh=hh, pt=pt, row=row: e.tensor_tensor(out=row.ap[0:1, hh * 512:(hh + 1) * 512], in0=pt[0:1, :],
                                                                           in1=mbrow.ap[0:1, hh * 512:(hh + 1) * 512], op=ALU.add),
                     reads=[self.ps_res[b], mbrow], writes=[row])
            for hh in range(2):
                b = 6 + hh
                pt = self.psum(b)
                S.op("pe", lambda e, hh=hh, pt=pt, row=row: e.matmul(pt, lhsT=self.ones_row.ap[0:1, :], rhs=row.ap[0:1, hh * 512:(hh + 1) * 512],
                                                                   start=True, stop=True),
                     reads=[row, self.ones_row], writes=[self.ps_res[b]])
                S.op("act", lambda e, hh=hh, pt=pt, dst=dst: e.activation(out=dst.ap[:, hh * 512:(hh + 1) * 512], in_=pt, func=AF.Copy),
                     reads=[self.ps_res[b]], writes=[dst])

    def load_w_bf16(self, w_dram, K, N, dst3, dst_tl, col0=0, ncols=None, stage_ring=None):
        S = self.S
        ncols = N if ncols is None else ncols
        src = w_dram.rearrange("(k p) n -> p k n", p=128)
        CH = 2048
        i = 0
        for k in range(K // 128):
            for c0 in range(0, ncols, CH):
                cw = min(CH, ncols - c0)
                st = stage_ring.next()
                S.dma("sp" if i % 2 == 0 else "act", st.ap[:, 0:cw], src[:, k, col0 + c0:col0 + c0 + cw], writes=[st])
                eng = ("act", "dve", "pool")[i % 3]
                if eng == "act":
                    S.op("act", lambda e, st=st, k=k, c0=c0, cw=cw: e.activation(out=dst3[:, k, c0:c0 + cw], in_=st.ap[:, 0:cw], func=AF.Copy),
                         reads=[st], writes=[dst_tl])
                else:
                    S.op(eng, lambda e, st=st, k=k, c0=c0, cw=cw: e.tensor_copy(out=dst3[:, k, c0:c0 + cw], in_=st.ap[:, 0:cw]),
                         reads=[st], writes=[dst_tl])
                i += 1

    def norm_tiles(self, src, src_res, Tn, which, xring, hring, small, TT):
        S = self.S
        gs3 = self.gs.ap.rearrange("p (k n) -> p k n", n=2)
        sh3 = self.sh.ap.rearrange("p (k n) -> p k n", n=2)
        for tt in range(Tn // TT):
            hT = hring.next()
            h3 = hT.ap.rearrange("p (k t) -> p k t", k=8)
            for s in range(TT // 128):
                ti = (tt * TT) // 128 + s
                xt = xring.next()
                xf, xb, ss = xt["xf"], xt["xb"], xt["ss"]
                S.dma("sp", xf.ap, src[ti * 128:(ti + 1) * 128, :], reads=[src_res[ti]], writes=[xf])
                S.op("act", lambda e, xf=xf, xb=xb, ss=ss: e.activation(out=xb.ap, in_=xf.ap, func=AF.Square, accum_out=ss.ap),
                     reads=[xf], writes=[xb, ss])
                S.op("dve", lambda e, ss=ss: e.tensor_scalar(out=ss.ap, in0=ss.ap, scalar1=1.0 / D, scalar2=EPS, op0=ALU.mult, op1=ALU.add),
                     reads=[ss], writes=[ss])
                S.op("act", lambda e, ss=ss: e.activation(out=ss.ap, in_=ss.ap, func=AF.Sqrt), reads=[ss], writes=[ss])
                S.op("dve", lambda e, ss=ss: e.reciprocal(out=ss.ap, in_=ss.ap), reads=[ss], writes=[ss])
                S.op("act", lambda e, xf=xf, xb=xb, ss=ss: e.activation(out=xb.ap, in_=xf.ap, func=AF.Copy, scale=ss.ap),
                     reads=[xf, ss], writes=[xb])
                b = self.tp_bank
                self.tp_bank = 6 + (self.tp_bank - 6 + 1) % 2
                pb = self.psum(b, 1, BF16)
                for j in range(8):
                    S.op("pe", lambda e, j=j, pb=pb, xb=xb: e.transpose(out=pb[:, j * 128:(j + 1) * 128], in_=xb.ap[:, j * 128:(j + 1) * 128],
                                                                        identity=self.ident.ap),
                         reads=[xb, self.ident], writes=[self.ps_res[b]])
                for j in range(8):
                    eng = "dve" if j % 2 == 0 else "pool"
                    eng = "dve"
                    S.op(eng, lambda e, j=j, pb=pb, s=s, h3=h3: e.tensor_scalar(out=h3[:, j, s * 128:(s + 1) * 128], in0=pb[:, j * 128:(j + 1) * 128],
                                                                         scalar1=gs3[:, j, which:which + 1], scalar2=sh3[:, j, which:which + 1],
                                                                         op0=ALU.mult, op1=ALU.add),
                         reads=[self.ps_res[b], self.gs, self.sh], writes=[hT])
            yield tt, hT, h3

    def make_norm_rings(self, TT, nh=2, nx=3):
        A = self.A
        xring = Ring([{"xf": Tl(A.f32(1024)), "xb": Tl(A.bf16(1024)), "ss": Tl(A.f32(1))} for _ in range(nx)])
        hring = Ring([Tl(A.bf16(8 * TT)) for _ in range(nh)])
        self.tp_bank = 6
        return xring, hring

    def out_proj(self, Y, C, w_bf3, w_tl, Tn, gt, src, dst, res_list, TT=512):
        S, A = self.S, self.A
        kc = C // 128
        TT = min(TT, Tn)
        yring = Ring([Tl(A.bf16(kc * TT)) for _ in range(2)])
        xring = Ring([Tl(A.f32(1024)) for _ in range(3)])
        Yv = Y.rearrange("(k p) t -> p k t", p=128)
        pbi = 0
        for tt in range(Tn // TT):
            yt = yring.next()
            y3 = yt.ap.rearrange("p (k t) -> p k t", k=kc)
            S.dma("sp", y3, Yv[:, :, tt * TT:(tt + 1) * TT], writes=[yt], reads=[self.Y_res])
            for s in range(TT // 128):
                ti = (tt * TT) // 128 + s
                xt = xring.next()
                S.dma("act", xt.ap, src[ti * 128:(ti + 1) * 128, :], reads=[res_list[ti]], writes=[xt])
                b0 = (pbi % 3) * 2
                pbi += 1
                for hh in range(2):
                    pt = self.psum(b0 + hh)
                    for k in range(kc):
                        S.op("pe", lambda e, k=k, hh=hh, pt=pt, s=s, y3=y3: e.matmul(pt, lhsT=y3[:, k, s * 128:(s + 1) * 128],
                                                                                 rhs=w_bf3[:, k, hh * 512:(hh + 1) * 512],
                                                                                 start=(k == 0), stop=(k == kc - 1)),
                             reads=[yt, w_tl], writes=[self.ps_res[b0 + hh]])
                pt2 = self.psum(b0, 2)
                S.op("dve", lambda e, pt2=pt2: e.tensor_tensor(out=pt2, in0=pt2, in1=gt.ap, op=ALU.mult),
                     reads=[self.ps_res[b0], self.ps_res[b0 + 1], gt], writes=[self.ps_res[b0], self.ps_res[b0 + 1]])
                S.op("dve", lambda e, pt2=pt2, xt=xt: e.tensor_tensor(out=xt.ap, in0=pt2, in1=xt.ap, op=ALU.add),
                     reads=[self.ps_res[b0], self.ps_res[b0 + 1], xt], writes=[xt])
                S.dma("sp", dst[ti * 128:(ti + 1) * 128, :], xt.ap, reads=[xt], writes=[res_list[ti]])

    def final_norm(self, src):
        S, A, T = self.S, self.A, self.T
        g = Tl(A.f32(1024))
        S.dma("sp", g.ap, self.W["final_norm_g"].partition_broadcast(128), writes=[g])
        ring = Ring([{"xf": Tl(A.f32(1024)), "sq": Tl(A.f32(1024)), "ss": Tl(A.f32(1))} for _ in range(3)])
        ores = Res()
        for ti in range(T // 128):
            r = ring.next()
            xf, sq, ss = r["xf"], r["sq"], r["ss"]
            S.dma("sp", xf.ap, src[ti * 128:(ti + 1) * 128, :], reads=[self.x_res[ti]], writes=[xf])
            S.op("act", lambda e, xf=xf, sq=sq, ss=ss: e.activation(out=sq.ap, in_=xf.ap, func=AF.Square, accum_out=ss.ap),
                 reads=[xf], writes=[sq, ss])
            S.op("dve", lambda e, ss=ss: e.tensor_scalar(out=ss.ap, in0=ss.ap, scalar1=1.0 / D, scalar2=EPS, op0=ALU.mult, op1=ALU.add),
                 reads=[ss], writes=[ss])
            S.op("act", lambda e, ss=ss: e.activation(out=ss.ap, in_=ss.ap, func=AF.Sqrt), reads=[ss], writes=[ss])
            S.op("dve", lambda e, ss=ss: e.reciprocal(out=ss.ap, in_=ss.ap), reads=[ss], writes=[ss])
            S.op("dve", lambda e, xf=xf, sq=sq, ss=ss: e.scalar_tensor_tensor(out=sq.ap, in0=xf.ap, scalar=ss.ap, in1=g.ap,
                                                                             op0=ALU.mult, op1=ALU.mult),
                 reads=[xf, ss, g], writes=[sq])
            S.dma("act", self.out[ti * 128:(ti + 1) * 128, :], sq.ap, reads=[sq], writes=[ores])

    def proj_to_dram(self, src, src_res, Tn, which, w3, w_tl, nchunks, U, U_res, TT):
        S, A = self.S, self.A
        xring, hring = self.make_norm_rings(TT)
        stg = Ring([Tl(A.f32(TT)) for _ in range(4)])
        bi = 0
        for tt, hT, h3 in self.norm_tiles(src, src_res, Tn, which, xring, hring, None, TT):
            for oc in range(nchunks):
                b = bi % 6
                bi += 1
                pt = self.psum(b)[:, 0:TT]
                for k in range(8):
                    S.op("pe", lambda e, k=k, oc=oc, pt=pt, h3=h3: e.matmul(pt, lhsT=w3[:, k, oc * 128:(oc + 1) * 128], rhs=h3[:, k, :],
                                                                         start=(k == 0), stop=(k == 7)),
                         reads=[hT, w_tl], writes=[self.ps_res[b]])
                st = stg.next()
                if oc % 2 == 0:
                    S.op("act", lambda e, pt=pt, st=st: e.activation(out=st.ap, in_=pt, func=AF.Copy), reads=[self.ps_res[b]], writes=[st])
                else:
                    S.op("dve", lambda e, pt=pt, st=st: e.tensor_copy(out=st.ap, in_=pt), reads=[self.ps_res[b]], writes=[st])
                S.dma("sp" if oc % 2 == 0 else "act", U[oc * 128:(oc + 1) * 128, tt * TT:(tt + 1) * TT], st.ap, reads=[st], writes=[U_res])

    def mix_lru(self, x_src, c_src):
        S, A, W, T, TC = self.S, self.A, self.W, self.T, self.TC
        U = self.scratch("lru_U", [4096, T]); UC = self.scratch("lru_UC", [4096, TC])
        Y = self.scratch("lru_Y", [2048, T], BF16); YC = self.scratch("lru_YC", [2048, TC], BF16)
        U_res, UC_res = FreeRes(), FreeRes()
        self.Y_res = FreeRes()
        w_tl = Tl(A.bf16(8 * 4096))
        w3 = w_tl.ap.rearrange("p (k n) -> p k n", k=8)
        stage = Ring([Tl(A.f32(2048)) for _ in range(3)])
        self.load_w_bf16(W["lru_w_in"], 1024, 4096, w3, w_tl, stage_ring=stage)
        mark = A.top
        self.proj_to_dram(c_src, self.c_res, TC, 1, w3, w_tl, 32, UC, UC_res, min(512, TC))
        A.top = mark
        S.barrier()
        self.proj_to_dram(x_src, self.x_res, T, 0, w3, w_tl, 32, U, U_res, min(512, T))
        S.barrier(); A.reset()
        def col(src_ap, n, pattern="(n p) -> p n"):
            t = Tl(A.f32(n))
            S.dma("sp", t.ap, src_ap.rearrange(pattern, p=128), writes=[t], slow=True)
            return t
        cw = [col(W["lru_conv_w"][k], 16) for k in range(4)]
        cb = col(W["lru_conv_b"], 16)
        br = [col(W["lru_b_r"][d], 16) for d in range(2)]
        bi_ = [col(W["lru_b_i"][d], 16) for d in range(2)]
        lam = [col(W["lru_lambda"][d], 16) for d in range(2)]
        cdec, cdec2 = [], []
        for d in range(2):
            t = Tl(A.f32(16)); t2 = Tl(A.f32(16))
            S.op("act", lambda e, t=t, d=d: e.activation(out=t.ap, in_=lam[d].ap, func=AF.Exp, scale=-1.0), reads=[lam[d]], writes=[t])
            S.op("act", lambda e, t=t: e.activation(out=t.ap, in_=t.ap, func=AF.Ln, bias=1.0), reads=[t], writes=[t])
            S.op("dve", lambda e, t=t, t2=t2: e.tensor_scalar(out=t2.ap, in0=t.ap, scalar1=-16.0, scalar2=None, op0=ALU.mult), reads=[t], writes=[t2])
            S.op("dve", lambda e, t=t: e.tensor_scalar(out=t.ap, in0=t.ap, scalar1=-8.0, scalar2=None, op0=ALU.mult), reads=[t], writes=[t])
            cdec.append(t); cdec2.append(t2)
        gw = Tl(A.bf16(2 * 2 * 16 * 128))
        gw5 = gw.ap.rearrange("p (d g n k) -> p d g n k", d=2, g=2, n=16)
        gst = Ring([Tl(A.f32(16 * 128)) for _ in range(2)])
        for d in range(2):
            for g, nm in enumerate(("lru_w_r", "lru_w_i")):
                st = gst.next()
                st3 = st.ap.rearrange("p (n k) -> p n k", n=16)
                S.dma("sp", st3, W[nm][d].rearrange("n j k -> j n k"), writes=[st])
                S.op("dve", lambda e, st3=st3, d=d, g=g: e.tensor_copy(out=gw5[:, d, g, :, :], in_=st3), reads=[st], writes=[gw])
        TM = max(T, TC)
        ubuf = Tl(A.f32(TM + 3))
        xl = Tl(A.f32(TM)); xlb = Tl(A.bf16(TM))
        TTS = min(1024, TM)
        tring = Ring([{k: Tl(A.f32(TTS)) for k in ("r", "i", "a", "hb", "g")} for _ in range(2)])
        yring = Ring([Tl(A.bf16(TTS)) for _ in range(2)])
        fin = [Tl(A.f32(1)), Tl(A.f32(1))]
        pb = [0]

        def seq(n, Useq, Useq_res, Tn, init, Yseq):
            tts = min(TTS, Tn)
            nt = Tn // tts
            S.op("pool", lambda e: e.memset(ubuf.ap[:, 0:2], 0.0), writes=[ubuf])
            S.op("pool", lambda e: e.memset(ubuf.ap[:, Tn + 2:Tn + 3], 0.0), writes=[ubuf])
            S.dma("sp", ubuf.ap[:, 2:Tn + 2], Useq[n * 128:(n + 1) * 128, :], reads=[Useq_res], writes=[ubuf])
            S.op("dve", lambda e: e.tensor_scalar(out=xl.ap[:, 0:Tn], in0=ubuf.ap[:, 0:Tn], scalar1=cw[0].ap[:, n:n + 1],
                                                  scalar2=cb.ap[:, n:n + 1], op0=ALU.mult, op1=ALU.add),
                 reads=[ubuf, cw[0], cb], writes=[xl])
            for k in range(1, 4):
                S.op("dve", lambda e, k=k: e.scalar_tensor_tensor(out=xl.ap[:, 0:Tn], in0=ubuf.ap[:, k:k + Tn], scalar=cw[k].ap[:, n:n + 1],
                                                                  in1=xl.ap[:, 0:Tn], op0=ALU.mult, op1=ALU.add),
                     reads=[ubuf, cw[k], xl], writes=[xl])
            S.op("act", lambda e: e.activation(out=xlb.ap[:, 0:Tn], in_=xl.ap[:, 0:Tn], func=AF.Copy), reads=[xl], writes=[xlb])
            hf = ubuf
            for d in range(2):
                order = list(range(nt)) if d == 0 else list(range(nt - 1, -1, -1))
                state = init[d]
                for tt in order:
                    sl = slice(tt * tts, (tt + 1) * tts)
                    tb = tring.next()
                    r, i_, a, hb, g = tb["r"], tb["i"], tb["a"], tb["hb"], tb["g"]
                    pb[0] += 1
                    nb = (tts + 511) // 512
                    bank0 = (pb[0] % 2) * 4
                    for gi in range(2):
                        for h in range(nb):
                            c0 = h * 512
                            cwid = min(512, tts - c0)
                            bank = bank0 + gi * 2 + h
                            pt = self.psum(bank)[:, 0:cwid]
                            S.op("pe", lambda e, pt=pt, gi=gi, c0=c0, cwid=cwid, d=d, tt=tt: e.matmul(
                                pt, lhsT=gw5[:, d, gi, n, :], rhs=xlb.ap[:, tt * tts + c0:tt * tts + c0 + cwid], start=True, stop=True),
                                 reads=[gw, xlb], writes=[self.ps_res[bank]])
                            dst = r if gi == 0 else i_
                            bias = (br if gi == 0 else bi_)[d]
                            S.op("act", lambda e, pt=pt, dst=dst, bias=bias, c0=c0, cwid=cwid: e.activation(
                                out=dst.ap[:, c0:c0 + cwid], in_=pt, func=AF.Sigmoid, bias=bias.ap[:, n:n + 1]),
                                 reads=[self.ps_res[bank], bias], writes=[dst])
                    S.op("act", lambda e, r=r, a=a, d=d: e.activation(out=a.ap[:, 0:tts], in_=r.ap[:, 0:tts], func=AF.Exp, scale=cdec[d].ap[:, n:n + 1]),
                         reads=[r, cdec[d]], writes=[a])
                    S.op("pool", lambda e, r=r, a=a: e.tensor_tensor(out=r.ap[:, 0:tts], in0=a.ap[:, 0:tts], in1=a.ap[:, 0:tts], op=ALU.mult),
                         reads=[a], writes=[r])
                    S.op("act", lambda e, r=r: e.activation(out=r.ap[:, 0:tts], in_=r.ap[:, 0:tts], func=AF.Sqrt, scale=-1.0, bias=1.0),
                         reads=[r], writes=[r])
                    S.op("pool", lambda e, i_=i_, sl=sl: e.tensor_tensor(out=i_.ap[:, 0:tts], in0=i_.ap[:, 0:tts], in1=xl.ap[:, sl], op=ALU.mult),
                         reads=[i_, xl], writes=[i_])
                    S.op("dve", lambda e, i_=i_, r=r: e.tensor_tensor(out=i_.ap[:, 0:tts], in0=i_.ap[:, 0:tts], in1=r.ap[:, 0:tts], op=ALU.mult),
                         reads=[i_, r], writes=[i_])
                    ini = state if isinstance(state, float) else state.ap
                    rds = [a, i_] + ([] if isinstance(state, float) else [state])
                    if d == 0:
                        S.op("dve", lambda e, a=a, i_=i_, ini=ini, sl=sl: e.tensor_tensor_scan(out=hf.ap[:, sl], data0=a.ap[:, 0:tts], data1=i_.ap[:, 0:tts],
                                                                                         initial=ini, op0=ALU.mult, op1=ALU.add),
                             reads=rds, writes=[hf])
                        state = Tl(hf.ap[:, (tt + 1) * tts - 1:(tt + 1) * tts], hf.res)
                    else:
                        S.op("dve", lambda e, a=a, i_=i_, ini=ini, hb=hb: e.tensor_tensor_scan(out=hb.ap[:, 0:tts][:, ::-1], data0=a.ap[:, 0:tts][:, ::-1],
                                                                                         data1=i_.ap[:, 0:tts][:, ::-1], initial=ini,
                                                                                         op0=ALU.mult, op1=ALU.add),
                             reads=rds, writes=[hb])
                        state = Tl(hb.ap[:, 0:1], hb.res)
                        if tt == 0:
                            S.op("act", lambda e, hb=hb: e.activation(out=fin[1].ap, in_=hb.ap[:, 0:1], func=AF.Copy), reads=[hb], writes=[fin[1]])
                        if Yseq is not None:
                            S.dma("act", g.ap[:, 0:tts], Useq[2048 + n * 128:2048 + (n + 1) * 128, sl], reads=[Useq_res], writes=[g])
                            S.op("act", lambda e, g=g: e.activation(out=g.ap[:, 0:tts], in_=g.ap[:, 0:tts], func=AF.Silu), reads=[g], writes=[g])
                            S.op("pool", lambda e, hb=hb, sl=sl, a=a: e.tensor_tensor(out=a.ap[:, 0:tts], in0=hb.ap[:, 0:tts], in1=hf.ap[:, sl], op=ALU.add),
                                 reads=[hb, hf], writes=[a])
                            yt = yring.next()
                            S.op("dve", lambda e, a=a, g=g, yt=yt: e.tensor_tensor(out=yt.ap[:, 0:tts], in0=a.ap[:, 0:tts], in1=g.ap[:, 0:tts], op=ALU.mult),
                                 reads=[a, g], writes=[yt])
                            S.dma("sp", Yseq[n * 128:(n + 1) * 128, sl], yt.ap[:, 0:tts], reads=[yt], writes=[self.Y_res])
                if d == 0:
                    S.op("act", lambda e: e.activation(out=fin[0].ap, in_=hf.ap[:, Tn - 1:Tn], func=AF.Copy), reads=[hf], writes=[fin[0]])

        for n in range(16):
            seq(n, UC, UC_res, TC, [0.0, 0.0], YC if self.need_ctx else None)
            seq(n, U, U_res, T, [fin[0], fin[1]], Y)
        S.barrier(); A.reset()
        wo = Tl(A.bf16(16 * 1024))
        wo3 = wo.ap.rearrange("p (k n) -> p k n", k=16)
        stage = Ring([Tl(A.f32(2048)) for _ in range(3)])
        self.load_w_bf16(W["lru_w_out"], 2048, 1024, wo3, wo, stage_ring=stage)
        self.out_proj(Y, 2048, wo3, wo, T, self.gt_bc, x_src, self.xs, self.x_res)
        if self.need_ctx:
            self.out_proj(YC, 2048, wo3, wo, TC, self.gtc_bc, c_src, self.cs, self.c_res)

    def mix_ret(self, x_src, c_src):
        S, A, W, T, TC = self.S, self.A, self.W, self.T, self.TC
        C = 128
        f32 = np.float32
        p_ = np.arange(128, dtype=f32)
        cst = np.zeros((128, 4 + 4 * 128), f32)
        cst[:, 0] = C - 1 - p_; cst[:, 1] = p_; cst[:, 2] = C
        ii = np.arange(128, dtype=f32)[None, :]; jj = np.arange(128, dtype=f32)[:, None]
        cst[:, 4:132] = ii + 1.0
        cst[:, 132:260] = C - ii
        cst[:, 260:388] = np.maximum(ii - jj, 0.0)
        cst[:, 388:516] = np.maximum(jj - ii, 0.0)
        cst_d = self.const("c_ret_cst", cst)
        inv = (f32(10000.0) ** (-np.arange(128, dtype=f32) / f32(128))).astype(f32)
        ang = (np.arange(T, dtype=f32)[:, None] * inv[None, :]).astype(f32)
        cs_, sn_ = np.cos(ang.astype(np.float64)).astype(f32), np.sin(ang.astype(np.float64)).astype(f32)
        tabs_x = dict(cF=self.const("c_ret_cF", cs_.T), sF=self.const("c_ret_sF", sn_.T),
                      cT=self.const("c_ret_cT", np.tile(cs_, (1, 2))), sT=self.const("c_ret_sT", np.tile(sn_, (1, 2))))
        tabs_c = dict(cF=self.const("c_ret_cFc", np.ones((128, TC), f32)), sF=self.const("c_ret_sFc", np.zeros((128, TC), f32)),
                      cT=self.const("c_ret_cTc", np.ones((TC, 256), f32)), sT=self.const("c_ret_sTc", np.zeros((TC, 256), f32)))

        def mk(T_, sfx):
            d = {}
            for nm in ("Q", "QF", "QB", "KT"):
                d[nm] = self.scratch("ret_%s%s" % (nm, sfx), [1024, T_], BF16)
            for nm in ("KF", "KB"):
                d[nm] = self.scratch("ret_%s%s" % (nm, sfx), [T_, 1024], BF16)
            d["V"] = self.scratch("ret_V" + sfx, [T_, 2048], BF16); d["G"] = self.scratch("ret_G" + sfx, [T_, 2048], BF16)
            d["OB"] = self.scratch("ret_OB" + sfx, [T_, 2048], F32); d["Y"] = self.scratch("ret_Y" + sfx, [2048, T_], BF16)
            d["res"] = FreeRes()
            return d
        DX, DC = mk(T, "x"), mk(TC, "c")
        self.Y_res = FreeRes()

        cstt = Tl(A.f32(516))
        S.dma("sp", cstt.ap, cst_d, writes=[cstt])
        lg = Tl(A.f32(8))
        S.dma("sp", lg.ap, W["ret_decay_logit"].rearrange("d h -> (d h)").partition_broadcast(128), writes=[lg])
        S.op("act", lambda e: e.activation(out=lg.ap, in_=lg.ap, func=AF.Exp, scale=-1.0), reads=[lg], writes=[lg])
        S.op("act", lambda e: e.activation(out=lg.ap, in_=lg.ap, func=AF.Ln, bias=1.0), reads=[lg], writes=[lg])
        S.op("dve", lambda e: e.tensor_scalar(out=lg.ap, in0=lg.ap, scalar1=-1.0, scalar2=None, op0=ALU.mult), reads=[lg], writes=[lg])
        kd = Tl(A.f32(8)); blk = Tl(A.f32(8))
        for d in range(2):
            for h in range(4):
                ix = d * 4 + h
                S.op("act", lambda e, d=d, ix=ix: e.activation(out=kd.ap[:, ix:ix + 1], in_=cstt.ap[:, d:d + 1], func=AF.Exp, scale=lg.ap[:, ix:ix + 1]),
                     reads=[cstt, lg], writes=[kd])
                S.op("act", lambda e, ix=ix: e.activation(out=blk.ap[:, ix:ix + 1], in_=cstt.ap[:, 2:3], func=AF.Exp, scale=lg.ap[:, ix:ix + 1]),
                     reads=[cstt, lg], writes=[blk])
        S.op("dve", lambda e: e.tensor_scalar(out=kd.ap, in0=kd.ap, scalar1=1.0 / 16.0, scalar2=None, op0=ALU.mult), reads=[kd], writes=[kd])
        onesf = Tl(A.f32(256))
        S.op("pool", lambda e: e.memset(onesf.ap, 1.0), writes=[onesf])
        dect = [Tl(A.f32(1024)), Tl(A.f32(1024))]
        qdt = [Tl(A.f32(4 * 512)), Tl(A.f32(4 * 512))]
        maskT = Tl(A.f32(4 * 128))
        mtmp = Tl(A.f32(128))
        for d in range(2):
            for h in range(4):
                ix = d * 4 + h
                S.op("dve", lambda e, d=d, h=h, ix=ix: e.tensor_scalar(out=dect[d].ap[:, h * 256:(h + 1) * 256], in0=onesf.ap, scalar1=kd.ap[:, ix:ix + 1],
                                                                   scalar2=None, op0=ALU.mult), reads=[onesf, kd], writes=[dect[d]])
                for r in range(4):
                    S.op("act", lambda e, d=d, h=h, ix=ix, r=r: e.activation(out=qdt[d].ap[:, h * 512 + r * 128:h * 512 + (r + 1) * 128],
                                                                          in_=cstt.ap[:, 4 + d * 128:4 + (d + 1) * 128], func=AF.Exp,
                                                                          scale=lg.ap[:, ix:ix + 1]), reads=[cstt, lg], writes=[qdt[d]])
        for h in range(4):
            S.op("act", lambda e, h=h: e.activation(out=maskT.ap[:, h * 128:(h + 1) * 128], in_=cstt.ap[:, 260:388], func=AF.Exp, scale=lg.ap[:, h:h + 1]),
                 reads=[cstt, lg], writes=[maskT])
            S.op("act", lambda e, h=h: e.activation(out=mtmp.ap, in_=cstt.ap[:, 388:516], func=AF.Exp, scale=lg.ap[:, 4 + h:5 + h]),
                 reads=[cstt, lg], writes=[mtmp])
            S.op("dve", lambda e, h=h: e.scalar_tensor_tensor(out=maskT.ap[:, h * 128:(h + 1) * 128], in0=maskT.ap[:, h * 128:(h + 1) * 128],
                                                              scalar=1.0 / 16.0, in1=mtmp.ap, op0=ALU.mult, op1=ALU.mult),
                 reads=[maskT, mtmp], writes=[maskT])
        base_mark = A.top

        def ret_proj(src, src_res, Tn, which, tabs, Dd, part, w3, w_tl):
            TT = min(512, Tn)
            xring, hring = self.make_norm_rings(TT, nh=2, nx=2)
            if part == 0:
                tabF = Ring([{"c": Tl(A.f32(TT)), "s": Tl(A.f32(TT))} for _ in range(2)])
                tmp = Ring([{k: Tl(A.f32(TT)) for k in ("t1", "t2", "o1", "o2")} for _ in range(2)])
                stb = Ring([Tl(A.bf16(TT)) for _ in range(6)])
            else:
                tabT = Ring([{"c": Tl(A.f32(256)), "s": Tl(A.f32(256))} for _ in range(2)])
                tk = Ring([{"u1": Tl(A.f32(256)), "u2": Tl(A.f32(256)), "ko": Tl(A.f32(512))} for _ in range(2)])
                kst = Ring([{"f": Tl(A.bf16(1024)), "b": Tl(A.bf16(1024))} for _ in range(2)])
                vst = Ring([Tl(A.bf16(2048)) for _ in range(2)])
                gst = Ring([Tl(A.bf16(2048)) for _ in range(2)])
            dres = Dd["res"]
            cnt = [0, 0]
            for tt, hT, h3 in self.norm_tiles(src, src_res, Tn, which, xring, hring, None, TT):
                cols = slice(tt * TT, (tt + 1) * TT)
                if part == 0:
                    tf = tabF.next()
                    S.dma("act", tf["c"].ap, tabs["cF"][:, cols], writes=[tf["c"]])
                    S.dma("act", tf["s"].ap, tabs["sF"][:, cols], writes=[tf["s"]])
                for qk in (range(2) if part == 0 else ()):
                    for h in range(4):
                        b0 = (cnt[0] % 2) * 2
                        cnt[0] += 1
                        for half in range(2):
                            oc = qk * 8 + h * 2 + half
                            pt = self.psum(b0 + half)[:, 0:TT]
                            for kk in range(8):
                                S.op("pe", lambda e, kk=kk, oc=oc, pt=pt, h3=h3: e.matmul(pt, lhsT=w3[:, kk, oc * 128:(oc + 1) * 128], rhs=h3[:, kk, :],
                                                                                      start=(kk == 0), stop=(kk == 7)),
                                     reads=[hT, w_tl], writes=[self.ps_res[b0 + half]])
                        x1 = self.psum(b0)[:, 0:TT]; x2 = self.psum(b0 + 1)[:, 0:TT]
                        r1, r2 = self.ps_res[b0], self.ps_res[b0 + 1]
                        tm = tmp.next()
                        t1, t2, o1, o2 = tm["t1"], tm["t2"], tm["o1"], tm["o2"]
                        S.op("dve", lambda e, x1=x1, t1=t1, tf=tf: e.tensor_tensor(out=t1.ap, in0=x1, in1=tf["c"].ap, op=ALU.mult), reads=[r1, tf["c"]], writes=[t1])
                        S.op("dve", lambda e, x2=x2, t2=t2, tf=tf: e.tensor_tensor(out=t2.ap, in0=x2, in1=tf["s"].ap, op=ALU.mult), reads=[r2, tf["s"]], writes=[t2])
                        S.op("pool", lambda e, t1=t1, t2=t2, o1=o1: e.tensor_tensor(out=o1.ap, in0=t1.ap, in1=t2.ap, op=ALU.subtract), reads=[t1, t2], writes=[o1])
                        S.op("dve", lambda e, x1=x1, t1=t1, tf=tf: e.tensor_tensor(out=t1.ap, in0=x1, in1=tf["s"].ap, op=ALU.mult), reads=[r1, tf["s"], o1], writes=[t1])
                        S.op("dve", lambda e, x2=x2, t2=t2, tf=tf: e.tensor_tensor(out=t2.ap, in0=x2, in1=tf["c"].ap, op=ALU.mult), reads=[r2, tf["c"], o1], writes=[t2])
                        S.op("pool", lambda e, t1=t1, t2=t2, o2=o2: e.tensor_tensor(out=o2.ap, in0=t1.ap, in1=t2.ap, op=ALU.add), reads=[t1, t2], writes=[o2])
                        for half, o in ((0, o1), (1, o2)):
                            row0 = h * 256 + half * 128
                            if qk == 0:
                                st = stb.next()
                                S.op("act", lambda e, st=st, o=o: e.activation(out=st.ap, in_=o.ap, func=AF.Copy), reads=[o], writes=[st])
                                S.dma("sp", Dd["Q"][row0:row0 + 128, cols], st.ap, reads=[st], writes=[dres])
                                for d, nm in ((0, "QF"), (1, "QB")):
                                    st = stb.next()
                                    S.op("pool" if d == 0 else "dve", lambda e, st=st, o=o, d=d, h=h: e.tensor_tensor(out=st.ap, in0=o.ap, in1=qdt[d].ap[:, h * 512:h * 512 + TT],
                                                                                                              op=ALU.mult), reads=[o, qdt[d]], writes=[st])
                                    S.dma("sp", Dd[nm][row0:row0 + 128, cols], st.ap, reads=[st], writes=[dres])
                            else:
                                st = stb.next()
                                S.op("act", lambda e, st=st, o=o: e.activation(out=st.ap, in_=o.ap, func=AF.Copy), reads=[o], writes=[st])
                                S.dma("sp", Dd["KT"][row0:row0 + 128, cols], st.ap, reads=[st], writes=[dres])
                for s in (range(TT // 128) if part == 1 else ()):
                    ti = tt * (TT // 128) + s
                    rows = slice(ti * 128, (ti + 1) * 128)
                    tT = tabT.next()
                    S.dma("act", tT["c"].ap, tabs["cT"][rows, :], writes=[tT["c"]])
                    S.dma("act", tT["s"].ap, tabs["sT"][rows, :], writes=[tT["s"]])
                    cT3 = tT["c"].ap.rearrange("p (a c) -> p a c", a=2); sT3 = tT["s"].ap.rearrange("p (a c) -> p a c", a=2)
                    ks = kst.next()

                    def tok_mm(col0, bank):
                        pt = self.psum(bank)
                        for kk in range(8):
                            S.op("pe", lambda e, kk=kk, pt=pt, s=s, h3=h3, col0=col0: e.matmul(pt, lhsT=h3[:, kk, s * 128:(s + 1) * 128],
                                                                                          rhs=w3[:, kk, col0:col0 + 512], start=(kk == 0), stop=(kk == 7)),
                                 reads=[hT, w_tl], writes=[self.ps_res[bank]])
                        return pt
                    for g in range(2):
                        bank = 4 + cnt[1] % 2
                        cnt[1] += 1
                        pt = tok_mm(g * 512, bank)
                        pr = self.ps_res[bank]
                        pv = pt.rearrange("p (a b c) -> p a b c", a=2, b=2)
                        x1, x2 = pv[:, :, 0, :], pv[:, :, 1, :]
                        t = tk.next()
                        u1, u2, ko = t["u1"], t["u2"], t["ko"]
                        u13 = u1.ap.rearrange("p (a c) -> p a c", a=2); u23 = u2.ap.rearrange("p (a c) -> p a c", a=2)
                        ko4 = ko.ap.rearrange("p (a b c) -> p a b c", a=2, b=2)
                        S.op("dve", lambda e, x1=x1, u13=u13, cT3=cT3: e.tensor_tensor(out=u13, in0=x1, in1=cT3, op=ALU.mult), reads=[pr, tT["c"]], writes=[u1])
                        S.op("dve", lambda e, x2=x2, u23=u23, sT3=sT3: e.tensor_tensor(out=u23, in0=x2, in1=sT3, op=ALU.mult), reads=[pr, tT["s"]], writes=[u2])
                        S.op("pool", lambda e, u13=u13, u23=u23, ko4=ko4: e.tensor_tensor(out=ko4[:, :, 0, :], in0=u13, in1=u23, op=ALU.subtract), reads=[u1, u2], writes=[ko])
                        S.op("dve", lambda e, x1=x1, u13=u13, sT3=sT3: e.tensor_tensor(out=u13, in0=x1, in1=sT3, op=ALU.mult), reads=[pr, tT["s"], ko], writes=[u1])
                        S.op("dve", lambda e, x2=x2, u23=u23, cT3=cT3: e.tensor_tensor(out=u23, in0=x2, in1=cT3, op=ALU.mult), reads=[pr, tT["c"], ko], writes=[u2])
                        S.op("pool", lambda e, u13=u13, u23=u23, ko4=ko4: e.tensor_tensor(out=ko4[:, :, 1, :], in0=u13, in1=u23, op=ALU.add), reads=[u1, u2], writes=[ko])
                        S.op("pool", lambda e, ko=ko, ks=ks, g=g: e.tensor_tensor(out=ks["f"].ap[:, g * 512:(g + 1) * 512], in0=ko.ap, in1=dect[0].ap[:, g * 512:(g + 1) * 512], op=ALU.mult),
                             reads=[ko, dect[0]], writes=[ks["f"]])
                        S.op("dve", lambda e, ko=ko, ks=ks, g=g: e.tensor_tensor(out=ks["b"].ap[:, g * 512:(g + 1) * 512], in0=ko.ap, in1=dect[1].ap[:, g * 512:(g + 1) * 512], op=ALU.mult),
                             reads=[ko, dect[1]], writes=[ks["b"]])
                    S.dma("sp", Dd["KF"][rows, :], ks["f"].ap, reads=[ks["f"]], writes=[dres])
                    S.dma("sp", Dd["KB"][rows, :], ks["b"].ap, reads=[ks["b"]], writes=[dres])
                    vs = vst.next(); gs_ = gst.next()
                    for g in range(4):
                        bank = 4 + cnt[1] % 2
                        cnt[1] += 1
                        pt = tok_mm(1024 + g * 512, bank)
                        S.op("act", lambda e, pt=pt, vs=vs, g=g: e.activation(out=vs.ap[:, g * 512:(g + 1) * 512], in_=pt, func=AF.Copy), reads=[self.ps_res[bank]], writes=[vs])
                    S.dma("sp", Dd["V"][rows, :], vs.ap, reads=[vs], writes=[dres])
                    for g in range(4):
                        bank = 4 + cnt[1] % 2
                        cnt[1] += 1
                        pt = tok_mm(3072 + g * 512, bank)
                        S.op("act", lambda e, pt=pt, gs_=gs_, g=g: e.activation(out=gs_.ap[:, g * 512:(g + 1) * 512], in_=pt, func=AF.Silu), reads=[self.ps_res[bank]], writes=[gs_])
                    S.dma("sp", Dd["G"][rows, :], gs_.ap, reads=[gs_], writes=[dres])

        for part, (c0, ncol) in enumerate(((0, 2048), (1024, 5120))):
            A.top = base_mark
            w_tl = Tl(A.bf16(8 * ncol))
            w3 = w_tl.ap.rearrange("p (k n) -> p k n", k=8)
            wmark = A.top
            stage = Ring([Tl(A.f32(2048)) for _ in range(2)])
            self.load_w_bf16(W["ret_w_in"], 1024, 6144, w3, w_tl, col0=c0, ncols=ncol, stage_ring=stage)
            S.barrier(); A.top = wmark
            ret_proj(c_src, self.c_res, TC, 1, tabs_c, DC, part, w3, w_tl)
            S.barrier(); A.top = wmark
            ret_proj(x_src, self.x_res, T, 0, tabs_x, DX, part, w3, w_tl)
            S.barrier()
        A.top = base_mark

        Sst = [[Tl(A.f32(512)) for _ in range(8)] for _ in range(2)]
        Sbf = [[Tl(A.bf16(512)) for _ in range(8)] for _ in range(2)]
        for d in range(2):
            for hd in range(8):
                S.op("pool", lambda e, d=d, hd=hd: e.memset(Sst[d][hd].ap, 0.0), writes=[Sst[d][hd]])
                S.op("pool", lambda e, d=d, hd=hd: e.memset(Sbf[d][hd].ap, 0.0), writes=[Sbf[d][hd]])
        pmark = A.top
        ucnt = [0]

        def state_update(d, kt_, vt_):
            for h in range(4):
                for dc in range(2):
                    hd = h * 2 + dc
                    bank = 6 + ucnt[0] % 2
                    ucnt[0] += 1
                    pt = self.psum(bank)
                    S.op("pe", lambda e, pt=pt, h=h, dc=dc, kt_=kt_, vt_=vt_: e.matmul(pt, lhsT=kt_.ap[:, h * 256 + dc * 128:h * 256 + (dc + 1) * 128],
                                                                                   rhs=vt_.ap[:, h * 512:(h + 1) * 512], start=True, stop=True),
                         reads=[kt_, vt_], writes=[self.ps_res[bank]])
                    st = Sst[d][hd]
                    S.op("dve", lambda e, pt=pt, st=st, d=d, h=h: e.scalar_tensor_tensor(out=st.ap, in0=st.ap, scalar=blk.ap[:, d * 4 + h:d * 4 + h + 1], in1=pt,
                                                                                    op0=ALU.mult, op1=ALU.add), reads=[st, blk, self.ps_res[bank]], writes=[st])
                    S.op("act", lambda e, st=st, d=d, hd=hd: e.activation(out=Sbf[d][hd].ap, in_=st.ap, func=AF.Copy), reads=[st], writes=[Sbf[d][hd]])

        def passB(Dd, Tn):
            nch = Tn // 128
            ring = Ring([{"q": Tl(A.bf16(1024)), "k": Tl(A.bf16(1024)), "v": Tl(A.bf16(2048)), "ob": Tl(A.f32(2048))} for _ in range(3)])
            bc = 0

            def bload(c):
                r = ring.next()
                cs = slice(c * 128, (c + 1) * 128)
                q3 = r["q"].ap.rearrange("p (a t) -> p a t", a=8)
                S.dma("sp", q3, Dd["QB"].rearrange("(a p) t -> p a t", p=128)[:, :, cs], reads=[Dd["res"]], writes=[r["q"]])
                S.dma("act", r["k"].ap, Dd["KB"][cs, :], reads=[Dd["res"]], writes=[r["k"]])
                S.dma("sp", r["v"].ap, Dd["V"][cs, :], reads=[Dd["res"]], writes=[r["v"]])
                return r, q3
            order = list(range(nch - 1, -1, -1))
            nxt = bload(order[0])
            for oi, c in enumerate(order):
                r, q3 = nxt
                if oi + 1 < len(order):
                    nxt = bload(order[oi + 1])
                q, k, v, ob = r["q"], r["k"], r["v"], r["ob"]
                cs = slice(c * 128, (c + 1) * 128)
                for h in range(4):
                    bank = bc % 4
                    bc += 1
                    pt = self.psum(bank)
                    for dc in range(2):
                        hd = h * 2 + dc
                        S.op("pe", lambda e, pt=pt, hd=hd, dc=dc, q3=q3: e.matmul(pt, lhsT=q3[:, hd, :], rhs=Sbf[1][hd].ap, start=(dc == 0), stop=(dc == 1)),
                             reads=[q, Sbf[1][hd]], writes=[self.ps_res[bank]])
                    if h % 2 == 0:
                        S.op("act", lambda e, pt=pt, ob=ob, h=h: e.activation(out=ob.ap[:, h * 512:(h + 1) * 512], in_=pt, func=AF.Copy), reads=[self.ps_res[bank]], writes=[ob])
                    else:
                        S.op("dve", lambda e, pt=pt, ob=ob, h=h: e.tensor_copy(out=ob.ap[:, h * 512:(h + 1) * 512], in_=pt), reads=[self.ps_res[bank]], writes=[ob])
                S.dma("sp", Dd["OB"][cs, :], ob.ap, reads=[ob], writes=[Dd["res"]])
                state_update(1, k, v)

        def passF(Dd, Tn, want_y):
            nch = Tn // 128
            ring = Ring([{"q": Tl(A.bf16(1024)), "qf": Tl(A.bf16(1024)), "kt": Tl(A.bf16(1024)), "k": Tl(A.bf16(1024)), "v": Tl(A.bf16(2048)),
                          "g": Tl(A.bf16(2048)), "ob": Tl(A.f32(2048)), "o": Tl(A.f32(2048)), "y": Tl(A.bf16(2048)), "yT": Tl(A.bf16(2048)),
                          "ss": Tl(A.f32(4)), "junk": Tl(A.bf16(512))} for _ in range(2)])
            pTr = Ring([Tl(A.bf16(128)) for _ in range(3)])
            bs = bo = 0

            def fload(c):
                r = ring.next()
                cs = slice(c * 128, (c + 1) * 128)
                q3 = r["q"].ap.rearrange("p (a t) -> p a t", a=8); qf3 = r["qf"].ap.rearrange("p (a t) -> p a t", a=8)
                kt3 = r["kt"].ap.rearrange("p (a t) -> p a t", a=8)
                fm = lambda nm: Dd[nm].rearrange("(a p) t -> p a t", p=128)[:, :, cs]
                S.dma("sp", q3, fm("Q"), reads=[Dd["res"]], writes=[r["q"]])
                S.dma("act", qf3, fm("QF"), reads=[Dd["res"]], writes=[r["qf"]])
                S.dma("sp", kt3, fm("KT"), reads=[Dd["res"]], writes=[r["kt"]])
                S.dma("act", r["k"].ap, Dd["KF"][cs, :], reads=[Dd["res"]], writes=[r["k"]])
                S.dma("sp", r["v"].ap, Dd["V"][cs, :], reads=[Dd["res"]], writes=[r["v"]])
                if want_y:
                    S.dma("act", r["g"].ap, Dd["G"][cs, :], reads=[Dd["res"]], writes=[r["g"]])
                    S.dma("sp", r["ob"].ap, Dd["OB"][cs, :], reads=[Dd["res"]], writes=[r["ob"]])
                return r, q3, qf3, kt3
            nxt = fload(0)
            for c in range(nch):
                r, q3, qf3, kt3 = nxt
                if c + 1 < nch:
                    nxt = fload(c + 1)
                q, qf, kt, k, v, g, ob, o, y, yT, ss, junk = (r[n_] for n_ in ("q", "qf", "kt", "k", "v", "g", "ob", "o", "y", "yT", "ss", "junk"))
                cs = slice(c * 128, (c + 1) * 128)
                if want_y:
                    def emit_S(h):
                        nonlocal bs
                        bS = bs % 2
                        bs += 1
                        pS = self.psum(bS)[:, 0:128]
                        for dc in range(2):
                            hd = h * 2 + dc
                            S.op("pe", lambda e, pS=pS, hd=hd, dc=dc, kt3=kt3, q3=q3: e.matmul(pS, lhsT=kt3[:, hd, :], rhs=q3[:, hd, :], start=(dc == 0), stop=(dc == 1)),
                                 reads=[kt, q], writes=[self.ps_res[bS]])
                        pT = pTr.next()
                        S.op("dve", lambda e, pS=pS, pT=pT, h=h: e.tensor_tensor(out=pT.ap, in0=pS, in1=maskT.ap[:, h * 128:(h + 1) * 128], op=ALU.mult),
                             reads=[self.ps_res[bS], maskT], writes=[pT])
                        return pT
                    pT_next = emit_S(0)
                    for h in range(4):
                        pT = pT_next
                        if h + 1 < 4:
                            pT_next = emit_S(h + 1)
                        bO = 2 + bo % 2
                        bo += 1
                        pO = self.psum(bO)
                        S.op("pe", lambda e, pO=pO, pT=pT, v=v, h=h: e.matmul(pO, lhsT=pT.ap, rhs=v.ap[:, h * 512:(h + 1) * 512], start=True, stop=False),
                             reads=[pT, v], writes=[self.ps_res[bO]])
                        for dc in range(2):
                            hd = h * 2 + dc
                            S.op("pe", lambda e, pO=pO, hd=hd, dc=dc, qf3=qf3: e.matmul(pO, lhsT=qf3[:, hd, :], rhs=Sbf[0][hd].ap, start=False, stop=(dc == 1)),
                                 reads=[qf, Sbf[0][hd]], writes=[self.ps_res[bO]])
                        S.op("dve", lambda e, pO=pO, o=o, ob=ob, h=h: e.tensor_tensor(out=o.ap[:, h * 512:(h + 1) * 512], in0=pO, in1=ob.ap[:, h * 512:(h + 1) * 512], op=ALU.add),
                             reads=[self.ps_res[bO], ob], writes=[o])
                        S.op("act", lambda e, o=o, junk=junk, ss=ss, h=h: e.activation(out=junk.ap, in_=o.ap[:, h * 512:(h + 1) * 512], func=AF.Square, accum_out=ss.ap[:, h:h + 1]),
                             reads=[o], writes=[junk, ss])
                    S.op("dve", lambda e, ss=ss: e.tensor_scalar(out=ss.ap, in0=ss.ap, scalar1=1.0 / 512.0, scalar2=EPS, op0=ALU.mult, op1=ALU.add), reads=[ss], writes=[ss])
                    S.op("act", lambda e, ss=ss: e.activation(out=ss.ap, in_=ss.ap, func=AF.Sqrt), reads=[ss], writes=[ss])
                    S.op("dve", lambda e, ss=ss: e.reciprocal(out=ss.ap, in_=ss.ap), reads=[ss], writes=[ss])
                    for h in range(4):
                        S.op("dve", lambda e, o=o, y=y, g=g, ss=ss, h=h: e.scalar_tensor_tensor(
                            out=y.ap[:, h * 512:(h + 1) * 512], in0=o.ap[:, h * 512:(h + 1) * 512], scalar=ss.ap[:, h:h + 1], in1=g.ap[:, h * 512:(h + 1) * 512],
                            op0=ALU.mult, op1=ALU.mult), reads=[o, ss, g], writes=[y])
                    for half in range(2):
                        bank = 4 + half
                        pb = self.psum(bank, 1, BF16)
                        for j in range(8):
                            jj_ = half * 8 + j
                            S.op("pe", lambda e, pb=pb, j=j, jj_=jj_, y=y: e.transpose(out=pb[:, j * 128:(j + 1) * 128], in_=y.ap[:, jj_ * 128:(jj_ + 1) * 128], identity=self.ident.ap),
                                 reads=[y, self.ident], writes=[self.ps_res[bank]])
                        if half == 0:
                            S.op("act", lambda e, pb=pb, yT=yT: e.activation(out=yT.ap[:, 0:1024], in_=pb, func=AF.Copy), reads=[self.ps_res[bank]], writes=[yT])
                        else:
                            S.op("dve", lambda e, pb=pb, yT=yT: e.tensor_copy(out=yT.ap[:, 1024:2048], in_=pb), reads=[self.ps_res[bank]], writes=[yT])
                    S.dma("sp", Dd["Y"].rearrange("(a p) t -> p a t", p=128)[:, :, cs], yT.ap.rearrange("p (a t) -> p a t", a=16), reads=[yT], writes=[self.Y_res])
                state_update(0, k, v)

        passB(DC, TC)
        S.barrier(); A.top = pmark
        passF(DC, TC, self.need_ctx)
        S.barrier(); A.top = pmark
        passB(DX, T)
        S.barrier(); A.top = pmark
        passF(DX, T, True)
        S.barrier(); A.top = base_mark
        wo = Tl(A.bf16(16 * 1024))
        wo3 = wo.ap.rearrange("p (k n) -> p k n", k=16)
        stage = Ring([Tl(A.f32(2048)) for _ in range(3)])
        self.load_w_bf16(W["ret_w_out"], 2048, 1024, wo3, wo, stage_ring=stage)
        self.out_proj(DX["Y"], 2048, wo3, wo, T, self.gt_bc, x_src, self.xs, self.x_res)
        if self.need_ctx:
            self.out_proj(DC["Y"], 2048, wo3, wo, TC, self.gtc_bc, c_src, self.cs, self.c_res)

    def mix_hy(self, x_src, c_src):
        S, A, W, T, TC = self.S, self.A, self.W, self.T, self.TC
        f32 = np.float32
        bf = ml_dtypes.bfloat16
        N = 16384
        TWO_PI = 2.0 * math.pi
        n_ = np.arange(128, dtype=np.float64)
        Fc = np.exp(-2j * np.pi * np.outer(n_, n_) / 128.0)
        Fre, Fim = Fc.real, Fc.imag
        twc = np.exp(-2j * np.pi * np.outer(n_, n_) / N)
        FS = np.concatenate([Fre, Fim], axis=1)
        cmat = np.concatenate([Fre, Fim, -Fim, Fre, -Fim, Fim, Fre, Fre / N, Fim / N], axis=1)
        cmat_d = self.const("c_hy_cmat", cmat.astype(bf))
        tw_d = self.const("c_hy_tw", np.concatenate([np.tile(twc.real, (1, 4)), np.tile(twc.imag, (1, 4))], axis=1).astype(f32))

        def fs_rows(L):
            M = L // 128
            rows = list(range(M)) + ([] if M == 64 else [])
            kr = list(range(M)) + list(range(128 - M, 128))
            return FS[:M].astype(bf), FS[kr].astype(bf)
        fss_x, fsk_x = fs_rows(T); fss_c, fsk_c = fs_rows(TC)
        fs_d = {"sx": self.const("c_hy_fssx", fss_x), "kx": self.const("c_hy_fskx", fsk_x),
                "sc": self.const("c_hy_fssc", fss_c), "kc": self.const("c_hy_fskc", fsk_c)}

        def zfeat(L):
            bands = 16
            t = np.linspace(0.0, 1.0, L, dtype=f32)[:, None]
            w = (f32(2.0 * math.pi) * np.arange(L, dtype=f32)[:, None] / f32(L)).astype(f32)
            fr = np.linspace(1e-4, bands - 1, bands, dtype=f32)[None, :]
            arg = (fr * w).astype(f32).astype(np.float64)
            z = np.concatenate([t, np.cos(arg).astype(f32), -np.sin(arg).astype(f32)], axis=-1)
            return np.ascontiguousarray(z.T.astype(f32)), np.ascontiguousarray(np.tile(t.T, (128, 1)).astype(f32))
        zT_x, tn_x = zfeat(T); zT_c, tn_c = zfeat(TC)
        zt_d = {"x": self.const("c_hy_zTx", zT_x), "c": self.const("c_hy_zTc", zT_c)}
        tn_d = {"x": self.const("c_hy_tnx", tn_x), "c": self.const("c_hy_tnc", tn_c)}
        deltas = np.abs(np.linspace(math.log(1e-2) / 1.5, math.log(1e-2) / 0.3, 1024, dtype=f32))
        nd_d = self.const("c_hy_ndelta", np.ascontiguousarray((-deltas).reshape(8, 128).T.astype(f32)))

        U = {"x": self.scratch("hy_Ux", [4096, T]), "c": self.scratch("hy_Uc", [4096, TC])}
        Z = {"x": self.scratch("hy_Zx", [3072, T]), "c": self.scratch("hy_Zc", [3072, TC])}
        KERN = {"x": self.scratch("hy_KERNx", [2, 1024, N]), "c": self.scratch("hy_KERNc", [2, 1024, N])}
        KF = {"x": self.scratch("hy_KFx", [2, 128, 1024, 256]), "c": self.scratch("hy_KFc", [2, 128, 1024, 256])}
        Y = {"x": self.scratch("hy_Yx", [1024, T], BF16), "c": self.scratch("hy_Yc", [1024, TC], BF16)}
        KC = self.scratch("hy_KC", [2, 1024, 2 * TC])
        identf_d = self.const("c_identf", np.eye(128, dtype=f32))
        jmat_d = self.const("c_jmat", np.eye(128, dtype=f32)[::-1].copy().astype(bf))
        Ures = {"x": FreeRes(), "c": FreeRes()}
        dres = FreeRes()
        self.Y_res = FreeRes()
        LEN = {"x": T, "c": TC}
        seqs = ["c", "x"] if self.need_ctx else ["c", "x"]

        w_tl = Tl(A.bf16(8 * 4096))
        w3 = w_tl.ap.rearrange("p (k n) -> p k n", k=8)
        stage = Ring([Tl(A.f32(2048)) for _ in range(3)])
        self.load_w_bf16(W["hy_w_in"], 1024, 4096, w3, w_tl, stage_ring=stage)
        mark = A.top
        self.proj_to_dram(c_src, self.c_res, TC, 1, w3, w_tl, 32, U["c"], Ures["c"], min(512, TC))
        A.top = mark
        S.barrier()
        self.proj_to_dram(x_src, self.x_res, T, 0, w3, w_tl, 32, U["x"], Ures["x"], min(512, T))
        S.barrier(); A.reset()

        def col(src_ap, n):
            t = Tl(A.f32(n))
            S.dma("sp", t.ap, src_ap.rearrange("(n p) -> p n", p=128), writes=[t], slow=True)
            return t
        cw = [col(W["hy_conv_w"][k], 24) for k in range(3)]
        cb = col(W["hy_conv_b"], 24)
        TM = max(T, TC)
        ur = Ring([Tl(A.f32(TM + 2)) for _ in range(2)])
        zr = Ring([Tl(A.f32(TM)) for _ in range(2)])
        for sq in ("c", "x"):
            Tn = LEN[sq]
            for n in range(24):
                ub = ur.next(); zb = zr.next()
                S.op("pool", lambda e, ub=ub: e.memset(ub.ap[:, 0:1], 0.0), writes=[ub])
                S.op("pool", lambda e, ub=ub, Tn=Tn: e.memset(ub.ap[:, Tn + 1:Tn + 2], 0.0), writes=[ub])
                S.dma("sp", ub.ap[:, 1:Tn + 1], U[sq][n * 128:(n + 1) * 128, :], reads=[Ures[sq]], writes=[ub])
                S.op("dve", lambda e, ub=ub, zb=zb, n=n, Tn=Tn: e.tensor_scalar(out=zb.ap[:, 0:Tn], in0=ub.ap[:, 0:Tn], scalar1=cw[0].ap[:, n:n + 1],
                                                                         scalar2=cb.ap[:, n:n + 1], op0=ALU.mult, op1=ALU.add),
                     reads=[ub, cw[0], cb], writes=[zb])
                for k in (1, 2):
                    S.op("dve", lambda e, ub=ub, zb=zb, n=n, k=k, Tn=Tn: e.scalar_tensor_tensor(out=zb.ap[:, 0:Tn], in0=ub.ap[:, k:k + Tn], scalar=cw[k].ap[:, n:n + 1],
                                                                                      in1=zb.ap[:, 0:Tn], op0=ALU.mult, op1=ALU.add),
                         reads=[ub, cw[k], zb], writes=[zb])
                S.dma("act", Z[sq][n * 128:(n + 1) * 128, :], zb.ap[:, 0:Tn], reads=[zb], writes=[dres])
        S.barrier(); A.reset()

        fw1 = Tl(A.f32(64)); fw2 = Tl(A.f32(64)); fw3 = Tl(A.f32(4096))
        S.dma("sp", fw1.ap[0:33, :], W["hy_fw1"], writes=[fw1])
        S.dma("sp", fw2.ap[0:64, :], W["hy_fw2"], writes=[fw2])
        S.dma("sp", fw3.ap[0:64, :], W["hy_fw3"], writes=[fw3])
        sm = Tl(A.f32(8))
        S.dma("sp", sm.ap[0:64, 0:1], W["hy_freq"].rearrange("(p o) -> p o", o=1), writes=[sm])
        S.dma("sp", sm.ap[0:64, 1:2], W["hy_fb1"].rearrange("(p o) -> p o", o=1), writes=[sm])
        S.dma("sp", sm.ap[0:64, 2:3], W["hy_fb2"].rearrange("(p o) -> p o", o=1), writes=[sm])
        OFF = math.pi + TWO_PI * 16
        for i in range(2):
            S.op("dve", lambda e, i=i: e.tensor_scalar(out=sm.ap[0:64, 3 + i:4 + i], in0=sm.ap[0:64, 1 + i:2 + i], scalar1=sm.ap[0:64, 0:1], scalar2=None, op0=ALU.mult),
                 reads=[sm], writes=[sm])
        nd = Tl(A.f32(8))
        S.dma("sp", nd.ap, nd_d, writes=[nd])
        skc = Tl(A.f32(16))
        S.dma("sp", skc.ap.rearrange("p (o n) -> p o n", o=2), W["hy_skip"].rearrange("o (n p) -> p o n", p=128), writes=[skc], slow=True)
        zero = Tl(A.f32(1))
        S.op("pool", lambda e: e.memset(zero.ap, 0.0), writes=[zero])
        MAGIC = 12582912.0
        hidT = Tl(A.f32(TM))
        f0 = Tl(A.f32(TM)); f1 = Tl(A.f32(TM)); rev = Tl(A.f32(TM))
        ztr = Ring([Tl(A.f32(512)) for _ in range(2)])
        h1r = Ring([{"a": Tl(A.f32(512)), "r": Tl(A.f32(512))} for _ in range(2)])
        tnr = Ring([Tl(A.f32(512)) for _ in range(2)])
        decr = Ring([Tl(A.f32(512)) for _ in range(2)])
        sums = Tl(A.f32(4))

        def sin_layer(pt, bias_col, dst_ap, TW, hb):
            a, r = hb["a"], hb["r"]
            S.op("dve", lambda e: e.tensor_scalar(out=a.ap[0:64, 0:TW], in0=pt, scalar1=sm.ap[0:64, 0:1], scalar2=sm.ap[0:64, bias_col:bias_col + 1],
                                                  op0=ALU.mult, op1=ALU.add), reads=[self.ps_res[0], sm], writes=[a])
            S.op("dve", lambda e: e.tensor_scalar(out=r.ap[0:64, 0:TW], in0=a.ap[0:64, 0:TW], scalar1=1.0 / TWO_PI, scalar2=MAGIC, op0=ALU.mult, op1=ALU.add),
                 reads=[a], writes=[r])
            S.op("dve", lambda e: e.tensor_scalar(out=r.ap[0:64, 0:TW], in0=r.ap[0:64, 0:TW], scalar1=-MAGIC, scalar2=None, op0=ALU.add), reads=[r], writes=[r])
            S.op("dve", lambda e: e.scalar_tensor_tensor(out=a.ap[0:64, 0:TW], in0=r.ap[0:64, 0:TW], scalar=-TWO_PI, in1=a.ap[0:64, 0:TW], op0=ALU.mult, op1=ALU.add),
                 reads=[r, a], writes=[a])
            S.op("act", lambda e: e.activation(out=dst_ap, in_=a.ap[0:64, 0:TW], func=AF.Sin), reads=[a], writes=[hidT, hb["r"]])

        for sq in ("c", "x"):
            L = LEN[sq]
            M = L // 128
            TW = min(512, L)
            for tt in range(L // TW):
                cs = slice(tt * TW, (tt + 1) * TW)
                zt = ztr.next(); hb = h1r.next()
                S.dma("sp", zt.ap[0:33, 0:TW], zt_d[sq][:, cs], writes=[zt])
                pt = self.psum(0)[0:64, 0:TW]
                S.op("pe", lambda e, pt=pt, zt=zt, TW=TW: e.matmul(pt, lhsT=fw1.ap[0:33, 0:64], rhs=zt.ap[0:33, 0:TW], start=True, stop=True),
                     reads=[fw1, zt], writes=[self.ps_res[0]])
                h1 = hb["r"]
                sin_layer(pt, 3, h1.ap[0:64, 0:TW], TW, hb)
                S.op("pe", lambda e, pt=pt, h1=h1, TW=TW: e.matmul(pt, lhsT=fw2.ap[0:64, 0:64], rhs=h1.ap[0:64, 0:TW], start=True, stop=True),
                     reads=[fw2, h1], writes=[self.ps_res[0]])
                hb2 = h1r.next()
                sin_layer(pt, 4, hidT.ap[0:64, cs], TW, hb2)
            for o in range(2):
                for cc in range(8):
                    fd = (f0, f1)
                    for dr in range(2):
                        j = o * 16 + dr * 8 + cc
                        for tt in range(L // TW):
                            cs = slice(tt * TW, (tt + 1) * TW)
                            b = 1 + (tt % 2)
                            pt = self.psum(b)[:, 0:TW]
                            S.op("pe", lambda e, pt=pt, j=j, cs=cs: e.matmul(pt, lhsT=fw3.ap[0:64, j * 128:(j + 1) * 128], rhs=hidT.ap[0:64, cs], start=True, stop=True),
                                 reads=[fw3, hidT], writes=[self.ps_res[b]])
                            tnt = tnr.next(); dc_ = decr.next()
                            S.dma("sp", tnt.ap[:, 0:TW], tn_d[sq][:, cs], writes=[tnt])
                            S.op("act", lambda e, tnt=tnt, dc_=dc_, cc=cc, TW=TW: e.activation(out=dc_.ap[:, 0:TW], in_=tnt.ap[:, 0:TW], func=AF.Exp, scale=nd.ap[:, cc:cc + 1]),
                                 reads=[tnt, nd], writes=[dc_])
                            S.op("dve", lambda e, pt=pt, dc_=dc_, dr=dr, cs=cs, TW=TW, fd=fd: e.tensor_tensor(out=fd[dr].ap[:, cs], in0=pt, in1=dc_.ap[:, 0:TW], op=ALU.mult),
                                 reads=[self.ps_res[b], dc_], writes=[fd[dr]])
                        s0 = dr
                        S.op("act", lambda e, dr=dr, s0=s0, L=L, fd=fd: e.activation(out=rev.ap[:, s0:L], in_=fd[dr].ap[:, s0:L], func=AF.Abs, accum_out=sums.ap[:, dr:dr + 1]),
                             reads=[fd[dr]], writes=[rev, sums])
                    S.op("dve", lambda e: e.tensor_tensor(out=sums.ap[:, 2:3], in0=sums.ap[:, 0:1], in1=sums.ap[:, 1:2], op=ALU.add), reads=[sums], writes=[sums])
                    S.op("dve", lambda e: e.reciprocal(out=sums.ap[:, 3:4], in_=sums.ap[:, 2:3]), reads=[sums], writes=[sums])
                    S.op("dve", lambda e, L=L: e.tensor_scalar(out=f0.ap[:, 0:L], in0=f0.ap[:, 0:L], scalar1=sums.ap[:, 3:4], scalar2=None, op0=ALU.mult),
                         reads=[f0, sums], writes=[f0])
                    S.op("dve", lambda e, o=o, cc=cc: e.tensor_scalar(out=f0.ap[:, 0:1], in0=f0.ap[:, 0:1], scalar1=skc.ap[:, o * 8 + cc:o * 8 + cc + 1], scalar2=None, op0=ALU.add),
                         reads=[f0, skc], writes=[f0])
                    rows = slice(cc * 128, (cc + 1) * 128)
                    S.op("dve", lambda e, L=L: e.tensor_scalar(out=rev.ap[:, 0:L - 1], in0=f1.ap[:, 1:L][:, ::-1], scalar1=sums.ap[:, 3:4], scalar2=None, op0=ALU.mult),
                         reads=[f1, sums], writes=[rev])
                    if sq == "c":
                        S.dma("sp", KC[o, rows, L - 1:2 * L - 1], f0.ap[:, 0:L], reads=[f0], writes=[dres])
                        S.dma("act", KC[o, rows, 0:L - 1], rev.ap[:, 0:L - 1], reads=[rev], writes=[dres])
                    else:
                        S.dma("sp", KERN[sq][o, rows, 0:L], f0.ap[:, 0:L], reads=[f0], writes=[dres])
                        S.dma("act", KERN[sq][o, rows, N - L + 1:N], rev.ap[:, 0:L - 1], reads=[rev], writes=[dres])
                        S.dma("sp", KERN[sq][o, rows, N - L:N - L + 1], zero.ap, reads=[zero], writes=[dres], slow=True)
        S.barrier(); A.reset()

        cm = Tl(A.bf16(9 * 128))
        S.dma("sp", cm.ap, cmat_d, writes=[cm])
        Fre_, Fim_, nFim_ = cm.ap[:, 0:128], cm.ap[:, 128:256], cm.ap[:, 256:384]
        GS1_, GS2_ = cm.ap[:, 384:640], cm.ap[:, 640:896]
        FreN_, FimN_ = cm.ap[:, 896:1024], cm.ap[:, 1024:1152]
        tw = Tl(A.f32(1024))
        S.dma("sp", tw.ap, tw_d, writes=[tw])
        twR = tw.ap[:, 0:512].rearrange("p (c k) -> p c k", c=4); twI = tw.ap[:, 512:1024].rearrange("p (c k) -> p c k", c=4)
        fs_t = {}
        for key, d in fs_d.items():
            t = Tl(A.bf16(256))
            R = d.shape[0]
            S.dma("sp", t.ap[0:R, :], d, writes=[t])
            fs_t[key] = (t, R)
        NCH = 2
        chs = []
        for ci in range(NCH):
            chs.append({"b": ci * 4,
                        "t": [Tl(A.f32(512)) for _ in range(4)],
                        "cre": Tl(A.bf16(512)), "cim": Tl(A.bf16(512)), "pre": Tl(A.bf16(512)), "pim": Tl(A.bf16(512)),
                        "zre": Tl(A.bf16(512)), "zim": Tl(A.bf16(512)),
                        "kf": Ring([Tl(A.f32(1024)) for _ in range(4)]),
                        "af": Ring([Tl(A.f32(512)) for _ in range(2)]), "ab": Ring([Tl(A.bf16(512)) for _ in range(3)]),
                        "x1": Ring([Tl(A.f32(512)) for _ in range(2)]), "x2": Ring([Tl(A.f32(512)) for _ in range(2)]),
                        "g": Ring([Tl(A.f32(512)) for _ in range(2)]), "y1": Ring([Tl(A.bf16(512)) for _ in range(2)]),
                        "y2": Tl(A.f32(512)), "yo": Ring([Tl(A.bf16(512)) for _ in range(2)]),
                        "kfo": Ring([Tl(A.f32(1024)) for _ in range(3)])})

        def v3(ap, c=4):
            return ap.rearrange("p (c k) -> p c k", c=c)

        def cmul(ch, src_re, src_im, rr, ri, are, aim, bre, bim, conj, out_re, out_im):
            t1, t2, t3, t4 = ch["t"]
            pr = [rr, ri]
            S.op("dve", lambda e: e.tensor_tensor(out=v3(t1.ap), in0=are, in1=bre, op=ALU.mult), reads=pr + src_re, writes=[t1])
            S.op("dve", lambda e: e.tensor_tensor(out=v3(t2.ap), in0=aim, in1=bim, op=ALU.mult), reads=pr + src_im, writes=[t2])
            S.op("pool", lambda e: e.tensor_tensor(out=out_re.ap, in0=t1.ap, in1=t2.ap, op=(ALU.add if conj else ALU.subtract)), reads=[t1, t2], writes=[out_re])
            S.op("dve", lambda e: e.tensor_tensor(out=v3(t3.ap), in0=aim, in1=bre, op=ALU.mult), reads=pr + src_re, writes=[t3])
            S.op("dve", lambda e: e.tensor_tensor(out=v3(t4.ap), in0=are, in1=bim, op=ALU.mult), reads=pr + src_im, writes=[t4])
            S.op("pool", lambda e: e.tensor_tensor(out=out_im.ap, in0=t3.ap, in1=t4.ap, op=(ALU.subtract if conj else ALU.add)), reads=[t3, t4], writes=[out_im])

        def fwd_fft(ch, a_tl, a3, R, fs):
            b = ch["b"]
            ps01 = self.psum(b, 2)
            for ci in range(4):
                S.op("pe", lambda e, ci=ci: e.matmul(ps01[:, ci * 256:(ci + 1) * 256], lhsT=a3[0:R, ci, :], rhs=fs.ap[0:R, :], start=True, stop=True),
                     reads=[a_tl, fs], writes=[self.ps_res[b + ci // 2]])
            yield
            p4 = ps01.rearrange("p (c r k) -> p c r k", c=4, r=2)
            rr = self.ps_res[b]; ri = self.ps_res[b + 1]
            cmul(ch, [tw], [tw], rr, ri, p4[:, :, 0, :], p4[:, :, 1, :], twR, twI, False, ch["cre"], ch["cim"])
            yield
            for bank, l1, l2 in ((b + 2, Fre_, nFim_), (b + 3, Fim_, Fre_)):
                pt = self.psum(bank)
                S.op("pe", lambda e, pt=pt, l1=l1: e.matmul(pt, lhsT=l1, rhs=ch["cre"].ap, start=True, stop=False), reads=[cm, ch["cre"]], writes=[self.ps_res[bank]])
                S.op("pe", lambda e, pt=pt, l2=l2: e.matmul(pt, lhsT=l2, rhs=ch["cim"].ap, start=False, stop=True), reads=[cm, ch["cim"]], writes=[self.ps_res[bank]])
            yield

        def inv_fft(ch, kf_tl, M):
            b = ch["b"]
            k4 = kf_tl.ap.rearrange("p (c r k) -> p c r k", c=4, r=2)
            cmul(ch, [kf_tl], [kf_tl], self.ps_res[b + 2], self.ps_res[b + 3], v3(self.psum(b + 2)), v3(self.psum(b + 3)), k4[:, :, 0, :], k4[:, :, 1, :],
                 False, ch["pre"], ch["pim"])
            yield
            ps01 = self.psum(b, 2)
            pre3, pim3 = v3(ch["pre"].ap), v3(ch["pim"].ap)
            for ci in range(4):
                S.op("pe", lambda e, ci=ci: e.matmul(ps01[:, ci * 256:(ci + 1) * 256], lhsT=pre3[:, ci, :], rhs=GS1_, start=True, stop=False),
                     reads=[ch["pre"], cm], writes=[self.ps_res[b + ci // 2]])
                S.op("pe", lambda e, ci=ci: e.matmul(ps01[:, ci * 256:(ci + 1) * 256], lhsT=pim3[:, ci, :], rhs=GS2_, start=False, stop=True),
                     reads=[ch["pim"], cm], writes=[self.ps_res[b + ci // 2]])
            yield
            p4 = ps01.rearrange("p (c r k) -> p c r k", c=4, r=2)
            cmul(ch, [tw], [tw], self.ps_res[b], self.ps_res[b + 1], p4[:, :, 0, :], p4[:, :, 1, :], twR, twI, True, ch["zre"], ch["zim"])
            yield
            pt = self.psum(b + 2)[0:M, :]
            S.op("pe", lambda e: e.matmul(pt, lhsT=FreN_[:, 0:M], rhs=ch["zre"].ap, start=True, stop=False), reads=[cm, ch["zre"]], writes=[self.ps_res[b + 2]])
            S.op("pe", lambda e: e.matmul(pt, lhsT=FimN_[:, 0:M], rhs=ch["zim"].ap, start=False, stop=True), reads=[cm, ch["zim"]], writes=[self.ps_res[b + 2]])
            yield

        def run_chains(gens):
            act = list(gens)
            while act:
                for g in list(act):
                    try:
                        next(g)
                    except StopIteration:
                        act.remove(g)

        def kern_chain(ch, sq, groups):
            L = LEN[sq]; M = L // 128
            fs, R = fs_t["k" + sq]
            b = ch["b"]
            def kload(og):
                o, g = og
                c0 = g * 4
                af = ch["kfo"].next()
                a3f = af.ap[:, 0:512].rearrange("p (c k) -> p c k", c=4)
                src = KERN[sq][o, c0:c0 + 4, :].rearrange("c (n1 n2) -> n1 c n2", n2=128)
                S.dma("sp", a3f[0:M], src[0:M], reads=[dres], writes=[af])
                if R > M:
                    S.dma("act", a3f[M:2 * M], src[128 - M:128], reads=[dres], writes=[af])
                ab = ch["ab"].next()
                S.op("act", lambda e, af=af, ab=ab: e.activation(out=ab.ap[0:R, :], in_=af.ap[0:R, 0:512], func=AF.Copy), reads=[af], writes=[ab])
                return ab
            nxt = kload(groups[0]) if groups else None
            for gi_, (o, g) in enumerate(groups):
                c0 = g * 4
                ab = nxt
                if gi_ + 1 < len(groups):
                    nxt = kload(groups[gi_ + 1])
                yield from fwd_fft(ch, ab, v3(ab.ap), R, fs)
                ko = ch["kf"].next()
                k4 = ko.ap.rearrange("p (c r k) -> p c r k", c=4, r=2)
                S.op("act", lambda e, k4=k4: e.activation(out=k4[:, :, 0, :], in_=v3(self.psum(b + 2)), func=AF.Copy), reads=[self.ps_res[b + 2]], writes=[ko])
                S.op("act", lambda e, k4=k4: e.activation(out=k4[:, :, 1, :], in_=v3(self.psum(b + 3)), func=AF.Copy), reads=[self.ps_res[b + 3]], writes=[ko])
                S.dma("sp", KF[sq][o, :, c0:c0 + 4, :], ko.ap.rearrange("p (c x) -> p c x", c=4), reads=[ko], writes=[dres])
                yield

        for sq in ("x",):
            allg = [(o, g) for o in range(2) for g in range(256)]
            run_chains([kern_chain(chs[i], sq, allg[i::NCH]) for i in range(NCH)])
        S.barrier()

        def conv_chain(ch, sq, groups):
            L = LEN[sq]; M = L // 128
            fs, R = fs_t["s" + sq]
            b = ch["b"]
            Zv = Z[sq].rearrange("c (n1 n2) -> n1 c n2", n2=128)
            Uv = U[sq].rearrange("c (n1 n2) -> n1 c n2", n2=128)
            Yv = Y[sq].rearrange("c (n1 n2) -> n1 c n2", n2=128)
            def cload(g):
                c0 = g * 4
                af = ch["af"].next(); x1 = ch["x1"].next(); x2 = ch["x2"].next(); gg = ch["g"].next(); ab = ch["ab"].next()
                S.dma("sp", v3(af.ap)[0:M], Zv[:, c0:c0 + 4, :], reads=[dres], writes=[af])
                S.dma("act", v3(x1.ap)[0:M], Zv[:, 1024 + c0:1024 + c0 + 4, :], reads=[dres], writes=[x1])
                S.dma("sp", v3(x2.ap)[0:M], Zv[:, 2048 + c0:2048 + c0 + 4, :], reads=[dres], writes=[x2])
                S.dma("act", v3(gg.ap)[0:M], Uv[:, 3072 + c0:3072 + c0 + 4, :], reads=[Ures[sq]], writes=[gg])
                S.op("act", lambda e, af=af, ab=ab: e.activation(out=ab.ap[0:M, :], in_=af.ap[0:M, :], func=AF.Copy), reads=[af], writes=[ab])
                S.op("act", lambda e, gg=gg: e.activation(out=gg.ap[0:M, :], in_=gg.ap[0:M, :], func=AF.Silu), reads=[gg], writes=[gg])
                kf0 = ch["kf"].next()
                S.dma("sp", kf0.ap.rearrange("p (c x) -> p c x", c=4), KF[sq][0, :, c0:c0 + 4, :], reads=[dres], writes=[kf0])
                kf1 = ch["kf"].next()
                S.dma("act", kf1.ap.rearrange("p (c x) -> p c x", c=4), KF[sq][1, :, c0:c0 + 4, :], reads=[dres], writes=[kf1])
                return x1, x2, gg, ab, kf0, kf1
            nxt = cload(groups[0]) if groups else None
            for gi_, g in enumerate(groups):
                c0 = g * 4
                x1, x2, gg, ab, kf0, kf1 = nxt
                if gi_ + 1 < len(groups):
                    nxt = cload(groups[gi_ + 1])
                yield from fwd_fft(ch, ab, v3(ab.ap), M, fs)
                yield from inv_fft(ch, kf0, M)
                y1 = ch["y1"].next()
                S.op("dve", lambda e, y1=y1, x1=x1: e.tensor_tensor(out=y1.ap[0:M, :], in0=self.psum(b + 2)[0:M, :], in1=x1.ap[0:M, :], op=ALU.mult),
                     reads=[self.ps_res[b + 2], x1], writes=[y1])
                yield
                yield from fwd_fft(ch, y1, v3(y1.ap), M, fs)
                yield from inv_fft(ch, kf1, M)
                y2 = ch["y2"]; yo = ch["yo"].next()
                S.op("dve", lambda e, y2=y2, x2=x2: e.tensor_tensor(out=y2.ap[0:M, :], in0=self.psum(b + 2)[0:M, :], in1=x2.ap[0:M, :], op=ALU.mult),
                     reads=[self.ps_res[b + 2], x2], writes=[y2])
                S.op("pool", lambda e, y2=y2, yo=yo, gg=gg: e.tensor_tensor(out=yo.ap[0:M, :], in0=y2.ap[0:M, :], in1=gg.ap[0:M, :], op=ALU.mult),
                     reads=[y2, gg], writes=[yo])
                S.dma("sp", Yv[:, c0:c0 + 4, :], v3(yo.ap)[0:M], reads=[yo], writes=[self.Y_res])
                yield

        for sq in ("x",):
            allg = list(range(256))
            run_chains([conv_chain(chs[i], sq, allg[i::NCH]) for i in range(NCH)])
        S.barrier(); A.reset()

        if self.need_ctx:
            Lc = TC; NH = Lc // 128
            identf = Tl(A.f32(128))
            S.dma("sp", identf.ap, identf_d, writes=[identf])
            tm = [Tl(A.f32(NH * 1024)) for _ in range(4)]
            t3 = lambda ap: ap.rearrange("p (h c) -> p h c", h=NH)
            ld = Ring([Tl(A.f32(Lc)) for _ in range(3)])
            tcnt = 0
            for n in range(32):
                src = Z["c"][n * 128:(n + 1) * 128, :] if n < 24 else U["c"][3072 + (n - 24) * 128:3072 + (n - 23) * 128, :]
                dstT = tm[n // 8]; cc = n % 8
                lt = ld.next()
                S.dma("sp" if n % 2 == 0 else "act", lt.ap, src, reads=[dres, Ures["c"]], writes=[lt])
                for h in range(NH):
                    bank = tcnt % 2
                    tcnt += 1
                    pt = self.psum(bank)[:, 0:128]
                    S.op("pe", lambda e, pt=pt, lt=lt, h=h: e.transpose(out=pt, in_=lt.ap[:, h * 128:(h + 1) * 128], identity=identf.ap),
                         reads=[lt, identf], writes=[self.ps_res[bank]])
                    if tcnt % 2 == 0:
                        S.op("act", lambda e, pt=pt, dstT=dstT, h=h, cc=cc: e.activation(out=t3(dstT.ap)[:, h, cc * 128:(cc + 1) * 128], in_=pt, func=AF.Copy),
                             reads=[self.ps_res[bank]], writes=[dstT])
                    else:
                        S.op("dve", lambda e, pt=pt, dstT=dstT, h=h, cc=cc: e.tensor_copy(out=t3(dstT.ap)[:, h, cc * 128:(cc + 1) * 128], in_=pt),
                             reads=[self.ps_res[bank]], writes=[dstT])
            vT, x1T, x2T, gT = tm
            S.op("act", lambda e: e.activation(out=gT.ap, in_=gT.ap, func=AF.Silu), reads=[gT], writes=[gT])
            nb = Tl(A.bf16(NH * 1024))
            ub = Tl(A.bf16(NH * 1024))
            ub3 = t3(ub.ap); nb3 = t3(nb.ap)
            jm = Tl(A.bf16(128))
            S.dma("sp", jm.ap, jmat_d, writes=[jm])
            S.op("act", lambda e: e.activation(out=nb.ap, in_=vT.ap, func=AF.Copy), reads=[vT], writes=[nb])

            def reverse_blocks():
                for sh in range(NH):
                    for hf in range(2):
                        bank = 2 * sh + hf
                        pt = self.psum(bank)
                        S.op("pe", lambda e, pt=pt, sh=sh, hf=hf: e.matmul(pt, lhsT=jm.ap, rhs=nb3[:, sh, hf * 512:(hf + 1) * 512], start=True, stop=True),
                             reads=[jm, nb], writes=[self.ps_res[bank]])
                        S.op("act", lambda e, pt=pt, sh=sh, hf=hf: e.activation(out=ub3[:, sh, hf * 512:(hf + 1) * 512], in_=pt, func=AF.Copy),
                             reads=[self.ps_res[bank]], writes=[ub])
            reverse_blocks()
            CG = 8
            trf = Ring([Tl(A.f32(CG * Lc)) for _ in range(4)])
            trb = Ring([Tl(A.bf16(CG * Lc)) for _ in range(6)])
            y2 = Tl(A.f32(1024)); yb = Tl(A.bf16(NH * 1024)); ycT = Tl(A.bf16(8 * Lc))
            ycT3 = ycT.ap.rearrange("p (k t) -> p k t", k=8)
            cvt = 0
            for o in range(2):
                for cg in range(1024 // CG):
                    tbs = []
                    for sh in range(NH):
                        tf = trf.next(); tb = trb.next()
                        off = (o * 1024 + cg * CG) * 2 * Lc + (Lc - 128) - sh * 128
                        src = bass.AP(KC.tensor, off, [[1, 128], [2 * Lc, CG], [1, Lc]])
                        S.dma("sp" if sh == 0 else "act", tf.ap.rearrange("p (c t) -> p c t", c=CG), src, reads=[dres], writes=[tf])
                        if cvt % 2 == 0:
                            S.op("act", lambda e, tf=tf, tb=tb: e.activation(out=tb.ap, in_=tf.ap, func=AF.Copy), reads=[tf], writes=[tb])
                        else:
                            S.op("dve", lambda e, tf=tf, tb=tb: e.tensor_copy(out=tb.ap, in_=tf.ap), reads=[tf], writes=[tb])
                        cvt += 1
                        tbs.append(tb)
                    for ci in range(CG):
                        c = cg * CG + ci
                        for th in range(NH):
                            bank = 4 + 2 * th + c // 512
                            pcol = self.psum(bank)[:, (c % 512):(c % 512) + 1]
                            for sh in range(NH):
                                tb = tbs[sh]
                                tb3 = tb.ap.rearrange("p (c t) -> p c t", c=CG)
                                S.op("pe", lambda e, pcol=pcol, tb3=tb3, ci=ci, th=th, sh=sh, c=c: e.matmul(pcol, lhsT=tb3[:, ci, th * 128:(th + 1) * 128], rhs=ub3[:, sh, c:c + 1],
                                                                                                  start=(sh == 0), stop=(sh == NH - 1)),
                                     reads=[tb, ub], writes=[self.ps_res[bank]])
                for th in range(NH):
                    pr = [self.ps_res[4 + 2 * th], self.ps_res[5 + 2 * th]]
                    pth = self.psum(4 + 2 * th, 2)
                    if o == 0:
                        S.op("dve", lambda e, pth=pth, th=th: e.tensor_tensor(out=nb3[:, th, :], in0=pth, in1=t3(x1T.ap)[:, th, :], op=ALU.mult),
                             reads=pr + [x1T], writes=[nb])
                        if th == NH - 1:
                            reverse_blocks()
                    else:
                        S.op("dve", lambda e, pth=pth, th=th: e.tensor_tensor(out=y2.ap, in0=pth, in1=t3(x2T.ap)[:, th, :], op=ALU.mult), reads=pr + [x2T], writes=[y2])
                        S.op("pool", lambda e, th=th: e.tensor_tensor(out=t3(yb.ap)[:, th, :], in0=y2.ap, in1=t3(gT.ap)[:, th, :], op=ALU.mult), reads=[y2, gT], writes=[yb])
                        pb = self.psum(th, 1, BF16)
                        for cc in range(8):
                            S.op("pe", lambda e, pb=pb, th=th, cc=cc: e.transpose(out=pb[:, cc * 128:(cc + 1) * 128], in_=t3(yb.ap)[:, th, cc * 128:(cc + 1) * 128], identity=self.ident.ap),
                                 reads=[yb, self.ident], writes=[self.ps_res[th]])
                        S.op("act", lambda e, pb=pb, th=th: e.activation(out=ycT3[:, :, th * 128:(th + 1) * 128], in_=pb.rearrange("p (k t) -> p k t", k=8), func=AF.Copy),
                             reads=[self.ps_res[th]], writes=[ycT])
            S.dma("sp", Y["c"].rearrange("(k p) t -> p k t", p=128), ycT3, reads=[ycT], writes=[self.Y_res])
            S.barrier(); A.reset()

        wo = Tl(A.bf16(8 * 1024))
        wo3 = wo.ap.rearrange("p (k n) -> p k n", k=8)
        stage = Ring([Tl(A.f32(2048)) for _ in range(3)])
        self.load_w_bf16(W["hy_w_out"], 1024, 1024, wo3, wo, stage_ring=stage)
        self.out_proj(Y["x"], 1024, wo3, wo, T, self.gt_bc, x_src, self.xs, self.x_res)
        if self.need_ctx:
            self.out_proj(Y["c"], 1024, wo3, wo, TC, self.gtc_bc, c_src, self.cs, self.c_res)

    def mix_att(self, x_src, c_src):
        S, A, W, T, TC = self.S, self.A, self.W, self.T, self.TC
        f32 = np.float32
        inv = (f32(10000.0) ** (-np.arange(32, dtype=f32) / f32(32))).astype(f32)
        rows = np.repeat(np.arange(T // GRID_W, dtype=f32), GRID_W); cols = np.tile(np.arange(GRID_W, dtype=f32), T // GRID_W)
        ang = np.concatenate([(rows[:, None] * inv[None, :]).astype(f32), (cols[:, None] * inv[None, :]).astype(f32)], axis=-1).astype(np.float64)
        cs_, sn_ = np.cos(ang).astype(f32).T, np.sin(ang).astype(f32).T
        tabs_x = (self.const("c_att_cos", np.concatenate([cs_, cs_], axis=0)), self.const("c_att_sin", np.concatenate([-sn_, sn_], axis=0)))
        tabs_c = (self.const("c_att_cosc", np.ones((128, TC), f32)), self.const("c_att_sinc", np.zeros((128, TC), f32)))
        QT = self.scratch("att_QT", [1024, T], BF16); GT = self.scratch("att_GT", [1024, T], BF16)
        Y = self.scratch("att_Y", [1024, T], BF16)
        dres = FreeRes()
        self.Y_res = FreeRes()
        NK = T + TC
        nkt = NK // 128
        KT = [Tl(A.bf16(NK)) for _ in range(2)]
        V = Tl(A.bf16(nkt * 256))
        V3 = V.ap.rearrange("p (k d) -> p k d", d=256)
        gq = Tl(A.f32(2))
        S.dma("sp", gq.ap[:, 0:1], W["att_q_norm_g"].rearrange("(p o) -> p o", o=1), writes=[gq])
        S.dma("sp", gq.ap[:, 1:2], W["att_k_norm_g"].rearrange("(p o) -> p o", o=1), writes=[gq])
        base_mark = A.top
        w_tl = Tl(A.bf16(8 * 2560))
        w3 = w_tl.ap.rearrange("p (k n) -> p k n", k=8)
        wmark = A.top
        stage = Ring([Tl(A.f32(2048)) for _ in range(2)])
        self.load_w_bf16(W["att_w_in"], 1024, 2560, w3, w_tl, stage_ring=stage)
        S.barrier(); A.top = wmark

        def att_proj(src, src_res, Tn, which, tabs, latent, key0):
            TT = min(512, Tn)
            xring, hring = self.make_norm_rings(TT, nh=2, nx=2)
            tab = Ring([{"c": Tl(A.f32(TT)), "s": Tl(A.f32(TT))} for _ in range(2)])
            tmp = Ring([{"sq": Tl(A.bf16(TT)), "rs": Tl(A.f32(TT)), "qn": Tl(A.f32(TT)), "sw": Tl(A.f32(TT)), "t1": Tl(A.f32(TT)), "t2": Tl(A.f32(TT))}
                        for _ in range(2)])
            stq = Ring([Tl(A.bf16(TT)) for _ in range(3)])
            cnt = [0, 0]
            for tt, hT, h3 in self.norm_tiles(src, src_res, Tn, which, xring, hring, None, TT):
                cols = slice(tt * TT, (tt + 1) * TT)
                tb = tab.next()
                S.dma("act", tb["c"].ap, tabs[0][:, cols], writes=[tb["c"]])
                S.dma("act", tb["s"].ap, tabs[1][:, cols], writes=[tb["s"]])
                heads = ([("q", i) for i in range(8)] if latent else []) + [("k", i) for i in range(2)]
                for kind, idx in heads:
                    oc = idx if kind == "q" else 8 + idx
                    gi = 0 if kind == "q" else 1
                    b = cnt[0] % 4
                    cnt[0] += 1
                    pt = self.psum(b)[:, 0:TT]
                    for kk in range(8):
                        S.op("pe", lambda e, kk=kk, oc=oc, pt=pt, h3=h3: e.matmul(pt, lhsT=w3[:, kk, oc * 128:(oc + 1) * 128], rhs=h3[:, kk, :],
                                                                              start=(kk == 0), stop=(kk == 7)), reads=[hT, w_tl], writes=[self.ps_res[b]])
                    tm = tmp.next()
                    sq, rs, qn, sw, t1, t2 = tm["sq"], tm["rs"], tm["qn"], tm["sw"], tm["t1"], tm["t2"]
                    S.op("act", lambda e, pt=pt, sq=sq: e.activation(out=sq.ap, in_=pt, func=AF.Square), reads=[self.ps_res[b]], writes=[sq])
                    b2 = 4 + cnt[1] % 2
                    cnt[1] += 1
                    p2 = self.psum(b2)[:, 0:TT]
                    S.op("pe", lambda e, p2=p2, sq=sq: e.matmul(p2, lhsT=self.ones_bf.ap, rhs=sq.ap, start=True, stop=True), reads=[sq, self.ones_bf], writes=[self.ps_res[b2]])
                    S.op("act", lambda e, p2=p2, rs=rs: e.activation(out=rs.ap, in_=p2, func=AF.Sqrt, scale=1.0 / 128.0, bias=self.eps_col.ap), reads=[self.ps_res[b2], self.eps_col], writes=[rs])
                    S.op("dve", lambda e, rs=rs: e.reciprocal(out=rs.ap, in_=rs.ap), reads=[rs], writes=[rs])
                    S.op("dve", lambda e, pt=pt, qn=qn, rs=rs, gi=gi: e.scalar_tensor_tensor(out=qn.ap, in0=pt, scalar=gq.ap[:, gi:gi + 1], in1=rs.ap, op0=ALU.mult, op1=ALU.mult),
                         reads=[self.ps_res[b], gq, rs], writes=[qn])
                    S.op("act", lambda e, qn=qn, sw=sw: e.activation(out=sw.ap[0:64, :], in_=qn.ap[64:128, :], func=AF.Copy), reads=[qn], writes=[sw])
                    S.op("pool", lambda e, qn=qn, sw=sw: e.tensor_copy(out=sw.ap[64:128, :], in_=qn.ap[0:64, :]), reads=[qn], writes=[sw])
                    S.op("pool", lambda e, qn=qn, t1=t1, tb=tb: e.tensor_tensor(out=t1.ap, in0=qn.ap, in1=tb["c"].ap, op=ALU.mult), reads=[qn, tb["c"]], writes=[t1])
                    S.op("dve", lambda e, sw=sw, t2=t2, tb=tb: e.tensor_tensor(out=t2.ap, in0=sw.ap, in1=tb["s"].ap, op=ALU.mult), reads=[sw, tb["s"]], writes=[t2])
                    if kind == "q":
                        st = stq.next()
                        S.op("pool", lambda e, t1=t1, t2=t2, st=st: e.tensor_tensor(out=st.ap, in0=t1.ap, in1=t2.ap, op=ALU.add), reads=[t1, t2], writes=[st])
                        S.dma("sp", QT[idx * 128:(idx + 1) * 128, cols], st.ap, reads=[st], writes=[dres])
                    else:
                        kdst = KT[idx].ap[:, key0 + tt * TT:key0 + (tt + 1) * TT]
                        S.op("pool", lambda e, t1=t1, t2=t2, kdst=kdst: e.tensor_tensor(out=kdst, in0=t1.ap, in1=t2.ap, op=ALU.add), reads=[t1, t2], writes=[KT[idx]])
                if latent:
                    for j in range(8):
                        b = cnt[0] % 4
                        cnt[0] += 1
                        pt = self.psum(b)[:, 0:TT]
                        for kk in range(8):
                            S.op("pe", lambda e, kk=kk, j=j, pt=pt, h3=h3: e.matmul(pt, lhsT=w3[:, kk, 1536 + j * 128:1536 + (j + 1) * 128], rhs=h3[:, kk, :],
                                                                                 start=(kk == 0), stop=(kk == 7)), reads=[hT, w_tl], writes=[self.ps_res[b]])
                        st = stq.next()
                        S.op("act", lambda e, pt=pt, st=st: e.activation(out=st.ap, in_=pt, func=AF.Silu), reads=[self.ps_res[b]], writes=[st])
                        S.dma("sp", GT[j * 128:(j + 1) * 128, cols], st.ap, reads=[st], writes=[dres])
                for s_ in range(TT // 128):
                    kti = (key0 + tt * TT) // 128 + s_
                    b2 = 6 + cnt[1] % 2
                    cnt[1] += 1
                    pv = self.psum(b2)[:, 0:256]
                    for kk in range(8):
                        S.op("pe", lambda e, kk=kk, pv=pv, h3=h3, s_=s_: e.matmul(pv, lhsT=h3[:, kk, s_ * 128:(s_ + 1) * 128], rhs=w3[:, kk, 1280:1536],
                                                                              start=(kk == 0), stop=(kk == 7)), reads=[hT, w_tl], writes=[self.ps_res[b2]])
                    S.op("act", lambda e, pv=pv, kti=kti: e.activation(out=V3[:, kti, :], in_=pv, func=AF.Copy), reads=[self.ps_res[b2]], writes=[V])

        att_proj(c_src, self.c_res, TC, 1, tabs_c, False, T)
        S.barrier(); A.top = wmark
        att_proj(x_src, self.x_res, T, 0, tabs_x, True, 0)
        S.barrier(); A.top = base_mark

        scale = 1.0 / math.sqrt(128.0)
        qr = Ring([{"q": Tl(A.bf16(1024)), "g": Tl(A.bf16(1024)), "y": Tl(A.bf16(1024))} for _ in range(2)])
        pTr = Ring([Tl(A.bf16(512)) for _ in range(7)])
        accs = Ring([{"d": Tl(A.f32(512)), "p": Tl(A.f32(512)), "rs": Tl(A.f32(512)), "o": Tl(A.f32(512))} for _ in range(2)])
        QTv = QT.rearrange("(h p) t -> p h t", p=128); GTv = GT.rearrange("(h p) t -> p h t", p=128); Yv = Y.rearrange("(h p) t -> p h t", p=128)
        gcnt = 0
        for qt in range(T // 128):
            cs = slice(qt * 128, (qt + 1) * 128)
            qb = qr.next()
            q, gt_, y = qb["q"], qb["g"], qb["y"]
            q3 = q.ap.rearrange("p (h t) -> p h t", h=8); g3 = gt_.ap.rearrange("p (h t) -> p h t", h=8)
            S.dma("sp", q3, QTv[:, :, cs], reads=[dres], writes=[q])
            S.dma("act", g3, GTv[:, :, cs], reads=[dres], writes=[gt_])
            for g in range(2):
                q2 = q.ap[:, g * 512:(g + 1) * 512]
                bO = 4 + gcnt % 2
                bSum = 6 + gcnt % 2
                gcnt += 1
                pO = self.psum(bO)
                ac = accs.next()
                LA = 3
                pTs = {}
                for kk_ in range(nkt + LA):
                    if kk_ < nkt:
                        kt = kk_
                        bS = kt % 4
                        pS = self.psum(bS)
                        S.op("pe", lambda e, pS=pS, g=g, kt=kt, q2=q2: e.matmul(pS, lhsT=KT[g].ap[:, kt * 128:(kt + 1) * 128], rhs=q2, start=True, stop=True),
                             reads=[KT[g], q], writes=[self.ps_res[bS]])
                        pT = pTr.next()
                        pTs[kt] = pT
                        S.op("act", lambda e, pS=pS, pT=pT: e.activation(out=pT.ap, in_=pS, func=AF.Exp, scale=scale), reads=[self.ps_res[bS]], writes=[pT])
                    kt = kk_ - LA
                    if kt >= 0:
                        pT = pTs.pop(kt)
                        S.op("pe", lambda e, pO=pO, g=g, kt=kt, pT=pT: e.matmul(pO, lhsT=V3[:, kt, g * 128:(g + 1) * 128], rhs=pT.ap, start=(kt == 0), stop=(kt == nkt - 1)),
                             reads=[V, pT], writes=[self.ps_res[bO]])
                        eng, at = ("dve", ac["d"]) if kt % 2 == 0 else ("pool", ac["p"])
                        if kt < 2:
                            S.op(eng, lambda e, at=at, pT=pT: e.tensor_copy(out=at.ap, in_=pT.ap), reads=[pT], writes=[at])
                        else:
                            S.op(eng, lambda e, at=at, pT=pT: e.tensor_tensor(out=at.ap, in0=at.ap, in1=pT.ap, op=ALU.add), reads=[at, pT], writes=[at])
                S.op("dve", lambda e, ac=ac: e.tensor_tensor(out=ac["d"].ap, in0=ac["d"].ap, in1=ac["p"].ap, op=ALU.add), reads=[ac["d"], ac["p"]], writes=[ac["d"]])
                pSum = self.psum(bSum)
                S.op("pe", lambda e, pSum=pSum, ac=ac: e.matmul(pSum, lhsT=self.ones_row.ap, rhs=ac["d"].ap, start=True, stop=True),
                     reads=[self.ones_row, ac["d"]], writes=[self.ps_res[bSum]])
                S.op("dve", lambda e, pSum=pSum, ac=ac: e.reciprocal(out=ac["rs"].ap, in_=pSum), reads=[self.ps_res[bSum]], writes=[ac["rs"]])
                S.op("dve", lambda e, pO=pO, ac=ac: e.tensor_tensor(out=ac["o"].ap, in0=pO, in1=ac["rs"].ap, op=ALU.mult), reads=[self.ps_res[bO], ac["rs"]], writes=[ac["o"]])
                S.op("pool", lambda e, ac=ac, y=y, gt_=gt_, g=g: e.tensor_tensor(out=y.ap[:, g * 512:(g + 1) * 512], in0=ac["o"].ap, in1=gt_.ap[:, g * 512:(g + 1) * 512], op=ALU.mult),
                     reads=[ac["o"], gt_], writes=[y])
            S.dma("sp", Yv[:, :, cs], y.ap.rearrange("p (h t) -> p h t", h=8), reads=[y], writes=[self.Y_res])
        S.barrier(); A.top = base_mark
        wo = Tl(A.bf16(8 * 1024))
        wo3 = wo.ap.rearrange("p (k n) -> p k n", k=8)
        stage = Ring([Tl(A.f32(2048)) for _ in range(3)])
        self.load_w_bf16(W["att_w_out"], 1024, 1024, wo3, wo, stage_ring=stage)
        self.out_proj(Y, 1024, wo3, wo, T, self.gt_bc, x_src, self.xs, self.x_res)


_CACHE = {}


def get_program(T, TC, nlayers=4, dbg=False):
    key = (T, TC, nlayers, dbg)
    if key not in _CACHE:
        b = Builder(T, TC, nlayers, dbg)
        nc = b.build()
        _CACHE[key] = (nc, b)
    return _CACHE[key]


def kernel(**inputs):
    x = np.asarray(inputs["x"], dtype=np.float32)
    B, T, _ = x.shape
    TC = inputs["ctx"].shape[1]
    nc, b = get_program(T, TC)
    in_maps = []
    for i in range(B):
        m = {}
        for name in b.ins:
            if name in b.host_consts:
                m[name] = b.host_consts[name]
            elif name == "x":
                m[name] = np.ascontiguousarray(x[i])
            elif name == "c":
                m[name] = np.ascontiguousarray(np.asarray(inputs["c"], dtype=np.float32)[i])
            elif name == "ctx":
                m[name] = np.ascontiguousarray(np.asarray(inputs["ctx"], dtype=np.float32)[i])
            else:
                m[name] = np.ascontiguousarray(np.asarray(inputs[name], dtype=np.float32))
        in_maps.append(m)
    res = run_bass_kernel_spmd(nc, in_maps, core_ids=list(range(B)))
    return np.stack([np.asarray(r["out"]) for r in res.results], axis=0).astype(np.float32)
```
